# Optimizing a Trainium2 kernel written in Bass

```python
import math
import jax, jax.numpy as jnp
from jax import lax
import numpy as np

D_MODEL = 1024
BATCH = 2
SEQ = 8192
DEPTH = 4

HEAD_DIM = 64
N_HEADS_PER_MIXER = 4
GROUP_WIDTH = N_HEADS_PER_MIXER * HEAD_DIM
N_MIXERS = 4
MIX_WIDTH = N_MIXERS * GROUP_WIDTH
DIFF_QK_DIM = HEAD_DIM // 2
Q_BLOCK = 128
SGU_CHUNK = 128
DILATED_PATTERNS = ((128, 1), (512, 4), (2048, 16))
DIL_BLOCK = 128
MLSTM_CHUNK = 128
CONV_WIDTH = 4
ROPE_THETA = 500000.0
ROT_FRACTION = 4
D_FF = -(-8 * D_MODEL // (3 * 256)) * 256
ALPHA = (2 * DEPTH) ** 0.25
BETA = (8 * DEPTH) ** -0.25
LN_EPS = 1e-5
SPLIT_SIZES = (GROUP_WIDTH,) * 3 + (GROUP_WIDTH,) * 2 + (GROUP_WIDTH,) * 3 + (GROUP_WIDTH,) * 4 + (N_HEADS_PER_MIXER, N_HEADS_PER_MIXER)
IN_WIDTH = sum(SPLIT_SIZES)
SPLIT_IDX = tuple(int(v) for v in np.cumsum(SPLIT_SIZES)[:-1])

kernel_name = "hymba_style_diff_sgu_dilated_mlstm"


def layer_norm(x, g, b):
    xf = x.astype(jnp.float32)
    mu = xf.mean(-1, keepdims=True)
    var = jnp.square(xf - mu).mean(-1, keepdims=True)
    return ((xf - mu) * lax.rsqrt(var + LN_EPS) * g + b).astype(x.dtype)


def rms_norm(x, g):
    xf = x.astype(jnp.float32)
    return (xf * lax.rsqrt(jnp.square(xf).mean(-1, keepdims=True) + LN_EPS) * g).astype(x.dtype)


def rope_tables(seq, head_dim):
    rot = head_dim // ROT_FRACTION
    pos = jnp.arange(seq, dtype=jnp.float32)
    inv = ROPE_THETA ** (-jnp.arange(0, rot, 2, dtype=jnp.float32) / rot)
    ang = pos[:, None] * inv[None, :]
    return jnp.cos(ang), jnp.sin(ang)


def partial_rope(x, cos, sin):
    half = cos.shape[-1]
    x1, x2, xp = x[..., :half], x[..., half:2 * half], x[..., 2 * half:]
    c = cos[None, :, None, :].astype(x.dtype)
    s = sin[None, :, None, :].astype(x.dtype)
    return jnp.concatenate([x1 * c - x2 * s, x2 * c + x1 * s, xp], axis=-1)


def diff_attention(q, k, v, lam):
    B, S, H, _, Dk = q.shape
    nq = S // Q_BLOCK
    scale = Dk ** -0.5
    qb = jnp.moveaxis(q.reshape(B, nq, Q_BLOCK, H, 2, Dk), 1, 0)
    kpos = jnp.arange(S)

    def block(args):
        qi, idx = args
        s = jnp.einsum('bqhmd,bkhmd->bhmqk', qi, k).astype(jnp.float32) * scale
        qpos = idx * Q_BLOCK + jnp.arange(Q_BLOCK)
        s = jnp.where((kpos[None, :] <= qpos[:, None])[None, None, None], s, -jnp.inf)
        p = jax.nn.softmax(s, axis=-1)
        a = p[:, :, 0] - lam * p[:, :, 1]
        return jnp.einsum('bhqk,bkhd->bqhd', a.astype(v.dtype), v)

    o = lax.map(block, (qb, jnp.arange(nq)))
    return jnp.moveaxis(o, 0, 1).reshape(B, S, H, v.shape[-1])


def spatial_gating(u, v, ln_g, ln_b, w_s, b_s):
    B, S, W = v.shape
    G, C, _ = w_s.shape
    nc = S // C
    u = jax.nn.gelu(u)
    v = layer_norm(jax.nn.gelu(v), ln_g, ln_b)
    vc = v.reshape(B, nc, C, G, W // G)
    w = w_s * jnp.tril(jnp.ones((C, C), w_s.dtype))[None]
    z = jnp.einsum('gts,bnsgd->bntgd', w, vc) + jnp.transpose(b_s)[None, None, :, :, None]
    return u * z.reshape(B, S, W)


def dilated_pattern(q, k, v, window, dilation):
    B, S, H, D = q.shape
    I = DIL_BLOCK
    span = dilation * I
    s_pad = -(-S // span) * span
    nb = s_pad // span

    def to_blocks(t):
        t = jnp.pad(t, ((0, 0), (0, s_pad - S), (0, 0), (0, 0)))
        return t.reshape(B, nb, I, dilation, H, D)

    def with_prev(t):
        prev = jnp.pad(t[:, :-1], ((0, 0), (1, 0), (0, 0), (0, 0), (0, 0), (0, 0)))
        return jnp.concatenate([prev, t], axis=2)

    qb = to_blocks(q)
    kk, vv = with_prev(to_blocks(k)), with_prev(to_blocks(v))
    s = jnp.einsum('bnirhd,bnjrhd->bnrhij', qb, kk).astype(jnp.float32)
    i_idx = jnp.arange(I)[:, None]
    j_idx = jnp.arange(2 * I)[None, :]
    dist = I + i_idx - j_idx
    band = (dist >= 0) & (dist <= window // dilation)
    start_ok = (jnp.arange(nb)[:, None, None] * I + j_idx[None] - I) >= 0
    valid = band[None] & start_ok
    s = jnp.where(valid[None, :, None, None], s, -jnp.inf)
    m = s.max(-1, keepdims=True)
    p = jnp.exp(s - m)
    den = p.sum(-1, keepdims=True)
    o = jnp.einsum('bnrhij,bnjrhd->bnirhd', (p / den).astype(v.dtype), vv)
    lse = (m + jnp.log(den))[..., 0]
    o = o.reshape(B, s_pad, H, D)[:, :S]
    lse = jnp.transpose(lse, (0, 1, 4, 2, 3)).reshape(B, s_pad, H)[:, :S]
    return o, lse


def dilated_mixture(q, k, v):
    outs, lses = [], []
    for window, dilation in DILATED_PATTERNS:
        o, l = dilated_pattern(q, k, v, window, dilation)
        outs.append(o)
        lses.append(l)
    wts = jax.nn.softmax(jnp.stack(lses, 0), axis=0)
    return jnp.einsum('pbsh,pbshd->bshd', wts.astype(q.dtype), jnp.stack(outs, 0))


def causal_depthwise_conv(x, w, b):
    K, C = w.shape
    y = lax.conv_general_dilated(x, w[:, None, :], window_strides=(1,), padding=[(K - 1, 0)],
                                 dimension_numbers=('NWC', 'WIO', 'NWC'), feature_group_count=C)
    return y + b


def mlstm_chunkwise(q, k, v, i_pre, f_pre):
    B, S, H, D = q.shape
    L = MLSTM_CHUNK
    nc = S // L
    f32 = jnp.float32
    qc = q.astype(f32).reshape(B, nc, L, H, D)
    kc = (k.astype(f32) * D ** -0.5).reshape(B, nc, L, H, D)
    vc = v.astype(f32).reshape(B, nc, L, H, D)
    ic = i_pre.astype(f32).reshape(B, nc, L, H)
    b = jnp.cumsum(jax.nn.log_sigmoid(f_pre.astype(f32)).reshape(B, nc, L, H), axis=2)
    b_last = b[:, :, -1]
    causal = jnp.tril(jnp.ones((L, L), bool))
    d_log = jnp.where(causal[None, None, :, :, None],
                      b[:, :, :, None] - b[:, :, None] + ic[:, :, None], -jnp.inf)
    g = b_last[:, :, None] - b + ic
    g_max = g.max(axis=2)
    w = jnp.exp(g - g_max[:, :, None])
    c_loc = jnp.einsum('bnsh,bnshd,bnshe->bnhde', w, vc, kc)
    n_loc = jnp.einsum('bnsh,bnshe->bnhe', w, kc)

    def step(carry, xs):
        c, n, m = carry
        cl, nl, gm, bl = xs
        m_new = jnp.maximum(bl + m, gm)
        a = jnp.exp(bl + m - m_new)
        e = jnp.exp(gm - m_new)
        return (a[..., None, None] * c + e[..., None, None] * cl,
                a[..., None] * n + e[..., None] * nl, m_new), (c, n, m)

    init = (jnp.zeros((B, H, D, D), f32), jnp.zeros((B, H, D), f32), jnp.zeros((B, H), f32))
    sw_ax = lambda t: jnp.moveaxis(t, 1, 0)
    _, (c_prev, n_prev, m_prev) = lax.scan(step, init, (sw_ax(c_loc), sw_ax(n_loc), sw_ax(g_max), sw_ax(b_last)))
    c_prev, n_prev, m_prev = sw_ax(c_prev), sw_ax(n_prev), sw_ax(m_prev)
    inter_log = b + m_prev[:, :, None]
    m_out = jnp.maximum(inter_log, d_log.max(axis=3))
    p = jnp.exp(d_log - m_out[:, :, :, None])
    sw = p * jnp.einsum('bnthd,bnshd->bntsh', qc, kc)
    e = jnp.exp(inter_log - m_out)
    num = jnp.einsum('bntsh,bnshd->bnthd', sw, vc) + e[..., None] * jnp.einsum('bnhde,bnthe->bnthd', c_prev, qc)
    den = sw.sum(axis=3) + e * jnp.einsum('bnhe,bnthe->bnth', n_prev, qc)
    h = num / jnp.maximum(jnp.abs(den), jnp.exp(-m_out))[..., None]
    return h.reshape(B, S, H, D)


def mixer_block(x, w_in, lam_vecs, lam_init, subln_g, sgu_ln_g, sgu_ln_b, sgu_w, sgu_b,
                conv_w, conv_b, gate_b, mnorm_g, w_out, rope_a, rope_c):
    B, S, _ = x.shape
    H = N_HEADS_PER_MIXER
    z = jnp.einsum('bsd,df->bsf', x, w_in)
    (a_q, a_k, a_v, b_u, b_v, c_q, c_k, c_v,
     d_q, d_k, d_v, d_o, d_i, d_f) = jnp.split(z, SPLIT_IDX, axis=-1)

    aq = partial_rope(a_q.reshape(B, S, 2 * H, DIFF_QK_DIM), *rope_a).reshape(B, S, H, 2, DIFF_QK_DIM)
    ak = partial_rope(a_k.reshape(B, S, 2 * H, DIFF_QK_DIM), *rope_a).reshape(B, S, H, 2, DIFF_QK_DIM)
    lv = lam_vecs.astype(jnp.float32)
    lam = jnp.exp(jnp.sum(lv[0] * lv[1])) - jnp.exp(jnp.sum(lv[2] * lv[3])) + lam_init
    out_a = diff_attention(aq, ak, a_v.reshape(B, S, H, HEAD_DIM), lam)
    out_a = rms_norm(out_a, subln_g) * (1.0 - lam_init)

    out_b = spatial_gating(b_u, b_v, sgu_ln_g, sgu_ln_b, sgu_w, sgu_b)

    cq = partial_rope(c_q.reshape(B, S, H, HEAD_DIM), *rope_c) * HEAD_DIM ** -0.5
    ck = partial_rope(c_k.reshape(B, S, H, HEAD_DIM), *rope_c)
    out_c = dilated_mixture(cq, ck, c_v.reshape(B, S, H, HEAD_DIM))

    qk = jax.nn.silu(causal_depthwise_conv(jnp.concatenate([d_q, d_k], -1), conv_w, conv_b))
    mq, mk = qk[..., :GROUP_WIDTH], qk[..., GROUP_WIDTH:]
    h = mlstm_chunkwise(mq.reshape(B, S, H, HEAD_DIM), mk.reshape(B, S, H, HEAD_DIM),
                        d_v.reshape(B, S, H, HEAD_DIM), d_i + gate_b[0], d_f + gate_b[1])
    h = layer_norm(h, mnorm_g, 0.0).astype(x.dtype)
    out_d = jax.nn.sigmoid(d_o.reshape(B, S, H, HEAD_DIM)) * h

    mixed = jnp.concatenate([out_a.reshape(B, S, GROUP_WIDTH).astype(x.dtype), out_b.astype(x.dtype),
                             out_c.reshape(B, S, GROUP_WIDTH).astype(x.dtype),
                             out_d.reshape(B, S, GROUP_WIDTH).astype(x.dtype)], axis=-1)
    return jnp.einsum('bsf,fd->bsd', mixed, w_out)


def swiglu(x, w_gate, w_up, w_down):
    g = jnp.einsum('bsd,df->bsf', x, w_gate)
    u = jnp.einsum('bsd,df->bsf', x, w_up)
    return jnp.einsum('bsf,fd->bsd', jax.nn.silu(g) * u, w_down)


def setup_inputs(seed: int = 0) -> dict:
    key = jax.random.key(seed)
    ks = jax.random.split(key, 24)
    L, H = DEPTH, N_HEADS_PER_MIXER
    nrm = lambda k, shape, scale: jax.random.normal(k, shape, jnp.float32) * scale
    return {
        "x": nrm(ks[0], (BATCH, SEQ, D_MODEL), 1.0),
        "w_in": nrm(ks[1], (L, D_MODEL, IN_WIDTH), D_MODEL ** -0.5),
        "diff_lambda": nrm(ks[2], (L, 4, DIFF_QK_DIM), 0.1),
        "diff_subln_g": 1.0 + nrm(ks[3], (L, HEAD_DIM), 0.02),
        "sgu_ln_g": 1.0 + nrm(ks[4], (L, GROUP_WIDTH), 0.02),
        "sgu_ln_b": nrm(ks[5], (L, GROUP_WIDTH), 0.02),
        "sgu_w": nrm(ks[6], (L, H, SGU_CHUNK, SGU_CHUNK), SGU_CHUNK ** -0.5),
        "sgu_b": 1.0 + nrm(ks[7], (L, H, SGU_CHUNK), 0.02),
        "mlstm_conv_w": nrm(ks[8], (L, CONV_WIDTH, 2 * GROUP_WIDTH), CONV_WIDTH ** -0.5),
        "mlstm_conv_b": nrm(ks[9], (L, 2 * GROUP_WIDTH), 0.02),
        "mlstm_gate_b": nrm(ks[10], (L, 2, H), 0.1) + jnp.array([0.0, 3.0], jnp.float32)[None, :, None],
        "mlstm_norm_g": 1.0 + nrm(ks[11], (L, HEAD_DIM), 0.02),
        "w_out": nrm(ks[12], (L, MIX_WIDTH, D_MODEL), MIX_WIDTH ** -0.5 * BETA),
        "ln1_g": 1.0 + nrm(ks[13], (L, D_MODEL), 0.02),
        "ln1_b": nrm(ks[14], (L, D_MODEL), 0.02),
        "w_gate": nrm(ks[15], (L, D_MODEL, D_FF), D_MODEL ** -0.5),
        "w_up": nrm(ks[16], (L, D_MODEL, D_FF), D_MODEL ** -0.5),
        "w_down": nrm(ks[17], (L, D_FF, D_MODEL), D_FF ** -0.5 * BETA),
        "ln2_g": 1.0 + nrm(ks[18], (L, D_MODEL), 0.02),
        "ln2_b": nrm(ks[19], (L, D_MODEL), 0.02),
    }


def reference(x, w_in, diff_lambda, diff_subln_g, sgu_ln_g, sgu_ln_b, sgu_w, sgu_b,
              mlstm_conv_w, mlstm_conv_b, mlstm_gate_b, mlstm_norm_g, w_out,
              ln1_g, ln1_b, w_gate, w_up, w_down, ln2_g, ln2_b):
    S = x.shape[1]
    rope_a = rope_tables(S, DIFF_QK_DIM)
    rope_c = rope_tables(S, HEAD_DIM)
    for l in range(DEPTH):
        lam_init = 0.8 - 0.6 * math.exp(-0.3 * l)
        h = mixer_block(x, w_in[l], diff_lambda[l], lam_init, diff_subln_g[l], sgu_ln_g[l], sgu_ln_b[l],
                        sgu_w[l], sgu_b[l], mlstm_conv_w[l], mlstm_conv_b[l], mlstm_gate_b[l],
                        mlstm_norm_g[l], w_out[l], rope_a, rope_c)
        x = layer_norm(ALPHA * x + h, ln1_g[l], ln1_b[l])
        f = swiglu(x, w_gate[l], w_up[l], w_down[l])
        x = layer_norm(ALPHA * x + f, ln2_g[l], ln2_b[l])
    return x
```

```python
import numpy as np
import concourse.bass as bass
import concourse.mybir as mybir

F32 = mybir.dt.float32
BF16 = mybir.dt.bfloat16
AF = mybir.ActivationFunctionType
ALU = mybir.AluOpType
AX = mybir.AxisListType


class _Eng:
    def __init__(self, S, name, eng):
        self.S, self.name, self.eng = S, name, eng
        self.sem = S.nc.alloc_semaphore("es_" + name)
        self.cnt = 0
        self.seen = {}

    def __getattr__(self, op):
        fn = getattr(self.eng, op)

        def call(*args, **kw):
            return self.S._emit(self, op, fn, args, kw)
        return call


class Sched:
    def __init__(self, nc):
        self.nc = nc
        self.pe = _Eng(self, "pe", nc.tensor)
        self.act = _Eng(self, "act", nc.scalar)
        self.dve = _Eng(self, "dve", nc.vector)
        self.pool = _Eng(self, "pool", nc.gpsimd)
        self.sp = _Eng(self, "sp", nc.sync)
        self.engs = [self.pe, self.act, self.dve, self.pool, self.sp]
        self.units = {}
        self.gran = {}
        self.dma_sems = {}
        self.all_sems = {}
        self.n_inst = 0
        self.free_dma = []
        self.cc_sem = None
        self.cc_cnt = 0

    def collective(self, kind, in_ap, out_ap, groups):
        E = self.pool
        reads, writes = self._units(in_ap), self._units(out_ap)
        self._deps(E, reads, writes, same_raw=False)
        if self.cc_sem is None:
            self.cc_sem = self.nc.alloc_semaphore("cc_sem")
        inst = self.nc.gpsimd.collective_compute(kind, ALU.bypass, replica_groups=groups, ins=[in_ap], outs=[out_ap])
        self.cc_cnt += 1
        inst.then_inc(self.cc_sem, 1)
        self._record((self.cc_sem, self.cc_cnt), reads, writes)
        return inst

    def gather_rows(self, out_ap, table_ap, idx_ap, element_offset, dep_ap=None):
        E = self.pool
        reads = self._units(dep_ap if dep_ap is not None else table_ap) + self._units(idx_ap)
        writes = self._units(out_ap)
        skey = (writes[0][0], 0)
        ds = self._dma_sem(skey)
        saved = []
        for key in writes:
            u = self._u(key)
            for k_, t_ in list(u["w"].items()):
                if t_[0] is ds[0]:
                    saved.append((u, k_, t_))
                    del u["w"][k_]
        self._deps(E, reads, writes, same_raw=False)
        inst = self.nc.gpsimd.indirect_dma_start(out=out_ap, out_offset=None, in_=table_ap,
                                                 in_offset=bass.IndirectOffsetOnAxis(ap=idx_ap, axis=0),
                                                 element_offset=element_offset)
        ds[1] += 16
        inst.then_inc(ds[0], 16)
        self._record((ds[0], ds[1]), reads, writes)
        return inst

    def _dma_sem(self, skey):
        ds = self.dma_sems.get(skey)
        if ds is None:
            if self.free_dma:
                ds = self.free_dma.pop()
            else:
                ds = [self.nc.alloc_semaphore("ds%d" % len(self.all_sems)), 0]
            self.dma_sems[skey] = ds
        return ds

    def set_gran(self, t, g):
        self.gran[t.name if hasattr(t, "name") else t] = g

    def _units(self, ap):
        name = ap.tensor.name
        g = self.gran.get(name)
        if g is None:
            return [(name, 0)]
        apl = ap.ap
        space = str(ap.space)
        off = int(ap.offset)
        if "DRAM" in space:
            lo = off
            hi = off + sum((c - 1) * s for s, c in apl)
        else:
            F = 1
            for d in ap.tensor.shape[1:]:
                F *= d
            lo = off % F
            hi = lo + sum((c - 1) * s for s, c in apl[1:])
        return [(name, i) for i in range(lo // g, hi // g + 1)]

    def _u(self, key):
        u = self.units.get(key)
        if u is None:
            u = self.units[key] = {"w": {}, "r": {}}
        return u

    def _wait(self, E, tok):
        sem, val = tok
        k = id(sem)
        if E.seen.get(k, 0) >= val:
            return
        E.eng.wait_ge(sem, val)
        E.seen[k] = val

    def _deps(self, E, reads, writes, same_raw=True):
        toks = []
        for key in reads:
            u = self._u(key)
            toks += list(u["w"].values())
        for key in writes:
            u = self._u(key)
            toks += list(u["w"].values()) + list(u["r"].values())
        for sem, val in toks:
            if sem is E.sem:
                continue
            self._wait(E, (sem, val))
        if same_raw and E is not self.pe:
            for key in reads:
                u = self._u(key)
                for sem, val in u["w"].values():
                    if sem is E.sem:
                        self._wait(E, (sem, val))

    def _record(self, tok, reads, writes):
        sem, val = tok
        k = id(sem)
        for key in reads:
            self._u(key)["r"][k] = tok
        for key in writes:
            u = self._u(key)
            u["w"] = {k: tok}
            u["r"] = {}
        self.all_sems[k] = tok

    def _emit(self, E, op, fn, args, kw):
        if op in ("dma_start",):
            return self._dma(E, fn, args, kw)
        lazy = kw.pop("lazy", False)
        aps = []
        out = kw.get("out", None)
        outs = []
        first = True
        for a in list(args) + [v for k_, v in kw.items()]:
            if isinstance(a, bass.AP):
                aps.append(a)
        if out is not None:
            outs = [out]
        elif args and isinstance(args[0], bass.AP):
            outs = [args[0]]
        if kw.get("accum_out") is not None:
            outs.append(kw["accum_out"])
        out_ids = [id(o) for o in outs]
        reads, writes = [], []
        for a in aps:
            if id(a) in out_ids:
                writes += self._units(a)
            else:
                reads += self._units(a)
        self._deps(E, reads, writes)
        inst = fn(*args, **kw)
        if lazy and kw.get("stop", True) is False:
            self._record((E.sem, E.cnt + 1), reads, writes)
            self.n_inst += 1
            return inst
        E.cnt += 1
        inst.then_inc(E.sem, 1)
        self._record((E.sem, E.cnt), reads, writes)
        self.n_inst += 1
        return inst

    def _dma(self, E, fn, args, kw):
        out = kw.get("out", args[0] if args else None)
        in_ = kw.get("in_", args[1] if len(args) > 1 else None)
        writes = self._units(out)
        reads = self._units(in_)
        if "DRAM" not in str(out.space):
            skey = (writes[0][0], 0)
        elif "DRAM" not in str(in_.space):
            skey = (reads[0][0], 0)
        else:
            skey = (writes[0][0], 0)
        self._deps(E, reads, writes, same_raw=False)
        ds = self._dma_sem(skey)
        inst = fn(*args, **kw)
        ds[1] += 16
        inst.then_inc(ds[0], 16)
        self._record((ds[0], ds[1]), reads, writes)
        self.n_inst += 1
        return inst

    def barrier(self, final=False):
        toks = [t for t in self.all_sems.values() if (final or t[0] is not self.cc_sem)]
        for E in self.engs:
            for tok in toks:
                if tok[0] is E.sem:
                    continue
                self._wait(E, tok)
        keep = {}
        for key, u in self.units.items():
            w = {k: t for k, t in u["w"].items() if t[0] is self.cc_sem}
            r = {k: t for k, t in u["r"].items() if t[0] is self.cc_sem}
            if (w or r) and not final:
                keep[key] = {"w": w, "r": r}
        self.units = keep
        self.free_dma += list(self.dma_sems.values())
        self.dma_sems = {}

    def finish(self):
        self.barrier(final=True)


from contextlib import ExitStack
from concourse.bass_utils import run_bass_kernel_spmd

D_MODEL = 1024
D_FF = 2816
DEPTH = 4
ALPHA = (2 * DEPTH) ** 0.25
LN_EPS = 1e-5


_PFX = [""]


def _sb(st, nc, name, shape, dt):
    return st.enter_context(nc.sbuf_tensor(_PFX[0] + name, shape, dt))


def _ps(st, nc, name, shape, dt=F32):
    return st.enter_context(nc.psum_tensor(_PFX[0] + name, shape, dt))


def _pipeline(n, stages, lag=1):
    ns = len(stages)
    for step in range(n + (ns - 1) * lag):
        for si, f in enumerate(stages):
            i = step - si * lag
            if 0 <= i < n:
                f(i)


def _make_ident(S, nc, ident):
    S.pool.memset(ident[:], 1.0)
    S.pool.affine_select(ident[:], ident[:], [[-1, 128]], ALU.is_equal, 0.0, base=0, channel_multiplier=1)


def _layernorm_rows(S, nc, t, width, stats, mv, g_bc, b_bc, out):
    nch = width // 512
    for c in range(nch):
        S.dve.bn_stats(stats[:, c, :], t[:, c * 512:(c + 1) * 512])
    S.dve.bn_aggr(mv[:, 0:2], stats[:, 0:nch, :])
    S.dve.tensor_scalar_add(mv[:, 2:3], mv[:, 1:2], LN_EPS)
    S.act.sqrt(mv[:, 2:3], mv[:, 2:3])
    S.dve.reciprocal(mv[:, 3:4], mv[:, 2:3])
    S.dve.tensor_scalar(t, t, mv[:, 0:1], mv[:, 3:4], ALU.subtract, ALU.mult)
    S.pool.tensor_tensor(t, t, g_bc, ALU.mult)
    S.pool.tensor_tensor(out, t, b_bc, ALU.add)


def _ffn_all(S, nc, T, P, ident):
    NT = T // 128
    NTT = T // 512
    mix_gather, xres, w_out, w_gate, w_up, w_down, lnp, y, yT = (P[k] for k in (
        "mix_gather", "xres", "w_out", "w_gate", "w_up", "w_down", "lnp", "y", "yT"))
    with ExitStack() as st0:
        lnbc = _sb(st0, nc, "lnbc", [128, 4, D_MODEL], F32)
        yacc = _sb(st0, nc, "yacc", [128, NT, D_MODEL], F32)
        x1T = _sb(st0, nc, "x1T", [128, 8, T], BF16)
        S.set_gran(yacc, D_MODEL)
        S.set_gran(lnbc, D_MODEL)
        for j in range(4):
            S.sp.dma_start(out=lnbc[:, j, :], in_=lnp[j].partition_broadcast(128))
        with ExitStack() as st:
            wout_b = _sb(st, nc, "wout_b", [128, 8, D_MODEL], BF16)
            mixb = [_sb(st, nc, "mixb%d" % i, [128, 8, 512], BF16) for i in range(2)]
            xr = [_sb(st, nc, "xr%d" % i, [128, D_MODEL], F32) for i in range(2)]
            t1 = [_sb(st, nc, "t1_%d" % i, [128, D_MODEL], F32) for i in range(3)]
            stats = [_sb(st, nc, "stats%d" % i, [128, 2, 6], F32) for i in range(2)]
            mv = [_sb(st, nc, "mv%d" % i, [128, 4], F32) for i in range(2)]
            psh = [_ps(st, nc, "psh%d" % i, [128, D_MODEL]) for i in range(2)]
            pst = [_ps(st, nc, "pst%d" % i, [128, D_MODEL]) for i in range(2)]
            S.pool.dma_start(out=wout_b[:], in_=w_out.rearrange("(c p) f -> p c f", p=128))
            def f0(i):
                b = i % 2
                tsl = slice(i * 128, (i + 1) * 128)
                mb = mixb[(i // 4) % 2]
                if i % 4 == 0:
                    mix_gather(mb, i // 4)
                S.sp.dma_start(out=xr[b][:], in_=xres[tsl, :])
                for half in range(2):
                    for k in range(8):
                        S.pe.matmul(psh[b][:, half * 512:(half + 1) * 512], mb[:, k, (i % 4) * 128:(i % 4 + 1) * 128],
                                    wout_b[:, k, half * 512:(half + 1) * 512], start=(k == 0), stop=(k == 7), lazy=True)
                S.dve.scalar_tensor_tensor(t1[i % 3][:], xr[b][:], ALPHA, psh[b][:], ALU.mult, ALU.add)

            def f1(i):
                b = i % 2
                _layernorm_rows(S, nc, t1[i % 3][:], D_MODEL, stats[b], mv[b], lnbc[:, 0, :], lnbc[:, 1, :], t1[i % 3][:])
                S.act.mul(yacc[:, i, :], t1[i % 3][:], ALPHA)

            def f2(i):
                b = i % 2
                tsl = slice(i * 128, (i + 1) * 128)
                for c in range(8):
                    S.pe.transpose(pst[b][:, c * 128:(c + 1) * 128], t1[i % 3][:, c * 128:(c + 1) * 128], ident[:])
                S.act.copy(x1T[:, 0:4, tsl], pst[b][:, 0:512].rearrange("p (c t) -> p c t", c=4))
                S.dve.tensor_copy(x1T[:, 4:8, tsl], pst[b][:, 512:1024].rearrange("p (c t) -> p c t", c=4))

            _pipeline(NT, [f0, f1, f2])
        S.barrier()
        with ExitStack() as st:
            FB = 256
            NFB = D_FF // FB
            wg_b = [_sb(st, nc, "wg_b%d" % i, [128, 8, FB], BF16) for i in range(2)]
            wu_b = [_sb(st, nc, "wu_b%d" % i, [128, 8, FB], BF16) for i in range(2)]
            wd_b = [_sb(st, nc, "wd_b%d" % i, [128, 2, D_MODEL], BF16) for i in range(2)]
            sg = [_sb(st, nc, "sg%d" % i, [128, 512], F32) for i in range(2)]
            dtmp = _sb(st, nc, "dtmp", [128, D_MODEL], F32)
            actT = [_sb(st, nc, "actT%d" % i, [128, 2, 512], BF16) for i in range(2)]
            psg = [_ps(st, nc, "psg%d" % i, [128, 512]) for i in range(2)]
            psu = [_ps(st, nc, "psu%d" % i, [128, 512]) for i in range(2)]
            psd = [_ps(st, nc, "psd%d" % i, [128, D_MODEL]) for i in range(2)]
            wg_v = w_gate.rearrange("(c p) f -> p c f", p=128)
            wu_v = w_up.rearrange("(c p) f -> p c f", p=128)
            wd_v = w_down.rearrange("(c p) f -> p c f", p=128)
            items = [(fb, tt) for fb in range(NFB) for tt in range(NTT)]

            def g0(it):
                fb, tt = items[it]
                wb = fb % 2
                ab = it % 2
                if tt == 0:
                    fsl = slice(fb * FB, (fb + 1) * FB)
                    S.pool.dma_start(out=wg_b[wb][:], in_=wg_v[:, :, fsl])
                    S.pool.dma_start(out=wu_b[wb][:], in_=wu_v[:, :, fsl])
                    S.pool.dma_start(out=wd_b[wb][:], in_=wd_v[:, 2 * fb:2 * fb + 2, :])
                tsl = slice(tt * 512, (tt + 1) * 512)
                for c2 in range(2):
                    for k in range(8):
                        S.pe.matmul(psg[c2][:], wg_b[wb][:, k, c2 * 128:(c2 + 1) * 128], x1T[:, k, tsl],
                                    start=(k == 0), stop=(k == 7), lazy=True)
                    for k in range(8):
                        S.pe.matmul(psu[c2][:], wu_b[wb][:, k, c2 * 128:(c2 + 1) * 128], x1T[:, k, tsl],
                                    start=(k == 0), stop=(k == 7), lazy=True)
                    S.act.activation(sg[c2][:], psg[c2][:], AF.Silu)
                    S.dve.tensor_tensor(actT[ab][:, c2, :], sg[c2][:], psu[c2][:], ALU.mult)

            def g1(it):
                fb, tt = items[it]
                wb = fb % 2
                ab = it % 2
                for s4 in range(4):
                    db = (it * 4 + s4) % 2
                    for half in range(2):
                        for c2 in range(2):
                            S.pe.matmul(psd[db][:, half * 512:(half + 1) * 512],
                                        actT[ab][:, c2, s4 * 128:(s4 + 1) * 128],
                                        wd_b[wb][:, c2, half * 512:(half + 1) * 512],
                                        start=(c2 == 0), stop=(c2 == 1), lazy=True)
                    ti = tt * 4 + s4
                    if s4 % 2 == 0:
                        S.dve.tensor_tensor(yacc[:, ti, :], yacc[:, ti, :], psd[db][:], ALU.add)
                    else:
                        S.act.copy(dtmp[:], psd[db][:])
                        S.pool.tensor_tensor(yacc[:, ti, :], yacc[:, ti, :], dtmp[:], ALU.add)

            _pipeline(len(items), [g0, g1])
        S.barrier()
        with ExitStack() as st:
            stats = [_sb(st, nc, "stats3_%d" % i, [128, 2, 6], F32) for i in range(2)]
            mv = [_sb(st, nc, "mv3_%d" % i, [128, 4], F32) for i in range(2)]
            ob = [_sb(st, nc, "ob%d" % i, [128, D_MODEL], F32) for i in range(2)]
            obT = [_sb(st, nc, "obT%d" % i, [128, 8, 128], BF16) for i in range(2)]
            pst3 = [_ps(st, nc, "pst3_%d" % i, [128, D_MODEL]) for i in range(2)]
            def h0(i):
                b = i % 2
                _layernorm_rows(S, nc, yacc[:, i, :], D_MODEL, stats[b], mv[b], lnbc[:, 2, :], lnbc[:, 3, :], ob[b][:])

            def h1(i):
                b = i % 2
                S.sp.dma_start(out=y[i * 128:(i + 1) * 128, :], in_=ob[b][:])
                if yT is not None:
                    for c in range(8):
                        S.pe.transpose(pst3[b][:, c * 128:(c + 1) * 128], ob[b][:, c * 128:(c + 1) * 128], ident[:])
                    S.act.copy(obT[b][:, 0:4, :], pst3[b][:, 0:512].rearrange("p (c t) -> p c t", c=4))
                    S.dve.tensor_copy(obT[b][:, 4:8, :], pst3[b][:, 512:1024].rearrange("p (c t) -> p c t", c=4))
                    S.sp.dma_start(out=yT(i), in_=obT[b][:])
                    P["after_tile"](i)

            _pipeline(NT, [h0, h1])


HD = 64
NTM = 580
NEG = -30000.0


def _gelu_tanh(S, nc, out, x, tmp):
    S.act.activation(tmp, x, AF.Square)
    S.dve.tensor_scalar(tmp, tmp, 0.044715, 1.0, ALU.mult, ALU.add)
    S.pool.tensor_tensor(tmp, tmp, x, ALU.mult)
    S.act.activation(tmp, tmp, AF.Sigmoid, scale=2.0 * 0.7978845608028654)
    S.dve.tensor_tensor(out, x, tmp, ALU.mult)


def _consts(S, nc, st0):
    C = {}
    C["pv"] = pv = _sb(st0, nc, "pv", [128, 16], F32)
    C["bv"] = bv = _sb(st0, nc, "bv", [128, 704], F32)
    C["ident"] = ident = _sb(st0, nc, "identf", [128, 128], F32)
    C["identb"] = identb = _sb(st0, nc, "identb", [128, 128], BF16)
    C["mask_le"] = mask_le = _sb(st0, nc, "mask_le", [128, 128], F32)
    C["mask_le_b"] = mask_le_b = _sb(st0, nc, "mask_le_b", [128, 128], BF16)
    C["mask_ge_b"] = mask_ge_b = _sb(st0, nc, "mask_ge_b", [128, 128], BF16)
    C["tri"] = tri = _sb(st0, nc, "tri", [128, 128], F32)
    _make_ident(S, nc, ident)
    S.dve.tensor_copy(identb[:], ident[:])
    S.pool.memset(mask_le[:], 0.0)
    S.pool.affine_select(mask_le[:], mask_le[:], [[1, 128]], ALU.is_ge, NEG, base=0, channel_multiplier=-1)
    S.dve.tensor_copy(mask_le_b[:], mask_le[:])
    S.pool.memset(tri[:], 1.0)
    S.pool.affine_select(tri[:], tri[:], [[1, 128]], ALU.is_ge, 0.0, base=0, channel_multiplier=-1)
    S.pool.memset(mask_ge_b[:], 0.0)
    S.pool.affine_select(mask_ge_b[:], mask_ge_b[:], [[-1, 128]], ALU.is_ge, NEG, base=0, channel_multiplier=1)
    return C


def _mixer_all(S, nc, SEQ, P, C, do=("A", "B", "C", "D")):
    NTT = SEQ // 512
    xsrc, w_fm, w_tm, ropeA, ropeC, pvec, bvec, sgu_wT, outf = (P[k] for k in (
        "xsrc", "w_fm", "w_tm", "ropeA", "ropeC", "pvec", "bvec", "sgu_wT", "outf"))
    sc_qa, sc_ka, sc_qc, sc_kc, sc_qkd, sc_g, sc_tm = (P[k] for k in ("sc_qa", "sc_ka", "sc_qc", "sc_kc", "sc_qkd", "sc_g", "sc_tm"))
    pv, bv, ident, identb, mask_le, mask_le_b, mask_ge_b, tri = (C[k] for k in (
        "pv", "bv", "ident", "identb", "mask_le", "mask_le_b", "mask_ge_b", "tri"))
    S.sp.dma_start(out=pv[:], in_=pvec)
    S.sp.dma_start(out=bv[:], in_=bvec.partition_broadcast(128))
    if True:
        with ExitStack() as st:
            wfm_b = _sb(st, nc, "wfm_b", [128, 8, 768], BF16)
            wtm_b = _sb(st, nc, "wtm_b", [128, 8, NTM], BF16)
            xb = [_sb(st, nc, "xb%d" % i, [128, 8, 512], BF16) for i in range(2)]
            rA = [_sb(st, nc, "rA%d" % i, [64, 2, 512], F32) for i in range(2)]
            rC = [_sb(st, nc, "rC%d" % i, [64, 2, 512], F32) for i in range(2)]
            ta = [_sb(st, nc, "ta%d" % i, [64, 512], F32) for i in range(2)]
            tb = [_sb(st, nc, "tb%d" % i, [64, 512], F32) for i in range(2)]
            stg = [_sb(st, nc, "stg%d" % i, [64, 512], BF16) for i in range(4)]
            stg5 = [_sb(st, nc, "stg5_%d" % i, [128, 512], F32) for i in range(2)]
            stg6 = [_sb(st, nc, "stg6_%d" % i, [2, 512], F32) for i in range(2)]
            stgt = [_sb(st, nc, "stgt%d" % i, [128, NTM], F32) for i in range(2)]
            ps1 = [_ps(st, nc, "ps1_%d" % i, [128, 512]) for i in range(2)]
            ps2 = [_ps(st, nc, "ps2_%d" % i, [128, 512]) for i in range(2)]
            ps5 = _ps(st, nc, "ps5", [128, 512])
            pstm = _ps(st, nc, "pstm", [128, 1024])
            S.pool.dma_start(out=wfm_b[:], in_=w_fm.rearrange("(c p) f -> p c f", p=128))
            S.pool.dma_start(out=wtm_b[:], in_=w_tm.rearrange("(c p) f -> p c f", p=128))
            rA_v = ropeA.rearrange("two r t -> r two t")
            rC_v = ropeC.rearrange("two r t -> r two t")
            for it_, tt in enumerate(P.get("tile_order", range(NTT))):
                b = it_ % 2
                tsl = slice(tt * 512, (tt + 1) * 512)
                xe, xap = xsrc(tt)
                xe.dma_start(out=xb[b][:], in_=xap)
                S.sp.dma_start(out=rA[b][:], in_=rA_v[:, :, tsl])
                S.sp.dma_start(out=rC[b][:], in_=rC_v[:, :, tsl])
                def do_pair(pair):
                    rt, dq, dk = ((rA[b], sc_qa, sc_ka), (rC[b], sc_qc, sc_kc))[pair]
                    pb = (2 * it_ + pair) % 2
                    g1 = 2 * pair
                    for k in range(8):
                        S.pe.matmul(ps1[pb][:], wfm_b[:, k, g1 * 128:(g1 + 1) * 128], xb[b][:, k, :], start=(k == 0), stop=(k == 7), lazy=True)
                    for k in range(8):
                        S.pe.matmul(ps2[pb][:], wfm_b[:, k, (g1 + 1) * 128:(g1 + 2) * 128], xb[b][:, k, :], start=(k == 0), stop=(k == 7), lazy=True)
                    for half, dst in enumerate((dq, dk)):
                        rows = slice(half * 64, (half + 1) * 64)
                        tbuf = half
                        S.dve.tensor_tensor(ta[tbuf][:], ps2[pb][rows, :], rt[:, 1, :], ALU.mult)
                        S.dve.tensor_tensor(tb[tbuf][:], ps1[pb][rows, :], rt[:, 0, :], ALU.mult)
                        sb_ = (4 * it_ + 2 * pair + half) % 4
                        S.pool.tensor_tensor(stg[sb_][:], ta[tbuf][:], tb[tbuf][:], ALU.add)
                        S.sp.dma_start(out=dst[:, tsl], in_=stg[sb_][:])

                def do_g5():
                    for k in range(8):
                        S.pe.matmul(ps5[:], wfm_b[:, k, 512:640], xb[b][:, k, :], start=(k == 0), stop=(k == 7), lazy=True)
                    S.act.copy(stg5[b][:], ps5[:])
                    S.sp.dma_start(out=sc_qkd[:, tsl], in_=stg5[b][:])

                def do_g6():
                    for k in range(8):
                        S.pe.matmul(ps5[0:2, :], wfm_b[:, k, 640:642], xb[b][:, k, :], start=(k == 0), stop=(k == 7), lazy=True)
                    S.act.copy(stg6[b][:], ps5[0:2, :])
                    S.sp.dma_start(out=sc_g[:, tsl], in_=stg6[b][:])

                def do_sub(sub):
                    tb_ = (4 * it_ + sub) % 2
                    for k in range(8):
                        S.pe.matmul(pstm[:, 0:512], xb[b][:, k, sub * 128:(sub + 1) * 128], wtm_b[:, k, 0:512], start=(k == 0), stop=(k == 7), lazy=True)
                    for k in range(8):
                        S.pe.matmul(pstm[:, 512:NTM], xb[b][:, k, sub * 128:(sub + 1) * 128], wtm_b[:, k, 512:NTM], start=(k == 0), stop=(k == 7), lazy=True)
                    S.act.copy(stgt[tb_][:], pstm[:, 0:NTM])
                    r0 = tt * 512 + sub * 128
                    S.sp.dma_start(out=sc_tm[r0:r0 + 128, :], in_=stgt[tb_][:])

                do_pair(0)
                do_sub(0)
                do_g5()
                do_sub(1)
                do_pair(1)
                do_sub(2)
                do_g6()
                do_sub(3)
        S.barrier()
        if "B" in do:
            _mixer_B(S, nc, SEQ, sc_tm, sgu_wT, pv, bv, ident, tri, outf)
            P["after"](1)
            S.barrier()
        if "A" in do:
            _mixer_A(S, nc, SEQ, sc_qa, sc_ka, sc_tm, pv, bv, outf)
            P["after"](0)
            S.barrier()
        if "C" in do:
            _mixer_C(S, nc, SEQ, sc_qc, sc_kc, sc_tm, identb, mask_le_b, mask_ge_b, outf)
            P["after"](2)
            S.barrier()
        if "D" in do:
            _mixer_D(S, nc, SEQ, sc_qkd, sc_g, sc_tm, pv, bv, ident, identb, mask_le, tri, outf)
            P["after"](3)


def _mixer_B(S, nc, SEQ, sc_tm, sgu_wT, pv, bv, ident, tri, outf):
    NIT = SEQ // 512
    with ExitStack() as st:
        wT = _sb(st, nc, "sg_wT", [128, 128], F32)
        wTb = _sb(st, nc, "sg_wTb", [128, 128], BF16)
        uv = [_sb(st, nc, "sg_uv%d" % i, [128, 4, 320], F32) for i in range(2)]
        gl = [_sb(st, nc, "sg_gl%d" % i, [128, 4, 320], F32) for i in range(2)]
        tmp = [_sb(st, nc, "sg_tmp%d" % i, [128, 4, 320], F32) for i in range(2)]
        stats = [_sb(st, nc, "sg_st%d" % i, [128, 4, 6], F32) for i in range(2)]
        mv = [_sb(st, nc, "sg_mv%d" % i, [128, 4, 2], F32) for i in range(2)]
        rs = [_sb(st, nc, "sg_rs%d" % i, [128, 4], F32) for i in range(2)]
        vn = [_sb(st, nc, "sg_vn%d" % i, [128, 4, 64], F32) for i in range(2)]
        vnb = [_sb(st, nc, "sg_vnb%d" % i, [128, 4, 64], BF16) for i in range(2)]
        ob = [_sb(st, nc, "sg_ob%d" % i, [128, 4, 64], F32) for i in range(2)]
        og = [_sb(st, nc, "sg_og%d" % i, [64, 512], BF16) for i in range(2)]
        psz = [_ps(st, nc, "sg_psz%d" % i, [128, 4, 64]) for i in range(2)]
        pst = [_ps(st, nc, "sg_pst%d" % i, [64, 512]) for i in range(2)]
        S.sp.dma_start(out=wT[:], in_=sgu_wT)
        S.dve.tensor_tensor(wTb[:], wT[:], tri[:], ALU.mult)
        gbc = bv[:, 0:64].unsqueeze(1).to_broadcast([128, 4, 64])
        bbc = bv[:, 64:128].unsqueeze(1).to_broadcast([128, 4, 64])

        def b0(it):
            b = it % 2
            r0 = it * 512
            S.sp.dma_start(out=uv[b][:], in_=sc_tm[r0:r0 + 512, 256:576].rearrange("(n p) c -> p n c", p=128))
            _gelu_tanh(S, nc, gl[b][:], uv[b][:], tmp[b][:])
            for k in range(4):
                S.dve.bn_stats(stats[b][:, k, :], gl[b][:, k, 64:320])
                S.dve.bn_aggr(mv[b][:, k, :], stats[b][:, k:k + 1, :])
            S.dve.tensor_scalar_add(rs[b][:], mv[b][:, :, 1], LN_EPS)
            S.act.sqrt(rs[b][:], rs[b][:])
            S.dve.reciprocal(rs[b][:], rs[b][:])
            S.dve.tensor_tensor(vn[b][:], gl[b][:, :, 64:128], mv[b][:, :, 0:1].to_broadcast([128, 4, 64]), ALU.subtract)
            S.dve.tensor_tensor(vn[b][:], vn[b][:], rs[b][:].unsqueeze(2).to_broadcast([128, 4, 64]), ALU.mult)
            S.pool.tensor_tensor(vn[b][:], vn[b][:], gbc, ALU.mult)
            S.pool.tensor_tensor(vnb[b][:], vn[b][:], bbc, ALU.add)

        def b1(it):
            b = it % 2
            for k in range(4):
                S.pe.matmul(psz[b][:, k, :], wTb[:], vnb[b][:, k, :], start=True, stop=True)
            S.dve.scalar_tensor_tensor(ob[b][:], psz[b][:], pv[:, 10:11], gl[b][:, :, 0:64], ALU.add, ALU.mult)

        def b2(it):
            b = it % 2
            for k in range(4):
                S.pe.transpose(pst[b][:, k * 128:(k + 1) * 128], ob[b][:, k, :], ident[:])
            S.act.copy(og[b][:], pst[b][:])
            S.sp.dma_start(out=outf(64, it * 512, (it + 1) * 512), in_=og[b][:])

        _pipeline(NIT, [b0, b1, b2])


def _mixer_A(S, nc, SEQ, sc_qa, sc_ka, sc_tm, pv, bv, outf):
    NTT = SEQ // 512
    NKB = SEQ // 128
    scale = 32 ** -0.5
    with ExitStack() as st:
        qT = _sb(st, nc, "A_qT", [64, SEQ], BF16)
        kT = _sb(st, nc, "A_kT", [64, SEQ], BF16)
        va = _sb(st, nc, "A_va", [128, NKB, 128], BF16)
        lam = _sb(st, nc, "A_lam", [64, 8], F32)
        lt = _sb(st, nc, "A_lt", [64, 64], F32)
        ones_ms = _sb(st, nc, "A_ones", [64, 64], F32)
        E = [_sb(st, nc, "A_E%d" % i, [128, 512], BF16) for i in range(4)]
        rd = [_sb(st, nc, "A_rd%d" % i, [64, 512], F32) for i in range(2)]
        o0 = _sb(st, nc, "A_o0", [64, 512], F32)
        o1 = _sb(st, nc, "A_o1", [64, 512], F32)
        sq = _sb(st, nc, "A_sq", [64, 512], F32)
        og = [_sb(st, nc, "A_og%d" % i, [64, 512], BF16) for i in range(2)]
        pss = [_ps(st, nc, "A_pss%d" % i, [128, 512]) for i in range(4)]
        pso = [_ps(st, nc, "A_pso%d" % i, [128, 512]) for i in range(4)]
        psm = pss[0]
        S.set_gran(va, 128)
        S.sp.dma_start(out=qT[:], in_=sc_qa)
        S.sp.dma_start(out=kT[:], in_=sc_ka)
        S.pool.memset(va[:, :, 64:128], 1.0)
        S.pool.dma_start(out=va[:, :, 0:64], in_=sc_tm[:, 0:64].rearrange("(n p) d -> p n d", p=128))
        S.pool.memset(ones_ms[:], 1.0 / 64.0)
        S.dve.tensor_tensor(lt[:, 0:32], bv[0:64, 192:224], bv[0:64, 224:256], ALU.mult)
        S.dve.tensor_tensor(lt[:, 32:64], bv[0:64, 256:288], bv[0:64, 288:320], ALU.mult)
        S.dve.tensor_reduce(lam[:, 0:1], lt[:, 0:32], AX.X, ALU.add)
        S.dve.tensor_reduce(lam[:, 1:2], lt[:, 32:64], AX.X, ALU.add)
        S.act.activation(lam[:, 2:4], lam[:, 0:2], AF.Exp)
        S.dve.tensor_tensor(lam[:, 4:5], lam[:, 2:3], lam[:, 3:4], ALU.subtract)
        S.dve.tensor_tensor(lam[:, 4:5], lam[:, 4:5], pv[0:64, 7:8], ALU.add)
        S.dve.tensor_scalar_mul(lam[:, 5:6], lam[:, 4:5], -1.0)
        S.dve.tensor_tensor(lam[:, 6:7], pv[0:64, 5:6], pv[0:64, 6:7], ALU.mult)
        blocks = []
        for t in range(NTT):
            nkb = 4 * (t + 1)
            for kb in range(nkb):
                blocks.append((t, kb, nkb))
        LA = 1
        NB_ = len(blocks)

        def front(i):
            t, kb, nkb = blocks[i]
            q0 = t * 512
            j = kb - 4 * t
            c0 = max(j, 0) * 128
            for m in range(2):
                rows = slice(32 * m, 32 * m + 32)
                e = (2 * i + m) % 4
                S.pe.matmul(pss[e][:, c0:512], kT[rows, kb * 128:(kb + 1) * 128], qT[rows, q0 + c0:q0 + 512],
                            start=True, stop=True)
            for m in range(2):
                e = (2 * i + m) % 4
                S.act.activation(E[e][:, c0:512], pss[e][:, c0:512], AF.Exp, scale=scale)
                if j >= 0:
                    S.pool.affine_select(E[e][:, c0:c0 + 128], E[e][:, c0:c0 + 128], [[1, 128]], ALU.is_ge, 0.0,
                                         base=0, channel_multiplier=-1)

        def back(i):
            t, kb, nkb = blocks[i]
            j = kb - 4 * t
            c0 = max(j, 0) * 128
            for m in range(2):
                e = (2 * i + m) % 4
                po = pso[(2 * t + m) % 4]
                S.pe.matmul(po[:, c0:512], va[:, kb, :], E[e][:, c0:512], start=(kb == 0), stop=(kb == nkb - 1))
            if kb == nkb - 1:
                epilogue(t)

        def epilogue(t):
            q0 = t * 512
            p0 = pso[(2 * t) % 4]
            p1 = pso[(2 * t + 1) % 4]
            S.act.activation(rd[0][:], p0[64:128, :], AF.Ln)
            S.act.activation(rd[0][:], rd[0][:], AF.Exp, scale=-1.0)
            S.dve.tensor_tensor(o0[:], p0[0:64, :], rd[0][:], ALU.mult)
            S.act.activation(rd[1][:], p1[64:128, :], AF.Ln)
            S.act.activation(rd[1][:], rd[1][:], AF.Exp, scale=-1.0)
            S.dve.tensor_tensor(o1[:], p1[0:64, :], rd[1][:], ALU.mult)
            S.dve.scalar_tensor_tensor(o0[:], o1[:], lam[:, 5:6], o0[:], ALU.mult, ALU.add)
            S.pool.tensor_tensor(sq[:], o0[:], o0[:], ALU.mult)
            S.pe.matmul(psm[0:64, :], ones_ms[:], sq[:], start=True, stop=True)
            S.dve.tensor_scalar_add(sq[:], psm[0:64, :], LN_EPS)
            S.act.activation(sq[:], sq[:], AF.Ln)
            S.act.activation(sq[:], sq[:], AF.Exp, scale=-0.5)
            S.pool.tensor_tensor(o1[:], o0[:], sq[:], ALU.mult)
            S.dve.tensor_scalar(og[t % 2][:], o1[:], lam[:, 6:7], None, ALU.mult)
            S.sp.dma_start(out=outf(0, q0, q0 + 512), in_=og[t % 2][:])

        for i in range(NB_ + LA):
            if i < NB_:
                front(i)
            if i - LA >= 0:
                back(i - LA)


import math as _math

ROPE_THETA = 500000.0


def _rope_tables(SEQ):
    pos = np.arange(SEQ, dtype=np.float32)

    def tab(rot, blk):
        half = rot // 2
        inv = (np.float32(ROPE_THETA) ** (-(np.arange(0, rot, 2, dtype=np.float32)) / np.float32(rot))).astype(np.float32)
        ang = (pos[:, None] * inv[None, :]).astype(np.float32)
        c, s = np.cos(ang).astype(np.float32).T, np.sin(ang).astype(np.float32).T
        C = np.ones((blk, SEQ), np.float32)
        Sn = np.zeros((blk, SEQ), np.float32)
        C[0:half] = c
        C[half:2 * half] = c
        Sn[0:half] = -s
        Sn[half:2 * half] = s
        return C, Sn
    Ca, Sa = tab(8, 32)
    Cc, Sc = tab(16, 64)
    ropeA = np.stack([np.concatenate([Ca, Ca], 0), np.concatenate([Sa, Sa], 0)]).astype(np.float32)
    ropeC = np.stack([Cc, Sc]).astype(np.float32)
    return np.ascontiguousarray(ropeA), np.ascontiguousarray(ropeC)


def _perm_idx(base, n, half):
    idx = np.arange(n)
    d = idx.copy()
    d[0:half] = idx[0:half] + half
    d[half:2 * half] = idx[half:2 * half] - half
    return base + d


def prep_mixer_inputs(inp, l, h, SEQ, ropes):
    w_in = inp["w_in"][l]
    c = lambda off: off + h * 64 + np.arange(64)
    aq, ak, av = c(0), c(256), c(512)
    bu = c(768)
    cq, ck, cv = c(1280), c(1536), c(1792)
    dq, dk, dv, do_ = c(2048), c(2304), c(2560), c(2816)
    aqp = np.concatenate([_perm_idx(aq[0], 32, 4), _perm_idx(aq[32], 32, 4)])
    akp = np.concatenate([_perm_idx(ak[0], 32, 4), _perm_idx(ak[32], 32, 4)])
    cqp = _perm_idx(cq[0], 64, 8)
    ckp = _perm_idx(ck[0], 64, 8)
    di, df = 3072 + h, 3076 + h
    g6 = np.concatenate([[di, df], np.full(126, di)])
    fm_cols = np.concatenate([aq, ak, aqp, akp, cq, ck, cqp, ckp, dq, dk, g6])
    bv_all = 1024 + np.concatenate([h * 64 + np.arange(64)] + [g * 64 + np.arange(64) for g in range(4) if g != h])
    tm_cols = np.concatenate([av, cv, dv, do_, bu, bv_all, [di, df, di, df]])
    assert fm_cols.size == 768 and tm_cols.size == NTM
    lam_init = 0.8 - 0.6 * _math.exp(-0.3 * l)
    pvec = np.zeros((128, 16), np.float32)
    chan = np.concatenate([h * 64 + np.arange(64), 256 + h * 64 + np.arange(64)])
    pvec[:, 0:4] = inp["mlstm_conv_w"][l][:, chan].T
    pvec[:, 4] = inp["mlstm_conv_b"][l][chan]
    pvec[:, 5] = np.tile(inp["diff_subln_g"][l], 2)
    pvec[:, 6] = 1.0 - lam_init
    pvec[:, 7] = lam_init
    pvec[:, 8] = inp["mlstm_gate_b"][l][0, h]
    pvec[:, 9] = inp["mlstm_gate_b"][l][1, h]
    pvec[:, 10] = inp["sgu_b"][l][h]
    bvec = np.zeros(704, np.float32)
    bvec[0:64] = inp["sgu_ln_g"][l][h * 64:(h + 1) * 64]
    bvec[64:128] = inp["sgu_ln_b"][l][h * 64:(h + 1) * 64]
    bvec[128:192] = inp["mlstm_norm_g"][l]
    bvec[192:320] = inp["diff_lambda"][l].reshape(-1)
    return {
        "w_fm": np.ascontiguousarray(w_in[:, fm_cols]),
        "w_tm": np.ascontiguousarray(w_in[:, tm_cols]),
        "ropeA": ropes[0], "ropeC": ropes[1],
        "pvec": pvec, "bvec": bvec,
        "sgu_wT": np.ascontiguousarray(inp["sgu_w"][l][h].T),
    }


def _mixer_C(S, nc, SEQ, sc_qc, sc_kc, sc_tm, identb, mask_le_b, mask_ge_b, outf):
    pats = (1, 4, 16)
    SB = 2048 if SEQ >= 2048 else SEQ
    NSB = SEQ // SB
    NBLK = SEQ // 128
    with ExitStack() as st:
        qT = _sb(st, nc, "C_qT", [64, SEQ], BF16)
        kT = _sb(st, nc, "C_kT", [64, SEQ], BF16)
        vd = [_sb(st, nc, "C_vd%d" % i, [128, NBLK, 128], BF16) for i in range(3)]
        acc = [_sb(st, nc, "C_acc%d" % i, [128, SB], F32) for i in range(2)]
        E = [_sb(st, nc, "C_E%d" % i, [128, 2, 128], BF16) for i in range(3)]
        rd = _sb(st, nc, "C_rd", [64, SB], F32)
        og = _sb(st, nc, "C_og", [64, SB], BF16)
        pss = [_ps(st, nc, "C_pss%d" % i, [128, 2, 128]) for i in range(3)]
        pso = [_ps(st, nc, "C_pso%d" % i, [128, 128]) for i in range(3)]
        for i in range(3):
            S.set_gran(vd[i], 128)
        S.sp.dma_start(out=qT[:], in_=sc_qc)
        S.sp.dma_start(out=kT[:], in_=sc_kc)
        for pi, dil in enumerate(pats):
            S.pool.memset(vd[pi][:, :, 64:128], 1.0)
            nb = SEQ // (128 * dil)
            src = sc_tm[:, 64:128].rearrange("(n j r) d -> j n r d", j=128, r=dil)
            dst = vd[pi][:, :, 0:64].rearrange("j (n r) d -> j n r d", r=dil)
            for n in range(nb):
                S.pool.dma_start(out=dst[:, n], in_=src[:, n])
        blocks = []
        for sb in range(NSB):
            first = True
            for pi, dil in enumerate(pats):
                span = 128 * dil
                for n in range(sb * SB // span, (sb + 1) * SB // span):
                    for r in range(dil):
                        blocks.append([sb, pi, dil, n, r, first, False])
                        first = False
            blocks[-1][6] = True

        def c0(i):
            sb, pi, dil, n, r, first, lastb = blocks[i]
            span = 128 * dil
            e = i % 3
            if first:
                S.pool.memset(acc[sb % 2][:], 0.0)
            qs = slice(n * span + r, (n + 1) * span, dil)
            if n >= 1:
                kprev = slice((n - 1) * span + r, n * span, dil)
                S.pe.matmul(pss[e][:, 0, :], kT[:, kprev], qT[:, qs], start=True, stop=False)
                S.pe.matmul(pss[e][:, 0, :], identb[:], mask_ge_b[:], start=False, stop=True)
            S.pe.matmul(pss[e][:, 1, :], kT[:, qs], qT[:, qs], start=True, stop=False)
            S.pe.matmul(pss[e][:, 1, :], identb[:], mask_le_b[:], start=False, stop=True)
            lo = 0 if n >= 1 else 1
            S.act.activation(E[e][:, lo:2, :], pss[e][:, lo:2, :], AF.Exp, scale=0.125)

        def c1(i):
            sb, pi, dil, n, r, first, lastb = blocks[i]
            span = 128 * dil
            e = i % 3
            a = acc[sb % 2]
            kbs = ([(0, (n - 1) * dil + r)] if n >= 1 else []) + [(1, n * dil + r)]
            for ii, (slot, blk) in enumerate(kbs):
                S.pe.matmul(pso[e][:], vd[pi][:, blk, :], E[e][:, slot, :], start=(ii == 0), stop=(ii == len(kbs) - 1))
            loc = slice(n * span + r - sb * SB, (n + 1) * span - sb * SB, dil)
            S.dve.tensor_tensor(a[:, loc], a[:, loc], pso[e][:], ALU.add)
            if lastb:
                S.act.activation(rd[:], a[64:128, :], AF.Ln)
                S.act.activation(rd[:], rd[:], AF.Exp, scale=-1.0)
                S.dve.tensor_tensor(og[:], a[0:64, :], rd[:], ALU.mult)
                PW = min(SB, 512)
                for pc_ in range(SB // PW):
                    S.sp.dma_start(out=outf(128, sb * SB + pc_ * PW, sb * SB + (pc_ + 1) * PW), in_=og[:, pc_ * PW:(pc_ + 1) * PW])

        _pipeline(len(blocks), [c0, c1], lag=2)


import math as _math

ROPE_THETA = 500000.0


def _rope_tables(SEQ):
    pos = np.arange(SEQ, dtype=np.float32)

    def tab(rot, blk):
        half = rot // 2
        inv = (np.float32(ROPE_THETA) ** (-(np.arange(0, rot, 2, dtype=np.float32)) / np.float32(rot))).astype(np.float32)
        ang = (pos[:, None] * inv[None, :]).astype(np.float32)
        c, s = np.cos(ang).astype(np.float32).T, np.sin(ang).astype(np.float32).T
        C = np.ones((blk, SEQ), np.float32)
        Sn = np.zeros((blk, SEQ), np.float32)
        C[0:half] = c
        C[half:2 * half] = c
        Sn[0:half] = -s
        Sn[half:2 * half] = s
        return C, Sn
    Ca, Sa = tab(8, 32)
    Cc, Sc = tab(16, 64)
    ropeA = np.stack([np.concatenate([Ca, Ca], 0), np.concatenate([Sa, Sa], 0)]).astype(np.float32)
    ropeC = np.stack([Cc, Sc]).astype(np.float32)
    return np.ascontiguousarray(ropeA), np.ascontiguousarray(ropeC)


def _perm_idx(base, n, half):
    idx = np.arange(n)
    d = idx.copy()
    d[0:half] = idx[0:half] + half
    d[half:2 * half] = idx[half:2 * half] - half
    return base + d


def prep_mixer_inputs(inp, l, h, SEQ, ropes):
    w_in = inp["w_in"][l]
    c = lambda off: off + h * 64 + np.arange(64)
    aq, ak, av = c(0), c(256), c(512)
    bu = c(768)
    cq, ck, cv = c(1280), c(1536), c(1792)
    dq, dk, dv, do_ = c(2048), c(2304), c(2560), c(2816)
    aqp = np.concatenate([_perm_idx(aq[0], 32, 4), _perm_idx(aq[32], 32, 4)])
    akp = np.concatenate([_perm_idx(ak[0], 32, 4), _perm_idx(ak[32], 32, 4)])
    cqp = _perm_idx(cq[0], 64, 8)
    ckp = _perm_idx(ck[0], 64, 8)
    di, df = 3072 + h, 3076 + h
    g6 = np.concatenate([[di, df], np.full(126, di)])
    fm_cols = np.concatenate([aq, ak, aqp, akp, cq, ck, cqp, ckp, dq, dk, g6])
    bv_all = 1024 + np.concatenate([h * 64 + np.arange(64)] + [g * 64 + np.arange(64) for g in range(4) if g != h])
    tm_cols = np.concatenate([av, cv, dv, do_, bu, bv_all, [di, df, di, df]])
    assert fm_cols.size == 768 and tm_cols.size == NTM
    lam_init = 0.8 - 0.6 * _math.exp(-0.3 * l)
    pvec = np.zeros((128, 16), np.float32)
    chan = np.concatenate([h * 64 + np.arange(64), 256 + h * 64 + np.arange(64)])
    pvec[:, 0:4] = inp["mlstm_conv_w"][l][:, chan].T
    pvec[:, 4] = inp["mlstm_conv_b"][l][chan]
    pvec[:, 5] = np.tile(inp["diff_subln_g"][l], 2)
    pvec[:, 6] = 1.0 - lam_init
    pvec[:, 7] = lam_init
    pvec[:, 8] = inp["mlstm_gate_b"][l][0, h]
    pvec[:, 9] = inp["mlstm_gate_b"][l][1, h]
    pvec[:, 10] = inp["sgu_b"][l][h]
    bvec = np.zeros(704, np.float32)
    bvec[0:64] = inp["sgu_ln_g"][l][h * 64:(h + 1) * 64]
    bvec[64:128] = inp["sgu_ln_b"][l][h * 64:(h + 1) * 64]
    bvec[128:192] = inp["mlstm_norm_g"][l]
    bvec[192:320] = inp["diff_lambda"][l].reshape(-1)
    return {
        "w_fm": np.ascontiguousarray(w_in[:, fm_cols]),
        "w_tm": np.ascontiguousarray(w_in[:, tm_cols]),
        "ropeA": ropes[0], "ropeC": ropes[1],
        "pvec": pvec, "bvec": bvec,
        "sgu_wT": np.ascontiguousarray(inp["sgu_w"][l][h].T),
    }


def _mixer_C(S, nc, SEQ, sc_qc, sc_kc, sc_tm, identb, mask_le_b, mask_ge_b, outf):
    pats = (1, 4, 16)
    SB = 2048 if SEQ >= 2048 else SEQ
    NSB = SEQ // SB
    NBLK = SEQ // 128
    with ExitStack() as st:
        qT = _sb(st, nc, "C_qT", [64, SEQ], BF16)
        kT = _sb(st, nc, "C_kT", [64, SEQ], BF16)
        vd = [_sb(st, nc, "C_vd%d" % i, [128, NBLK, 128], BF16) for i in range(3)]
        acc = [_sb(st, nc, "C_acc%d" % i, [128, SB], F32) for i in range(2)]
        E = [_sb(st, nc, "C_E%d" % i, [128, 2, 128], BF16) for i in range(3)]
        rd = _sb(st, nc, "C_rd", [64, SB], F32)
        og = _sb(st, nc, "C_og", [64, SB], BF16)
        pss = [_ps(st, nc, "C_pss%d" % i, [128, 2, 128]) for i in range(3)]
        pso = [_ps(st, nc, "C_pso%d" % i, [128, 128]) for i in range(3)]
        for i in range(3):
            S.set_gran(vd[i], 128)
        S.sp.dma_start(out=qT[:], in_=sc_qc)
        S.sp.dma_start(out=kT[:], in_=sc_kc)
        for pi, dil in enumerate(pats):
            S.pool.memset(vd[pi][:, :, 64:128], 1.0)
            nb = SEQ // (128 * dil)
            src = sc_tm[:, 64:128].rearrange("(n j r) d -> j n r d", j=128, r=dil)
            dst = vd[pi][:, :, 0:64].rearrange("j (n r) d -> j n r d", r=dil)
            for n in range(nb):
                S.pool.dma_start(out=dst[:, n], in_=src[:, n])
        ei = 0
        for sb in range(NSB):
            a = acc[sb % 2]
            S.pool.memset(a[:], 0.0)
            for pi, dil in enumerate(pats):
                span = 128 * dil
                for n in range(sb * SB // span, (sb + 1) * SB // span):
                    for r in range(dil):
                        e = ei % 3
                        ei += 1
                        qs = slice(n * span + r, (n + 1) * span, dil)
                        kcur = qs
                        kbs = []
                        if n >= 1:
                            kprev = slice((n - 1) * span + r, n * span, dil)
                            S.pe.matmul(pss[e][:, 0, :], kT[:, kprev], qT[:, qs], start=True, stop=False)
                            S.pe.matmul(pss[e][:, 0, :], identb[:], mask_ge_b[:], start=False, stop=True)
                            kbs.append((0, (n - 1) * dil + r))
                        S.pe.matmul(pss[e][:, 1, :], kT[:, kcur], qT[:, qs], start=True, stop=False)
                        S.pe.matmul(pss[e][:, 1, :], identb[:], mask_le_b[:], start=False, stop=True)
                        kbs.append((1, n * dil + r))
                        lo = kbs[0][0]
                        S.act.activation(E[e][:, lo:2, :], pss[e][:, lo:2, :], AF.Exp, scale=0.125)
                        for ii, (slot, blk) in enumerate(kbs):
                            S.pe.matmul(pso[e][:], vd[pi][:, blk, :], E[e][:, slot, :], start=(ii == 0), stop=(ii == len(kbs) - 1))
                        loc = slice(n * span + r - sb * SB, (n + 1) * span - sb * SB, dil)
                        S.dve.tensor_tensor(a[:, loc], a[:, loc], pso[e][:], ALU.add)
            S.dve.reciprocal(rd[:], a[64:128, :])
            S.dve.tensor_tensor(og[:], a[0:64, :], rd[:], ALU.mult)
            PW = min(SB, 512)
            for pc_ in range(SB // PW):
                S.sp.dma_start(out=outf(128, sb * SB + pc_ * PW, sb * SB + (pc_ + 1) * PW), in_=og[:, pc_ * PW:(pc_ + 1) * PW])


def _mixer_D(S, nc, SEQ, sc_qkd, sc_g, sc_tm, pv, bv, ident, identb, mask_le, tri, outf):
    NCH = SEQ // 128
    with ExitStack() as st:
        qTb = _sb(st, nc, "D_qT", [64, SEQ], BF16)
        kTb = _sb(st, nc, "D_kT", [64, SEQ], BF16)
        sm = _sb(st, nc, "D_sm", [128, 8], F32)
        Mfull = _sb(st, nc, "D_Mfull", [NCH, 128], F32)
        OH = _sb(st, nc, "D_OH", [NCH, NCH, 128], F32)
        bc = _sb(st, nc, "D_bc", [128, 3, NCH], F32)
        Xcol = _sb(st, nc, "D_Xcol", [128, NCH], F32)
        rcol = _sb(st, nc, "D_rcol", [128, NCH], F32)
        wcol = _sb(st, nc, "D_wcol", [128, NCH], F32)
        mprev = bc[:, 0, :]
        aexp = bc[:, 2, :]
        S.dve.tensor_scalar_mul(sm[:, 0:1], pv[:, 9:10], -1.0)
        S.pool.memset(OH[:], 1.0)
        S.pool.affine_select(OH[:], OH[:], [[-1, NCH], [0, 128]], ALU.is_equal, 0.0, base=0, channel_multiplier=1)
        with ExitStack() as s1:
            gi = _sb(s1, nc, "D_gi", [NCH, 128], F32)
            gf = _sb(s1, nc, "D_gf", [NCH, 128], F32)
            Bc = _sb(s1, nc, "D_Bc", [NCH, 128], F32)
            rr = _sb(s1, nc, "D_rr", [NCH, 128], F32)
            Ml = _sb(s1, nc, "D_Ml", [NCH, 128], F32)
            Xn = _sb(s1, nc, "D_Xn", [NCH, 128], F32)
            zer = _sb(s1, nc, "D_zer", [NCH, 128], F32)
            col2 = _sb(s1, nc, "D_col2", [NCH, 2], F32)
            rows = _sb(s1, nc, "D_rows", [1, 2, NCH], F32)
            r3 = _sb(s1, nc, "D_r3", [1, 3, NCH], F32)
            mall = _sb(s1, nc, "D_mall", [1, NCH], F32)
            mpc = _sb(s1, nc, "D_mpc", [NCH, 1], F32)
            ones1 = _sb(s1, nc, "D_ones1", [1, 128], F32)
            pA = _ps(s1, nc, "D_pA", [128, 3 * NCH])
            pB = _ps(s1, nc, "D_pB", [128, 128])
            S.pool.memset(zer[:], 0.0)
            S.pool.memset(ones1[:], 1.0)
            S.sp.dma_start(out=gi[:], in_=sc_g[0].rearrange("(n t) -> n t", t=128))
            S.sp.dma_start(out=gf[:], in_=sc_g[1].rearrange("(n t) -> n t", t=128))
            S.act.activation(gf[:], gf[:], AF.Exp, scale=-1.0, bias=sm[0:NCH, 0:1])
            S.act.activation(gf[:], gf[:], AF.Ln, bias=1.0)
            S.dve.tensor_tensor_scan(Bc[:], gf[:], zer[:], 0.0, ALU.add, ALU.max)
            S.dve.scalar_tensor_tensor(rr[:], gi[:], pv[0:NCH, 8:9], Bc[:], ALU.add, ALU.add)
            S.dve.tensor_tensor_scan(Ml[:], rr[:], rr[:], -1e30, ALU.max, ALU.max)
            S.dve.tensor_copy(col2[:, 0:1], Ml[:, 127:128])
            S.dve.tensor_scalar_mul(col2[:, 1:2], Bc[:, 127:128], -1.0)
            S.pe.transpose(pA[0:1, 0:NCH], col2[:, 0:1], ident[0:NCH, 0:NCH])
            S.pe.transpose(pA[0:1, NCH:2 * NCH], col2[:, 1:2], ident[0:NCH, 0:NCH])
            S.dve.tensor_copy(rows[:].rearrange("p a n -> p (a n)"), pA[0:1, 0:2 * NCH])
            S.dve.tensor_tensor_scan(mall[:], rows[:, 0, :], rows[:, 1, :], 0.0, ALU.max, ALU.add)
            S.dve.memset(r3[:, 0, 0:1], 0.0)
            if NCH > 1:
                S.dve.tensor_copy(r3[:, 0, 1:NCH], mall[:, 0:NCH - 1])
            S.dve.tensor_tensor(r3[:, 1, :], r3[:, 0, :], rows[:, 0, :], ALU.max)
            S.dve.tensor_tensor(r3[:, 2, :], r3[:, 0, :], r3[:, 1, :], ALU.subtract)
            S.act.activation(r3[:, 2, :], r3[:, 2, :], AF.Exp)
            S.pe.matmul(pA[:, 0:3 * NCH], ones1[:], r3[:].rearrange("p a n -> p (a n)"), start=True, stop=True)
            S.dve.tensor_copy(bc[:].rearrange("p a n -> p (a n)"), pA[:, 0:3 * NCH])
            S.pe.transpose(pB[0:NCH, 0:1], r3[:, 0, :], ident[0:1, 0:1])
            S.dve.tensor_copy(mpc[:], pB[0:NCH, 0:1])
            S.dve.tensor_scalar(Mfull[:], Ml[:], mpc[:, 0:1], None, ALU.max)
            S.dve.tensor_tensor(Xn[:], Bc[:], Mfull[:], ALU.subtract)
            S.act.activation(Xn[:], Xn[:], AF.Exp)
            S.pe.transpose(pB[:, 0:NCH], Xn[:], ident[0:NCH, 0:NCH])
            S.dve.tensor_copy(Xcol[:], pB[:, 0:NCH])
            S.pe.transpose(pB[:, 0:NCH], rr[:], ident[0:NCH, 0:NCH])
            S.dve.tensor_copy(rcol[:], pB[:, 0:NCH])
            S.dve.tensor_tensor(wcol[:], rcol[:], bc[:, 1, :], ALU.subtract)
            S.act.activation(wcol[:], wcol[:], AF.Exp)
            S.dve.tensor_scalar_mul(wcol[:], wcol[:], 0.125)
        S.barrier()
        with ExitStack() as s2:
            xin = _sb(s2, nc, "D_xin", [128, SEQ + 4], F32)
            cacc = _sb(s2, nc, "D_cacc", [128, SEQ], F32)
            S.pool.memset(xin[:, 0:3], 0.0)
            S.sp.dma_start(out=xin[:, 3:SEQ + 3], in_=sc_qkd)
            S.dve.tensor_scalar(cacc[:], xin[:, 0:SEQ], pv[:, 0:1], None, ALU.mult)
            for j in range(1, 4):
                S.dve.scalar_tensor_tensor(cacc[:], xin[:, j:j + SEQ], pv[:, j:j + 1], cacc[:], ALU.mult, ALU.add)
            S.act.activation(qTb[:], cacc[0:64, :], AF.Silu, bias=pv[0:64, 4:5])
            S.act.activation(kTb[:], cacc[64:128, :], AF.Silu, bias=pv[64:128, 4:5])
        S.barrier()
        with ExitStack() as s3:
            vaug = _sb(s3, nc, "D_vaug", [128, NCH, 66], BF16)
            ktm = _sb(s3, nc, "D_ktm", [128, NCH, 64], BF16)
            kw = _sb(s3, nc, "D_kw", [128, NCH, 64], BF16)
            Cst = _sb(s3, nc, "D_Cst", [64, 66], F32)
            Cb = [_sb(s3, nc, "D_Cb%d" % i, [64, 66], BF16) for i in range(2)]
            tmp = [_sb(s3, nc, "D_tmp%d" % i, [128, 128], F32) for i in range(2)]
            pp = [_sb(s3, nc, "D_pp%d" % i, [128, 128], F32) for i in range(2)]
            swT = [_sb(s3, nc, "D_swT%d" % i, [128, 128], BF16) for i in range(2)]
            ech = [_sb(s3, nc, "D_ech%d" % i, [64, 128], F32) for i in range(2)]
            qe = [_sb(s3, nc, "D_qe%d" % i, [64, 128], BF16) for i in range(2)]
            dsg = [_sb(s3, nc, "D_dsg%d" % i, [128, 4, 64], F32) for i in range(2)]
            hh = [_sb(s3, nc, "D_hh%d" % i, [128, 64], F32) for i in range(2)]
            dd = [_sb(s3, nc, "D_dd%d" % i, [128, 4], F32) for i in range(2)]
            stats = [_sb(s3, nc, "D_st%d" % i, [128, 1, 6], F32) for i in range(2)]
            mv = [_sb(s3, nc, "D_mv%d" % i, [128, 4], F32) for i in range(2)]
            og = [_sb(s3, nc, "D_og%d" % i, [64, 512], BF16) for i in range(2)]
            pkt = [_ps(s3, nc, "D_pkt%d" % i, [128, 64], BF16) for i in range(1)]
            pqk = [_ps(s3, nc, "D_pqk%d" % i, [128, 128]) for i in range(2)]
            ph = [_ps(s3, nc, "D_ph%d" % i, [128, 66]) for i in range(2)]
            pc = _ps(s3, nc, "D_pc", [64, 66])
            pt = _ps(s3, nc, "D_pt", [64, 128])
            pM = _ps(s3, nc, "D_pM", [128, 128])
            S.set_gran(vaug, 66)
            S.set_gran(ktm, 64)
            S.pool.memset(vaug[:, :, 64:66], 1.0)
            S.pool.dma_start(out=vaug[:, :, 0:64], in_=sc_tm[:, 128:192].rearrange("(n s) d -> s n d", s=128))
            S.pool.memset(Cst[:], 0.0)
            for n in range(NCH):
                c = slice(n * 128, (n + 1) * 128)
                S.pe.transpose(pkt[0][:], kTb[:, c], identb[0:64, 0:64])
                S.act.copy(ktm[:, n, :], pkt[0][:])
            wb = wcol[:].unsqueeze(2).to_broadcast([128, NCH, 64])
            S.pool.tensor_tensor(kw[:], ktm[:], wb, ALU.mult)
            def d0(n):
                b = n % 2
                c = slice(n * 128, (n + 1) * 128)
                if n % 4 == 0:
                    g4 = (n // 4) % 2
                    nn = min(4, NCH - n)
                    S.sp.dma_start(out=dsg[g4][:, 0:nn, :],
                                   in_=sc_tm[n * 128:(n + nn) * 128, 192:256].rearrange("(n s) d -> s n d", s=128))
                    S.act.activation(dsg[g4][:, 0:nn, :], dsg[g4][:, 0:nn, :], AF.Exp, scale=-1.0)
                    S.pool.tensor_scalar_add(dsg[g4][:, 0:nn, :], dsg[g4][:, 0:nn, :], 1.0)
                    S.dve.reciprocal(dsg[g4][:, 0:nn, :], dsg[g4][:, 0:nn, :])
                S.pe.matmul(pqk[b][:], kTb[:, c], qTb[:, c], start=True, stop=True)
                S.pe.matmul(pM[:], OH[:, n, :], Mfull[:], start=True, stop=True)
                S.dve.scalar_tensor_tensor(tmp[b][:], mask_le[:], rcol[:, n:n + 1], pM[:], ALU.add, ALU.subtract)
                S.act.activation(pp[b][:], tmp[b][:], AF.Exp)
                S.dve.scalar_tensor_tensor(swT[b][:], pqk[b][:], 0.125, pp[b][:], ALU.mult, ALU.mult)
                if n > 0:
                    S.act.activation(ech[b][:], pM[0:64, :], AF.Exp, scale=-1.0, bias=mprev[0:64, n:n + 1])
                    S.pool.tensor_tensor(qe[b][:], qTb[:, c], ech[b][:], ALU.mult)

            def d1(n):
                b = n % 2
                S.pe.matmul(ph[b][:, 0:65], swT[b][:], vaug[:, n, 0:65], start=True, stop=(n == 0))
                if n > 0:
                    S.pe.matmul(ph[b][:, 0:65], qe[b][:], Cb[(n - 1) % 2][:, 0:65], start=False, stop=True)
                if n < NCH - 1:
                    S.pe.matmul(pc[:, 0:65], kw[:, n, :], vaug[:, n, 0:65], start=True, stop=True)
                    S.dve.scalar_tensor_tensor(Cst[:, 0:65], Cst[:, 0:65], aexp[0:64, n:n + 1], pc[:, 0:65], ALU.mult, ALU.add)
                    S.act.copy(Cb[n % 2][:, 0:65], Cst[:, 0:65])

            def d2(n):
                b = n % 2
                S.dve.tensor_scalar_mul(dd[b][:, 3:4], ph[b][:, 64:65], -1.0)
                S.dve.tensor_tensor(dd[b][:, 0:1], dd[b][:, 3:4], ph[b][:, 64:65], ALU.max)
                S.dve.tensor_tensor(dd[b][:, 1:2], dd[b][:, 0:1], Xcol[:, n:n + 1], ALU.max)
                S.dve.reciprocal(dd[b][:, 2:3], dd[b][:, 1:2])
                S.dve.tensor_scalar(hh[b][:], ph[b][:, 0:64], dd[b][:, 2:3], None, ALU.mult)
                S.dve.bn_stats(stats[b][:, 0, :], hh[b][:])
                S.dve.bn_aggr(mv[b][:, 0:2], stats[b][:])
                S.dve.tensor_scalar_add(mv[b][:, 2:3], mv[b][:, 1:2], LN_EPS)
                S.act.activation(mv[b][:, 2:3], mv[b][:, 2:3], AF.Ln)
                S.act.activation(mv[b][:, 3:4], mv[b][:, 2:3], AF.Exp, scale=-0.5)
                S.dve.tensor_scalar(hh[b][:], hh[b][:], mv[b][:, 0:1], mv[b][:, 3:4], ALU.subtract, ALU.mult)
                S.pool.tensor_tensor(hh[b][:], hh[b][:], bv[:, 128:192], ALU.mult)
                S.pool.tensor_tensor(hh[b][:], hh[b][:], dsg[(n // 4) % 2][:, n % 4, :], ALU.mult)

            def d3(n):
                b = n % 2
                S.pe.transpose(pt[:], hh[b][:], ident[:])
                g = (n // 4) % 2
                S.act.copy(og[g][:, (n % 4) * 128:(n % 4 + 1) * 128], pt[:])
                if n % 4 == 3 or n == NCH - 1:
                    n0 = (n // 4) * 4
                    S.sp.dma_start(out=outf(192, n0 * 128, (n + 1) * 128), in_=og[g][:, 0:(n - n0 + 1) * 128])

            _pipeline(NCH, [d0, d1, d2, d3])


I32 = mybir.dt.int32
GROUPS = [[0, 1, 2, 3], [4, 5, 6, 7]]


def build_fused(SEQ, depth=DEPTH, do=("A", "B", "C", "D"), ffn=True, exch=True):
    T = SEQ // 4
    NTILE = SEQ // 128
    nc = bass.Bass("TRN2", target_bir_lowering=False)
    L = depth
    dr = lambda name, shape, dt=F32: nc.dram_tensor(name, shape, dt, kind="ExternalInput").ap()
    x0g = dr("x0g", [4 * D_MODEL * (T // 512), 512])
    xres0 = dr("xres0", [T, D_MODEL])
    w_fm = dr("w_fm", [L, D_MODEL, 768])
    w_tm = dr("w_tm", [L, D_MODEL, NTM])
    ropeA = dr("ropeA", [2, 64, SEQ])
    ropeC = dr("ropeC", [2, 64, SEQ])
    pvec = dr("pvec", [L, 128, 16])
    bvec = dr("bvec", [L, 704])
    sgu_wT = dr("sgu_wT", [L, 128, 128])
    w_out = dr("w_out", [L, D_MODEL, D_MODEL])
    w_gate = dr("w_gate", [L, D_MODEL, D_FF])
    w_up = dr("w_up", [L, D_MODEL, D_FF])
    w_down = dr("w_down", [L, D_FF, D_MODEL])
    lnp = dr("lnp", [L, 4, D_MODEL])
    gidx = dr("gidx", [128, 8], I32)
    y = nc.dram_tensor("y", [T, D_MODEL], F32, kind="ExternalOutput").ap()
    P0 = dict(
        sc_qa=nc.dram_tensor("sc_qa", [64, SEQ], BF16).ap(), sc_ka=nc.dram_tensor("sc_ka", [64, SEQ], BF16).ap(),
        sc_qc=nc.dram_tensor("sc_qc", [64, SEQ], BF16).ap(), sc_kc=nc.dram_tensor("sc_kc", [64, SEQ], BF16).ap(),
        sc_qkd=nc.dram_tensor("sc_qkd", [128, SEQ], F32).ap(), sc_g=nc.dram_tensor("sc_g", [2, SEQ], F32).ap(),
        sc_tm=nc.dram_tensor("sc_tm", [SEQ, NTM], F32).ap())
    NTC = T // 512
    mixb = nc.dram_tensor("mixb", [256, SEQ], BF16).ap()
    mixg = nc.dram_tensor("mixg", [4 * 256, SEQ], BF16).ap()
    yTb = nc.dram_tensor("yTb", [NTC * D_MODEL, 512], BF16).ap()
    xTg = nc.dram_tensor("xTg", [NTC * 4 * D_MODEL, 512], BF16).ap()
    xres_d = nc.dram_tensor("xres_d", [T, D_MODEL], F32).ap()
    S = Sched(nc)
    S.set_gran(mixb, 64 * SEQ)
    S.set_gran(mixg, 256 * SEQ)
    S.set_gran(yTb, D_MODEL * 512)
    S.set_gran(xTg, 4 * D_MODEL * 512)
    with ExitStack() as stc:
        _PFX[0] = ""
        C = _consts(S, nc, stc)
        gi = _sb(stc, nc, "gidx_sb", [128, 8], I32)
        S.sp.dma_start(out=gi[:], in_=gidx)
        mixtab = mixg.rearrange("f (n t) -> (f n) t", t=512)
        for l in range(L):
            _PFX[0] = "L%d_" % l
            xg, xeng = (x0g, S.pool) if l == 0 else (xTg, S.sp)

            def xsrc(tt, xg=xg, xeng=xeng):
                r, tc = divmod(tt, NTC)
                r0 = (tc * 4 + r) * D_MODEL
                return xeng, xg[r0:r0 + D_MODEL, :].rearrange("(c p) t -> p c t", p=128)

            def outf(r0, c0, c1):
                return mixb[r0:r0 + 64, c0:c1]

            def after(m):
                if exch:
                    cw = min(SEQ, 2048)
                    S.collective("AllGather", mixb[64 * m:64 * m + 64, :].rearrange("r (a b) -> (r a) b", b=cw),
                                 mixg[256 * m:256 * m + 256, :].rearrange("r (a b) -> (r a) b", b=cw), GROUPS)
            P = dict(P0)
            P.update(xsrc=xsrc, w_fm=w_fm[l], w_tm=w_tm[l], ropeA=ropeA, ropeC=ropeC, pvec=pvec[l], bvec=bvec[l],
                     sgu_wT=sgu_wT[l], outf=outf, after=after,
                     tile_order=[r * NTC + tc for tc in range(NTC) for r in range(4)])
            _mixer_all(S, nc, SEQ, P, C, do)
            S.barrier()
            last = (l == L - 1)

            def mix_gather(tile, i):
                for c in range(8):
                    m = c // 2
                    S.gather_rows(tile[:, c, :], mixtab, gi[:, c:c + 1], i * 512, dep_ap=mixg[256 * m:256 * m + 256, :])

            def yT(i):
                tc = i // 4
                return yTb[tc * D_MODEL:(tc + 1) * D_MODEL, :].rearrange("(c p) t -> p c t", p=128)[:, :, (i % 4) * 128:(i % 4 + 1) * 128]

            def after_tile(i):
                if i % 4 == 3 and exch:
                    tc = i // 4
                    S.collective("AllGather", yTb[tc * D_MODEL:(tc + 1) * D_MODEL, :],
                                 xTg[tc * 4 * D_MODEL:(tc + 1) * 4 * D_MODEL, :], GROUPS)
            PF = dict(mix_gather=mix_gather, xres=(xres0 if l == 0 else xres_d),
                      w_out=w_out[l], w_gate=w_gate[l], w_up=w_up[l], w_down=w_down[l], lnp=lnp[l],
                      y=(y if last else xres_d), yT=(None if last else yT), after_tile=after_tile)
            if ffn:
                _ffn_all(S, nc, T, PF, C["ident"])
            S.barrier()
        S.finish()
    return nc


_PROGS = {}
BATCH = 2
SEQ_FULL = 8192
NCORES = 8


def _fused_inputs(inp, SEQ, depth):
    T = SEQ // 4
    ropes = _rope_tables(SEQ)
    x = inp["x"]
    hm = [[prep_mixer_inputs(inp, l, h, SEQ, ropes) for l in range(depth)] for h in range(4)]
    w_out_p = inp["w_out"][:depth]
    lnp = np.stack([np.stack([inp["ln1_g"][l], inp["ln1_b"][l], inp["ln2_g"][l], inp["ln2_b"][l]]) for l in range(depth)])
    maps = []
    for core in range(NCORES):
        b, j = divmod(core, 4)
        xb = x[b, :SEQ]
        NTC = T // 512
        x0g = np.ascontiguousarray(xb.reshape(4, NTC, 512, D_MODEL).transpose(1, 0, 3, 2).reshape(NTC * 4 * D_MODEL, 512))
        gidx = ((np.arange(8)[None, :] * 128 + np.arange(128)[:, None]) * (SEQ // 512) + j * (T // 512)).astype(np.int32)
        m = {
            "x0g": x0g, "xres0": np.ascontiguousarray(xb[j * T:(j + 1) * T]),
            "w_fm": np.stack([hm[j][l]["w_fm"] for l in range(depth)]),
            "w_tm": np.stack([hm[j][l]["w_tm"] for l in range(depth)]),
            "ropeA": ropes[0], "ropeC": ropes[1],
            "pvec": np.stack([hm[j][l]["pvec"] for l in range(depth)]),
            "bvec": np.stack([hm[j][l]["bvec"] for l in range(depth)]),
            "sgu_wT": np.stack([hm[j][l]["sgu_wT"] for l in range(depth)]),
            "w_out": w_out_p, "w_gate": inp["w_gate"][:depth], "w_up": inp["w_up"][:depth], "w_down": inp["w_down"][:depth],
            "lnp": lnp.astype(np.float32), "gidx": gidx,
        }
        maps.append({k: np.ascontiguousarray(v) for k, v in m.items()})
    return maps


def run_fused(inp, SEQ, depth):
    key = ("fused", SEQ, depth)
    if key not in _PROGS:
        _PROGS[key] = build_fused(SEQ, depth)
    maps = _fused_inputs(inp, SEQ, depth)
    res = run_bass_kernel_spmd(_PROGS[key], maps, core_ids=list(range(NCORES)))
    T = SEQ // 4
    out = np.empty((BATCH, SEQ, D_MODEL), np.float32)
    for core in range(NCORES):
        b, j = divmod(core, 4)
        out[b, j * T:(j + 1) * T] = res.results[core]["y"]
    return out


def kernel(**inputs):
    inp = {k: np.asarray(v, dtype=np.float32) for k, v in inputs.items()}
    return run_fused(inp, SEQ_FULL, DEPTH)
```

```python
import numpy as np
import concourse.bass as bass
import concourse.mybir as mybir

F32 = mybir.dt.float32
BF16 = mybir.dt.bfloat16
AF = mybir.ActivationFunctionType
ALU = mybir.AluOpType
AX = mybir.AxisListType


class _Eng:
    def __init__(self, S, name, eng):
        self.S, self.name, self.eng = S, name, eng
        self.sem = S.nc.alloc_semaphore("es_" + name)
        self.cnt = 0
        self.seen = {}

    def __getattr__(self, op):
        fn = getattr(self.eng, op)

        def call(*args, **kw):
            return self.S._emit(self, op, fn, args, kw)
        return call


class Sched:
    def __init__(self, nc):
        self.nc = nc
        self.pe = _Eng(self, "pe", nc.tensor)
        self.act = _Eng(self, "act", nc.scalar)
        self.dve = _Eng(self, "dve", nc.vector)
        self.pool = _Eng(self, "pool", nc.gpsimd)
        self.sp = _Eng(self, "sp", nc.sync)
        self.engs = [self.pe, self.act, self.dve, self.pool, self.sp]
        self.units = {}
        self.gran = {}
        self.dma_sems = {}
        self.all_sems = {}
        self.n_inst = 0
        self.free_dma = []
        self.cc_sem = None
        self.cc_cnt = 0

    def collective(self, kind, in_ap, out_ap, groups):
        E = self.pool
        reads, writes = self._units(in_ap), self._units(out_ap)
        self._deps(E, reads, writes, same_raw=False)
        if self.cc_sem is None:
            self.cc_sem = self.nc.alloc_semaphore("cc_sem")
        inst = self.nc.gpsimd.collective_compute(kind, ALU.bypass, replica_groups=groups, ins=[in_ap], outs=[out_ap])
        self.cc_cnt += 1
        inst.then_inc(self.cc_sem, 1)
        self._record((self.cc_sem, self.cc_cnt), reads, writes)
        return inst

    def gather_rows(self, out_ap, table_ap, idx_ap, element_offset, dep_ap=None):
        E = self.pool
        reads = self._units(dep_ap if dep_ap is not None else table_ap) + self._units(idx_ap)
        writes = self._units(out_ap)
        skey = (writes[0][0], 0)
        ds = self._dma_sem(skey)
        saved = []
        for key in writes:
            u = self._u(key)
            for k_, t_ in list(u["w"].items()):
                if t_[0] is ds[0]:
                    saved.append((u, k_, t_))
                    del u["w"][k_]
        self._deps(E, reads, writes, same_raw=False)
        inst = self.nc.gpsimd.indirect_dma_start(out=out_ap, out_offset=None, in_=table_ap,
                                                 in_offset=bass.IndirectOffsetOnAxis(ap=idx_ap, axis=0),
                                                 element_offset=element_offset)
        ds[1] += 16
        inst.then_inc(ds[0], 16)
        self._record((ds[0], ds[1]), reads, writes)
        return inst

    def _dma_sem(self, skey):
        ds = self.dma_sems.get(skey)
        if ds is None:
            if self.free_dma:
                ds = self.free_dma.pop()
            else:
                ds = [self.nc.alloc_semaphore("ds%d" % len(self.all_sems)), 0]
            self.dma_sems[skey] = ds
        return ds

    def set_gran(self, t, g):
        self.gran[t.name if hasattr(t, "name") else t] = g

    def _units(self, ap):
        name = ap.tensor.name
        g = self.gran.get(name)
        if g is None:
            return [(name, 0)]
        apl = ap.ap
        space = str(ap.space)
        off = int(ap.offset)
        if "DRAM" in space:
            lo = off
            hi = off + sum((c - 1) * s for s, c in apl)
        else:
            F = 1
            for d in ap.tensor.shape[1:]:
                F *= d
            lo = off % F
            hi = lo + sum((c - 1) * s for s, c in apl[1:])
        return [(name, i) for i in range(lo // g, hi // g + 1)]

    def _u(self, key):
        u = self.units.get(key)
        if u is None:
            u = self.units[key] = {"w": {}, "r": {}}
        return u

    def _wait(self, E, tok):
        sem, val = tok
        k = id(sem)
        if E.seen.get(k, 0) >= val:
            return
        E.eng.wait_ge(sem, val)
        E.seen[k] = val

    def _deps(self, E, reads, writes, same_raw=True):
        toks = []
        for key in reads:
            u = self._u(key)
            toks += list(u["w"].values())
        for key in writes:
            u = self._u(key)
            toks += list(u["w"].values()) + list(u["r"].values())
        for sem, val in toks:
            if sem is E.sem:
                continue
            self._wait(E, (sem, val))
        if same_raw and E is not self.pe:
            for key in reads:
                u = self._u(key)
                for sem, val in u["w"].values():
                    if sem is E.sem:
                        self._wait(E, (sem, val))

    def _record(self, tok, reads, writes):
        sem, val = tok
        k = id(sem)
        for key in reads:
            self._u(key)["r"][k] = tok
        for key in writes:
            u = self._u(key)
            u["w"] = {k: tok}
            u["r"] = {}
        self.all_sems[k] = tok

    def _emit(self, E, op, fn, args, kw):
        if op in ("dma_start",):
            return self._dma(E, fn, args, kw)
        lazy = kw.pop("lazy", False)
        aps = []
        out = kw.get("out", None)
        outs = []
        first = True
        for a in list(args) + [v for k_, v in kw.items()]:
            if isinstance(a, bass.AP):
                aps.append(a)
        if out is not None:
            outs = [out]
        elif args and isinstance(args[0], bass.AP):
            outs = [args[0]]
        if kw.get("accum_out") is not None:
            outs.append(kw["accum_out"])
        out_ids = [id(o) for o in outs]
        reads, writes = [], []
        for a in aps:
            if id(a) in out_ids:
                writes += self._units(a)
            else:
                reads += self._units(a)
        self._deps(E, reads, writes)
        inst = fn(*args, **kw)
        if lazy and kw.get("stop", True) is False:
            self._record((E.sem, E.cnt + 1), reads, writes)
            self.n_inst += 1
            return inst
        E.cnt += 1
        inst.then_inc(E.sem, 1)
        self._record((E.sem, E.cnt), reads, writes)
        self.n_inst += 1
        return inst

    def _dma(self, E, fn, args, kw):
        out = kw.get("out", args[0] if args else None)
        in_ = kw.get("in_", args[1] if len(args) > 1 else None)
        writes = self._units(out)
        reads = self._units(in_)
        if "DRAM" not in str(out.space):
            skey = (writes[0][0], 0)
        elif "DRAM" not in str(in_.space):
            skey = (reads[0][0], 0)
        else:
            skey = (writes[0][0], 0)
        self._deps(E, reads, writes, same_raw=False)
        ds = self._dma_sem(skey)
        inst = fn(*args, **kw)
        ds[1] += 16
        inst.then_inc(ds[0], 16)
        self._record((ds[0], ds[1]), reads, writes)
        self.n_inst += 1
        return inst

    def barrier(self, final=False):
        toks = [t for t in self.all_sems.values() if (final or t[0] is not self.cc_sem)]
        for E in self.engs:
            for tok in toks:
                if tok[0] is E.sem:
                    continue
                self._wait(E, tok)
        keep = {}
        for key, u in self.units.items():
            w = {k: t for k, t in u["w"].items() if t[0] is self.cc_sem}
            r = {k: t for k, t in u["r"].items() if t[0] is self.cc_sem}
            if (w or r) and not final:
                keep[key] = {"w": w, "r": r}
        self.units = keep
        self.free_dma += list(self.dma_sems.values())
        self.dma_sems = {}

    def finish(self):
        self.barrier(final=True)


from contextlib import ExitStack
from concourse.bass_utils import run_bass_kernel_spmd

D_MODEL = 1024
D_FF = 2816
DEPTH = 4
ALPHA = (2 * DEPTH) ** 0.25
LN_EPS = 1e-5


_PFX = [""]


def _sb(st, nc, name, shape, dt):
    return st.enter_context(nc.sbuf_tensor(_PFX[0] + name, shape, dt))


def _ps(st, nc, name, shape, dt=F32):
    return st.enter_context(nc.psum_tensor(_PFX[0] + name, shape, dt))


def _pipeline(n, stages, lag=1):
    ns = len(stages)
    for step in range(n + (ns - 1) * lag):
        for si, f in enumerate(stages):
            i = step - si * lag
            if 0 <= i < n:
                f(i)


def _make_ident(S, nc, ident):
    S.pool.memset(ident[:], 1.0)
    S.pool.affine_select(ident[:], ident[:], [[-1, 128]], ALU.is_equal, 0.0, base=0, channel_multiplier=1)


def _layernorm_rows(S, nc, t, width, stats, mv, g_bc, b_bc, out):
    nch = width // 512
    for c in range(nch):
        S.dve.bn_stats(stats[:, c, :], t[:, c * 512:(c + 1) * 512])
    S.dve.bn_aggr(mv[:, 0:2], stats[:, 0:nch, :])
    S.dve.tensor_scalar_add(mv[:, 2:3], mv[:, 1:2], LN_EPS)
    S.act.sqrt(mv[:, 2:3], mv[:, 2:3])
    S.dve.reciprocal(mv[:, 3:4], mv[:, 2:3])
    S.dve.tensor_scalar(t, t, mv[:, 0:1], mv[:, 3:4], ALU.subtract, ALU.mult)
    S.pool.tensor_tensor(t, t, g_bc, ALU.mult)
    S.pool.tensor_tensor(out, t, b_bc, ALU.add)


def _ffn_all(S, nc, T, P, ident):
    NT = T // 128
    NTT = T // 512
    mix_gather, xres, w_out, w_gate, w_up, w_down, lnp, y, yT = (P[k] for k in (
        "mix_gather", "xres", "w_out", "w_gate", "w_up", "w_down", "lnp", "y", "yT"))
    with ExitStack() as st0:
        lnbc = _sb(st0, nc, "lnbc", [128, 4, D_MODEL], F32)
        yacc = _sb(st0, nc, "yacc", [128, NT, D_MODEL], F32)
        x1T = _sb(st0, nc, "x1T", [128, 8, T], BF16)
        S.set_gran(yacc, D_MODEL)
        S.set_gran(lnbc, D_MODEL)
        for j in range(4):
            S.sp.dma_start(out=lnbc[:, j, :], in_=lnp[j].partition_broadcast(128))
        with ExitStack() as st:
            wout_b = _sb(st, nc, "wout_b", [128, 8, D_MODEL], BF16)
            mixb = [_sb(st, nc, "mixb%d" % i, [128, 8, 512], BF16) for i in range(2)]
            xr = [_sb(st, nc, "xr%d" % i, [128, D_MODEL], F32) for i in range(2)]
            t1 = [_sb(st, nc, "t1_%d" % i, [128, D_MODEL], F32) for i in range(3)]
            stats = [_sb(st, nc, "stats%d" % i, [128, 2, 6], F32) for i in range(2)]
            mv = [_sb(st, nc, "mv%d" % i, [128, 4], F32) for i in range(2)]
            psh = [_ps(st, nc, "psh%d" % i, [128, D_MODEL]) for i in range(2)]
            pst = [_ps(st, nc, "pst%d" % i, [128, D_MODEL]) for i in range(2)]
            S.pool.dma_start(out=wout_b[:], in_=w_out.rearrange("(c p) f -> p c f", p=128))
            def f0(i):
                b = i % 2
                tsl = slice(i * 128, (i + 1) * 128)
                mb = mixb[(i // 4) % 2]
                if i % 4 == 0:
                    mix_gather(mb, i // 4)
                S.sp.dma_start(out=xr[b][:], in_=xres[tsl, :])
                for half in range(2):
                    for k in range(8):
                        S.pe.matmul(psh[b][:, half * 512:(half + 1) * 512], mb[:, k, (i % 4) * 128:(i % 4 + 1) * 128],
                                    wout_b[:, k, half * 512:(half + 1) * 512], start=(k == 0), stop=(k == 7), lazy=True)
                S.dve.scalar_tensor_tensor(t1[i % 3][:], xr[b][:], ALPHA, psh[b][:], ALU.mult, ALU.add)

            def f1(i):
                b = i % 2
                _layernorm_rows(S, nc, t1[i % 3][:], D_MODEL, stats[b], mv[b], lnbc[:, 0, :], lnbc[:, 1, :], t1[i % 3][:])
                S.act.mul(yacc[:, i, :], t1[i % 3][:], ALPHA)

            def f2(i):
                b = i % 2
                tsl = slice(i * 128, (i + 1) * 128)
                for c in range(8):
                    S.pe.transpose(pst[b][:, c * 128:(c + 1) * 128], t1[i % 3][:, c * 128:(c + 1) * 128], ident[:])
                S.act.copy(x1T[:, 0:4, tsl], pst[b][:, 0:512].rearrange("p (c t) -> p c t", c=4))
                S.dve.tensor_copy(x1T[:, 4:8, tsl], pst[b][:, 512:1024].rearrange("p (c t) -> p c t", c=4))

            _pipeline(NT, [f0, f1, f2])
        S.barrier()
        with ExitStack() as st:
            FB = 256
            NFB = D_FF // FB
            wg_b = [_sb(st, nc, "wg_b%d" % i, [128, 8, FB], BF16) for i in range(2)]
            wu_b = [_sb(st, nc, "wu_b%d" % i, [128, 8, FB], BF16) for i in range(2)]
            wd_b = [_sb(st, nc, "wd_b%d" % i, [128, 2, D_MODEL], BF16) for i in range(2)]
            sg = [_sb(st, nc, "sg%d" % i, [128, 512], F32) for i in range(2)]
            actT = [_sb(st, nc, "actT%d" % i, [128, 2, 512], BF16) for i in range(2)]
            psg = [_ps(st, nc, "psg%d" % i, [128, 512]) for i in range(2)]
            psu = [_ps(st, nc, "psu%d" % i, [128, 512]) for i in range(2)]
            psd = [_ps(st, nc, "psd%d" % i, [128, D_MODEL]) for i in range(2)]
            wg_v = w_gate.rearrange("(c p) f -> p c f", p=128)
            wu_v = w_up.rearrange("(c p) f -> p c f", p=128)
            wd_v = w_down.rearrange("(c p) f -> p c f", p=128)
            items = [(fb, tt) for fb in range(NFB) for tt in range(NTT)]

            def g0(it):
                fb, tt = items[it]
                wb = fb % 2
                ab = it % 2
                if tt == 0:
                    fsl = slice(fb * FB, (fb + 1) * FB)
                    S.pool.dma_start(out=wg_b[wb][:], in_=wg_v[:, :, fsl])
                    S.pool.dma_start(out=wu_b[wb][:], in_=wu_v[:, :, fsl])
                    S.pool.dma_start(out=wd_b[wb][:], in_=wd_v[:, 2 * fb:2 * fb + 2, :])
                tsl = slice(tt * 512, (tt + 1) * 512)
                for c2 in range(2):
                    for k in range(8):
                        S.pe.matmul(psg[c2][:], wg_b[wb][:, k, c2 * 128:(c2 + 1) * 128], x1T[:, k, tsl],
                                    start=(k == 0), stop=(k == 7), lazy=True)
                    for k in range(8):
                        S.pe.matmul(psu[c2][:], wu_b[wb][:, k, c2 * 128:(c2 + 1) * 128], x1T[:, k, tsl],
                                    start=(k == 0), stop=(k == 7), lazy=True)
                    S.act.activation(sg[c2][:], psg[c2][:], AF.Silu)
                    S.dve.tensor_tensor(actT[ab][:, c2, :], sg[c2][:], psu[c2][:], ALU.mult)

            def g1(it):
                fb, tt = items[it]
                wb = fb % 2
                ab = it % 2
                for s4 in range(4):
                    db = (it * 4 + s4) % 2
                    for half in range(2):
                        for c2 in range(2):
                            S.pe.matmul(psd[db][:, half * 512:(half + 1) * 512],
                                        actT[ab][:, c2, s4 * 128:(s4 + 1) * 128],
                                        wd_b[wb][:, c2, half * 512:(half + 1) * 512],
                                        start=(c2 == 0), stop=(c2 == 1), lazy=True)
                    ti = tt * 4 + s4
                    S.dve.tensor_tensor(yacc[:, ti, :], yacc[:, ti, :], psd[db][:], ALU.add)

            _pipeline(len(items), [g0, g1])
        S.barrier()
        with ExitStack() as st:
            stats = [_sb(st, nc, "stats3_%d" % i, [128, 2, 6], F32) for i in range(2)]
            mv = [_sb(st, nc, "mv3_%d" % i, [128, 4], F32) for i in range(2)]
            ob = [_sb(st, nc, "ob%d" % i, [128, D_MODEL], F32) for i in range(2)]
            obT = [_sb(st, nc, "obT%d" % i, [128, 8, 128], BF16) for i in range(2)]
            pst3 = [_ps(st, nc, "pst3_%d" % i, [128, D_MODEL]) for i in range(2)]
            def h0(i):
                b = i % 2
                _layernorm_rows(S, nc, yacc[:, i, :], D_MODEL, stats[b], mv[b], lnbc[:, 2, :], lnbc[:, 3, :], ob[b][:])

            def h1(i):
                b = i % 2
                S.sp.dma_start(out=y[i * 128:(i + 1) * 128, :], in_=ob[b][:])
                if yT is not None:
                    for c in range(8):
                        S.pe.transpose(pst3[b][:, c * 128:(c + 1) * 128], ob[b][:, c * 128:(c + 1) * 128], ident[:])
                    S.act.copy(obT[b][:, 0:4, :], pst3[b][:, 0:512].rearrange("p (c t) -> p c t", c=4))
                    S.dve.tensor_copy(obT[b][:, 4:8, :], pst3[b][:, 512:1024].rearrange("p (c t) -> p c t", c=4))
                    S.sp.dma_start(out=yT(i), in_=obT[b][:])
                    P["after_tile"](i)

            _pipeline(NT, [h0, h1])


HD = 64
NTM = 580
NEG = -30000.0


def _gelu_tanh(S, nc, out, x, tmp):
    S.act.activation(tmp, x, AF.Square)
    S.dve.tensor_scalar(tmp, tmp, 0.044715, 1.0, ALU.mult, ALU.add)
    S.pool.tensor_tensor(tmp, tmp, x, ALU.mult)
    S.act.activation(tmp, tmp, AF.Sigmoid, scale=2.0 * 0.7978845608028654)
    S.dve.tensor_tensor(out, x, tmp, ALU.mult)


def _consts(S, nc, st0):
    C = {}
    C["pv"] = pv = _sb(st0, nc, "pv", [128, 16], F32)
    C["bv"] = bv = _sb(st0, nc, "bv", [128, 704], F32)
    C["ident"] = ident = _sb(st0, nc, "identf", [128, 128], F32)
    C["identb"] = identb = _sb(st0, nc, "identb", [128, 128], BF16)
    C["mask_le"] = mask_le = _sb(st0, nc, "mask_le", [128, 128], F32)
    C["mask_le_b"] = mask_le_b = _sb(st0, nc, "mask_le_b", [128, 128], BF16)
    C["mask_ge_b"] = mask_ge_b = _sb(st0, nc, "mask_ge_b", [128, 128], BF16)
    C["tri"] = tri = _sb(st0, nc, "tri", [128, 128], F32)
    _make_ident(S, nc, ident)
    S.dve.tensor_copy(identb[:], ident[:])
    S.pool.memset(mask_le[:], 0.0)
    S.pool.affine_select(mask_le[:], mask_le[:], [[1, 128]], ALU.is_ge, NEG, base=0, channel_multiplier=-1)
    S.dve.tensor_copy(mask_le_b[:], mask_le[:])
    S.pool.memset(tri[:], 1.0)
    S.pool.affine_select(tri[:], tri[:], [[1, 128]], ALU.is_ge, 0.0, base=0, channel_multiplier=-1)
    S.pool.memset(mask_ge_b[:], 0.0)
    S.pool.affine_select(mask_ge_b[:], mask_ge_b[:], [[-1, 128]], ALU.is_ge, NEG, base=0, channel_multiplier=1)
    return C


def _mixer_all(S, nc, SEQ, P, C, do=("A", "B", "C", "D")):
    NTT = SEQ // 512
    xsrc, w_fm, w_tm, ropeA, ropeC, pvec, bvec, sgu_wT, outf = (P[k] for k in (
        "xsrc", "w_fm", "w_tm", "ropeA", "ropeC", "pvec", "bvec", "sgu_wT", "outf"))
    sc_qa, sc_ka, sc_qc, sc_kc, sc_qkd, sc_g, sc_tm = (P[k] for k in ("sc_qa", "sc_ka", "sc_qc", "sc_kc", "sc_qkd", "sc_g", "sc_tm"))
    pv, bv, ident, identb, mask_le, mask_le_b, mask_ge_b, tri = (C[k] for k in (
        "pv", "bv", "ident", "identb", "mask_le", "mask_le_b", "mask_ge_b", "tri"))
    S.sp.dma_start(out=pv[:], in_=pvec)
    S.sp.dma_start(out=bv[:], in_=bvec.partition_broadcast(128))
    if True:
        with ExitStack() as st:
            wfm_b = _sb(st, nc, "wfm_b", [128, 8, 768], BF16)
            wtm_b = _sb(st, nc, "wtm_b", [128, 8, NTM], BF16)
            xb = [_sb(st, nc, "xb%d" % i, [128, 8, 512], BF16) for i in range(2)]
            rA = [_sb(st, nc, "rA%d" % i, [64, 2, 512], F32) for i in range(2)]
            rC = [_sb(st, nc, "rC%d" % i, [64, 2, 512], F32) for i in range(2)]
            ta = [_sb(st, nc, "ta%d" % i, [64, 512], F32) for i in range(2)]
            tb = [_sb(st, nc, "tb%d" % i, [64, 512], F32) for i in range(2)]
            stg = [_sb(st, nc, "stg%d" % i, [64, 512], BF16) for i in range(4)]
            stg5 = [_sb(st, nc, "stg5_%d" % i, [128, 512], F32) for i in range(2)]
            stg6 = [_sb(st, nc, "stg6_%d" % i, [2, 512], F32) for i in range(2)]
            stgt = [_sb(st, nc, "stgt%d" % i, [128, NTM], F32) for i in range(2)]
            ps1 = [_ps(st, nc, "ps1_%d" % i, [128, 512]) for i in range(2)]
            ps2 = [_ps(st, nc, "ps2_%d" % i, [128, 512]) for i in range(2)]
            ps5 = _ps(st, nc, "ps5", [128, 512])
            pstm = _ps(st, nc, "pstm", [128, 1024])
            S.pool.dma_start(out=wfm_b[:], in_=w_fm.rearrange("(c p) f -> p c f", p=128))
            S.pool.dma_start(out=wtm_b[:], in_=w_tm.rearrange("(c p) f -> p c f", p=128))
            rA_v = ropeA.rearrange("two r t -> r two t")
            rC_v = ropeC.rearrange("two r t -> r two t")
            for it_, tt in enumerate(P.get("tile_order", range(NTT))):
                b = it_ % 2
                tsl = slice(tt * 512, (tt + 1) * 512)
                xe, xap = xsrc(tt)
                xe.dma_start(out=xb[b][:], in_=xap)
                S.sp.dma_start(out=rA[b][:], in_=rA_v[:, :, tsl])
                S.sp.dma_start(out=rC[b][:], in_=rC_v[:, :, tsl])
                def do_pair(pair):
                    rt, dq, dk = ((rA[b], sc_qa, sc_ka), (rC[b], sc_qc, sc_kc))[pair]
                    pb = (2 * it_ + pair) % 2
                    g1 = 2 * pair
                    for k in range(8):
                        S.pe.matmul(ps1[pb][:], wfm_b[:, k, g1 * 128:(g1 + 1) * 128], xb[b][:, k, :], start=(k == 0), stop=(k == 7), lazy=True)
                    for k in range(8):
                        S.pe.matmul(ps2[pb][:], wfm_b[:, k, (g1 + 1) * 128:(g1 + 2) * 128], xb[b][:, k, :], start=(k == 0), stop=(k == 7), lazy=True)
                    for half, dst in enumerate((dq, dk)):
                        rows = slice(half * 64, (half + 1) * 64)
                        tbuf = half
                        S.dve.tensor_tensor(ta[tbuf][:], ps2[pb][rows, :], rt[:, 1, :], ALU.mult)
                        S.dve.tensor_tensor(tb[tbuf][:], ps1[pb][rows, :], rt[:, 0, :], ALU.mult)
                        sb_ = (4 * it_ + 2 * pair + half) % 4
                        S.pool.tensor_tensor(stg[sb_][:], ta[tbuf][:], tb[tbuf][:], ALU.add)
                        S.sp.dma_start(out=dst[:, tsl], in_=stg[sb_][:])

                def do_g5():
                    for k in range(8):
                        S.pe.matmul(ps5[:], wfm_b[:, k, 512:640], xb[b][:, k, :], start=(k == 0), stop=(k == 7), lazy=True)
                    S.act.copy(stg5[b][:], ps5[:])
                    S.sp.dma_start(out=sc_qkd[:, tsl], in_=stg5[b][:])

                def do_g6():
                    for k in range(8):
                        S.pe.matmul(ps5[0:2, :], wfm_b[:, k, 640:642], xb[b][:, k, :], start=(k == 0), stop=(k == 7), lazy=True)
                    S.act.copy(stg6[b][:], ps5[0:2, :])
                    S.sp.dma_start(out=sc_g[:, tsl], in_=stg6[b][:])

                def do_sub(sub):
                    tb_ = (4 * it_ + sub) % 2
                    for k in range(8):
                        S.pe.matmul(pstm[:, 0:512], xb[b][:, k, sub * 128:(sub + 1) * 128], wtm_b[:, k, 0:512], start=(k == 0), stop=(k == 7), lazy=True)
                    for k in range(8):
                        S.pe.matmul(pstm[:, 512:NTM], xb[b][:, k, sub * 128:(sub + 1) * 128], wtm_b[:, k, 512:NTM], start=(k == 0), stop=(k == 7), lazy=True)
                    S.act.copy(stgt[tb_][:], pstm[:, 0:NTM])
                    r0 = tt * 512 + sub * 128
                    S.sp.dma_start(out=sc_tm[r0:r0 + 128, :], in_=stgt[tb_][:])

                do_pair(0)
                do_sub(0)
                do_g5()
                do_sub(1)
                do_pair(1)
                do_sub(2)
                do_g6()
                do_sub(3)
        S.barrier()
        if "B" in do:
            _mixer_B(S, nc, SEQ, sc_tm, sgu_wT, pv, bv, ident, tri, outf)
            P["after"](1)
            S.barrier()
        if "A" in do:
            _mixer_A(S, nc, SEQ, sc_qa, sc_ka, sc_tm, pv, bv, outf)
            P["after"](0)
            S.barrier()
        if "C" in do:
            _mixer_C(S, nc, SEQ, sc_qc, sc_kc, sc_tm, identb, mask_le_b, mask_ge_b, outf)
            P["after"](2)
            S.barrier()
        if "D" in do:
            _mixer_D(S, nc, SEQ, sc_qkd, sc_g, sc_tm, pv, bv, ident, identb, mask_le, tri, outf)
            P["after"](3)


def _mixer_B(S, nc, SEQ, sc_tm, sgu_wT, pv, bv, ident, tri, outf):
    NIT = SEQ // 512
    with ExitStack() as st:
        wT = _sb(st, nc, "sg_wT", [128, 128], F32)
        wTb = _sb(st, nc, "sg_wTb", [128, 128], BF16)
        uv = [_sb(st, nc, "sg_uv%d" % i, [128, 4, 320], F32) for i in range(2)]
        gl = [_sb(st, nc, "sg_gl%d" % i, [128, 4, 320], F32) for i in range(2)]
        tmp = [_sb(st, nc, "sg_tmp%d" % i, [128, 4, 320], F32) for i in range(2)]
        stats = [_sb(st, nc, "sg_st%d" % i, [128, 4, 6], F32) for i in range(2)]
        mv = [_sb(st, nc, "sg_mv%d" % i, [128, 4, 2], F32) for i in range(2)]
        rs = [_sb(st, nc, "sg_rs%d" % i, [128, 4], F32) for i in range(2)]
        vn = [_sb(st, nc, "sg_vn%d" % i, [128, 4, 64], F32) for i in range(2)]
        vnb = [_sb(st, nc, "sg_vnb%d" % i, [128, 4, 64], BF16) for i in range(2)]
        ob = [_sb(st, nc, "sg_ob%d" % i, [128, 4, 64], F32) for i in range(2)]
        og = [_sb(st, nc, "sg_og%d" % i, [64, 512], BF16) for i in range(2)]
        psz = [_ps(st, nc, "sg_psz%d" % i, [128, 4, 64]) for i in range(2)]
        pst = [_ps(st, nc, "sg_pst%d" % i, [64, 512]) for i in range(2)]
        S.sp.dma_start(out=wT[:], in_=sgu_wT)
        S.dve.tensor_tensor(wTb[:], wT[:], tri[:], ALU.mult)
        gbc = bv[:, 0:64].unsqueeze(1).to_broadcast([128, 4, 64])
        bbc = bv[:, 64:128].unsqueeze(1).to_broadcast([128, 4, 64])

        def b0(it):
            b = it % 2
            r0 = it * 512
            S.sp.dma_start(out=uv[b][:], in_=sc_tm[r0:r0 + 512, 256:576].rearrange("(n p) c -> p n c", p=128))
            _gelu_tanh(S, nc, gl[b][:], uv[b][:], tmp[b][:])
            for k in range(4):
                S.dve.bn_stats(stats[b][:, k, :], gl[b][:, k, 64:320])
                S.dve.bn_aggr(mv[b][:, k, :], stats[b][:, k:k + 1, :])
            S.dve.tensor_scalar_add(rs[b][:], mv[b][:, :, 1], LN_EPS)
            S.act.sqrt(rs[b][:], rs[b][:])
            S.dve.reciprocal(rs[b][:], rs[b][:])
            S.dve.tensor_tensor(vn[b][:], gl[b][:, :, 64:128], mv[b][:, :, 0:1].to_broadcast([128, 4, 64]), ALU.subtract)
            S.dve.tensor_tensor(vn[b][:], vn[b][:], rs[b][:].unsqueeze(2).to_broadcast([128, 4, 64]), ALU.mult)
            S.pool.tensor_tensor(vn[b][:], vn[b][:], gbc, ALU.mult)
            S.pool.tensor_tensor(vnb[b][:], vn[b][:], bbc, ALU.add)

        def b1(it):
            b = it % 2
            for k in range(4):
                S.pe.matmul(psz[b][:, k, :], wTb[:], vnb[b][:, k, :], start=True, stop=True)
            S.dve.scalar_tensor_tensor(ob[b][:], psz[b][:], pv[:, 10:11], gl[b][:, :, 0:64], ALU.add, ALU.mult)

        def b2(it):
            b = it % 2
            for k in range(4):
                S.pe.transpose(pst[b][:, k * 128:(k + 1) * 128], ob[b][:, k, :], ident[:])
            S.act.copy(og[b][:], pst[b][:])
            S.sp.dma_start(out=outf(64, it * 512, (it + 1) * 512), in_=og[b][:])

        _pipeline(NIT, [b0, b1, b2])


def _mixer_A(S, nc, SEQ, sc_qa, sc_ka, sc_tm, pv, bv, outf):
    NTT = SEQ // 512
    NKB = SEQ // 128
    scale = 32 ** -0.5
    with ExitStack() as st:
        qT = _sb(st, nc, "A_qT", [64, SEQ], BF16)
        kT = _sb(st, nc, "A_kT", [64, SEQ], BF16)
        va = _sb(st, nc, "A_va", [128, NKB, 128], BF16)
        lam = _sb(st, nc, "A_lam", [64, 8], F32)
        lt = _sb(st, nc, "A_lt", [64, 64], F32)
        ones_ms = _sb(st, nc, "A_ones", [64, 64], F32)
        E = [_sb(st, nc, "A_E%d" % i, [128, 512], BF16) for i in range(4)]
        rd = [_sb(st, nc, "A_rd%d" % i, [64, 512], F32) for i in range(2)]
        o0 = _sb(st, nc, "A_o0", [64, 512], F32)
        o1 = _sb(st, nc, "A_o1", [64, 512], F32)
        sq = _sb(st, nc, "A_sq", [64, 512], F32)
        og = [_sb(st, nc, "A_og%d" % i, [64, 512], BF16) for i in range(2)]
        pss = [_ps(st, nc, "A_pss%d" % i, [128, 512]) for i in range(4)]
        pso = [_ps(st, nc, "A_pso%d" % i, [128, 512]) for i in range(4)]
        psm = pss[0]
        S.set_gran(va, 128)
        S.sp.dma_start(out=qT[:], in_=sc_qa)
        S.sp.dma_start(out=kT[:], in_=sc_ka)
        S.pool.memset(va[:, :, 64:128], 1.0)
        S.pool.dma_start(out=va[:, :, 0:64], in_=sc_tm[:, 0:64].rearrange("(n p) d -> p n d", p=128))
        S.pool.memset(ones_ms[:], 1.0 / 64.0)
        S.dve.tensor_tensor(lt[:, 0:32], bv[0:64, 192:224], bv[0:64, 224:256], ALU.mult)
        S.dve.tensor_tensor(lt[:, 32:64], bv[0:64, 256:288], bv[0:64, 288:320], ALU.mult)
        S.dve.tensor_reduce(lam[:, 0:1], lt[:, 0:32], AX.X, ALU.add)
        S.dve.tensor_reduce(lam[:, 1:2], lt[:, 32:64], AX.X, ALU.add)
        S.act.activation(lam[:, 2:4], lam[:, 0:2], AF.Exp)
        S.dve.tensor_tensor(lam[:, 4:5], lam[:, 2:3], lam[:, 3:4], ALU.subtract)
        S.dve.tensor_tensor(lam[:, 4:5], lam[:, 4:5], pv[0:64, 7:8], ALU.add)
        S.dve.tensor_scalar_mul(lam[:, 5:6], lam[:, 4:5], -1.0)
        S.dve.tensor_tensor(lam[:, 6:7], pv[0:64, 5:6], pv[0:64, 6:7], ALU.mult)
        blocks = []
        for t in range(NTT):
            nkb = 4 * (t + 1)
            for kb in range(nkb):
                blocks.append((t, kb, nkb))
        LA = 1
        NB_ = len(blocks)

        def front(i):
            t, kb, nkb = blocks[i]
            q0 = t * 512
            j = kb - 4 * t
            c0 = max(j, 0) * 128
            for m in range(2):
                rows = slice(32 * m, 32 * m + 32)
                e = (2 * i + m) % 4
                S.pe.matmul(pss[e][:, c0:512], kT[rows, kb * 128:(kb + 1) * 128], qT[rows, q0 + c0:q0 + 512],
                            start=True, stop=True)
            for m in range(2):
                e = (2 * i + m) % 4
                S.act.activation(E[e][:, c0:512], pss[e][:, c0:512], AF.Exp, scale=scale)
                if j >= 0:
                    S.pool.affine_select(E[e][:, c0:c0 + 128], E[e][:, c0:c0 + 128], [[1, 128]], ALU.is_ge, 0.0,
                                         base=0, channel_multiplier=-1)

        def back(i):
            t, kb, nkb = blocks[i]
            j = kb - 4 * t
            c0 = max(j, 0) * 128
            for m in range(2):
                e = (2 * i + m) % 4
                po = pso[(2 * t + m) % 4]
                S.pe.matmul(po[:, c0:512], va[:, kb, :], E[e][:, c0:512], start=(kb == 0), stop=(kb == nkb - 1))
            if kb == nkb - 1:
                epilogue(t)

        def epilogue(t):
            q0 = t * 512
            p0 = pso[(2 * t) % 4]
            p1 = pso[(2 * t + 1) % 4]
            S.act.activation(rd[0][:], p0[64:128, :], AF.Ln)
            S.act.activation(rd[0][:], rd[0][:], AF.Exp, scale=-1.0)
            S.dve.tensor_tensor(o0[:], p0[0:64, :], rd[0][:], ALU.mult)
            S.act.activation(rd[1][:], p1[64:128, :], AF.Ln)
            S.act.activation(rd[1][:], rd[1][:], AF.Exp, scale=-1.0)
            S.dve.tensor_tensor(o1[:], p1[0:64, :], rd[1][:], ALU.mult)
            S.dve.scalar_tensor_tensor(o0[:], o1[:], lam[:, 5:6], o0[:], ALU.mult, ALU.add)
            S.pool.tensor_tensor(sq[:], o0[:], o0[:], ALU.mult)
            S.pe.matmul(psm[0:64, :], ones_ms[:], sq[:], start=True, stop=True)
            S.dve.tensor_scalar_add(sq[:], psm[0:64, :], LN_EPS)
            S.act.activation(sq[:], sq[:], AF.Ln)
            S.act.activation(sq[:], sq[:], AF.Exp, scale=-0.5)
            S.pool.tensor_tensor(o1[:], o0[:], sq[:], ALU.mult)
            S.dve.tensor_scalar(og[t % 2][:], o1[:], lam[:, 6:7], None, ALU.mult)
            S.sp.dma_start(out=outf(0, q0, q0 + 512), in_=og[t % 2][:])

        for i in range(NB_ + LA):
            if i < NB_:
                front(i)
            if i - LA >= 0:
                back(i - LA)


import math as _math

ROPE_THETA = 500000.0


def _rope_tables(SEQ):
    pos = np.arange(SEQ, dtype=np.float32)

    def tab(rot, blk):
        half = rot // 2
        inv = (np.float32(ROPE_THETA) ** (-(np.arange(0, rot, 2, dtype=np.float32)) / np.float32(rot))).astype(np.float32)
        ang = (pos[:, None] * inv[None, :]).astype(np.float32)
        c, s = np.cos(ang).astype(np.float32).T, np.sin(ang).astype(np.float32).T
        C = np.ones((blk, SEQ), np.float32)
        Sn = np.zeros((blk, SEQ), np.float32)
        C[0:half] = c
        C[half:2 * half] = c
        Sn[0:half] = -s
        Sn[half:2 * half] = s
        return C, Sn
    Ca, Sa = tab(8, 32)
    Cc, Sc = tab(16, 64)
    ropeA = np.stack([np.concatenate([Ca, Ca], 0), np.concatenate([Sa, Sa], 0)]).astype(np.float32)
    ropeC = np.stack([Cc, Sc]).astype(np.float32)
    return np.ascontiguousarray(ropeA), np.ascontiguousarray(ropeC)


def _perm_idx(base, n, half):
    idx = np.arange(n)
    d = idx.copy()
    d[0:half] = idx[0:half] + half
    d[half:2 * half] = idx[half:2 * half] - half
    return base + d


def prep_mixer_inputs(inp, l, h, SEQ, ropes):
    w_in = inp["w_in"][l]
    c = lambda off: off + h * 64 + np.arange(64)
    aq, ak, av = c(0), c(256), c(512)
    bu = c(768)
    cq, ck, cv = c(1280), c(1536), c(1792)
    dq, dk, dv, do_ = c(2048), c(2304), c(2560), c(2816)
    aqp = np.concatenate([_perm_idx(aq[0], 32, 4), _perm_idx(aq[32], 32, 4)])
    akp = np.concatenate([_perm_idx(ak[0], 32, 4), _perm_idx(ak[32], 32, 4)])
    cqp = _perm_idx(cq[0], 64, 8)
    ckp = _perm_idx(ck[0], 64, 8)
    di, df = 3072 + h, 3076 + h
    g6 = np.concatenate([[di, df], np.full(126, di)])
    fm_cols = np.concatenate([aq, ak, aqp, akp, cq, ck, cqp, ckp, dq, dk, g6])
    bv_all = 1024 + np.concatenate([h * 64 + np.arange(64)] + [g * 64 + np.arange(64) for g in range(4) if g != h])
    tm_cols = np.concatenate([av, cv, dv, do_, bu, bv_all, [di, df, di, df]])
    assert fm_cols.size == 768 and tm_cols.size == NTM
    lam_init = 0.8 - 0.6 * _math.exp(-0.3 * l)
    pvec = np.zeros((128, 16), np.float32)
    chan = np.concatenate([h * 64 + np.arange(64), 256 + h * 64 + np.arange(64)])
    pvec[:, 0:4] = inp["mlstm_conv_w"][l][:, chan].T
    pvec[:, 4] = inp["mlstm_conv_b"][l][chan]
    pvec[:, 5] = np.tile(inp["diff_subln_g"][l], 2)
    pvec[:, 6] = 1.0 - lam_init
    pvec[:, 7] = lam_init
    pvec[:, 8] = inp["mlstm_gate_b"][l][0, h]
    pvec[:, 9] = inp["mlstm_gate_b"][l][1, h]
    pvec[:, 10] = inp["sgu_b"][l][h]
    bvec = np.zeros(704, np.float32)
    bvec[0:64] = inp["sgu_ln_g"][l][h * 64:(h + 1) * 64]
    bvec[64:128] = inp["sgu_ln_b"][l][h * 64:(h + 1) * 64]
    bvec[128:192] = inp["mlstm_norm_g"][l]
    bvec[192:320] = inp["diff_lambda"][l].reshape(-1)
    return {
        "w_fm": np.ascontiguousarray(w_in[:, fm_cols]),
        "w_tm": np.ascontiguousarray(w_in[:, tm_cols]),
        "ropeA": ropes[0], "ropeC": ropes[1],
        "pvec": pvec, "bvec": bvec,
        "sgu_wT": np.ascontiguousarray(inp["sgu_w"][l][h].T),
    }


def _mixer_C(S, nc, SEQ, sc_qc, sc_kc, sc_tm, identb, mask_le_b, mask_ge_b, outf):
    pats = (1, 4, 16)
    SB = 2048 if SEQ >= 2048 else SEQ
    NSB = SEQ // SB
    NBLK = SEQ // 128
    with ExitStack() as st:
        qT = _sb(st, nc, "C_qT", [64, SEQ], BF16)
        kT = _sb(st, nc, "C_kT", [64, SEQ], BF16)
        vd = [_sb(st, nc, "C_vd%d" % i, [128, NBLK, 128], BF16) for i in range(3)]
        acc = [_sb(st, nc, "C_acc%d" % i, [128, SB], F32) for i in range(2)]
        E = [_sb(st, nc, "C_E%d" % i, [128, 2, 128], BF16) for i in range(3)]
        rd = _sb(st, nc, "C_rd", [64, SB], F32)
        og = _sb(st, nc, "C_og", [64, SB], BF16)
        pss = [_ps(st, nc, "C_pss%d" % i, [128, 2, 128]) for i in range(3)]
        pso = [_ps(st, nc, "C_pso%d" % i, [128, 128]) for i in range(3)]
        for i in range(3):
            S.set_gran(vd[i], 128)
        S.sp.dma_start(out=qT[:], in_=sc_qc)
        S.sp.dma_start(out=kT[:], in_=sc_kc)
        for pi, dil in enumerate(pats):
            S.pool.memset(vd[pi][:, :, 64:128], 1.0)
            nb = SEQ // (128 * dil)
            src = sc_tm[:, 64:128].rearrange("(n j r) d -> j n r d", j=128, r=dil)
            dst = vd[pi][:, :, 0:64].rearrange("j (n r) d -> j n r d", r=dil)
            for n in range(nb):
                S.pool.dma_start(out=dst[:, n], in_=src[:, n])
        blocks = []
        for sb in range(NSB):
            first = True
            for pi, dil in enumerate(pats):
                span = 128 * dil
                for n in range(sb * SB // span, (sb + 1) * SB // span):
                    for r in range(dil):
                        blocks.append([sb, pi, dil, n, r, first, False])
                        first = False
            blocks[-1][6] = True

        def c0(i):
            sb, pi, dil, n, r, first, lastb = blocks[i]
            span = 128 * dil
            e = i % 3
            if first:
                S.pool.memset(acc[sb % 2][:], 0.0)
            qs = slice(n * span + r, (n + 1) * span, dil)
            if n >= 1:
                kprev = slice((n - 1) * span + r, n * span, dil)
                S.pe.matmul(pss[e][:, 0, :], kT[:, kprev], qT[:, qs], start=True, stop=False)
                S.pe.matmul(pss[e][:, 0, :], identb[:], mask_ge_b[:], start=False, stop=True)
            S.pe.matmul(pss[e][:, 1, :], kT[:, qs], qT[:, qs], start=True, stop=False)
            S.pe.matmul(pss[e][:, 1, :], identb[:], mask_le_b[:], start=False, stop=True)
            lo = 0 if n >= 1 else 1
            S.act.activation(E[e][:, lo:2, :], pss[e][:, lo:2, :], AF.Exp, scale=0.125)

        def c1(i):
            sb, pi, dil, n, r, first, lastb = blocks[i]
            span = 128 * dil
            e = i % 3
            a = acc[sb % 2]
            kbs = ([(0, (n - 1) * dil + r)] if n >= 1 else []) + [(1, n * dil + r)]
            for ii, (slot, blk) in enumerate(kbs):
                S.pe.matmul(pso[e][:], vd[pi][:, blk, :], E[e][:, slot, :], start=(ii == 0), stop=(ii == len(kbs) - 1))
            loc = slice(n * span + r - sb * SB, (n + 1) * span - sb * SB, dil)
            S.dve.tensor_tensor(a[:, loc], a[:, loc], pso[e][:], ALU.add)
            if lastb:
                S.act.activation(rd[:], a[64:128, :], AF.Ln)
                S.act.activation(rd[:], rd[:], AF.Exp, scale=-1.0)
                S.dve.tensor_tensor(og[:], a[0:64, :], rd[:], ALU.mult)
                PW = min(SB, 512)
                for pc_ in range(SB // PW):
                    S.sp.dma_start(out=outf(128, sb * SB + pc_ * PW, sb * SB + (pc_ + 1) * PW), in_=og[:, pc_ * PW:(pc_ + 1) * PW])

        _pipeline(len(blocks), [c0, c1], lag=2)


import math as _math

ROPE_THETA = 500000.0


def _rope_tables(SEQ):
    pos = np.arange(SEQ, dtype=np.float32)

    def tab(rot, blk):
        half = rot // 2
        inv = (np.float32(ROPE_THETA) ** (-(np.arange(0, rot, 2, dtype=np.float32)) / np.float32(rot))).astype(np.float32)
        ang = (pos[:, None] * inv[None, :]).astype(np.float32)
        c, s = np.cos(ang).astype(np.float32).T, np.sin(ang).astype(np.float32).T
        C = np.ones((blk, SEQ), np.float32)
        Sn = np.zeros((blk, SEQ), np.float32)
        C[0:half] = c
        C[half:2 * half] = c
        Sn[0:half] = -s
        Sn[half:2 * half] = s
        return C, Sn
    Ca, Sa = tab(8, 32)
    Cc, Sc = tab(16, 64)
    ropeA = np.stack([np.concatenate([Ca, Ca], 0), np.concatenate([Sa, Sa], 0)]).astype(np.float32)
    ropeC = np.stack([Cc, Sc]).astype(np.float32)
    return np.ascontiguousarray(ropeA), np.ascontiguousarray(ropeC)


def _perm_idx(base, n, half):
    idx = np.arange(n)
    d = idx.copy()
    d[0:half] = idx[0:half] + half
    d[half:2 * half] = idx[half:2 * half] - half
    return base + d


def prep_mixer_inputs(inp, l, h, SEQ, ropes):
    w_in = inp["w_in"][l]
    c = lambda off: off + h * 64 + np.arange(64)
    aq, ak, av = c(0), c(256), c(512)
    bu = c(768)
    cq, ck, cv = c(1280), c(1536), c(1792)
    dq, dk, dv, do_ = c(2048), c(2304), c(2560), c(2816)
    aqp = np.concatenate([_perm_idx(aq[0], 32, 4), _perm_idx(aq[32], 32, 4)])
    akp = np.concatenate([_perm_idx(ak[0], 32, 4), _perm_idx(ak[32], 32, 4)])
    cqp = _perm_idx(cq[0], 64, 8)
    ckp = _perm_idx(ck[0], 64, 8)
    di, df = 3072 + h, 3076 + h
    g6 = np.concatenate([[di, df], np.full(126, di)])
    fm_cols = np.concatenate([aq, ak, aqp, akp, cq, ck, cqp, ckp, dq, dk, g6])
    bv_all = 1024 + np.concatenate([h * 64 + np.arange(64)] + [g * 64 + np.arange(64) for g in range(4) if g != h])
    tm_cols = np.concatenate([av, cv, dv, do_, bu, bv_all, [di, df, di, df]])
    assert fm_cols.size == 768 and tm_cols.size == NTM
    lam_init = 0.8 - 0.6 * _math.exp(-0.3 * l)
    pvec = np.zeros((128, 16), np.float32)
    chan = np.concatenate([h * 64 + np.arange(64), 256 + h * 64 + np.arange(64)])
    pvec[:, 0:4] = inp["mlstm_conv_w"][l][:, chan].T
    pvec[:, 4] = inp["mlstm_conv_b"][l][chan]
    pvec[:, 5] = np.tile(inp["diff_subln_g"][l], 2)
    pvec[:, 6] = 1.0 - lam_init
    pvec[:, 7] = lam_init
    pvec[:, 8] = inp["mlstm_gate_b"][l][0, h]
    pvec[:, 9] = inp["mlstm_gate_b"][l][1, h]
    pvec[:, 10] = inp["sgu_b"][l][h]
    bvec = np.zeros(704, np.float32)
    bvec[0:64] = inp["sgu_ln_g"][l][h * 64:(h + 1) * 64]
    bvec[64:128] = inp["sgu_ln_b"][l][h * 64:(h + 1) * 64]
    bvec[128:192] = inp["mlstm_norm_g"][l]
    bvec[192:320] = inp["diff_lambda"][l].reshape(-1)
    return {
        "w_fm": np.ascontiguousarray(w_in[:, fm_cols]),
        "w_tm": np.ascontiguousarray(w_in[:, tm_cols]),
        "ropeA": ropes[0], "ropeC": ropes[1],
        "pvec": pvec, "bvec": bvec,
        "sgu_wT": np.ascontiguousarray(inp["sgu_w"][l][h].T),
    }


def _mixer_C(S, nc, SEQ, sc_qc, sc_kc, sc_tm, identb, mask_le_b, mask_ge_b, outf):
    pats = (1, 4, 16)
    SB = 2048 if SEQ >= 2048 else SEQ
    NSB = SEQ // SB
    NBLK = SEQ // 128
    with ExitStack() as st:
        qT = _sb(st, nc, "C_qT", [64, SEQ], BF16)
        kT = _sb(st, nc, "C_kT", [64, SEQ], BF16)
        vd = [_sb(st, nc, "C_vd%d" % i, [128, NBLK, 128], BF16) for i in range(3)]
        acc = [_sb(st, nc, "C_acc%d" % i, [128, SB], F32) for i in range(2)]
        E = [_sb(st, nc, "C_E%d" % i, [128, 2, 128], BF16) for i in range(3)]
        rd = _sb(st, nc, "C_rd", [64, SB], F32)
        og = _sb(st, nc, "C_og", [64, SB], BF16)
        pss = [_ps(st, nc, "C_pss%d" % i, [128, 2, 128]) for i in range(3)]
        pso = [_ps(st, nc, "C_pso%d" % i, [128, 128]) for i in range(3)]
        for i in range(3):
            S.set_gran(vd[i], 128)
        S.sp.dma_start(out=qT[:], in_=sc_qc)
        S.sp.dma_start(out=kT[:], in_=sc_kc)
        for pi, dil in enumerate(pats):
            S.pool.memset(vd[pi][:, :, 64:128], 1.0)
            nb = SEQ // (128 * dil)
            src = sc_tm[:, 64:128].rearrange("(n j r) d -> j n r d", j=128, r=dil)
            dst = vd[pi][:, :, 0:64].rearrange("j (n r) d -> j n r d", r=dil)
            for n in range(nb):
                S.pool.dma_start(out=dst[:, n], in_=src[:, n])
        ei = 0
        for sb in range(NSB):
            a = acc[sb % 2]
            S.pool.memset(a[:], 0.0)
            for pi, dil in enumerate(pats):
                span = 128 * dil
                for n in range(sb * SB // span, (sb + 1) * SB // span):
                    for r in range(dil):
                        e = ei % 3
                        ei += 1
                        qs = slice(n * span + r, (n + 1) * span, dil)
                        kcur = qs
                        kbs = []
                        if n >= 1:
                            kprev = slice((n - 1) * span + r, n * span, dil)
                            S.pe.matmul(pss[e][:, 0, :], kT[:, kprev], qT[:, qs], start=True, stop=False)
                            S.pe.matmul(pss[e][:, 0, :], identb[:], mask_ge_b[:], start=False, stop=True)
                            kbs.append((0, (n - 1) * dil + r))
                        S.pe.matmul(pss[e][:, 1, :], kT[:, kcur], qT[:, qs], start=True, stop=False)
                        S.pe.matmul(pss[e][:, 1, :], identb[:], mask_le_b[:], start=False, stop=True)
                        kbs.append((1, n * dil + r))
                        lo = kbs[0][0]
                        S.act.activation(E[e][:, lo:2, :], pss[e][:, lo:2, :], AF.Exp, scale=0.125)
                        for ii, (slot, blk) in enumerate(kbs):
                            S.pe.matmul(pso[e][:], vd[pi][:, blk, :], E[e][:, slot, :], start=(ii == 0), stop=(ii == len(kbs) - 1))
                        loc = slice(n * span + r - sb * SB, (n + 1) * span - sb * SB, dil)
                        S.dve.tensor_tensor(a[:, loc], a[:, loc], pso[e][:], ALU.add)
            S.dve.reciprocal(rd[:], a[64:128, :])
            S.dve.tensor_tensor(og[:], a[0:64, :], rd[:], ALU.mult)
            PW = min(SB, 512)
            for pc_ in range(SB // PW):
                S.sp.dma_start(out=outf(128, sb * SB + pc_ * PW, sb * SB + (pc_ + 1) * PW), in_=og[:, pc_ * PW:(pc_ + 1) * PW])


def _mixer_D(S, nc, SEQ, sc_qkd, sc_g, sc_tm, pv, bv, ident, identb, mask_le, tri, outf):
    NCH = SEQ // 128
    with ExitStack() as st:
        qTb = _sb(st, nc, "D_qT", [64, SEQ], BF16)
        kTb = _sb(st, nc, "D_kT", [64, SEQ], BF16)
        sm = _sb(st, nc, "D_sm", [128, 8], F32)
        Mfull = _sb(st, nc, "D_Mfull", [NCH, 128], F32)
        OH = _sb(st, nc, "D_OH", [NCH, NCH, 128], F32)
        bc = _sb(st, nc, "D_bc", [128, 3, NCH], F32)
        Xcol = _sb(st, nc, "D_Xcol", [128, NCH], F32)
        rcol = _sb(st, nc, "D_rcol", [128, NCH], F32)
        wcol = _sb(st, nc, "D_wcol", [128, NCH], F32)
        mprev = bc[:, 0, :]
        aexp = bc[:, 2, :]
        S.dve.tensor_scalar_mul(sm[:, 0:1], pv[:, 9:10], -1.0)
        S.pool.memset(OH[:], 1.0)
        S.pool.affine_select(OH[:], OH[:], [[-1, NCH], [0, 128]], ALU.is_equal, 0.0, base=0, channel_multiplier=1)
        with ExitStack() as s1:
            gi = _sb(s1, nc, "D_gi", [NCH, 128], F32)
            gf = _sb(s1, nc, "D_gf", [NCH, 128], F32)
            Bc = _sb(s1, nc, "D_Bc", [NCH, 128], F32)
            rr = _sb(s1, nc, "D_rr", [NCH, 128], F32)
            Ml = _sb(s1, nc, "D_Ml", [NCH, 128], F32)
            Xn = _sb(s1, nc, "D_Xn", [NCH, 128], F32)
            zer = _sb(s1, nc, "D_zer", [NCH, 128], F32)
            col2 = _sb(s1, nc, "D_col2", [NCH, 2], F32)
            rows = _sb(s1, nc, "D_rows", [1, 2, NCH], F32)
            r3 = _sb(s1, nc, "D_r3", [1, 3, NCH], F32)
            mall = _sb(s1, nc, "D_mall", [1, NCH], F32)
            mpc = _sb(s1, nc, "D_mpc", [NCH, 1], F32)
            ones1 = _sb(s1, nc, "D_ones1", [1, 128], F32)
            pA = _ps(s1, nc, "D_pA", [128, 3 * NCH])
            pB = _ps(s1, nc, "D_pB", [128, 128])
            S.pool.memset(zer[:], 0.0)
            S.pool.memset(ones1[:], 1.0)
            S.sp.dma_start(out=gi[:], in_=sc_g[0].rearrange("(n t) -> n t", t=128))
            S.sp.dma_start(out=gf[:], in_=sc_g[1].rearrange("(n t) -> n t", t=128))
            S.act.activation(gf[:], gf[:], AF.Exp, scale=-1.0, bias=sm[0:NCH, 0:1])
            S.act.activation(gf[:], gf[:], AF.Ln, bias=1.0)
            S.dve.tensor_tensor_scan(Bc[:], gf[:], zer[:], 0.0, ALU.add, ALU.max)
            S.dve.scalar_tensor_tensor(rr[:], gi[:], pv[0:NCH, 8:9], Bc[:], ALU.add, ALU.add)
            S.dve.tensor_tensor_scan(Ml[:], rr[:], rr[:], -1e30, ALU.max, ALU.max)
            S.dve.tensor_copy(col2[:, 0:1], Ml[:, 127:128])
            S.dve.tensor_scalar_mul(col2[:, 1:2], Bc[:, 127:128], -1.0)
            S.pe.transpose(pA[0:1, 0:NCH], col2[:, 0:1], ident[0:NCH, 0:NCH])
            S.pe.transpose(pA[0:1, NCH:2 * NCH], col2[:, 1:2], ident[0:NCH, 0:NCH])
            S.dve.tensor_copy(rows[:].rearrange("p a n -> p (a n)"), pA[0:1, 0:2 * NCH])
            S.dve.tensor_tensor_scan(mall[:], rows[:, 0, :], rows[:, 1, :], 0.0, ALU.max, ALU.add)
            S.dve.memset(r3[:, 0, 0:1], 0.0)
            if NCH > 1:
                S.dve.tensor_copy(r3[:, 0, 1:NCH], mall[:, 0:NCH - 1])
            S.dve.tensor_tensor(r3[:, 1, :], r3[:, 0, :], rows[:, 0, :], ALU.max)
            S.dve.tensor_tensor(r3[:, 2, :], r3[:, 0, :], r3[:, 1, :], ALU.subtract)
            S.act.activation(r3[:, 2, :], r3[:, 2, :], AF.Exp)
            S.pe.matmul(pA[:, 0:3 * NCH], ones1[:], r3[:].rearrange("p a n -> p (a n)"), start=True, stop=True)
            S.dve.tensor_copy(bc[:].rearrange("p a n -> p (a n)"), pA[:, 0:3 * NCH])
            S.pe.transpose(pB[0:NCH, 0:1], r3[:, 0, :], ident[0:1, 0:1])
            S.dve.tensor_copy(mpc[:], pB[0:NCH, 0:1])
            S.dve.tensor_scalar(Mfull[:], Ml[:], mpc[:, 0:1], None, ALU.max)
            S.dve.tensor_tensor(Xn[:], Bc[:], Mfull[:], ALU.subtract)
            S.act.activation(Xn[:], Xn[:], AF.Exp)
            S.pe.transpose(pB[:, 0:NCH], Xn[:], ident[0:NCH, 0:NCH])
            S.dve.tensor_copy(Xcol[:], pB[:, 0:NCH])
            S.pe.transpose(pB[:, 0:NCH], rr[:], ident[0:NCH, 0:NCH])
            S.dve.tensor_copy(rcol[:], pB[:, 0:NCH])
            S.dve.tensor_tensor(wcol[:], rcol[:], bc[:, 1, :], ALU.subtract)
            S.act.activation(wcol[:], wcol[:], AF.Exp)
            S.dve.tensor_scalar_mul(wcol[:], wcol[:], 0.125)
        S.barrier()
        with ExitStack() as s2:
            xin = _sb(s2, nc, "D_xin", [128, SEQ + 4], F32)
            cacc = _sb(s2, nc, "D_cacc", [128, SEQ], F32)
            S.pool.memset(xin[:, 0:3], 0.0)
            S.sp.dma_start(out=xin[:, 3:SEQ + 3], in_=sc_qkd)
            S.dve.tensor_scalar(cacc[:], xin[:, 0:SEQ], pv[:, 0:1], None, ALU.mult)
            for j in range(1, 4):
                S.dve.scalar_tensor_tensor(cacc[:], xin[:, j:j + SEQ], pv[:, j:j + 1], cacc[:], ALU.mult, ALU.add)
            S.act.activation(qTb[:], cacc[0:64, :], AF.Silu, bias=pv[0:64, 4:5])
            S.act.activation(kTb[:], cacc[64:128, :], AF.Silu, bias=pv[64:128, 4:5])
        S.barrier()
        with ExitStack() as s3:
            vaug = _sb(s3, nc, "D_vaug", [128, NCH, 66], BF16)
            ktm = _sb(s3, nc, "D_ktm", [128, NCH, 64], BF16)
            kw = _sb(s3, nc, "D_kw", [128, NCH, 64], BF16)
            Cst = _sb(s3, nc, "D_Cst", [64, 66], F32)
            Cb = [_sb(s3, nc, "D_Cb%d" % i, [64, 66], BF16) for i in range(2)]
            tmp = [_sb(s3, nc, "D_tmp%d" % i, [128, 128], F32) for i in range(2)]
            pp = [_sb(s3, nc, "D_pp%d" % i, [128, 128], F32) for i in range(2)]
            swT = [_sb(s3, nc, "D_swT%d" % i, [128, 128], BF16) for i in range(2)]
            ech = [_sb(s3, nc, "D_ech%d" % i, [64, 128], F32) for i in range(2)]
            qe = [_sb(s3, nc, "D_qe%d" % i, [64, 128], BF16) for i in range(2)]
            dsg = [_sb(s3, nc, "D_dsg%d" % i, [128, 4, 64], F32) for i in range(2)]
            hh = [_sb(s3, nc, "D_hh%d" % i, [128, 64], F32) for i in range(2)]
            dd = [_sb(s3, nc, "D_dd%d" % i, [128, 4], F32) for i in range(2)]
            stats = [_sb(s3, nc, "D_st%d" % i, [128, 1, 6], F32) for i in range(2)]
            mv = [_sb(s3, nc, "D_mv%d" % i, [128, 4], F32) for i in range(2)]
            og = [_sb(s3, nc, "D_og%d" % i, [64, 512], BF16) for i in range(2)]
            pkt = [_ps(s3, nc, "D_pkt%d" % i, [128, 64], BF16) for i in range(1)]
            pqk = [_ps(s3, nc, "D_pqk%d" % i, [128, 128]) for i in range(2)]
            ph = [_ps(s3, nc, "D_ph%d" % i, [128, 66]) for i in range(2)]
            pc = _ps(s3, nc, "D_pc", [64, 66])
            pt = _ps(s3, nc, "D_pt", [64, 128])
            pM = _ps(s3, nc, "D_pM", [128, 128])
            S.set_gran(vaug, 66)
            S.set_gran(ktm, 64)
            S.pool.memset(vaug[:, :, 64:66], 1.0)
            S.pool.dma_start(out=vaug[:, :, 0:64], in_=sc_tm[:, 128:192].rearrange("(n s) d -> s n d", s=128))
            S.pool.memset(Cst[:], 0.0)
            for n in range(NCH):
                c = slice(n * 128, (n + 1) * 128)
                S.pe.transpose(pkt[0][:], kTb[:, c], identb[0:64, 0:64])
                S.act.copy(ktm[:, n, :], pkt[0][:])
            wb = wcol[:].unsqueeze(2).to_broadcast([128, NCH, 64])
            S.pool.tensor_tensor(kw[:], ktm[:], wb, ALU.mult)
            def d0(n):
                b = n % 2
                c = slice(n * 128, (n + 1) * 128)
                if n % 4 == 0:
                    g4 = (n // 4) % 2
                    nn = min(4, NCH - n)
                    S.sp.dma_start(out=dsg[g4][:, 0:nn, :],
                                   in_=sc_tm[n * 128:(n + nn) * 128, 192:256].rearrange("(n s) d -> s n d", s=128))
                    S.act.activation(dsg[g4][:, 0:nn, :], dsg[g4][:, 0:nn, :], AF.Exp, scale=-1.0)
                    S.pool.tensor_scalar_add(dsg[g4][:, 0:nn, :], dsg[g4][:, 0:nn, :], 1.0)
                    S.dve.reciprocal(dsg[g4][:, 0:nn, :], dsg[g4][:, 0:nn, :])
                S.pe.matmul(pqk[b][:], kTb[:, c], qTb[:, c], start=True, stop=True)
                S.pe.matmul(pM[:], OH[:, n, :], Mfull[:], start=True, stop=True)
                S.dve.scalar_tensor_tensor(tmp[b][:], mask_le[:], rcol[:, n:n + 1], pM[:], ALU.add, ALU.subtract)
                S.act.activation(pp[b][:], tmp[b][:], AF.Exp)
                S.dve.scalar_tensor_tensor(swT[b][:], pqk[b][:], 0.125, pp[b][:], ALU.mult, ALU.mult)
                if n > 0:
                    S.act.activation(ech[b][:], pM[0:64, :], AF.Exp, scale=-1.0, bias=mprev[0:64, n:n + 1])
                    S.pool.tensor_tensor(qe[b][:], qTb[:, c], ech[b][:], ALU.mult)

            def d1(n):
                b = n % 2
                S.pe.matmul(ph[b][:, 0:65], swT[b][:], vaug[:, n, 0:65], start=True, stop=(n == 0))
                if n > 0:
                    S.pe.matmul(ph[b][:, 0:65], qe[b][:], Cb[(n - 1) % 2][:, 0:65], start=False, stop=True)
                if n < NCH - 1:
                    S.pe.matmul(pc[:, 0:65], kw[:, n, :], vaug[:, n, 0:65], start=True, stop=True)
                    S.dve.scalar_tensor_tensor(Cst[:, 0:65], Cst[:, 0:65], aexp[0:64, n:n + 1], pc[:, 0:65], ALU.mult, ALU.add)
                    S.act.copy(Cb[n % 2][:, 0:65], Cst[:, 0:65])

            def d2(n):
                b = n % 2
                S.dve.tensor_scalar_mul(dd[b][:, 3:4], ph[b][:, 64:65], -1.0)
                S.dve.tensor_tensor(dd[b][:, 0:1], dd[b][:, 3:4], ph[b][:, 64:65], ALU.max)
                S.dve.tensor_tensor(dd[b][:, 1:2], dd[b][:, 0:1], Xcol[:, n:n + 1], ALU.max)
                S.dve.reciprocal(dd[b][:, 2:3], dd[b][:, 1:2])
                S.dve.tensor_scalar(hh[b][:], ph[b][:, 0:64], dd[b][:, 2:3], None, ALU.mult)
                S.dve.bn_stats(stats[b][:, 0, :], hh[b][:])
                S.dve.bn_aggr(mv[b][:, 0:2], stats[b][:])
                S.dve.tensor_scalar_add(mv[b][:, 2:3], mv[b][:, 1:2], LN_EPS)
                S.act.activation(mv[b][:, 2:3], mv[b][:, 2:3], AF.Ln)
                S.act.activation(mv[b][:, 3:4], mv[b][:, 2:3], AF.Exp, scale=-0.5)
                S.dve.tensor_scalar(hh[b][:], hh[b][:], mv[b][:, 0:1], mv[b][:, 3:4], ALU.subtract, ALU.mult)
                S.pool.tensor_tensor(hh[b][:], hh[b][:], bv[:, 128:192], ALU.mult)
                S.pool.tensor_tensor(hh[b][:], hh[b][:], dsg[(n // 4) % 2][:, n % 4, :], ALU.mult)

            def d3(n):
                b = n % 2
                S.pe.transpose(pt[:], hh[b][:], ident[:])
                g = (n // 4) % 2
                S.act.copy(og[g][:, (n % 4) * 128:(n % 4 + 1) * 128], pt[:])
                if n % 4 == 3 or n == NCH - 1:
                    n0 = (n // 4) * 4
                    S.sp.dma_start(out=outf(192, n0 * 128, (n + 1) * 128), in_=og[g][:, 0:(n - n0 + 1) * 128])

            _pipeline(NCH, [d0, d1, d2, d3])


I32 = mybir.dt.int32
GROUPS = [[0, 1, 2, 3], [4, 5, 6, 7]]


def build_fused(SEQ, depth=DEPTH, do=("A", "B", "C", "D"), ffn=True, exch=True):
    T = SEQ // 4
    NTILE = SEQ // 128
    nc = bass.Bass("TRN2", target_bir_lowering=False)
    L = depth
    dr = lambda name, shape, dt=F32: nc.dram_tensor(name, shape, dt, kind="ExternalInput").ap()
    x0g = dr("x0g", [4 * D_MODEL * (T // 512), 512])
    xres0 = dr("xres0", [T, D_MODEL])
    w_fm = dr("w_fm", [L, D_MODEL, 768])
    w_tm = dr("w_tm", [L, D_MODEL, NTM])
    ropeA = dr("ropeA", [2, 64, SEQ])
    ropeC = dr("ropeC", [2, 64, SEQ])
    pvec = dr("pvec", [L, 128, 16])
    bvec = dr("bvec", [L, 704])
    sgu_wT = dr("sgu_wT", [L, 128, 128])
    w_out = dr("w_out", [L, D_MODEL, D_MODEL])
    w_gate = dr("w_gate", [L, D_MODEL, D_FF])
    w_up = dr("w_up", [L, D_MODEL, D_FF])
    w_down = dr("w_down", [L, D_FF, D_MODEL])
    lnp = dr("lnp", [L, 4, D_MODEL])
    gidx = dr("gidx", [128, 8], I32)
    y = nc.dram_tensor("y", [T, D_MODEL], F32, kind="ExternalOutput").ap()
    P0 = dict(
        sc_qa=nc.dram_tensor("sc_qa", [64, SEQ], BF16).ap(), sc_ka=nc.dram_tensor("sc_ka", [64, SEQ], BF16).ap(),
        sc_qc=nc.dram_tensor("sc_qc", [64, SEQ], BF16).ap(), sc_kc=nc.dram_tensor("sc_kc", [64, SEQ], BF16).ap(),
        sc_qkd=nc.dram_tensor("sc_qkd", [128, SEQ], F32).ap(), sc_g=nc.dram_tensor("sc_g", [2, SEQ], F32).ap(),
        sc_tm=nc.dram_tensor("sc_tm", [SEQ, NTM], F32).ap())
    NTC = T // 512
    mixb = nc.dram_tensor("mixb", [256, SEQ], BF16).ap()
    mixg = nc.dram_tensor("mixg", [4 * 256, SEQ], BF16).ap()
    yTb = nc.dram_tensor("yTb", [NTC * D_MODEL, 512], BF16).ap()
    xTg = nc.dram_tensor("xTg", [NTC * 4 * D_MODEL, 512], BF16).ap()
    xres_d = nc.dram_tensor("xres_d", [T, D_MODEL], F32).ap()
    S = Sched(nc)
    S.set_gran(mixb, 64 * SEQ)
    S.set_gran(mixg, 256 * SEQ)
    S.set_gran(yTb, D_MODEL * 512)
    S.set_gran(xTg, 4 * D_MODEL * 512)
    with ExitStack() as stc:
        _PFX[0] = ""
        C = _consts(S, nc, stc)
        gi = _sb(stc, nc, "gidx_sb", [128, 8], I32)
        S.sp.dma_start(out=gi[:], in_=gidx)
        mixtab = mixg.rearrange("f (n t) -> (f n) t", t=512)
        for l in range(L):
            _PFX[0] = "L%d_" % l
            xg, xeng = (x0g, S.pool) if l == 0 else (xTg, S.sp)

            def xsrc(tt, xg=xg, xeng=xeng):
                r, tc = divmod(tt, NTC)
                r0 = (tc * 4 + r) * D_MODEL
                return xeng, xg[r0:r0 + D_MODEL, :].rearrange("(c p) t -> p c t", p=128)

            def outf(r0, c0, c1):
                return mixb[r0:r0 + 64, c0:c1]

            def after(m):
                if exch:
                    cw = min(SEQ, 2048)
                    S.collective("AllGather", mixb[64 * m:64 * m + 64, :].rearrange("r (a b) -> (r a) b", b=cw),
                                 mixg[256 * m:256 * m + 256, :].rearrange("r (a b) -> (r a) b", b=cw), GROUPS)
            P = dict(P0)
            P.update(xsrc=xsrc, w_fm=w_fm[l], w_tm=w_tm[l], ropeA=ropeA, ropeC=ropeC, pvec=pvec[l], bvec=bvec[l],
                     sgu_wT=sgu_wT[l], outf=outf, after=after,
                     tile_order=[r * NTC + tc for tc in range(NTC) for r in range(4)])
            _mixer_all(S, nc, SEQ, P, C, do)
            S.barrier()
            last = (l == L - 1)

            def mix_gather(tile, i):
                for c in range(8):
                    m = c // 2
                    S.gather_rows(tile[:, c, :], mixtab, gi[:, c:c + 1], i * 512, dep_ap=mixg[256 * m:256 * m + 256, :])

            def yT(i):
                tc = i // 4
                return yTb[tc * D_MODEL:(tc + 1) * D_MODEL, :].rearrange("(c p) t -> p c t", p=128)[:, :, (i % 4) * 128:(i % 4 + 1) * 128]

            def after_tile(i):
                if i % 4 == 3 and exch:
                    tc = i // 4
                    S.collective("AllGather", yTb[tc * D_MODEL:(tc + 1) * D_MODEL, :],
                                 xTg[tc * 4 * D_MODEL:(tc + 1) * 4 * D_MODEL, :], GROUPS)
            PF = dict(mix_gather=mix_gather, xres=(xres0 if l == 0 else xres_d),
                      w_out=w_out[l], w_gate=w_gate[l], w_up=w_up[l], w_down=w_down[l], lnp=lnp[l],
                      y=(y if last else xres_d), yT=(None if last else yT), after_tile=after_tile)
            if ffn:
                _ffn_all(S, nc, T, PF, C["ident"])
            S.barrier()
        S.finish()
    return nc


_PROGS = {}
BATCH = 2
SEQ_FULL = 8192
NCORES = 8


def _fused_inputs(inp, SEQ, depth):
    T = SEQ // 4
    ropes = _rope_tables(SEQ)
    x = inp["x"]
    hm = [[prep_mixer_inputs(inp, l, h, SEQ, ropes) for l in range(depth)] for h in range(4)]
    w_out_p = inp["w_out"][:depth]
    lnp = np.stack([np.stack([inp["ln1_g"][l], inp["ln1_b"][l], inp["ln2_g"][l], inp["ln2_b"][l]]) for l in range(depth)])
    maps = []
    for core in range(NCORES):
        b, j = divmod(core, 4)
        xb = x[b, :SEQ]
        NTC = T // 512
        x0g = np.ascontiguousarray(xb.reshape(4, NTC, 512, D_MODEL).transpose(1, 0, 3, 2).reshape(NTC * 4 * D_MODEL, 512))
        gidx = ((np.arange(8)[None, :] * 128 + np.arange(128)[:, None]) * (SEQ // 512) + j * (T // 512)).astype(np.int32)
        m = {
            "x0g": x0g, "xres0": np.ascontiguousarray(xb[j * T:(j + 1) * T]),
            "w_fm": np.stack([hm[j][l]["w_fm"] for l in range(depth)]),
            "w_tm": np.stack([hm[j][l]["w_tm"] for l in range(depth)]),
            "ropeA": ropes[0], "ropeC": ropes[1],
            "pvec": np.stack([hm[j][l]["pvec"] for l in range(depth)]),
            "bvec": np.stack([hm[j][l]["bvec"] for l in range(depth)]),
            "sgu_wT": np.stack([hm[j][l]["sgu_wT"] for l in range(depth)]),
            "w_out": w_out_p, "w_gate": inp["w_gate"][:depth], "w_up": inp["w_up"][:depth], "w_down": inp["w_down"][:depth],
            "lnp": lnp.astype(np.float32), "gidx": gidx,
        }
        maps.append({k: np.ascontiguousarray(v) for k, v in m.items()})
    return maps


def run_fused(inp, SEQ, depth):
    key = ("fused", SEQ, depth)
    if key not in _PROGS:
        _PROGS[key] = build_fused(SEQ, depth)
    maps = _fused_inputs(inp, SEQ, depth)
    res = run_bass_kernel_spmd(_PROGS[key], maps, core_ids=list(range(NCORES)))
    T = SEQ // 4
    out = np.empty((BATCH, SEQ, D_MODEL), np.float32)
    for core in range(NCORES):
        b, j = divmod(core, 4)
        out[b, j * T:(j + 1) * T] = res.results[core]["y"]
    return out


def kernel(**inputs):
    inp = {k: np.asarray(v, dtype=np.float32) for k, v in inputs.items()}
    return run_fused(inp, SEQ_FULL, DEPTH)
```

```python
import numpy as np
import concourse.bass as bass
import concourse.mybir as mybir

F32 = mybir.dt.float32
BF16 = mybir.dt.bfloat16
AF = mybir.ActivationFunctionType
ALU = mybir.AluOpType
AX = mybir.AxisListType


class _Eng:
    def __init__(self, S, name, eng):
        self.S, self.name, self.eng = S, name, eng
        self.sem = S.nc.alloc_semaphore("es_" + name)
        self.cnt = 0
        self.seen = {}

    def __getattr__(self, op):
        fn = getattr(self.eng, op)

        def call(*args, **kw):
            return self.S._emit(self, op, fn, args, kw)
        return call


class Sched:
    def __init__(self, nc):
        self.nc = nc
        self.pe = _Eng(self, "pe", nc.tensor)
        self.act = _Eng(self, "act", nc.scalar)
        self.dve = _Eng(self, "dve", nc.vector)
        self.pool = _Eng(self, "pool", nc.gpsimd)
        self.sp = _Eng(self, "sp", nc.sync)
        self.engs = [self.pe, self.act, self.dve, self.pool, self.sp]
        self.units = {}
        self.gran = {}
        self.dma_sems = {}
        self.all_sems = {}
        self.n_inst = 0
        self.free_dma = []
        self.cc_sem = None
        self.cc_cnt = 0

    def collective(self, kind, in_ap, out_ap, groups):
        E = self.pool
        reads, writes = self._units(in_ap), self._units(out_ap)
        self._deps(E, reads, writes, same_raw=False)
        if self.cc_sem is None:
            self.cc_sem = self.nc.alloc_semaphore("cc_sem")
        inst = self.nc.gpsimd.collective_compute(kind, ALU.bypass, replica_groups=groups, ins=[in_ap], outs=[out_ap])
        self.cc_cnt += 1
        inst.then_inc(self.cc_sem, 1)
        self._record((self.cc_sem, self.cc_cnt), reads, writes)
        return inst

    def gather_rows(self, out_ap, table_ap, idx_ap, element_offset, dep_ap=None):
        E = self.pool
        reads = self._units(dep_ap if dep_ap is not None else table_ap) + self._units(idx_ap)
        writes = self._units(out_ap)
        skey = (writes[0][0], 0)
        ds = self._dma_sem(skey)
        saved = []
        for key in writes:
            u = self._u(key)
            for k_, t_ in list(u["w"].items()):
                if t_[0] is ds[0]:
                    saved.append((u, k_, t_))
                    del u["w"][k_]
        self._deps(E, reads, writes, same_raw=False)
        inst = self.nc.gpsimd.indirect_dma_start(out=out_ap, out_offset=None, in_=table_ap,
                                                 in_offset=bass.IndirectOffsetOnAxis(ap=idx_ap, axis=0),
                                                 element_offset=element_offset)
        ds[1] += 16
        inst.then_inc(ds[0], 16)
        self._record((ds[0], ds[1]), reads, writes)
        return inst

    def _dma_sem(self, skey):
        ds = self.dma_sems.get(skey)
        if ds is None:
            if self.free_dma:
                ds = self.free_dma.pop()
            else:
                ds = [self.nc.alloc_semaphore("ds%d" % len(self.all_sems)), 0]
            self.dma_sems[skey] = ds
        return ds

    def set_gran(self, t, g):
        self.gran[t.name if hasattr(t, "name") else t] = g

    def _units(self, ap):
        name = ap.tensor.name
        g = self.gran.get(name)
        if g is None:
            return [(name, 0)]
        apl = ap.ap
        space = str(ap.space)
        off = int(ap.offset)
        if "DRAM" in space:
            lo = off
            hi = off + sum((c - 1) * s for s, c in apl)
        else:
            F = 1
            for d in ap.tensor.shape[1:]:
                F *= d
            lo = off % F
            hi = lo + sum((c - 1) * s for s, c in apl[1:])
        return [(name, i) for i in range(lo // g, hi // g + 1)]

    def _u(self, key):
        u = self.units.get(key)
        if u is None:
            u = self.units[key] = {"w": {}, "r": {}}
        return u

    def _wait(self, E, tok):
        sem, val = tok
        k = id(sem)
        if E.seen.get(k, 0) >= val:
            return
        E.eng.wait_ge(sem, val)
        E.seen[k] = val

    def _deps(self, E, reads, writes, same_raw=True):
        toks = []
        for key in reads:
            u = self._u(key)
            toks += list(u["w"].values())
        for key in writes:
            u = self._u(key)
            toks += list(u["w"].values()) + list(u["r"].values())
        for sem, val in toks:
            if sem is E.sem:
                continue
            self._wait(E, (sem, val))
        if same_raw and E is not self.pe:
            for key in reads:
                u = self._u(key)
                for sem, val in u["w"].values():
                    if sem is E.sem:
                        self._wait(E, (sem, val))

    def _record(self, tok, reads, writes):
        sem, val = tok
        k = id(sem)
        for key in reads:
            self._u(key)["r"][k] = tok
        for key in writes:
            u = self._u(key)
            u["w"] = {k: tok}
            u["r"] = {}
        self.all_sems[k] = tok

    def _emit(self, E, op, fn, args, kw):
        if op in ("dma_start",):
            return self._dma(E, fn, args, kw)
        lazy = kw.pop("lazy", False)
        aps = []
        out = kw.get("out", None)
        outs = []
        first = True
        for a in list(args) + [v for k_, v in kw.items()]:
            if isinstance(a, bass.AP):
                aps.append(a)
        if out is not None:
            outs = [out]
        elif args and isinstance(args[0], bass.AP):
            outs = [args[0]]
        if kw.get("accum_out") is not None:
            outs.append(kw["accum_out"])
        out_ids = [id(o) for o in outs]
        reads, writes = [], []
        for a in aps:
            if id(a) in out_ids:
                writes += self._units(a)
            else:
                reads += self._units(a)
        self._deps(E, reads, writes)
        inst = fn(*args, **kw)
        if lazy and kw.get("stop", True) is False:
            self._record((E.sem, E.cnt + 1), reads, writes)
            self.n_inst += 1
            return inst
        E.cnt += 1
        inst.then_inc(E.sem, 1)
        self._record((E.sem, E.cnt), reads, writes)
        self.n_inst += 1
        return inst

    def _dma(self, E, fn, args, kw):
        out = kw.get("out", args[0] if args else None)
        in_ = kw.get("in_", args[1] if len(args) > 1 else None)
        writes = self._units(out)
        reads = self._units(in_)
        if "DRAM" not in str(out.space):
            skey = (writes[0][0], 0)
        elif "DRAM" not in str(in_.space):
            skey = (reads[0][0], 0)
        else:
            skey = (writes[0][0], 0)
        self._deps(E, reads, writes, same_raw=False)
        ds = self._dma_sem(skey)
        inst = fn(*args, **kw)
        ds[1] += 16
        inst.then_inc(ds[0], 16)
        self._record((ds[0], ds[1]), reads, writes)
        self.n_inst += 1
        return inst

    def barrier(self, final=False):
        toks = [t for t in self.all_sems.values() if (final or t[0] is not self.cc_sem)]
        for E in self.engs:
            for tok in toks:
                if tok[0] is E.sem:
                    continue
                self._wait(E, tok)
        keep = {}
        for key, u in self.units.items():
            w = {k: t for k, t in u["w"].items() if t[0] is self.cc_sem}
            r = {k: t for k, t in u["r"].items() if t[0] is self.cc_sem}
            if (w or r) and not final:
                keep[key] = {"w": w, "r": r}
        self.units = keep
        self.free_dma += list(self.dma_sems.values())
        self.dma_sems = {}

    def finish(self):
        self.barrier(final=True)


from contextlib import ExitStack
from concourse.bass_utils import run_bass_kernel_spmd

D_MODEL = 1024
D_FF = 2816
DEPTH = 4
ALPHA = (2 * DEPTH) ** 0.25
LN_EPS = 1e-5


_PFX = [""]


def _sb(st, nc, name, shape, dt):
    return st.enter_context(nc.sbuf_tensor(_PFX[0] + name, shape, dt))


def _ps(st, nc, name, shape, dt=F32):
    return st.enter_context(nc.psum_tensor(_PFX[0] + name, shape, dt))


def _pipeline(n, stages, lag=1):
    ns = len(stages)
    for step in range(n + (ns - 1) * lag):
        for si, f in enumerate(stages):
            i = step - si * lag
            if 0 <= i < n:
                f(i)


def _make_ident(S, nc, ident):
    S.pool.memset(ident[:], 1.0)
    S.pool.affine_select(ident[:], ident[:], [[-1, 128]], ALU.is_equal, 0.0, base=0, channel_multiplier=1)


def _layernorm_rows(S, nc, t, width, stats, mv, g_bc, b_bc, out):
    nch = width // 512
    for c in range(nch):
        S.dve.bn_stats(stats[:, c, :], t[:, c * 512:(c + 1) * 512])
    S.dve.bn_aggr(mv[:, 0:2], stats[:, 0:nch, :])
    S.dve.tensor_scalar_add(mv[:, 2:3], mv[:, 1:2], LN_EPS)
    S.act.sqrt(mv[:, 2:3], mv[:, 2:3])
    S.dve.reciprocal(mv[:, 3:4], mv[:, 2:3])
    S.dve.tensor_scalar(t, t, mv[:, 0:1], mv[:, 3:4], ALU.subtract, ALU.mult)
    S.pool.tensor_tensor(t, t, g_bc, ALU.mult)
    S.pool.tensor_tensor(out, t, b_bc, ALU.add)


def _ffn_all(S, nc, T, P, ident):
    NT = T // 128
    NTT = T // 512
    mix_gather, xres, w_out, w_gate, w_up, w_down, lnp, y, yT = (P[k] for k in (
        "mix_gather", "xres", "w_out", "w_gate", "w_up", "w_down", "lnp", "y", "yT"))
    with ExitStack() as st0:
        lnbc = _sb(st0, nc, "lnbc", [128, 4, D_MODEL], F32)
        yacc = _sb(st0, nc, "yacc", [128, NT, D_MODEL], F32)
        x1T = _sb(st0, nc, "x1T", [128, 8, T], BF16)
        S.set_gran(yacc, D_MODEL)
        S.set_gran(lnbc, D_MODEL)
        for j in range(4):
            S.sp.dma_start(out=lnbc[:, j, :], in_=lnp[j].partition_broadcast(128))
        with ExitStack() as st:
            wout_b = _sb(st, nc, "wout_b", [128, 8, D_MODEL], BF16)
            mixb = [_sb(st, nc, "mixb%d" % i, [128, 8, 512], BF16) for i in range(2)]
            xr = [_sb(st, nc, "xr%d" % i, [128, D_MODEL], F32) for i in range(2)]
            t1 = [_sb(st, nc, "t1_%d" % i, [128, D_MODEL], F32) for i in range(3)]
            stats = [_sb(st, nc, "stats%d" % i, [128, 2, 6], F32) for i in range(2)]
            mv = [_sb(st, nc, "mv%d" % i, [128, 4], F32) for i in range(2)]
            psh = [_ps(st, nc, "psh%d" % i, [128, D_MODEL]) for i in range(2)]
            pst = [_ps(st, nc, "pst%d" % i, [128, D_MODEL]) for i in range(2)]
            S.pool.dma_start(out=wout_b[:], in_=w_out.rearrange("(c p) f -> p c f", p=128))
            def f0(i):
                b = i % 2
                tsl = slice(i * 128, (i + 1) * 128)
                mb = mixb[(i // 4) % 2]
                if i % 4 == 0:
                    mix_gather(mb, i // 4)
                S.sp.dma_start(out=xr[b][:], in_=xres[tsl, :])
                for half in range(2):
                    for k in range(8):
                        S.pe.matmul(psh[b][:, half * 512:(half + 1) * 512], mb[:, k, (i % 4) * 128:(i % 4 + 1) * 128],
                                    wout_b[:, k, half * 512:(half + 1) * 512], start=(k == 0), stop=(k == 7), lazy=True)
                S.dve.scalar_tensor_tensor(t1[i % 3][:], xr[b][:], ALPHA, psh[b][:], ALU.mult, ALU.add)

            def f1(i):
                b = i % 2
                _layernorm_rows(S, nc, t1[i % 3][:], D_MODEL, stats[b], mv[b], lnbc[:, 0, :], lnbc[:, 1, :], t1[i % 3][:])
                S.act.mul(yacc[:, i, :], t1[i % 3][:], ALPHA)

            def f2(i):
                b = i % 2
                tsl = slice(i * 128, (i + 1) * 128)
                for c in range(8):
                    S.pe.transpose(pst[b][:, c * 128:(c + 1) * 128], t1[i % 3][:, c * 128:(c + 1) * 128], ident[:])
                S.act.copy(x1T[:, 0:4, tsl], pst[b][:, 0:512].rearrange("p (c t) -> p c t", c=4))
                S.dve.tensor_copy(x1T[:, 4:8, tsl], pst[b][:, 512:1024].rearrange("p (c t) -> p c t", c=4))

            _pipeline(NT, [f0, f1, f2])
        S.barrier()
        with ExitStack() as st:
            FB = 256
            NFB = D_FF // FB
            wg_b = [_sb(st, nc, "wg_b%d" % i, [128, 8, FB], BF16) for i in range(2)]
            wu_b = [_sb(st, nc, "wu_b%d" % i, [128, 8, FB], BF16) for i in range(2)]
            wd_b = [_sb(st, nc, "wd_b%d" % i, [128, 2, D_MODEL], BF16) for i in range(2)]
            sg = [_sb(st, nc, "sg%d" % i, [128, 512], F32) for i in range(2)]
            actT = [_sb(st, nc, "actT%d" % i, [128, 2, 512], BF16) for i in range(2)]
            psg = [_ps(st, nc, "psg%d" % i, [128, 512]) for i in range(2)]
            psu = [_ps(st, nc, "psu%d" % i, [128, 512]) for i in range(2)]
            psd = [_ps(st, nc, "psd%d" % i, [128, D_MODEL]) for i in range(2)]
            wg_v = w_gate.rearrange("(c p) f -> p c f", p=128)
            wu_v = w_up.rearrange("(c p) f -> p c f", p=128)
            wd_v = w_down.rearrange("(c p) f -> p c f", p=128)
            items = [(fb, tt) for fb in range(NFB) for tt in range(NTT)]

            def g0(it):
                fb, tt = items[it]
                wb = fb % 2
                ab = it % 2
                if tt == 0:
                    fsl = slice(fb * FB, (fb + 1) * FB)
                    S.pool.dma_start(out=wg_b[wb][:], in_=wg_v[:, :, fsl])
                    S.pool.dma_start(out=wu_b[wb][:], in_=wu_v[:, :, fsl])
                    S.pool.dma_start(out=wd_b[wb][:], in_=wd_v[:, 2 * fb:2 * fb + 2, :])
                tsl = slice(tt * 512, (tt + 1) * 512)
                for c2 in range(2):
                    for k in range(8):
                        S.pe.matmul(psg[c2][:], wg_b[wb][:, k, c2 * 128:(c2 + 1) * 128], x1T[:, k, tsl],
                                    start=(k == 0), stop=(k == 7), lazy=True)
                    for k in range(8):
                        S.pe.matmul(psu[c2][:], wu_b[wb][:, k, c2 * 128:(c2 + 1) * 128], x1T[:, k, tsl],
                                    start=(k == 0), stop=(k == 7), lazy=True)
                    S.act.activation(sg[c2][:], psg[c2][:], AF.Silu)
                    S.dve.tensor_tensor(actT[ab][:, c2, :], sg[c2][:], psu[c2][:], ALU.mult)

            def g1(it):
                fb, tt = items[it]
                wb = fb % 2
                ab = it % 2
                for s4 in range(4):
                    db = (it * 4 + s4) % 2
                    for half in range(2):
                        for c2 in range(2):
                            S.pe.matmul(psd[db][:, half * 512:(half + 1) * 512],
                                        actT[ab][:, c2, s4 * 128:(s4 + 1) * 128],
                                        wd_b[wb][:, c2, half * 512:(half + 1) * 512],
                                        start=(c2 == 0), stop=(c2 == 1), lazy=True)
                    ti = tt * 4 + s4
                    S.dve.tensor_tensor(yacc[:, ti, :], yacc[:, ti, :], psd[db][:], ALU.add)

            _pipeline(len(items), [g0, g1])
        S.barrier()
        with ExitStack() as st:
            stats = [_sb(st, nc, "stats3_%d" % i, [128, 2, 6], F32) for i in range(2)]
            mv = [_sb(st, nc, "mv3_%d" % i, [128, 4], F32) for i in range(2)]
            ob = [_sb(st, nc, "ob%d" % i, [128, D_MODEL], F32) for i in range(2)]
            obT = [_sb(st, nc, "obT%d" % i, [128, 8, 128], BF16) for i in range(2)]
            pst3 = [_ps(st, nc, "pst3_%d" % i, [128, D_MODEL]) for i in range(2)]
            def h0(i):
                b = i % 2
                _layernorm_rows(S, nc, yacc[:, i, :], D_MODEL, stats[b], mv[b], lnbc[:, 2, :], lnbc[:, 3, :], ob[b][:])

            def h1(i):
                b = i % 2
                S.sp.dma_start(out=y[i * 128:(i + 1) * 128, :], in_=ob[b][:])
                if yT is not None:
                    for c in range(8):
                        S.pe.transpose(pst3[b][:, c * 128:(c + 1) * 128], ob[b][:, c * 128:(c + 1) * 128], ident[:])
                    S.act.copy(obT[b][:, 0:4, :], pst3[b][:, 0:512].rearrange("p (c t) -> p c t", c=4))
                    S.dve.tensor_copy(obT[b][:, 4:8, :], pst3[b][:, 512:1024].rearrange("p (c t) -> p c t", c=4))
                    S.sp.dma_start(out=yT(i), in_=obT[b][:])
                    P["after_tile"](i)

            _pipeline(NT, [h0, h1])


HD = 64
NTM = 580
NEG = -30000.0


def _gelu_tanh(S, nc, out, x, tmp):
    S.act.activation(tmp, x, AF.Square)
    S.dve.tensor_scalar(tmp, tmp, 0.044715, 1.0, ALU.mult, ALU.add)
    S.pool.tensor_tensor(tmp, tmp, x, ALU.mult)
    S.act.activation(tmp, tmp, AF.Sigmoid, scale=2.0 * 0.7978845608028654)
    S.dve.tensor_tensor(out, x, tmp, ALU.mult)


def _consts(S, nc, st0):
    C = {}
    C["pv"] = pv = _sb(st0, nc, "pv", [128, 16], F32)
    C["bv"] = bv = _sb(st0, nc, "bv", [128, 704], F32)
    C["ident"] = ident = _sb(st0, nc, "identf", [128, 128], F32)
    C["identb"] = identb = _sb(st0, nc, "identb", [128, 128], BF16)
    C["mask_le"] = mask_le = _sb(st0, nc, "mask_le", [128, 128], F32)
    C["mask_le_b"] = mask_le_b = _sb(st0, nc, "mask_le_b", [128, 128], BF16)
    C["mask_ge_b"] = mask_ge_b = _sb(st0, nc, "mask_ge_b", [128, 128], BF16)
    C["tri"] = tri = _sb(st0, nc, "tri", [128, 128], F32)
    _make_ident(S, nc, ident)
    S.dve.tensor_copy(identb[:], ident[:])
    S.pool.memset(mask_le[:], 0.0)
    S.pool.affine_select(mask_le[:], mask_le[:], [[1, 128]], ALU.is_ge, NEG, base=0, channel_multiplier=-1)
    S.dve.tensor_copy(mask_le_b[:], mask_le[:])
    S.pool.memset(tri[:], 1.0)
    S.pool.affine_select(tri[:], tri[:], [[1, 128]], ALU.is_ge, 0.0, base=0, channel_multiplier=-1)
    S.pool.memset(mask_ge_b[:], 0.0)
    S.pool.affine_select(mask_ge_b[:], mask_ge_b[:], [[-1, 128]], ALU.is_ge, NEG, base=0, channel_multiplier=1)
    return C


def _mixer_all(S, nc, SEQ, P, C, do=("A", "B", "C", "D")):
    NTT = SEQ // 512
    xsrc, w_fm, w_tm, ropeA, ropeC, pvec, bvec, sgu_wT, outf = (P[k] for k in (
        "xsrc", "w_fm", "w_tm", "ropeA", "ropeC", "pvec", "bvec", "sgu_wT", "outf"))
    sc_qa, sc_ka, sc_qc, sc_kc, sc_qkd, sc_g, sc_tm = (P[k] for k in ("sc_qa", "sc_ka", "sc_qc", "sc_kc", "sc_qkd", "sc_g", "sc_tm"))
    pv, bv, ident, identb, mask_le, mask_le_b, mask_ge_b, tri = (C[k] for k in (
        "pv", "bv", "ident", "identb", "mask_le", "mask_le_b", "mask_ge_b", "tri"))
    S.sp.dma_start(out=pv[:], in_=pvec)
    S.sp.dma_start(out=bv[:], in_=bvec.partition_broadcast(128))
    if True:
        with ExitStack() as st:
            wfm_b = _sb(st, nc, "wfm_b", [128, 8, 768], BF16)
            wtm_b = _sb(st, nc, "wtm_b", [128, 8, NTM], BF16)
            xb = [_sb(st, nc, "xb%d" % i, [128, 8, 512], BF16) for i in range(2)]
            rA = [_sb(st, nc, "rA%d" % i, [64, 2, 512], F32) for i in range(2)]
            rC = [_sb(st, nc, "rC%d" % i, [64, 2, 512], F32) for i in range(2)]
            ta = [_sb(st, nc, "ta%d" % i, [64, 512], F32) for i in range(2)]
            tb = [_sb(st, nc, "tb%d" % i, [64, 512], F32) for i in range(2)]
            stg = [_sb(st, nc, "stg%d" % i, [64, 512], BF16) for i in range(4)]
            stg5 = [_sb(st, nc, "stg5_%d" % i, [128, 512], F32) for i in range(2)]
            stg6 = [_sb(st, nc, "stg6_%d" % i, [2, 512], F32) for i in range(2)]
            stgt = [_sb(st, nc, "stgt%d" % i, [128, NTM], F32) for i in range(2)]
            ps1 = [_ps(st, nc, "ps1_%d" % i, [128, 512]) for i in range(2)]
            ps2 = [_ps(st, nc, "ps2_%d" % i, [128, 512]) for i in range(2)]
            ps5 = _ps(st, nc, "ps5", [128, 512])
            pstm = _ps(st, nc, "pstm", [128, 1024])
            S.pool.dma_start(out=wfm_b[:], in_=w_fm.rearrange("(c p) f -> p c f", p=128))
            S.pool.dma_start(out=wtm_b[:], in_=w_tm.rearrange("(c p) f -> p c f", p=128))
            rA_v = ropeA.rearrange("two r t -> r two t")
            rC_v = ropeC.rearrange("two r t -> r two t")
            for it_, tt in enumerate(P.get("tile_order", range(NTT))):
                b = it_ % 2
                tsl = slice(tt * 512, (tt + 1) * 512)
                xe, xap = xsrc(tt)
                xe.dma_start(out=xb[b][:], in_=xap)
                S.sp.dma_start(out=rA[b][:], in_=rA_v[:, :, tsl])
                S.sp.dma_start(out=rC[b][:], in_=rC_v[:, :, tsl])
                def do_pair(pair):
                    rt, dq, dk = ((rA[b], sc_qa, sc_ka), (rC[b], sc_qc, sc_kc))[pair]
                    pb = (2 * it_ + pair) % 2
                    g1 = 2 * pair
                    for k in range(8):
                        S.pe.matmul(ps1[pb][:], wfm_b[:, k, g1 * 128:(g1 + 1) * 128], xb[b][:, k, :], start=(k == 0), stop=(k == 7), lazy=True)
                    for k in range(8):
                        S.pe.matmul(ps2[pb][:], wfm_b[:, k, (g1 + 1) * 128:(g1 + 2) * 128], xb[b][:, k, :], start=(k == 0), stop=(k == 7), lazy=True)
                    for half, dst in enumerate((dq, dk)):
                        rows = slice(half * 64, (half + 1) * 64)
                        tbuf = half
                        S.dve.tensor_tensor(ta[tbuf][:], ps2[pb][rows, :], rt[:, 1, :], ALU.mult)
                        S.dve.tensor_tensor(tb[tbuf][:], ps1[pb][rows, :], rt[:, 0, :], ALU.mult)
                        sb_ = (4 * it_ + 2 * pair + half) % 4
                        S.pool.tensor_tensor(stg[sb_][:], ta[tbuf][:], tb[tbuf][:], ALU.add)
                        S.sp.dma_start(out=dst[:, tsl], in_=stg[sb_][:])

                def do_g5():
                    for k in range(8):
                        S.pe.matmul(ps5[:], wfm_b[:, k, 512:640], xb[b][:, k, :], start=(k == 0), stop=(k == 7), lazy=True)
                    S.act.copy(stg5[b][:], ps5[:])
                    S.sp.dma_start(out=sc_qkd[:, tsl], in_=stg5[b][:])

                def do_g6():
                    for k in range(8):
                        S.pe.matmul(ps5[0:2, :], wfm_b[:, k, 640:642], xb[b][:, k, :], start=(k == 0), stop=(k == 7), lazy=True)
                    S.act.copy(stg6[b][:], ps5[0:2, :])
                    S.sp.dma_start(out=sc_g[:, tsl], in_=stg6[b][:])

                def do_sub(sub):
                    tb_ = (4 * it_ + sub) % 2
                    for k in range(8):
                        S.pe.matmul(pstm[:, 0:512], xb[b][:, k, sub * 128:(sub + 1) * 128], wtm_b[:, k, 0:512], start=(k == 0), stop=(k == 7), lazy=True)
                    for k in range(8):
                        S.pe.matmul(pstm[:, 512:NTM], xb[b][:, k, sub * 128:(sub + 1) * 128], wtm_b[:, k, 512:NTM], start=(k == 0), stop=(k == 7), lazy=True)
                    S.act.copy(stgt[tb_][:], pstm[:, 0:NTM])
                    r0 = tt * 512 + sub * 128
                    S.sp.dma_start(out=sc_tm[r0:r0 + 128, :], in_=stgt[tb_][:])

                do_pair(0)
                do_sub(0)
                do_g5()
                do_sub(1)
                do_pair(1)
                do_sub(2)
                do_g6()
                do_sub(3)
        S.barrier()
        if "B" in do:
            _mixer_B(S, nc, SEQ, sc_tm, sgu_wT, pv, bv, ident, tri, outf)
            P["after"](1)
            S.barrier()
        if "A" in do:
            _mixer_A(S, nc, SEQ, sc_qa, sc_ka, sc_tm, pv, bv, outf)
            P["after"](0)
            S.barrier()
        if "C" in do:
            _mixer_C(S, nc, SEQ, sc_qc, sc_kc, sc_tm, identb, mask_le_b, mask_ge_b, outf)
            P["after"](2)
            S.barrier()
        if "D" in do:
            _mixer_D(S, nc, SEQ, sc_qkd, sc_g, sc_tm, pv, bv, ident, identb, mask_le, tri, outf)
            P["after"](3)


def _mixer_B(S, nc, SEQ, sc_tm, sgu_wT, pv, bv, ident, tri, outf):
    NIT = SEQ // 512
    with ExitStack() as st:
        wT = _sb(st, nc, "sg_wT", [128, 128], F32)
        wTb = _sb(st, nc, "sg_wTb", [128, 128], BF16)
        uv = [_sb(st, nc, "sg_uv%d" % i, [128, 4, 320], F32) for i in range(2)]
        gl = [_sb(st, nc, "sg_gl%d" % i, [128, 4, 320], F32) for i in range(2)]
        tmp = [_sb(st, nc, "sg_tmp%d" % i, [128, 4, 320], F32) for i in range(2)]
        stats = [_sb(st, nc, "sg_st%d" % i, [128, 4, 6], F32) for i in range(2)]
        mv = [_sb(st, nc, "sg_mv%d" % i, [128, 4, 2], F32) for i in range(2)]
        rs = [_sb(st, nc, "sg_rs%d" % i, [128, 4], F32) for i in range(2)]
        vn = [_sb(st, nc, "sg_vn%d" % i, [128, 4, 64], F32) for i in range(2)]
        vnb = [_sb(st, nc, "sg_vnb%d" % i, [128, 4, 64], BF16) for i in range(2)]
        ob = [_sb(st, nc, "sg_ob%d" % i, [128, 4, 64], F32) for i in range(2)]
        og = [_sb(st, nc, "sg_og%d" % i, [64, 512], BF16) for i in range(2)]
        psz = [_ps(st, nc, "sg_psz%d" % i, [128, 4, 64]) for i in range(2)]
        pst = [_ps(st, nc, "sg_pst%d" % i, [64, 512]) for i in range(2)]
        S.sp.dma_start(out=wT[:], in_=sgu_wT)
        S.dve.tensor_tensor(wTb[:], wT[:], tri[:], ALU.mult)
        gbc = bv[:, 0:64].unsqueeze(1).to_broadcast([128, 4, 64])
        bbc = bv[:, 64:128].unsqueeze(1).to_broadcast([128, 4, 64])

        def b0(it):
            b = it % 2
            r0 = it * 512
            S.sp.dma_start(out=uv[b][:], in_=sc_tm[r0:r0 + 512, 256:576].rearrange("(n p) c -> p n c", p=128))
            _gelu_tanh(S, nc, gl[b][:], uv[b][:], tmp[b][:])
            for k in range(4):
                S.dve.bn_stats(stats[b][:, k, :], gl[b][:, k, 64:320])
                S.dve.bn_aggr(mv[b][:, k, :], stats[b][:, k:k + 1, :])
            S.dve.tensor_scalar_add(rs[b][:], mv[b][:, :, 1], LN_EPS)
            S.act.sqrt(rs[b][:], rs[b][:])
            S.dve.reciprocal(rs[b][:], rs[b][:])
            S.dve.tensor_tensor(vn[b][:], gl[b][:, :, 64:128], mv[b][:, :, 0:1].to_broadcast([128, 4, 64]), ALU.subtract)
            S.dve.tensor_tensor(vn[b][:], vn[b][:], rs[b][:].unsqueeze(2).to_broadcast([128, 4, 64]), ALU.mult)
            S.pool.tensor_tensor(vn[b][:], vn[b][:], gbc, ALU.mult)
            S.pool.tensor_tensor(vnb[b][:], vn[b][:], bbc, ALU.add)

        def b1(it):
            b = it % 2
            for k in range(4):
                S.pe.matmul(psz[b][:, k, :], wTb[:], vnb[b][:, k, :], start=True, stop=True)
            S.dve.scalar_tensor_tensor(ob[b][:], psz[b][:], pv[:, 10:11], gl[b][:, :, 0:64], ALU.add, ALU.mult)

        def b2(it):
            b = it % 2
            for k in range(4):
                S.pe.transpose(pst[b][:, k * 128:(k + 1) * 128], ob[b][:, k, :], ident[:])
            S.act.copy(og[b][:], pst[b][:])
            S.sp.dma_start(out=outf(64, it * 512, (it + 1) * 512), in_=og[b][:])

        _pipeline(NIT, [b0, b1, b2])


def _mixer_A(S, nc, SEQ, sc_qa, sc_ka, sc_tm, pv, bv, outf):
    NTT = SEQ // 512
    NKB = SEQ // 128
    scale = 32 ** -0.5
    with ExitStack() as st:
        qT = _sb(st, nc, "A_qT", [64, SEQ], BF16)
        kT = _sb(st, nc, "A_kT", [64, SEQ], BF16)
        va = _sb(st, nc, "A_va", [128, NKB, 128], BF16)
        lam = _sb(st, nc, "A_lam", [64, 8], F32)
        lt = _sb(st, nc, "A_lt", [64, 64], F32)
        ones_ms = _sb(st, nc, "A_ones", [64, 64], F32)
        E = [_sb(st, nc, "A_E%d" % i, [128, 512], BF16) for i in range(6)]
        rd = [_sb(st, nc, "A_rd%d" % i, [64, 512], F32) for i in range(2)]
        o0 = _sb(st, nc, "A_o0", [64, 512], F32)
        o1 = _sb(st, nc, "A_o1", [64, 512], F32)
        sq = _sb(st, nc, "A_sq", [64, 512], F32)
        og = [_sb(st, nc, "A_og%d" % i, [64, 512], BF16) for i in range(2)]
        pss = [_ps(st, nc, "A_pss%d" % i, [128, 512]) for i in range(6)]
        pso = [_ps(st, nc, "A_pso%d" % i, [128, 512]) for i in range(2)]
        psm = pss[0]
        S.set_gran(va, 128)
        S.sp.dma_start(out=qT[:], in_=sc_qa)
        S.sp.dma_start(out=kT[:], in_=sc_ka)
        S.pool.memset(va[:, :, 64:128], 1.0)
        S.pool.dma_start(out=va[:, :, 0:64], in_=sc_tm[:, 0:64].rearrange("(n p) d -> p n d", p=128))
        S.pool.memset(ones_ms[:], 1.0 / 64.0)
        S.dve.tensor_tensor(lt[:, 0:32], bv[0:64, 192:224], bv[0:64, 224:256], ALU.mult)
        S.dve.tensor_tensor(lt[:, 32:64], bv[0:64, 256:288], bv[0:64, 288:320], ALU.mult)
        S.dve.tensor_reduce(lam[:, 0:1], lt[:, 0:32], AX.X, ALU.add)
        S.dve.tensor_reduce(lam[:, 1:2], lt[:, 32:64], AX.X, ALU.add)
        S.act.activation(lam[:, 2:4], lam[:, 0:2], AF.Exp)
        S.dve.tensor_tensor(lam[:, 4:5], lam[:, 2:3], lam[:, 3:4], ALU.subtract)
        S.dve.tensor_tensor(lam[:, 4:5], lam[:, 4:5], pv[0:64, 7:8], ALU.add)
        S.dve.tensor_scalar_mul(lam[:, 5:6], lam[:, 4:5], -1.0)
        S.dve.tensor_tensor(lam[:, 6:7], pv[0:64, 5:6], pv[0:64, 6:7], ALU.mult)
        blocks = []
        for t in range(NTT):
            nkb = 4 * (t + 1)
            for kb in range(nkb):
                blocks.append((t, kb, nkb))
        LA = 2
        NB_ = len(blocks)

        def front(i):
            t, kb, nkb = blocks[i]
            q0 = t * 512
            j = kb - 4 * t
            c0 = max(j, 0) * 128
            for m in range(2):
                rows = slice(32 * m, 32 * m + 32)
                e = (2 * i + m) % 6
                S.pe.matmul(pss[e][:, c0:512], kT[rows, kb * 128:(kb + 1) * 128], qT[rows, q0 + c0:q0 + 512],
                            start=True, stop=True)
            for m in range(2):
                e = (2 * i + m) % 6
                S.act.activation(E[e][:, c0:512], pss[e][:, c0:512], AF.Exp, scale=scale)
                if j >= 0:
                    S.pool.affine_select(E[e][:, c0:c0 + 128], E[e][:, c0:c0 + 128], [[1, 128]], ALU.is_ge, 0.0,
                                         base=0, channel_multiplier=-1)

        def back(i):
            t, kb, nkb = blocks[i]
            j = kb - 4 * t
            c0 = max(j, 0) * 128
            for m in range(2):
                e = (2 * i + m) % 6
                po = pso[m]
                S.pe.matmul(po[:, c0:512], va[:, kb, :], E[e][:, c0:512], start=(kb == 0), stop=(kb == nkb - 1))
            if kb == nkb - 1:
                epilogue(t)

        def epilogue(t):
            q0 = t * 512
            p0 = pso[0]
            p1 = pso[1]
            S.act.activation(rd[0][:], p0[64:128, :], AF.Ln)
            S.act.activation(rd[0][:], rd[0][:], AF.Exp, scale=-1.0)
            S.dve.tensor_tensor(o0[:], p0[0:64, :], rd[0][:], ALU.mult)
            S.act.activation(rd[1][:], p1[64:128, :], AF.Ln)
            S.act.activation(rd[1][:], rd[1][:], AF.Exp, scale=-1.0)
            S.dve.tensor_tensor(o1[:], p1[0:64, :], rd[1][:], ALU.mult)
            S.dve.scalar_tensor_tensor(o0[:], o1[:], lam[:, 5:6], o0[:], ALU.mult, ALU.add)
            S.pool.tensor_tensor(sq[:], o0[:], o0[:], ALU.mult)
            S.pe.matmul(psm[0:64, :], ones_ms[:], sq[:], start=True, stop=True)
            S.dve.tensor_scalar_add(sq[:], psm[0:64, :], LN_EPS)
            S.act.activation(sq[:], sq[:], AF.Ln)
            S.act.activation(sq[:], sq[:], AF.Exp, scale=-0.5)
            S.pool.tensor_tensor(o1[:], o0[:], sq[:], ALU.mult)
            S.dve.tensor_scalar(og[t % 2][:], o1[:], lam[:, 6:7], None, ALU.mult)
            S.sp.dma_start(out=outf(0, q0, q0 + 512), in_=og[t % 2][:])

        for i in range(NB_ + LA):
            if i < NB_:
                front(i)
            if i - LA >= 0:
                back(i - LA)


import math as _math

ROPE_THETA = 500000.0


def _rope_tables(SEQ):
    pos = np.arange(SEQ, dtype=np.float32)

    def tab(rot, blk):
        half = rot // 2
        inv = (np.float32(ROPE_THETA) ** (-(np.arange(0, rot, 2, dtype=np.float32)) / np.float32(rot))).astype(np.float32)
        ang = (pos[:, None] * inv[None, :]).astype(np.float32)
        c, s = np.cos(ang).astype(np.float32).T, np.sin(ang).astype(np.float32).T
        C = np.ones((blk, SEQ), np.float32)
        Sn = np.zeros((blk, SEQ), np.float32)
        C[0:half] = c
        C[half:2 * half] = c
        Sn[0:half] = -s
        Sn[half:2 * half] = s
        return C, Sn
    Ca, Sa = tab(8, 32)
    Cc, Sc = tab(16, 64)
    ropeA = np.stack([np.concatenate([Ca, Ca], 0), np.concatenate([Sa, Sa], 0)]).astype(np.float32)
    ropeC = np.stack([Cc, Sc]).astype(np.float32)
    return np.ascontiguousarray(ropeA), np.ascontiguousarray(ropeC)


def _perm_idx(base, n, half):
    idx = np.arange(n)
    d = idx.copy()
    d[0:half] = idx[0:half] + half
    d[half:2 * half] = idx[half:2 * half] - half
    return base + d


def prep_mixer_inputs(inp, l, h, SEQ, ropes):
    w_in = inp["w_in"][l]
    c = lambda off: off + h * 64 + np.arange(64)
    aq, ak, av = c(0), c(256), c(512)
    bu = c(768)
    cq, ck, cv = c(1280), c(1536), c(1792)
    dq, dk, dv, do_ = c(2048), c(2304), c(2560), c(2816)
    aqp = np.concatenate([_perm_idx(aq[0], 32, 4), _perm_idx(aq[32], 32, 4)])
    akp = np.concatenate([_perm_idx(ak[0], 32, 4), _perm_idx(ak[32], 32, 4)])
    cqp = _perm_idx(cq[0], 64, 8)
    ckp = _perm_idx(ck[0], 64, 8)
    di, df = 3072 + h, 3076 + h
    g6 = np.concatenate([[di, df], np.full(126, di)])
    fm_cols = np.concatenate([aq, ak, aqp, akp, cq, ck, cqp, ckp, dq, dk, g6])
    bv_all = 1024 + np.concatenate([h * 64 + np.arange(64)] + [g * 64 + np.arange(64) for g in range(4) if g != h])
    tm_cols = np.concatenate([av, cv, dv, do_, bu, bv_all, [di, df, di, df]])
    assert fm_cols.size == 768 and tm_cols.size == NTM
    lam_init = 0.8 - 0.6 * _math.exp(-0.3 * l)
    pvec = np.zeros((128, 16), np.float32)
    chan = np.concatenate([h * 64 + np.arange(64), 256 + h * 64 + np.arange(64)])
    pvec[:, 0:4] = inp["mlstm_conv_w"][l][:, chan].T
    pvec[:, 4] = inp["mlstm_conv_b"][l][chan]
    pvec[:, 5] = np.tile(inp["diff_subln_g"][l], 2)
    pvec[:, 6] = 1.0 - lam_init
    pvec[:, 7] = lam_init
    pvec[:, 8] = inp["mlstm_gate_b"][l][0, h]
    pvec[:, 9] = inp["mlstm_gate_b"][l][1, h]
    pvec[:, 10] = inp["sgu_b"][l][h]
    bvec = np.zeros(704, np.float32)
    bvec[0:64] = inp["sgu_ln_g"][l][h * 64:(h + 1) * 64]
    bvec[64:128] = inp["sgu_ln_b"][l][h * 64:(h + 1) * 64]
    bvec[128:192] = inp["mlstm_norm_g"][l]
    bvec[192:320] = inp["diff_lambda"][l].reshape(-1)
    return {
        "w_fm": np.ascontiguousarray(w_in[:, fm_cols]),
        "w_tm": np.ascontiguousarray(w_in[:, tm_cols]),
        "ropeA": ropes[0], "ropeC": ropes[1],
        "pvec": pvec, "bvec": bvec,
        "sgu_wT": np.ascontiguousarray(inp["sgu_w"][l][h].T),
    }


def _mixer_C(S, nc, SEQ, sc_qc, sc_kc, sc_tm, identb, mask_le_b, mask_ge_b, outf):
    pats = (1, 4, 16)
    SB = 2048 if SEQ >= 2048 else SEQ
    NSB = SEQ // SB
    NBLK = SEQ // 128
    with ExitStack() as st:
        qT = _sb(st, nc, "C_qT", [64, SEQ], BF16)
        kT = _sb(st, nc, "C_kT", [64, SEQ], BF16)
        vd = [_sb(st, nc, "C_vd%d" % i, [128, NBLK, 128], BF16) for i in range(3)]
        acc = [_sb(st, nc, "C_acc%d" % i, [128, SB], F32) for i in range(2)]
        E = [_sb(st, nc, "C_E%d" % i, [128, 2, 128], BF16) for i in range(3)]
        rd = _sb(st, nc, "C_rd", [64, SB], F32)
        og = _sb(st, nc, "C_og", [64, SB], BF16)
        pss = [_ps(st, nc, "C_pss%d" % i, [128, 2, 128]) for i in range(3)]
        pso = [_ps(st, nc, "C_pso%d" % i, [128, 128]) for i in range(3)]
        for i in range(3):
            S.set_gran(vd[i], 128)
        S.sp.dma_start(out=qT[:], in_=sc_qc)
        S.sp.dma_start(out=kT[:], in_=sc_kc)
        for pi, dil in enumerate(pats):
            S.pool.memset(vd[pi][:, :, 64:128], 1.0)
            nb = SEQ // (128 * dil)
            src = sc_tm[:, 64:128].rearrange("(n j r) d -> j n r d", j=128, r=dil)
            dst = vd[pi][:, :, 0:64].rearrange("j (n r) d -> j n r d", r=dil)
            for n in range(nb):
                S.pool.dma_start(out=dst[:, n], in_=src[:, n])
        blocks = []
        for sb in range(NSB):
            first = True
            for pi, dil in enumerate(pats):
                span = 128 * dil
                for n in range(sb * SB // span, (sb + 1) * SB // span):
                    for r in range(dil):
                        blocks.append([sb, pi, dil, n, r, first, False])
                        first = False
            blocks[-1][6] = True

        def c0(i):
            sb, pi, dil, n, r, first, lastb = blocks[i]
            span = 128 * dil
            e = i % 3
            if first:
                S.pool.memset(acc[sb % 2][:], 0.0)
            qs = slice(n * span + r, (n + 1) * span, dil)
            if n >= 1:
                kprev = slice((n - 1) * span + r, n * span, dil)
                S.pe.matmul(pss[e][:, 0, :], kT[:, kprev], qT[:, qs], start=True, stop=False)
                S.pe.matmul(pss[e][:, 0, :], identb[:], mask_ge_b[:], start=False, stop=True)
            S.pe.matmul(pss[e][:, 1, :], kT[:, qs], qT[:, qs], start=True, stop=False)
            S.pe.matmul(pss[e][:, 1, :], identb[:], mask_le_b[:], start=False, stop=True)
            lo = 0 if n >= 1 else 1
            S.act.activation(E[e][:, lo:2, :], pss[e][:, lo:2, :], AF.Exp, scale=0.125)

        def c1(i):
            sb, pi, dil, n, r, first, lastb = blocks[i]
            span = 128 * dil
            e = i % 3
            a = acc[sb % 2]
            kbs = ([(0, (n - 1) * dil + r)] if n >= 1 else []) + [(1, n * dil + r)]
            for ii, (slot, blk) in enumerate(kbs):
                S.pe.matmul(pso[e][:], vd[pi][:, blk, :], E[e][:, slot, :], start=(ii == 0), stop=(ii == len(kbs) - 1))
            loc = slice(n * span + r - sb * SB, (n + 1) * span - sb * SB, dil)
            S.dve.tensor_tensor(a[:, loc], a[:, loc], pso[e][:], ALU.add)
            if lastb:
                S.act.activation(rd[:], a[64:128, :], AF.Ln)
                S.act.activation(rd[:], rd[:], AF.Exp, scale=-1.0)
                S.dve.tensor_tensor(og[:], a[0:64, :], rd[:], ALU.mult)
                PW = min(SB, 512)
                for pc_ in range(SB // PW):
                    S.sp.dma_start(out=outf(128, sb * SB + pc_ * PW, sb * SB + (pc_ + 1) * PW), in_=og[:, pc_ * PW:(pc_ + 1) * PW])

        _pipeline(len(blocks), [c0, c1], lag=2)


import math as _math

ROPE_THETA = 500000.0


def _rope_tables(SEQ):
    pos = np.arange(SEQ, dtype=np.float32)

    def tab(rot, blk):
        half = rot // 2
        inv = (np.float32(ROPE_THETA) ** (-(np.arange(0, rot, 2, dtype=np.float32)) / np.float32(rot))).astype(np.float32)
        ang = (pos[:, None] * inv[None, :]).astype(np.float32)
        c, s = np.cos(ang).astype(np.float32).T, np.sin(ang).astype(np.float32).T
        C = np.ones((blk, SEQ), np.float32)
        Sn = np.zeros((blk, SEQ), np.float32)
        C[0:half] = c
        C[half:2 * half] = c
        Sn[0:half] = -s
        Sn[half:2 * half] = s
        return C, Sn
    Ca, Sa = tab(8, 32)
    Cc, Sc = tab(16, 64)
    ropeA = np.stack([np.concatenate([Ca, Ca], 0), np.concatenate([Sa, Sa], 0)]).astype(np.float32)
    ropeC = np.stack([Cc, Sc]).astype(np.float32)
    return np.ascontiguousarray(ropeA), np.ascontiguousarray(ropeC)


def _perm_idx(base, n, half):
    idx = np.arange(n)
    d = idx.copy()
    d[0:half] = idx[0:half] + half
    d[half:2 * half] = idx[half:2 * half] - half
    return base + d


def prep_mixer_inputs(inp, l, h, SEQ, ropes):
    w_in = inp["w_in"][l]
    c = lambda off: off + h * 64 + np.arange(64)
    aq, ak, av = c(0), c(256), c(512)
    bu = c(768)
    cq, ck, cv = c(1280), c(1536), c(1792)
    dq, dk, dv, do_ = c(2048), c(2304), c(2560), c(2816)
    aqp = np.concatenate([_perm_idx(aq[0], 32, 4), _perm_idx(aq[32], 32, 4)])
    akp = np.concatenate([_perm_idx(ak[0], 32, 4), _perm_idx(ak[32], 32, 4)])
    cqp = _perm_idx(cq[0], 64, 8)
    ckp = _perm_idx(ck[0], 64, 8)
    di, df = 3072 + h, 3076 + h
    g6 = np.concatenate([[di, df], np.full(126, di)])
    fm_cols = np.concatenate([aq, ak, aqp, akp, cq, ck, cqp, ckp, dq, dk, g6])
    bv_all = 1024 + np.concatenate([h * 64 + np.arange(64)] + [g * 64 + np.arange(64) for g in range(4) if g != h])
    tm_cols = np.concatenate([av, cv, dv, do_, bu, bv_all, [di, df, di, df]])
    assert fm_cols.size == 768 and tm_cols.size == NTM
    lam_init = 0.8 - 0.6 * _math.exp(-0.3 * l)
    pvec = np.zeros((128, 16), np.float32)
    chan = np.concatenate([h * 64 + np.arange(64), 256 + h * 64 + np.arange(64)])
    pvec[:, 0:4] = inp["mlstm_conv_w"][l][:, chan].T
    pvec[:, 4] = inp["mlstm_conv_b"][l][chan]
    pvec[:, 5] = np.tile(inp["diff_subln_g"][l], 2)
    pvec[:, 6] = 1.0 - lam_init
    pvec[:, 7] = lam_init
    pvec[:, 8] = inp["mlstm_gate_b"][l][0, h]
    pvec[:, 9] = inp["mlstm_gate_b"][l][1, h]
    pvec[:, 10] = inp["sgu_b"][l][h]
    bvec = np.zeros(704, np.float32)
    bvec[0:64] = inp["sgu_ln_g"][l][h * 64:(h + 1) * 64]
    bvec[64:128] = inp["sgu_ln_b"][l][h * 64:(h + 1) * 64]
    bvec[128:192] = inp["mlstm_norm_g"][l]
    bvec[192:320] = inp["diff_lambda"][l].reshape(-1)
    return {
        "w_fm": np.ascontiguousarray(w_in[:, fm_cols]),
        "w_tm": np.ascontiguousarray(w_in[:, tm_cols]),
        "ropeA": ropes[0], "ropeC": ropes[1],
        "pvec": pvec, "bvec": bvec,
        "sgu_wT": np.ascontiguousarray(inp["sgu_w"][l][h].T),
    }


def _mixer_C(S, nc, SEQ, sc_qc, sc_kc, sc_tm, identb, mask_le_b, mask_ge_b, outf):
    pats = (1, 4, 16)
    SB = 2048 if SEQ >= 2048 else SEQ
    NSB = SEQ // SB
    NBLK = SEQ // 128
    with ExitStack() as st:
        qT = _sb(st, nc, "C_qT", [64, SEQ], BF16)
        kT = _sb(st, nc, "C_kT", [64, SEQ], BF16)
        vd = [_sb(st, nc, "C_vd%d" % i, [128, NBLK, 128], BF16) for i in range(3)]
        acc = [_sb(st, nc, "C_acc%d" % i, [128, SB], F32) for i in range(2)]
        E = [_sb(st, nc, "C_E%d" % i, [128, 2, 128], BF16) for i in range(3)]
        rd = _sb(st, nc, "C_rd", [64, SB], F32)
        og = _sb(st, nc, "C_og", [64, SB], BF16)
        pss = [_ps(st, nc, "C_pss%d" % i, [128, 2, 128]) for i in range(3)]
        pso = [_ps(st, nc, "C_pso%d" % i, [128, 128]) for i in range(3)]
        for i in range(3):
            S.set_gran(vd[i], 128)
        S.sp.dma_start(out=qT[:], in_=sc_qc)
        S.sp.dma_start(out=kT[:], in_=sc_kc)
        for pi, dil in enumerate(pats):
            S.pool.memset(vd[pi][:, :, 64:128], 1.0)
            nb = SEQ // (128 * dil)
            src = sc_tm[:, 64:128].rearrange("(n j r) d -> j n r d", j=128, r=dil)
            dst = vd[pi][:, :, 0:64].rearrange("j (n r) d -> j n r d", r=dil)
            for n in range(nb):
                S.pool.dma_start(out=dst[:, n], in_=src[:, n])
        ei = 0
        for sb in range(NSB):
            a = acc[sb % 2]
            S.pool.memset(a[:], 0.0)
            for pi, dil in enumerate(pats):
                span = 128 * dil
                for n in range(sb * SB // span, (sb + 1) * SB // span):
                    for r in range(dil):
                        e = ei % 3
                        ei += 1
                        qs = slice(n * span + r, (n + 1) * span, dil)
                        kcur = qs
                        kbs = []
                        if n >= 1:
                            kprev = slice((n - 1) * span + r, n * span, dil)
                            S.pe.matmul(pss[e][:, 0, :], kT[:, kprev], qT[:, qs], start=True, stop=False)
                            S.pe.matmul(pss[e][:, 0, :], identb[:], mask_ge_b[:], start=False, stop=True)
                            kbs.append((0, (n - 1) * dil + r))
                        S.pe.matmul(pss[e][:, 1, :], kT[:, kcur], qT[:, qs], start=True, stop=False)
                        S.pe.matmul(pss[e][:, 1, :], identb[:], mask_le_b[:], start=False, stop=True)
                        kbs.append((1, n * dil + r))
                        lo = kbs[0][0]
                        S.act.activation(E[e][:, lo:2, :], pss[e][:, lo:2, :], AF.Exp, scale=0.125)
                        for ii, (slot, blk) in enumerate(kbs):
                            S.pe.matmul(pso[e][:], vd[pi][:, blk, :], E[e][:, slot, :], start=(ii == 0), stop=(ii == len(kbs) - 1))
                        loc = slice(n * span + r - sb * SB, (n + 1) * span - sb * SB, dil)
                        S.dve.tensor_tensor(a[:, loc], a[:, loc], pso[e][:], ALU.add)
            S.dve.reciprocal(rd[:], a[64:128, :])
            S.dve.tensor_tensor(og[:], a[0:64, :], rd[:], ALU.mult)
            PW = min(SB, 512)
            for pc_ in range(SB // PW):
                S.sp.dma_start(out=outf(128, sb * SB + pc_ * PW, sb * SB + (pc_ + 1) * PW), in_=og[:, pc_ * PW:(pc_ + 1) * PW])


def _mixer_D(S, nc, SEQ, sc_qkd, sc_g, sc_tm, pv, bv, ident, identb, mask_le, tri, outf):
    NCH = SEQ // 128
    with ExitStack() as st:
        qTb = _sb(st, nc, "D_qT", [64, SEQ], BF16)
        kTb = _sb(st, nc, "D_kT", [64, SEQ], BF16)
        sm = _sb(st, nc, "D_sm", [128, 8], F32)
        Mfull = _sb(st, nc, "D_Mfull", [NCH, 128], F32)
        OH = _sb(st, nc, "D_OH", [NCH, NCH, 128], F32)
        bc = _sb(st, nc, "D_bc", [128, 3, NCH], F32)
        Xcol = _sb(st, nc, "D_Xcol", [128, NCH], F32)
        rcol = _sb(st, nc, "D_rcol", [128, NCH], F32)
        wcol = _sb(st, nc, "D_wcol", [128, NCH], F32)
        mprev = bc[:, 0, :]
        aexp = bc[:, 2, :]
        S.dve.tensor_scalar_mul(sm[:, 0:1], pv[:, 9:10], -1.0)
        S.pool.memset(OH[:], 1.0)
        S.pool.affine_select(OH[:], OH[:], [[-1, NCH], [0, 128]], ALU.is_equal, 0.0, base=0, channel_multiplier=1)
        with ExitStack() as s1:
            gi = _sb(s1, nc, "D_gi", [NCH, 128], F32)
            gf = _sb(s1, nc, "D_gf", [NCH, 128], F32)
            Bc = _sb(s1, nc, "D_Bc", [NCH, 128], F32)
            rr = _sb(s1, nc, "D_rr", [NCH, 128], F32)
            Ml = _sb(s1, nc, "D_Ml", [NCH, 128], F32)
            Xn = _sb(s1, nc, "D_Xn", [NCH, 128], F32)
            zer = _sb(s1, nc, "D_zer", [NCH, 128], F32)
            col2 = _sb(s1, nc, "D_col2", [NCH, 2], F32)
            rows = _sb(s1, nc, "D_rows", [1, 2, NCH], F32)
            r3 = _sb(s1, nc, "D_r3", [1, 3, NCH], F32)
            mall = _sb(s1, nc, "D_mall", [1, NCH], F32)
            mpc = _sb(s1, nc, "D_mpc", [NCH, 1], F32)
            ones1 = _sb(s1, nc, "D_ones1", [1, 128], F32)
            pA = _ps(s1, nc, "D_pA", [128, 3 * NCH])
            pB = _ps(s1, nc, "D_pB", [128, 128])
            S.pool.memset(zer[:], 0.0)
            S.pool.memset(ones1[:], 1.0)
            S.sp.dma_start(out=gi[:], in_=sc_g[0].rearrange("(n t) -> n t", t=128))
            S.sp.dma_start(out=gf[:], in_=sc_g[1].rearrange("(n t) -> n t", t=128))
            S.act.activation(gf[:], gf[:], AF.Exp, scale=-1.0, bias=sm[0:NCH, 0:1])
            S.act.activation(gf[:], gf[:], AF.Ln, bias=1.0)
            S.dve.tensor_tensor_scan(Bc[:], gf[:], zer[:], 0.0, ALU.add, ALU.max)
            S.dve.scalar_tensor_tensor(rr[:], gi[:], pv[0:NCH, 8:9], Bc[:], ALU.add, ALU.add)
            S.dve.tensor_tensor_scan(Ml[:], rr[:], rr[:], -1e30, ALU.max, ALU.max)
            S.dve.tensor_copy(col2[:, 0:1], Ml[:, 127:128])
            S.dve.tensor_scalar_mul(col2[:, 1:2], Bc[:, 127:128], -1.0)
            S.pe.transpose(pA[0:1, 0:NCH], col2[:, 0:1], ident[0:NCH, 0:NCH])
            S.pe.transpose(pA[0:1, NCH:2 * NCH], col2[:, 1:2], ident[0:NCH, 0:NCH])
            S.dve.tensor_copy(rows[:].rearrange("p a n -> p (a n)"), pA[0:1, 0:2 * NCH])
            S.dve.tensor_tensor_scan(mall[:], rows[:, 0, :], rows[:, 1, :], 0.0, ALU.max, ALU.add)
            S.dve.memset(r3[:, 0, 0:1], 0.0)
            if NCH > 1:
                S.dve.tensor_copy(r3[:, 0, 1:NCH], mall[:, 0:NCH - 1])
            S.dve.tensor_tensor(r3[:, 1, :], r3[:, 0, :], rows[:, 0, :], ALU.max)
            S.dve.tensor_tensor(r3[:, 2, :], r3[:, 0, :], r3[:, 1, :], ALU.subtract)
            S.act.activation(r3[:, 2, :], r3[:, 2, :], AF.Exp)
            S.pe.matmul(pA[:, 0:3 * NCH], ones1[:], r3[:].rearrange("p a n -> p (a n)"), start=True, stop=True)
            S.dve.tensor_copy(bc[:].rearrange("p a n -> p (a n)"), pA[:, 0:3 * NCH])
            S.pe.transpose(pB[0:NCH, 0:1], r3[:, 0, :], ident[0:1, 0:1])
            S.dve.tensor_copy(mpc[:], pB[0:NCH, 0:1])
            S.dve.tensor_scalar(Mfull[:], Ml[:], mpc[:, 0:1], None, ALU.max)
            S.dve.tensor_tensor(Xn[:], Bc[:], Mfull[:], ALU.subtract)
            S.act.activation(Xn[:], Xn[:], AF.Exp)
            S.pe.transpose(pB[:, 0:NCH], Xn[:], ident[0:NCH, 0:NCH])
            S.dve.tensor_copy(Xcol[:], pB[:, 0:NCH])
            S.pe.transpose(pB[:, 0:NCH], rr[:], ident[0:NCH, 0:NCH])
            S.dve.tensor_copy(rcol[:], pB[:, 0:NCH])
            S.dve.tensor_tensor(wcol[:], rcol[:], bc[:, 1, :], ALU.subtract)
            S.act.activation(wcol[:], wcol[:], AF.Exp)
            S.dve.tensor_scalar_mul(wcol[:], wcol[:], 0.125)
        S.barrier()
        with ExitStack() as s2:
            xin = _sb(s2, nc, "D_xin", [128, SEQ + 4], F32)
            cacc = _sb(s2, nc, "D_cacc", [128, SEQ], F32)
            S.pool.memset(xin[:, 0:3], 0.0)
            S.sp.dma_start(out=xin[:, 3:SEQ + 3], in_=sc_qkd)
            S.dve.tensor_scalar(cacc[:], xin[:, 0:SEQ], pv[:, 0:1], None, ALU.mult)
            for j in range(1, 4):
                S.dve.scalar_tensor_tensor(cacc[:], xin[:, j:j + SEQ], pv[:, j:j + 1], cacc[:], ALU.mult, ALU.add)
            S.act.activation(qTb[:], cacc[0:64, :], AF.Silu, bias=pv[0:64, 4:5])
            S.act.activation(kTb[:], cacc[64:128, :], AF.Silu, bias=pv[64:128, 4:5])
        S.barrier()
        with ExitStack() as s3:
            vaug = _sb(s3, nc, "D_vaug", [128, NCH, 66], BF16)
            ktm = _sb(s3, nc, "D_ktm", [128, NCH, 64], BF16)
            kw = _sb(s3, nc, "D_kw", [128, NCH, 64], BF16)
            Cst = _sb(s3, nc, "D_Cst", [64, 66], F32)
            Cb = [_sb(s3, nc, "D_Cb%d" % i, [64, 66], BF16) for i in range(2)]
            tmp = [_sb(s3, nc, "D_tmp%d" % i, [128, 128], F32) for i in range(2)]
            pp = [_sb(s3, nc, "D_pp%d" % i, [128, 128], F32) for i in range(2)]
            swT = [_sb(s3, nc, "D_swT%d" % i, [128, 128], BF16) for i in range(2)]
            ech = [_sb(s3, nc, "D_ech%d" % i, [64, 128], F32) for i in range(2)]
            qe = [_sb(s3, nc, "D_qe%d" % i, [64, 128], BF16) for i in range(2)]
            dsg = [_sb(s3, nc, "D_dsg%d" % i, [128, 4, 64], F32) for i in range(2)]
            hh = [_sb(s3, nc, "D_hh%d" % i, [128, 64], F32) for i in range(2)]
            dd = [_sb(s3, nc, "D_dd%d" % i, [128, 4], F32) for i in range(2)]
            stats = [_sb(s3, nc, "D_st%d" % i, [128, 1, 6], F32) for i in range(2)]
            mv = [_sb(s3, nc, "D_mv%d" % i, [128, 4], F32) for i in range(2)]
            og = [_sb(s3, nc, "D_og%d" % i, [64, 512], BF16) for i in range(2)]
            pkt = [_ps(s3, nc, "D_pkt%d" % i, [128, 64], BF16) for i in range(1)]
            pqk = [_ps(s3, nc, "D_pqk%d" % i, [128, 128]) for i in range(2)]
            ph = [_ps(s3, nc, "D_ph%d" % i, [128, 66]) for i in range(2)]
            pc = _ps(s3, nc, "D_pc", [64, 66])
            pt = _ps(s3, nc, "D_pt", [64, 128])
            pM = _ps(s3, nc, "D_pM", [128, 128])
            S.set_gran(vaug, 66)
            S.set_gran(ktm, 64)
            S.pool.memset(vaug[:, :, 64:66], 1.0)
            S.pool.dma_start(out=vaug[:, :, 0:64], in_=sc_tm[:, 128:192].rearrange("(n s) d -> s n d", s=128))
            S.pool.memset(Cst[:], 0.0)
            for n in range(NCH):
                c = slice(n * 128, (n + 1) * 128)
                S.pe.transpose(pkt[0][:], kTb[:, c], identb[0:64, 0:64])
                S.act.copy(ktm[:, n, :], pkt[0][:])
            wb = wcol[:].unsqueeze(2).to_broadcast([128, NCH, 64])
            S.pool.tensor_tensor(kw[:], ktm[:], wb, ALU.mult)
            def d0(n):
                b = n % 2
                c = slice(n * 128, (n + 1) * 128)
                if n % 4 == 0:
                    g4 = (n // 4) % 2
                    nn = min(4, NCH - n)
                    S.sp.dma_start(out=dsg[g4][:, 0:nn, :],
                                   in_=sc_tm[n * 128:(n + nn) * 128, 192:256].rearrange("(n s) d -> s n d", s=128))
                    S.act.activation(dsg[g4][:, 0:nn, :], dsg[g4][:, 0:nn, :], AF.Exp, scale=-1.0)
                    S.pool.tensor_scalar_add(dsg[g4][:, 0:nn, :], dsg[g4][:, 0:nn, :], 1.0)
                    S.dve.reciprocal(dsg[g4][:, 0:nn, :], dsg[g4][:, 0:nn, :])
                S.pe.matmul(pqk[b][:], kTb[:, c], qTb[:, c], start=True, stop=True)
                S.pe.matmul(pM[:], OH[:, n, :], Mfull[:], start=True, stop=True)
                S.dve.scalar_tensor_tensor(tmp[b][:], mask_le[:], rcol[:, n:n + 1], pM[:], ALU.add, ALU.subtract)
                S.act.activation(pp[b][:], tmp[b][:], AF.Exp)
                S.dve.scalar_tensor_tensor(swT[b][:], pqk[b][:], 0.125, pp[b][:], ALU.mult, ALU.mult)
                if n > 0:
                    S.act.activation(ech[b][:], pM[0:64, :], AF.Exp, scale=-1.0, bias=mprev[0:64, n:n + 1])
                    S.pool.tensor_tensor(qe[b][:], qTb[:, c], ech[b][:], ALU.mult)

            def d1(n):
                b = n % 2
                S.pe.matmul(ph[b][:, 0:65], swT[b][:], vaug[:, n, 0:65], start=True, stop=(n == 0))
                if n > 0:
                    S.pe.matmul(ph[b][:, 0:65], qe[b][:], Cb[(n - 1) % 2][:, 0:65], start=False, stop=True)
                if n < NCH - 1:
                    S.pe.matmul(pc[:, 0:65], kw[:, n, :], vaug[:, n, 0:65], start=True, stop=True)
                    S.dve.scalar_tensor_tensor(Cst[:, 0:65], Cst[:, 0:65], aexp[0:64, n:n + 1], pc[:, 0:65], ALU.mult, ALU.add)
                    S.act.copy(Cb[n % 2][:, 0:65], Cst[:, 0:65])

            def d2(n):
                b = n % 2
                S.dve.tensor_scalar_mul(dd[b][:, 3:4], ph[b][:, 64:65], -1.0)
                S.dve.tensor_tensor(dd[b][:, 0:1], dd[b][:, 3:4], ph[b][:, 64:65], ALU.max)
                S.dve.tensor_tensor(dd[b][:, 1:2], dd[b][:, 0:1], Xcol[:, n:n + 1], ALU.max)
                S.dve.reciprocal(dd[b][:, 2:3], dd[b][:, 1:2])
                S.dve.tensor_scalar(hh[b][:], ph[b][:, 0:64], dd[b][:, 2:3], None, ALU.mult)
                S.dve.bn_stats(stats[b][:, 0, :], hh[b][:])
                S.dve.bn_aggr(mv[b][:, 0:2], stats[b][:])
                S.dve.tensor_scalar_add(mv[b][:, 2:3], mv[b][:, 1:2], LN_EPS)
                S.act.activation(mv[b][:, 2:3], mv[b][:, 2:3], AF.Ln)
                S.act.activation(mv[b][:, 3:4], mv[b][:, 2:3], AF.Exp, scale=-0.5)
                S.dve.tensor_scalar(hh[b][:], hh[b][:], mv[b][:, 0:1], mv[b][:, 3:4], ALU.subtract, ALU.mult)
                S.pool.tensor_tensor(hh[b][:], hh[b][:], bv[:, 128:192], ALU.mult)
                S.pool.tensor_tensor(hh[b][:], hh[b][:], dsg[(n // 4) % 2][:, n % 4, :], ALU.mult)

            def d3(n):
                b = n % 2
                S.pe.transpose(pt[:], hh[b][:], ident[:])
                g = (n // 4) % 2
                S.act.copy(og[g][:, (n % 4) * 128:(n % 4 + 1) * 128], pt[:])
                if n % 4 == 3 or n == NCH - 1:
                    n0 = (n // 4) * 4
                    S.sp.dma_start(out=outf(192, n0 * 128, (n + 1) * 128), in_=og[g][:, 0:(n - n0 + 1) * 128])

            _pipeline(NCH, [d0, d1, d2, d3])


I32 = mybir.dt.int32
GROUPS = [[0, 1, 2, 3], [4, 5, 6, 7]]


def build_fused(SEQ, depth=DEPTH, do=("A", "B", "C", "D"), ffn=True, exch=True):
    T = SEQ // 4
    NTILE = SEQ // 128
    nc = bass.Bass("TRN2", target_bir_lowering=False)
    L = depth
    dr = lambda name, shape, dt=F32: nc.dram_tensor(name, shape, dt, kind="ExternalInput").ap()
    x0g = dr("x0g", [4 * D_MODEL * (T // 512), 512])
    xres0 = dr("xres0", [T, D_MODEL])
    w_fm = dr("w_fm", [L, D_MODEL, 768])
    w_tm = dr("w_tm", [L, D_MODEL, NTM])
    ropeA = dr("ropeA", [2, 64, SEQ])
    ropeC = dr("ropeC", [2, 64, SEQ])
    pvec = dr("pvec", [L, 128, 16])
    bvec = dr("bvec", [L, 704])
    sgu_wT = dr("sgu_wT", [L, 128, 128])
    w_out = dr("w_out", [L, D_MODEL, D_MODEL])
    w_gate = dr("w_gate", [L, D_MODEL, D_FF])
    w_up = dr("w_up", [L, D_MODEL, D_FF])
    w_down = dr("w_down", [L, D_FF, D_MODEL])
    lnp = dr("lnp", [L, 4, D_MODEL])
    gidx = dr("gidx", [128, 8], I32)
    y = nc.dram_tensor("y", [T, D_MODEL], F32, kind="ExternalOutput").ap()
    P0 = dict(
        sc_qa=nc.dram_tensor("sc_qa", [64, SEQ], BF16).ap(), sc_ka=nc.dram_tensor("sc_ka", [64, SEQ], BF16).ap(),
        sc_qc=nc.dram_tensor("sc_qc", [64, SEQ], BF16).ap(), sc_kc=nc.dram_tensor("sc_kc", [64, SEQ], BF16).ap(),
        sc_qkd=nc.dram_tensor("sc_qkd", [128, SEQ], F32).ap(), sc_g=nc.dram_tensor("sc_g", [2, SEQ], F32).ap(),
        sc_tm=nc.dram_tensor("sc_tm", [SEQ, NTM], F32).ap())
    NTC = T // 512
    mixb = nc.dram_tensor("mixb", [256, SEQ], BF16).ap()
    mixg = nc.dram_tensor("mixg", [4 * 256, SEQ], BF16).ap()
    yTb = nc.dram_tensor("yTb", [NTC * D_MODEL, 512], BF16).ap()
    xTg = nc.dram_tensor("xTg", [NTC * 4 * D_MODEL, 512], BF16).ap()
    xres_d = nc.dram_tensor("xres_d", [T, D_MODEL], F32).ap()
    S = Sched(nc)
    S.set_gran(mixb, 64 * SEQ)
    S.set_gran(mixg, 256 * SEQ)
    S.set_gran(yTb, D_MODEL * 512)
    S.set_gran(xTg, 4 * D_MODEL * 512)
    with ExitStack() as stc:
        _PFX[0] = ""
        C = _consts(S, nc, stc)
        gi = _sb(stc, nc, "gidx_sb", [128, 8], I32)
        S.sp.dma_start(out=gi[:], in_=gidx)
        mixtab = mixg.rearrange("f (n t) -> (f n) t", t=512)
        for l in range(L):
            _PFX[0] = "L%d_" % l
            xg, xeng = (x0g, S.pool) if l == 0 else (xTg, S.sp)

            def xsrc(tt, xg=xg, xeng=xeng):
                r, tc = divmod(tt, NTC)
                r0 = (tc * 4 + r) * D_MODEL
                return xeng, xg[r0:r0 + D_MODEL, :].rearrange("(c p) t -> p c t", p=128)

            def outf(r0, c0, c1):
                return mixb[r0:r0 + 64, c0:c1]

            def after(m):
                if exch:
                    cw = min(SEQ, 2048)
                    S.collective("AllGather", mixb[64 * m:64 * m + 64, :].rearrange("r (a b) -> (r a) b", b=cw),
                                 mixg[256 * m:256 * m + 256, :].rearrange("r (a b) -> (r a) b", b=cw), GROUPS)
            P = dict(P0)
            P.update(xsrc=xsrc, w_fm=w_fm[l], w_tm=w_tm[l], ropeA=ropeA, ropeC=ropeC, pvec=pvec[l], bvec=bvec[l],
                     sgu_wT=sgu_wT[l], outf=outf, after=after,
                     tile_order=[r * NTC + tc for tc in range(NTC) for r in range(4)])
            _mixer_all(S, nc, SEQ, P, C, do)
            S.barrier()
            last = (l == L - 1)

            def mix_gather(tile, i):
                for c in range(8):
                    m = c // 2
                    S.gather_rows(tile[:, c, :], mixtab, gi[:, c:c + 1], i * 512, dep_ap=mixg[256 * m:256 * m + 256, :])

            def yT(i):
                tc = i // 4
                return yTb[tc * D_MODEL:(tc + 1) * D_MODEL, :].rearrange("(c p) t -> p c t", p=128)[:, :, (i % 4) * 128:(i % 4 + 1) * 128]

            def after_tile(i):
                if i % 4 == 3 and exch:
                    tc = i // 4
                    S.collective("AllGather", yTb[tc * D_MODEL:(tc + 1) * D_MODEL, :],
                                 xTg[tc * 4 * D_MODEL:(tc + 1) * 4 * D_MODEL, :], GROUPS)
            PF = dict(mix_gather=mix_gather, xres=(xres0 if l == 0 else xres_d),
                      w_out=w_out[l], w_gate=w_gate[l], w_up=w_up[l], w_down=w_down[l], lnp=lnp[l],
                      y=(y if last else xres_d), yT=(None if last else yT), after_tile=after_tile)
            if ffn:
                _ffn_all(S, nc, T, PF, C["ident"])
            S.barrier()
        S.finish()
    return nc


_PROGS = {}
BATCH = 2
SEQ_FULL = 8192
NCORES = 8


def _fused_inputs(inp, SEQ, depth):
    T = SEQ // 4
    ropes = _rope_tables(SEQ)
    x = inp["x"]
    hm = [[prep_mixer_inputs(inp, l, h, SEQ, ropes) for l in range(depth)] for h in range(4)]
    w_out_p = inp["w_out"][:depth]
    lnp = np.stack([np.stack([inp["ln1_g"][l], inp["ln1_b"][l], inp["ln2_g"][l], inp["ln2_b"][l]]) for l in range(depth)])
    maps = []
    for core in range(NCORES):
        b, j = divmod(core, 4)
        xb = x[b, :SEQ]
        NTC = T // 512
        x0g = np.ascontiguousarray(xb.reshape(4, NTC, 512, D_MODEL).transpose(1, 0, 3, 2).reshape(NTC * 4 * D_MODEL, 512))
        gidx = ((np.arange(8)[None, :] * 128 + np.arange(128)[:, None]) * (SEQ // 512) + j * (T // 512)).astype(np.int32)
        m = {
            "x0g": x0g, "xres0": np.ascontiguousarray(xb[j * T:(j + 1) * T]),
            "w_fm": np.stack([hm[j][l]["w_fm"] for l in range(depth)]),
            "w_tm": np.stack([hm[j][l]["w_tm"] for l in range(depth)]),
            "ropeA": ropes[0], "ropeC": ropes[1],
            "pvec": np.stack([hm[j][l]["pvec"] for l in range(depth)]),
            "bvec": np.stack([hm[j][l]["bvec"] for l in range(depth)]),
            "sgu_wT": np.stack([hm[j][l]["sgu_wT"] for l in range(depth)]),
            "w_out": w_out_p, "w_gate": inp["w_gate"][:depth], "w_up": inp["w_up"][:depth], "w_down": inp["w_down"][:depth],
            "lnp": lnp.astype(np.float32), "gidx": gidx,
        }
        maps.append({k: np.ascontiguousarray(v) for k, v in m.items()})
    return maps


def run_fused(inp, SEQ, depth):
    key = ("fused", SEQ, depth)
    if key not in _PROGS:
        _PROGS[key] = build_fused(SEQ, depth)
    maps = _fused_inputs(inp, SEQ, depth)
    res = run_bass_kernel_spmd(_PROGS[key], maps, core_ids=list(range(NCORES)))
    T = SEQ // 4
    out = np.empty((BATCH, SEQ, D_MODEL), np.float32)
    for core in range(NCORES):
        b, j = divmod(core, 4)
        out[b, j * T:(j + 1) * T] = res.results[core]["y"]
    return out


def kernel(**inputs):
    inp = {k: np.asarray(v, dtype=np.float32) for k, v in inputs.items()}
    return run_fused(inp, SEQ_FULL, DEPTH)
```

```python
import numpy as np
import concourse.bass as bass
import concourse.mybir as mybir

F32 = mybir.dt.float32
BF16 = mybir.dt.bfloat16
AF = mybir.ActivationFunctionType
ALU = mybir.AluOpType
AX = mybir.AxisListType


class _Eng:
    def __init__(self, S, name, eng):
        self.S, self.name, self.eng = S, name, eng
        self.sem = S.nc.alloc_semaphore("es_" + name)
        self.cnt = 0
        self.seen = {}

    def __getattr__(self, op):
        fn = getattr(self.eng, op)

        def call(*args, **kw):
            return self.S._emit(self, op, fn, args, kw)
        return call


class Sched:
    def __init__(self, nc):
        self.nc = nc
        self.pe = _Eng(self, "pe", nc.tensor)
        self.act = _Eng(self, "act", nc.scalar)
        self.dve = _Eng(self, "dve", nc.vector)
        self.pool = _Eng(self, "pool", nc.gpsimd)
        self.sp = _Eng(self, "sp", nc.sync)
        self.engs = [self.pe, self.act, self.dve, self.pool, self.sp]
        self.units = {}
        self.gran = {}
        self.dma_sems = {}
        self.all_sems = {}
        self.n_inst = 0
        self.free_dma = []
        self.cc_sem = None
        self.cc_cnt = 0

    def collective(self, kind, in_ap, out_ap, groups):
        E = self.pool
        reads, writes = self._units(in_ap), self._units(out_ap)
        self._deps(E, reads, writes, same_raw=False)
        if self.cc_sem is None:
            self.cc_sem = self.nc.alloc_semaphore("cc_sem")
        inst = self.nc.gpsimd.collective_compute(kind, ALU.bypass, replica_groups=groups, ins=[in_ap], outs=[out_ap])
        self.cc_cnt += 1
        inst.then_inc(self.cc_sem, 1)
        self._record((self.cc_sem, self.cc_cnt), reads, writes)
        return inst

    def gather_rows(self, out_ap, table_ap, idx_ap, element_offset, dep_ap=None):
        E = self.pool
        reads = self._units(dep_ap if dep_ap is not None else table_ap) + self._units(idx_ap)
        writes = self._units(out_ap)
        skey = (writes[0][0], 0)
        ds = self._dma_sem(skey)
        saved = []
        for key in writes:
            u = self._u(key)
            for k_, t_ in list(u["w"].items()):
                if t_[0] is ds[0]:
                    saved.append((u, k_, t_))
                    del u["w"][k_]
        self._deps(E, reads, writes, same_raw=False)
        inst = self.nc.gpsimd.indirect_dma_start(out=out_ap, out_offset=None, in_=table_ap,
                                                 in_offset=bass.IndirectOffsetOnAxis(ap=idx_ap, axis=0),
                                                 element_offset=element_offset)
        ds[1] += 16
        inst.then_inc(ds[0], 16)
        self._record((ds[0], ds[1]), reads, writes)
        return inst

    def _dma_sem(self, skey):
        ds = self.dma_sems.get(skey)
        if ds is None:
            if self.free_dma:
                ds = self.free_dma.pop()
            else:
                ds = [self.nc.alloc_semaphore("ds%d" % len(self.all_sems)), 0]
            self.dma_sems[skey] = ds
        return ds

    def set_gran(self, t, g):
        self.gran[t.name if hasattr(t, "name") else t] = g

    def _units(self, ap):
        name = ap.tensor.name
        g = self.gran.get(name)
        if g is None:
            return [(name, 0)]
        apl = ap.ap
        space = str(ap.space)
        off = int(ap.offset)
        if "DRAM" in space:
            lo = off
            hi = off + sum((c - 1) * s for s, c in apl)
        else:
            F = 1
            for d in ap.tensor.shape[1:]:
                F *= d
            lo = off % F
            hi = lo + sum((c - 1) * s for s, c in apl[1:])
        return [(name, i) for i in range(lo // g, hi // g + 1)]

    def _u(self, key):
        u = self.units.get(key)
        if u is None:
            u = self.units[key] = {"w": {}, "r": {}}
        return u

    def _wait(self, E, tok):
        sem, val = tok
        k = id(sem)
        if E.seen.get(k, 0) >= val:
            return
        E.eng.wait_ge(sem, val)
        E.seen[k] = val

    def _deps(self, E, reads, writes, same_raw=True):
        toks = []
        for key in reads:
            u = self._u(key)
            toks += list(u["w"].values())
        for key in writes:
            u = self._u(key)
            toks += list(u["w"].values()) + list(u["r"].values())
        for sem, val in toks:
            if sem is E.sem:
                continue
            self._wait(E, (sem, val))
        if same_raw and E is not self.pe:
            for key in reads:
                u = self._u(key)
                for sem, val in u["w"].values():
                    if sem is E.sem:
                        self._wait(E, (sem, val))

    def _record(self, tok, reads, writes):
        sem, val = tok
        k = id(sem)
        for key in reads:
            self._u(key)["r"][k] = tok
        for key in writes:
            u = self._u(key)
            u["w"] = {k: tok}
            u["r"] = {}
        self.all_sems[k] = tok

    def _emit(self, E, op, fn, args, kw):
        if op in ("dma_start",):
            return self._dma(E, fn, args, kw)
        lazy = kw.pop("lazy", False)
        aps = []
        out = kw.get("out", None)
        outs = []
        first = True
        for a in list(args) + [v for k_, v in kw.items()]:
            if isinstance(a, bass.AP):
                aps.append(a)
        if out is not None:
            outs = [out]
        elif args and isinstance(args[0], bass.AP):
            outs = [args[0]]
        if kw.get("accum_out") is not None:
            outs.append(kw["accum_out"])
        out_ids = [id(o) for o in outs]
        reads, writes = [], []
        for a in aps:
            if id(a) in out_ids:
                writes += self._units(a)
            else:
                reads += self._units(a)
        self._deps(E, reads, writes)
        inst = fn(*args, **kw)
        if lazy and kw.get("stop", True) is False:
            self._record((E.sem, E.cnt + 1), reads, writes)
            self.n_inst += 1
            return inst
        E.cnt += 1
        inst.then_inc(E.sem, 1)
        self._record((E.sem, E.cnt), reads, writes)
        self.n_inst += 1
        return inst

    def _dma(self, E, fn, args, kw):
        out = kw.get("out", args[0] if args else None)
        in_ = kw.get("in_", args[1] if len(args) > 1 else None)
        writes = self._units(out)
        reads = self._units(in_)
        if "DRAM" not in str(out.space):
            skey = (writes[0][0], 0)
        elif "DRAM" not in str(in_.space):
            skey = (reads[0][0], 0)
        else:
            skey = (writes[0][0], 0)
        self._deps(E, reads, writes, same_raw=False)
        ds = self._dma_sem(skey)
        inst = fn(*args, **kw)
        ds[1] += 16
        inst.then_inc(ds[0], 16)
        self._record((ds[0], ds[1]), reads, writes)
        self.n_inst += 1
        return inst

    def barrier(self, final=False):
        toks = [t for t in self.all_sems.values() if (final or t[0] is not self.cc_sem)]
        for E in self.engs:
            for tok in toks:
                if tok[0] is E.sem:
                    continue
                self._wait(E, tok)
        keep = {}
        for key, u in self.units.items():
            w = {k: t for k, t in u["w"].items() if t[0] is self.cc_sem}
            r = {k: t for k, t in u["r"].items() if t[0] is self.cc_sem}
            if (w or r) and not final:
                keep[key] = {"w": w, "r": r}
        self.units = keep
        self.free_dma += list(self.dma_sems.values())
        self.dma_sems = {}

    def finish(self):
        self.barrier(final=True)


from contextlib import ExitStack
from concourse.bass_utils import run_bass_kernel_spmd

D_MODEL = 1024
D_FF = 2816
DEPTH = 4
ALPHA = (2 * DEPTH) ** 0.25
LN_EPS = 1e-5


_PFX = [""]


def _sb(st, nc, name, shape, dt):
    return st.enter_context(nc.sbuf_tensor(_PFX[0] + name, shape, dt))


def _ps(st, nc, name, shape, dt=F32):
    return st.enter_context(nc.psum_tensor(_PFX[0] + name, shape, dt))


def _pipeline(n, stages, lag=1):
    ns = len(stages)
    for step in range(n + (ns - 1) * lag):
        for si, f in enumerate(stages):
            i = step - si * lag
            if 0 <= i < n:
                f(i)


def _make_ident(S, nc, ident):
    S.pool.memset(ident[:], 1.0)
    S.pool.affine_select(ident[:], ident[:], [[-1, 128]], ALU.is_equal, 0.0, base=0, channel_multiplier=1)


def _layernorm_rows(S, nc, t, width, stats, mv, g_bc, b_bc, out):
    nch = width // 512
    for c in range(nch):
        S.dve.bn_stats(stats[:, c, :], t[:, c * 512:(c + 1) * 512])
    S.dve.bn_aggr(mv[:, 0:2], stats[:, 0:nch, :])
    S.dve.tensor_scalar_add(mv[:, 2:3], mv[:, 1:2], LN_EPS)
    S.act.sqrt(mv[:, 2:3], mv[:, 2:3])
    S.dve.reciprocal(mv[:, 3:4], mv[:, 2:3])
    S.dve.tensor_scalar(t, t, mv[:, 0:1], mv[:, 3:4], ALU.subtract, ALU.mult)
    S.pool.tensor_tensor(t, t, g_bc, ALU.mult)
    S.pool.tensor_tensor(out, t, b_bc, ALU.add)


def _ffn_all(S, nc, T, P, ident):
    NT = T // 128
    NTT = T // 512
    mix_gather, xres, w_out, w_gate, w_up, w_down, lnp, y, yT = (P[k] for k in (
        "mix_gather", "xres", "w_out", "w_gate", "w_up", "w_down", "lnp", "y", "yT"))
    with ExitStack() as st0:
        lnbc = _sb(st0, nc, "lnbc", [128, 4, D_MODEL], F32)
        yacc = _sb(st0, nc, "yacc", [128, NT, D_MODEL], F32)
        x1T = _sb(st0, nc, "x1T", [128, 8, T], BF16)
        S.set_gran(yacc, D_MODEL)
        S.set_gran(lnbc, D_MODEL)
        for j in range(4):
            S.sp.dma_start(out=lnbc[:, j, :], in_=lnp[j].partition_broadcast(128))
        with ExitStack() as st:
            wout_b = _sb(st, nc, "wout_b", [128, 8, D_MODEL], BF16)
            mixb = [_sb(st, nc, "mixb%d" % i, [128, 8, 512], BF16) for i in range(2)]
            xr = [_sb(st, nc, "xr%d" % i, [128, D_MODEL], F32) for i in range(2)]
            t1 = [_sb(st, nc, "t1_%d" % i, [128, D_MODEL], F32) for i in range(3)]
            stats = [_sb(st, nc, "stats%d" % i, [128, 2, 6], F32) for i in range(2)]
            mv = [_sb(st, nc, "mv%d" % i, [128, 4], F32) for i in range(2)]
            psh = [_ps(st, nc, "psh%d" % i, [128, D_MODEL]) for i in range(2)]
            pst = [_ps(st, nc, "pst%d" % i, [128, D_MODEL]) for i in range(2)]
            S.pool.dma_start(out=wout_b[:], in_=w_out.rearrange("(c p) f -> p c f", p=128))
            def f0(i):
                b = i % 2
                tsl = slice(i * 128, (i + 1) * 128)
                mb = mixb[(i // 4) % 2]
                if i % 4 == 0:
                    mix_gather(mb, i // 4)
                S.sp.dma_start(out=xr[b][:], in_=xres[tsl, :])
                for half in range(2):
                    for k in range(8):
                        S.pe.matmul(psh[b][:, half * 512:(half + 1) * 512], mb[:, k, (i % 4) * 128:(i % 4 + 1) * 128],
                                    wout_b[:, k, half * 512:(half + 1) * 512], start=(k == 0), stop=(k == 7), lazy=True)
                S.dve.scalar_tensor_tensor(t1[i % 3][:], xr[b][:], ALPHA, psh[b][:], ALU.mult, ALU.add)

            def f1(i):
                b = i % 2
                _layernorm_rows(S, nc, t1[i % 3][:], D_MODEL, stats[b], mv[b], lnbc[:, 0, :], lnbc[:, 1, :], t1[i % 3][:])
                S.act.mul(yacc[:, i, :], t1[i % 3][:], ALPHA)

            def f2(i):
                b = i % 2
                tsl = slice(i * 128, (i + 1) * 128)
                for c in range(8):
                    S.pe.transpose(pst[b][:, c * 128:(c + 1) * 128], t1[i % 3][:, c * 128:(c + 1) * 128], ident[:])
                S.act.copy(x1T[:, 0:4, tsl], pst[b][:, 0:512].rearrange("p (c t) -> p c t", c=4))
                S.dve.tensor_copy(x1T[:, 4:8, tsl], pst[b][:, 512:1024].rearrange("p (c t) -> p c t", c=4))

            _pipeline(NT, [f0, f1, f2])
        S.barrier()
        with ExitStack() as st:
            FB = 256
            NFB = D_FF // FB
            wg_b = [_sb(st, nc, "wg_b%d" % i, [128, 8, FB], BF16) for i in range(2)]
            wu_b = [_sb(st, nc, "wu_b%d" % i, [128, 8, FB], BF16) for i in range(2)]
            wd_b = [_sb(st, nc, "wd_b%d" % i, [128, 2, D_MODEL], BF16) for i in range(2)]
            sg = [_sb(st, nc, "sg%d" % i, [128, 512], F32) for i in range(2)]
            actT = [_sb(st, nc, "actT%d" % i, [128, 2, 512], BF16) for i in range(2)]
            psg = [_ps(st, nc, "psg%d" % i, [128, 512]) for i in range(2)]
            psu = [_ps(st, nc, "psu%d" % i, [128, 512]) for i in range(2)]
            psd = [_ps(st, nc, "psd%d" % i, [128, D_MODEL]) for i in range(2)]
            wg_v = w_gate.rearrange("(c p) f -> p c f", p=128)
            wu_v = w_up.rearrange("(c p) f -> p c f", p=128)
            wd_v = w_down.rearrange("(c p) f -> p c f", p=128)
            items = [(fb, tt) for fb in range(NFB) for tt in range(NTT)]

            def g0(it):
                fb, tt = items[it]
                wb = fb % 2
                ab = it % 2
                if tt == 0:
                    fsl = slice(fb * FB, (fb + 1) * FB)
                    S.pool.dma_start(out=wg_b[wb][:], in_=wg_v[:, :, fsl])
                    S.pool.dma_start(out=wu_b[wb][:], in_=wu_v[:, :, fsl])
                    S.pool.dma_start(out=wd_b[wb][:], in_=wd_v[:, 2 * fb:2 * fb + 2, :])
                tsl = slice(tt * 512, (tt + 1) * 512)
                for c2 in range(2):
                    for k in range(8):
                        S.pe.matmul(psg[c2][:], wg_b[wb][:, k, c2 * 128:(c2 + 1) * 128], x1T[:, k, tsl],
                                    start=(k == 0), stop=(k == 7), lazy=True)
                    for k in range(8):
                        S.pe.matmul(psu[c2][:], wu_b[wb][:, k, c2 * 128:(c2 + 1) * 128], x1T[:, k, tsl],
                                    start=(k == 0), stop=(k == 7), lazy=True)
                    S.act.activation(sg[c2][:], psg[c2][:], AF.Silu)
                    S.dve.tensor_tensor(actT[ab][:, c2, :], sg[c2][:], psu[c2][:], ALU.mult)

            def g1(it):
                fb, tt = items[it]
                wb = fb % 2
                ab = it % 2
                for s4 in range(4):
                    db = (it * 4 + s4) % 2
                    for half in range(2):
                        for c2 in range(2):
                            S.pe.matmul(psd[db][:, half * 512:(half + 1) * 512],
                                        actT[ab][:, c2, s4 * 128:(s4 + 1) * 128],
                                        wd_b[wb][:, c2, half * 512:(half + 1) * 512],
                                        start=(c2 == 0), stop=(c2 == 1), lazy=True)
                    ti = tt * 4 + s4
                    S.dve.tensor_tensor(yacc[:, ti, :], yacc[:, ti, :], psd[db][:], ALU.add)

            _pipeline(len(items), [g0, g1])
        S.barrier()
        with ExitStack() as st:
            stats = [_sb(st, nc, "stats3_%d" % i, [128, 2, 6], F32) for i in range(2)]
            mv = [_sb(st, nc, "mv3_%d" % i, [128, 4], F32) for i in range(2)]
            ob = [_sb(st, nc, "ob%d" % i, [128, D_MODEL], F32) for i in range(2)]
            obT = [_sb(st, nc, "obT%d" % i, [128, 8, 128], BF16) for i in range(2)]
            pst3 = [_ps(st, nc, "pst3_%d" % i, [128, D_MODEL]) for i in range(2)]
            def h0(i):
                b = i % 2
                _layernorm_rows(S, nc, yacc[:, i, :], D_MODEL, stats[b], mv[b], lnbc[:, 2, :], lnbc[:, 3, :], ob[b][:])

            def h1(i):
                b = i % 2
                S.sp.dma_start(out=y[i * 128:(i + 1) * 128, :], in_=ob[b][:])
                if yT is not None:
                    for c in range(8):
                        S.pe.transpose(pst3[b][:, c * 128:(c + 1) * 128], ob[b][:, c * 128:(c + 1) * 128], ident[:])
                    S.act.copy(obT[b][:, 0:4, :], pst3[b][:, 0:512].rearrange("p (c t) -> p c t", c=4))
                    S.dve.tensor_copy(obT[b][:, 4:8, :], pst3[b][:, 512:1024].rearrange("p (c t) -> p c t", c=4))
                    S.sp.dma_start(out=yT(i), in_=obT[b][:])
                    P["after_tile"](i)

            _pipeline(NT, [h0, h1])


HD = 64
NTM = 580
NEG = -30000.0


def _gelu_tanh(S, nc, out, x, tmp):
    S.act.activation(tmp, x, AF.Square)
    S.dve.tensor_scalar(tmp, tmp, 0.044715, 1.0, ALU.mult, ALU.add)
    S.pool.tensor_tensor(tmp, tmp, x, ALU.mult)
    S.act.activation(tmp, tmp, AF.Sigmoid, scale=2.0 * 0.7978845608028654)
    S.dve.tensor_tensor(out, x, tmp, ALU.mult)


def _consts(S, nc, st0):
    C = {}
    C["pv"] = pv = _sb(st0, nc, "pv", [128, 16], F32)
    C["bv"] = bv = _sb(st0, nc, "bv", [128, 704], F32)
    C["ident"] = ident = _sb(st0, nc, "identf", [128, 128], F32)
    C["identb"] = identb = _sb(st0, nc, "identb", [128, 128], BF16)
    C["mask_le"] = mask_le = _sb(st0, nc, "mask_le", [128, 128], F32)
    C["mask_le_b"] = mask_le_b = _sb(st0, nc, "mask_le_b", [128, 128], BF16)
    C["mask_ge_b"] = mask_ge_b = _sb(st0, nc, "mask_ge_b", [128, 128], BF16)
    C["tri"] = tri = _sb(st0, nc, "tri", [128, 128], F32)
    _make_ident(S, nc, ident)
    S.dve.tensor_copy(identb[:], ident[:])
    S.pool.memset(mask_le[:], 0.0)
    S.pool.affine_select(mask_le[:], mask_le[:], [[1, 128]], ALU.is_ge, NEG, base=0, channel_multiplier=-1)
    S.dve.tensor_copy(mask_le_b[:], mask_le[:])
    S.pool.memset(tri[:], 1.0)
    S.pool.affine_select(tri[:], tri[:], [[1, 128]], ALU.is_ge, 0.0, base=0, channel_multiplier=-1)
    S.pool.memset(mask_ge_b[:], 0.0)
    S.pool.affine_select(mask_ge_b[:], mask_ge_b[:], [[-1, 128]], ALU.is_ge, NEG, base=0, channel_multiplier=1)
    return C


def _mixer_all(S, nc, SEQ, P, C, do=("A", "B", "C", "D")):
    NTT = SEQ // 512
    xsrc, w_fm, w_tm, ropeA, ropeC, pvec, bvec, sgu_wT, outf = (P[k] for k in (
        "xsrc", "w_fm", "w_tm", "ropeA", "ropeC", "pvec", "bvec", "sgu_wT", "outf"))
    sc_qa, sc_ka, sc_qc, sc_kc, sc_qkd, sc_g, sc_tm = (P[k] for k in ("sc_qa", "sc_ka", "sc_qc", "sc_kc", "sc_qkd", "sc_g", "sc_tm"))
    pv, bv, ident, identb, mask_le, mask_le_b, mask_ge_b, tri = (C[k] for k in (
        "pv", "bv", "ident", "identb", "mask_le", "mask_le_b", "mask_ge_b", "tri"))
    S.sp.dma_start(out=pv[:], in_=pvec)
    S.sp.dma_start(out=bv[:], in_=bvec.partition_broadcast(128))
    if True:
        with ExitStack() as st:
            wfm_b = _sb(st, nc, "wfm_b", [128, 8, 768], BF16)
            wtm_b = _sb(st, nc, "wtm_b", [128, 8, NTM], BF16)
            xb = [_sb(st, nc, "xb%d" % i, [128, 8, 512], BF16) for i in range(2)]
            rA = [_sb(st, nc, "rA%d" % i, [64, 2, 512], F32) for i in range(2)]
            rC = [_sb(st, nc, "rC%d" % i, [64, 2, 512], F32) for i in range(2)]
            ta = [_sb(st, nc, "ta%d" % i, [64, 512], F32) for i in range(2)]
            tb = [_sb(st, nc, "tb%d" % i, [64, 512], F32) for i in range(2)]
            stg = [_sb(st, nc, "stg%d" % i, [64, 512], BF16) for i in range(4)]
            stg5 = [_sb(st, nc, "stg5_%d" % i, [128, 512], F32) for i in range(2)]
            stg6 = [_sb(st, nc, "stg6_%d" % i, [2, 512], F32) for i in range(2)]
            stgt = [_sb(st, nc, "stgt%d" % i, [128, NTM], F32) for i in range(2)]
            ps1 = [_ps(st, nc, "ps1_%d" % i, [128, 512]) for i in range(2)]
            ps2 = [_ps(st, nc, "ps2_%d" % i, [128, 512]) for i in range(2)]
            ps5 = _ps(st, nc, "ps5", [128, 512])
            pstm = _ps(st, nc, "pstm", [128, 1024])
            S.pool.dma_start(out=wfm_b[:], in_=w_fm.rearrange("(c p) f -> p c f", p=128))
            S.pool.dma_start(out=wtm_b[:], in_=w_tm.rearrange("(c p) f -> p c f", p=128))
            rA_v = ropeA.rearrange("two r t -> r two t")
            rC_v = ropeC.rearrange("two r t -> r two t")
            for it_, tt in enumerate(P.get("tile_order", range(NTT))):
                b = it_ % 2
                tsl = slice(tt * 512, (tt + 1) * 512)
                xe, xap = xsrc(tt)
                xe.dma_start(out=xb[b][:], in_=xap)
                S.sp.dma_start(out=rA[b][:], in_=rA_v[:, :, tsl])
                S.sp.dma_start(out=rC[b][:], in_=rC_v[:, :, tsl])
                def do_pair(pair):
                    rt, dq, dk = ((rA[b], sc_qa, sc_ka), (rC[b], sc_qc, sc_kc))[pair]
                    pb = (2 * it_ + pair) % 2
                    g1 = 2 * pair
                    for k in range(8):
                        S.pe.matmul(ps1[pb][:], wfm_b[:, k, g1 * 128:(g1 + 1) * 128], xb[b][:, k, :], start=(k == 0), stop=(k == 7), lazy=True)
                    for k in range(8):
                        S.pe.matmul(ps2[pb][:], wfm_b[:, k, (g1 + 1) * 128:(g1 + 2) * 128], xb[b][:, k, :], start=(k == 0), stop=(k == 7), lazy=True)
                    for half, dst in enumerate((dq, dk)):
                        rows = slice(half * 64, (half + 1) * 64)
                        tbuf = half
                        S.dve.tensor_tensor(ta[tbuf][:], ps2[pb][rows, :], rt[:, 1, :], ALU.mult)
                        S.dve.tensor_tensor(tb[tbuf][:], ps1[pb][rows, :], rt[:, 0, :], ALU.mult)
                        sb_ = (4 * it_ + 2 * pair + half) % 4
                        S.pool.tensor_tensor(stg[sb_][:], ta[tbuf][:], tb[tbuf][:], ALU.add)
                        S.sp.dma_start(out=dst[:, tsl], in_=stg[sb_][:])

                def do_g5():
                    for k in range(8):
                        S.pe.matmul(ps5[:], wfm_b[:, k, 512:640], xb[b][:, k, :], start=(k == 0), stop=(k == 7), lazy=True)
                    S.act.copy(stg5[b][:], ps5[:])
                    S.sp.dma_start(out=sc_qkd[:, tsl], in_=stg5[b][:])

                def do_g6():
                    for k in range(8):
                        S.pe.matmul(ps5[0:2, :], wfm_b[:, k, 640:642], xb[b][:, k, :], start=(k == 0), stop=(k == 7), lazy=True)
                    S.act.copy(stg6[b][:], ps5[0:2, :])
                    S.sp.dma_start(out=sc_g[:, tsl], in_=stg6[b][:])

                def do_sub(sub):
                    tb_ = (4 * it_ + sub) % 2
                    for k in range(8):
                        S.pe.matmul(pstm[:, 0:512], xb[b][:, k, sub * 128:(sub + 1) * 128], wtm_b[:, k, 0:512], start=(k == 0), stop=(k == 7), lazy=True)
                    for k in range(8):
                        S.pe.matmul(pstm[:, 512:NTM], xb[b][:, k, sub * 128:(sub + 1) * 128], wtm_b[:, k, 512:NTM], start=(k == 0), stop=(k == 7), lazy=True)
                    S.act.copy(stgt[tb_][:], pstm[:, 0:NTM])
                    r0 = tt * 512 + sub * 128
                    S.sp.dma_start(out=sc_tm[r0:r0 + 128, :], in_=stgt[tb_][:])

                do_pair(0)
                do_sub(0)
                do_g5()
                do_sub(1)
                do_pair(1)
                do_sub(2)
                do_g6()
                do_sub(3)
        S.barrier()
        if "B" in do:
            _mixer_B(S, nc, SEQ, sc_tm, sgu_wT, pv, bv, ident, tri, outf)
            P["after"](1)
            S.barrier()
        if "A" in do:
            _mixer_A(S, nc, SEQ, sc_qa, sc_ka, sc_tm, pv, bv, outf)
            P["after"](0)
            S.barrier()
        if "C" in do:
            _mixer_C(S, nc, SEQ, sc_qc, sc_kc, sc_tm, identb, mask_le_b, mask_ge_b, outf)
            P["after"](2)
            S.barrier()
        if "D" in do:
            _mixer_D(S, nc, SEQ, sc_qkd, sc_g, sc_tm, pv, bv, ident, identb, mask_le, tri, outf)
            P["after"](3)


def _mixer_B(S, nc, SEQ, sc_tm, sgu_wT, pv, bv, ident, tri, outf):
    NIT = SEQ // 512
    with ExitStack() as st:
        wT = _sb(st, nc, "sg_wT", [128, 128], F32)
        wTb = _sb(st, nc, "sg_wTb", [128, 128], BF16)
        uv = [_sb(st, nc, "sg_uv%d" % i, [128, 4, 320], F32) for i in range(2)]
        gl = [_sb(st, nc, "sg_gl%d" % i, [128, 4, 320], F32) for i in range(2)]
        tmp = [_sb(st, nc, "sg_tmp%d" % i, [128, 4, 320], F32) for i in range(2)]
        stats = [_sb(st, nc, "sg_st%d" % i, [128, 4, 6], F32) for i in range(2)]
        mv = [_sb(st, nc, "sg_mv%d" % i, [128, 4, 2], F32) for i in range(2)]
        rs = [_sb(st, nc, "sg_rs%d" % i, [128, 4], F32) for i in range(2)]
        vn = [_sb(st, nc, "sg_vn%d" % i, [128, 4, 64], F32) for i in range(2)]
        vnb = [_sb(st, nc, "sg_vnb%d" % i, [128, 4, 64], BF16) for i in range(2)]
        ob = [_sb(st, nc, "sg_ob%d" % i, [128, 4, 64], F32) for i in range(2)]
        og = [_sb(st, nc, "sg_og%d" % i, [64, 512], BF16) for i in range(2)]
        psz = [_ps(st, nc, "sg_psz%d" % i, [128, 4, 64]) for i in range(2)]
        pst = [_ps(st, nc, "sg_pst%d" % i, [64, 512]) for i in range(2)]
        S.sp.dma_start(out=wT[:], in_=sgu_wT)
        S.dve.tensor_tensor(wTb[:], wT[:], tri[:], ALU.mult)
        gbc = bv[:, 0:64].unsqueeze(1).to_broadcast([128, 4, 64])
        bbc = bv[:, 64:128].unsqueeze(1).to_broadcast([128, 4, 64])

        def b0(it):
            b = it % 2
            r0 = it * 512
            S.sp.dma_start(out=uv[b][:], in_=sc_tm[r0:r0 + 512, 256:576].rearrange("(n p) c -> p n c", p=128))
            _gelu_tanh(S, nc, gl[b][:], uv[b][:], tmp[b][:])
            for k in range(4):
                S.dve.bn_stats(stats[b][:, k, :], gl[b][:, k, 64:320])
                S.dve.bn_aggr(mv[b][:, k, :], stats[b][:, k:k + 1, :])
            S.dve.tensor_scalar_add(rs[b][:], mv[b][:, :, 1], LN_EPS)
            S.act.sqrt(rs[b][:], rs[b][:])
            S.dve.reciprocal(rs[b][:], rs[b][:])
            S.dve.tensor_tensor(vn[b][:], gl[b][:, :, 64:128], mv[b][:, :, 0:1].to_broadcast([128, 4, 64]), ALU.subtract)
            S.dve.tensor_tensor(vn[b][:], vn[b][:], rs[b][:].unsqueeze(2).to_broadcast([128, 4, 64]), ALU.mult)
            S.pool.tensor_tensor(vn[b][:], vn[b][:], gbc, ALU.mult)
            S.pool.tensor_tensor(vnb[b][:], vn[b][:], bbc, ALU.add)

        def b1(it):
            b = it % 2
            for k in range(4):
                S.pe.matmul(psz[b][:, k, :], wTb[:], vnb[b][:, k, :], start=True, stop=True)
            S.dve.scalar_tensor_tensor(ob[b][:], psz[b][:], pv[:, 10:11], gl[b][:, :, 0:64], ALU.add, ALU.mult)

        def b2(it):
            b = it % 2
            for k in range(4):
                S.pe.transpose(pst[b][:, k * 128:(k + 1) * 128], ob[b][:, k, :], ident[:])
            S.act.copy(og[b][:], pst[b][:])
            S.sp.dma_start(out=outf(64, it * 512, (it + 1) * 512), in_=og[b][:])

        _pipeline(NIT, [b0, b1, b2])


def _mixer_A(S, nc, SEQ, sc_qa, sc_ka, sc_tm, pv, bv, outf):
    NTT = SEQ // 512
    NKB = SEQ // 128
    scale = 32 ** -0.5
    with ExitStack() as st:
        qT = _sb(st, nc, "A_qT", [64, SEQ], BF16)
        kT = _sb(st, nc, "A_kT", [64, SEQ], BF16)
        va = _sb(st, nc, "A_va", [128, NKB, 128], BF16)
        lam = _sb(st, nc, "A_lam", [64, 8], F32)
        lt = _sb(st, nc, "A_lt", [64, 64], F32)
        ones_ms = _sb(st, nc, "A_ones", [64, 64], F32)
        E = [_sb(st, nc, "A_E%d" % i, [128, 512], BF16) for i in range(6)]
        rd = [_sb(st, nc, "A_rd%d" % i, [64, 512], F32) for i in range(2)]
        o0 = _sb(st, nc, "A_o0", [64, 512], F32)
        o1 = _sb(st, nc, "A_o1", [64, 512], F32)
        sq = _sb(st, nc, "A_sq", [64, 512], F32)
        og = [_sb(st, nc, "A_og%d" % i, [64, 512], BF16) for i in range(2)]
        pss = [_ps(st, nc, "A_pss%d" % i, [128, 512]) for i in range(6)]
        pso = [_ps(st, nc, "A_pso%d" % i, [128, 512]) for i in range(2)]
        psm = pss[0]
        S.set_gran(va, 128)
        S.sp.dma_start(out=qT[:], in_=sc_qa)
        S.sp.dma_start(out=kT[:], in_=sc_ka)
        S.pool.memset(va[:, :, 64:128], 1.0)
        S.pool.dma_start(out=va[:, :, 0:64], in_=sc_tm[:, 0:64].rearrange("(n p) d -> p n d", p=128))
        S.pool.memset(ones_ms[:], 1.0 / 64.0)
        S.dve.tensor_tensor(lt[:, 0:32], bv[0:64, 192:224], bv[0:64, 224:256], ALU.mult)
        S.dve.tensor_tensor(lt[:, 32:64], bv[0:64, 256:288], bv[0:64, 288:320], ALU.mult)
        S.dve.tensor_reduce(lam[:, 0:1], lt[:, 0:32], AX.X, ALU.add)
        S.dve.tensor_reduce(lam[:, 1:2], lt[:, 32:64], AX.X, ALU.add)
        S.act.activation(lam[:, 2:4], lam[:, 0:2], AF.Exp)
        S.dve.tensor_tensor(lam[:, 4:5], lam[:, 2:3], lam[:, 3:4], ALU.subtract)
        S.dve.tensor_tensor(lam[:, 4:5], lam[:, 4:5], pv[0:64, 7:8], ALU.add)
        S.dve.tensor_scalar_mul(lam[:, 5:6], lam[:, 4:5], -1.0)
        S.dve.tensor_tensor(lam[:, 6:7], pv[0:64, 5:6], pv[0:64, 6:7], ALU.mult)
        blocks = []
        for t in range(NTT):
            nkb = 4 * (t + 1)
            for kb in range(nkb):
                blocks.append((t, kb, nkb))
        LA = 2
        NB_ = len(blocks)

        def front(i):
            t, kb, nkb = blocks[i]
            q0 = t * 512
            j = kb - 4 * t
            c0 = max(j, 0) * 128
            for m in range(2):
                rows = slice(32 * m, 32 * m + 32)
                e = (2 * i + m) % 6
                S.pe.matmul(pss[e][:, c0:512], kT[rows, kb * 128:(kb + 1) * 128], qT[rows, q0 + c0:q0 + 512],
                            start=True, stop=True)
            for m in range(2):
                e = (2 * i + m) % 6
                S.act.activation(E[e][:, c0:512], pss[e][:, c0:512], AF.Exp, scale=scale)
                if j >= 0:
                    S.pool.affine_select(E[e][:, c0:c0 + 128], E[e][:, c0:c0 + 128], [[1, 128]], ALU.is_ge, 0.0,
                                         base=0, channel_multiplier=-1)

        def back(i):
            t, kb, nkb = blocks[i]
            j = kb - 4 * t
            c0 = max(j, 0) * 128
            for m in range(2):
                e = (2 * i + m) % 6
                po = pso[m]
                S.pe.matmul(po[:, c0:512], va[:, kb, :], E[e][:, c0:512], start=(kb == 0), stop=(kb == nkb - 1))
            if kb == nkb - 1:
                epilogue(t)

        def epilogue(t):
            q0 = t * 512
            p0 = pso[0]
            p1 = pso[1]
            S.act.activation(rd[0][:], p0[64:128, :], AF.Ln)
            S.act.activation(rd[0][:], rd[0][:], AF.Exp, scale=-1.0)
            S.dve.tensor_tensor(o0[:], p0[0:64, :], rd[0][:], ALU.mult)
            S.act.activation(rd[1][:], p1[64:128, :], AF.Ln)
            S.act.activation(rd[1][:], rd[1][:], AF.Exp, scale=-1.0)
            S.dve.tensor_tensor(o1[:], p1[0:64, :], rd[1][:], ALU.mult)
            S.dve.scalar_tensor_tensor(o0[:], o1[:], lam[:, 5:6], o0[:], ALU.mult, ALU.add)
            S.pool.tensor_tensor(sq[:], o0[:], o0[:], ALU.mult)
            S.pe.matmul(psm[0:64, :], ones_ms[:], sq[:], start=True, stop=True)
            S.dve.tensor_scalar_add(sq[:], psm[0:64, :], LN_EPS)
            S.act.activation(sq[:], sq[:], AF.Ln)
            S.act.activation(sq[:], sq[:], AF.Exp, scale=-0.5)
            S.pool.tensor_tensor(o1[:], o0[:], sq[:], ALU.mult)
            S.dve.tensor_scalar(og[t % 2][:], o1[:], lam[:, 6:7], None, ALU.mult)
            S.sp.dma_start(out=outf(0, q0, q0 + 512), in_=og[t % 2][:])

        for i in range(NB_ + LA):
            if i < NB_:
                front(i)
            if i - LA >= 0:
                back(i - LA)


import math as _math

ROPE_THETA = 500000.0


def _rope_tables(SEQ):
    pos = np.arange(SEQ, dtype=np.float32)

    def tab(rot, blk):
        half = rot // 2
        inv = (np.float32(ROPE_THETA) ** (-(np.arange(0, rot, 2, dtype=np.float32)) / np.float32(rot))).astype(np.float32)
        ang = (pos[:, None] * inv[None, :]).astype(np.float32)
        c, s = np.cos(ang).astype(np.float32).T, np.sin(ang).astype(np.float32).T
        C = np.ones((blk, SEQ), np.float32)
        Sn = np.zeros((blk, SEQ), np.float32)
        C[0:half] = c
        C[half:2 * half] = c
        Sn[0:half] = -s
        Sn[half:2 * half] = s
        return C, Sn
    Ca, Sa = tab(8, 32)
    Cc, Sc = tab(16, 64)
    ropeA = np.stack([np.concatenate([Ca, Ca], 0), np.concatenate([Sa, Sa], 0)]).astype(np.float32)
    ropeC = np.stack([Cc, Sc]).astype(np.float32)
    return np.ascontiguousarray(ropeA), np.ascontiguousarray(ropeC)


def _perm_idx(base, n, half):
    idx = np.arange(n)
    d = idx.copy()
    d[0:half] = idx[0:half] + half
    d[half:2 * half] = idx[half:2 * half] - half
    return base + d


def prep_mixer_inputs(inp, l, h, SEQ, ropes):
    w_in = inp["w_in"][l]
    c = lambda off: off + h * 64 + np.arange(64)
    aq, ak, av = c(0), c(256), c(512)
    bu = c(768)
    cq, ck, cv = c(1280), c(1536), c(1792)
    dq, dk, dv, do_ = c(2048), c(2304), c(2560), c(2816)
    aqp = np.concatenate([_perm_idx(aq[0], 32, 4), _perm_idx(aq[32], 32, 4)])
    akp = np.concatenate([_perm_idx(ak[0], 32, 4), _perm_idx(ak[32], 32, 4)])
    cqp = _perm_idx(cq[0], 64, 8)
    ckp = _perm_idx(ck[0], 64, 8)
    di, df = 3072 + h, 3076 + h
    g6 = np.concatenate([[di, df], np.full(126, di)])
    fm_cols = np.concatenate([aq, ak, aqp, akp, cq, ck, cqp, ckp, dq, dk, g6])
    bv_all = 1024 + np.concatenate([h * 64 + np.arange(64)] + [g * 64 + np.arange(64) for g in range(4) if g != h])
    tm_cols = np.concatenate([av, cv, dv, do_, bu, bv_all, [di, df, di, df]])
    assert fm_cols.size == 768 and tm_cols.size == NTM
    lam_init = 0.8 - 0.6 * _math.exp(-0.3 * l)
    pvec = np.zeros((128, 16), np.float32)
    chan = np.concatenate([h * 64 + np.arange(64), 256 + h * 64 + np.arange(64)])
    pvec[:, 0:4] = inp["mlstm_conv_w"][l][:, chan].T
    pvec[:, 4] = inp["mlstm_conv_b"][l][chan]
    pvec[:, 5] = np.tile(inp["diff_subln_g"][l], 2)
    pvec[:, 6] = 1.0 - lam_init
    pvec[:, 7] = lam_init
    pvec[:, 8] = inp["mlstm_gate_b"][l][0, h]
    pvec[:, 9] = inp["mlstm_gate_b"][l][1, h]
    pvec[:, 10] = inp["sgu_b"][l][h]
    bvec = np.zeros(704, np.float32)
    bvec[0:64] = inp["sgu_ln_g"][l][h * 64:(h + 1) * 64]
    bvec[64:128] = inp["sgu_ln_b"][l][h * 64:(h + 1) * 64]
    bvec[128:192] = inp["mlstm_norm_g"][l]
    bvec[192:320] = inp["diff_lambda"][l].reshape(-1)
    return {
        "w_fm": np.ascontiguousarray(w_in[:, fm_cols]),
        "w_tm": np.ascontiguousarray(w_in[:, tm_cols]),
        "ropeA": ropes[0], "ropeC": ropes[1],
        "pvec": pvec, "bvec": bvec,
        "sgu_wT": np.ascontiguousarray(inp["sgu_w"][l][h].T),
    }


def _mixer_C(S, nc, SEQ, sc_qc, sc_kc, sc_tm, identb, mask_le_b, mask_ge_b, outf):
    pats = (1, 4, 16)
    SB = 2048 if SEQ >= 2048 else SEQ
    NSB = SEQ // SB
    NBLK = SEQ // 128
    with ExitStack() as st:
        qT = _sb(st, nc, "C_qT", [64, SEQ], BF16)
        kT = _sb(st, nc, "C_kT", [64, SEQ], BF16)
        vd = [_sb(st, nc, "C_vd%d" % i, [128, NBLK, 128], BF16) for i in range(3)]
        acc = [_sb(st, nc, "C_acc%d" % i, [128, SB], F32) for i in range(2)]
        E = [_sb(st, nc, "C_E%d" % i, [128, 2, 128], BF16) for i in range(3)]
        rd = _sb(st, nc, "C_rd", [64, SB], F32)
        og = _sb(st, nc, "C_og", [64, SB], BF16)
        pss = [_ps(st, nc, "C_pss%d" % i, [128, 2, 128]) for i in range(3)]
        pso = [_ps(st, nc, "C_pso%d" % i, [128, 128]) for i in range(3)]
        for i in range(3):
            S.set_gran(vd[i], 128)
        S.sp.dma_start(out=qT[:], in_=sc_qc)
        S.sp.dma_start(out=kT[:], in_=sc_kc)
        for pi, dil in enumerate(pats):
            S.pool.memset(vd[pi][:, :, 64:128], 1.0)
            nb = SEQ // (128 * dil)
            src = sc_tm[:, 64:128].rearrange("(n j r) d -> j n r d", j=128, r=dil)
            dst = vd[pi][:, :, 0:64].rearrange("j (n r) d -> j n r d", r=dil)
            for n in range(nb):
                S.pool.dma_start(out=dst[:, n], in_=src[:, n])
        blocks = []
        for sb in range(NSB):
            first = True
            for pi, dil in enumerate(pats):
                span = 128 * dil
                for n in range(sb * SB // span, (sb + 1) * SB // span):
                    for r in range(dil):
                        blocks.append([sb, pi, dil, n, r, first, False])
                        first = False
            blocks[-1][6] = True

        def c0(i):
            sb, pi, dil, n, r, first, lastb = blocks[i]
            span = 128 * dil
            e = i % 3
            if first:
                S.pool.memset(acc[sb % 2][:], 0.0)
            qs = slice(n * span + r, (n + 1) * span, dil)
            if n >= 1:
                kprev = slice((n - 1) * span + r, n * span, dil)
                S.pe.matmul(pss[e][:, 0, :], kT[:, kprev], qT[:, qs], start=True, stop=False)
                S.pe.matmul(pss[e][:, 0, :], identb[:], mask_ge_b[:], start=False, stop=True)
            S.pe.matmul(pss[e][:, 1, :], kT[:, qs], qT[:, qs], start=True, stop=False)
            S.pe.matmul(pss[e][:, 1, :], identb[:], mask_le_b[:], start=False, stop=True)
            lo = 0 if n >= 1 else 1
            S.act.activation(E[e][:, lo:2, :], pss[e][:, lo:2, :], AF.Exp, scale=0.125)

        def c1(i):
            sb, pi, dil, n, r, first, lastb = blocks[i]
            span = 128 * dil
            e = i % 3
            a = acc[sb % 2]
            kbs = ([(0, (n - 1) * dil + r)] if n >= 1 else []) + [(1, n * dil + r)]
            for ii, (slot, blk) in enumerate(kbs):
                S.pe.matmul(pso[e][:], vd[pi][:, blk, :], E[e][:, slot, :], start=(ii == 0), stop=(ii == len(kbs) - 1))
            loc = slice(n * span + r - sb * SB, (n + 1) * span - sb * SB, dil)
            S.dve.tensor_tensor(a[:, loc], a[:, loc], pso[e][:], ALU.add)
            if lastb:
                S.act.activation(rd[:], a[64:128, :], AF.Ln)
                S.act.activation(rd[:], rd[:], AF.Exp, scale=-1.0)
                S.dve.tensor_tensor(og[:], a[0:64, :], rd[:], ALU.mult)
                PW = min(SB, 512)
                for pc_ in range(SB // PW):
                    S.sp.dma_start(out=outf(128, sb * SB + pc_ * PW, sb * SB + (pc_ + 1) * PW), in_=og[:, pc_ * PW:(pc_ + 1) * PW])

        _pipeline(len(blocks), [c0, c1], lag=2)


import math as _math

ROPE_THETA = 500000.0


def _rope_tables(SEQ):
    pos = np.arange(SEQ, dtype=np.float32)

    def tab(rot, blk):
        half = rot // 2
        inv = (np.float32(ROPE_THETA) ** (-(np.arange(0, rot, 2, dtype=np.float32)) / np.float32(rot))).astype(np.float32)
        ang = (pos[:, None] * inv[None, :]).astype(np.float32)
        c, s = np.cos(ang).astype(np.float32).T, np.sin(ang).astype(np.float32).T
        C = np.ones((blk, SEQ), np.float32)
        Sn = np.zeros((blk, SEQ), np.float32)
        C[0:half] = c
        C[half:2 * half] = c
        Sn[0:half] = -s
        Sn[half:2 * half] = s
        return C, Sn
    Ca, Sa = tab(8, 32)
    Cc, Sc = tab(16, 64)
    ropeA = np.stack([np.concatenate([Ca, Ca], 0), np.concatenate([Sa, Sa], 0)]).astype(np.float32)
    ropeC = np.stack([Cc, Sc]).astype(np.float32)
    return np.ascontiguousarray(ropeA), np.ascontiguousarray(ropeC)


def _perm_idx(base, n, half):
    idx = np.arange(n)
    d = idx.copy()
    d[0:half] = idx[0:half] + half
    d[half:2 * half] = idx[half:2 * half] - half
    return base + d


def prep_mixer_inputs(inp, l, h, SEQ, ropes):
    w_in = inp["w_in"][l]
    c = lambda off: off + h * 64 + np.arange(64)
    aq, ak, av = c(0), c(256), c(512)
    bu = c(768)
    cq, ck, cv = c(1280), c(1536), c(1792)
    dq, dk, dv, do_ = c(2048), c(2304), c(2560), c(2816)
    aqp = np.concatenate([_perm_idx(aq[0], 32, 4), _perm_idx(aq[32], 32, 4)])
    akp = np.concatenate([_perm_idx(ak[0], 32, 4), _perm_idx(ak[32], 32, 4)])
    cqp = _perm_idx(cq[0], 64, 8)
    ckp = _perm_idx(ck[0], 64, 8)
    di, df = 3072 + h, 3076 + h
    g6 = np.concatenate([[di, df], np.full(126, di)])
    fm_cols = np.concatenate([aq, ak, aqp, akp, cq, ck, cqp, ckp, dq, dk, g6])
    bv_all = 1024 + np.concatenate([h * 64 + np.arange(64)] + [g * 64 + np.arange(64) for g in range(4) if g != h])
    tm_cols = np.concatenate([av, cv, dv, do_, bu, bv_all, [di, df, di, df]])
    assert fm_cols.size == 768 and tm_cols.size == NTM
    lam_init = 0.8 - 0.6 * _math.exp(-0.3 * l)
    pvec = np.zeros((128, 16), np.float32)
    chan = np.concatenate([h * 64 + np.arange(64), 256 + h * 64 + np.arange(64)])
    pvec[:, 0:4] = inp["mlstm_conv_w"][l][:, chan].T
    pvec[:, 4] = inp["mlstm_conv_b"][l][chan]
    pvec[:, 5] = np.tile(inp["diff_subln_g"][l], 2)
    pvec[:, 6] = 1.0 - lam_init
    pvec[:, 7] = lam_init
    pvec[:, 8] = inp["mlstm_gate_b"][l][0, h]
    pvec[:, 9] = inp["mlstm_gate_b"][l][1, h]
    pvec[:, 10] = inp["sgu_b"][l][h]
    bvec = np.zeros(704, np.float32)
    bvec[0:64] = inp["sgu_ln_g"][l][h * 64:(h + 1) * 64]
    bvec[64:128] = inp["sgu_ln_b"][l][h * 64:(h + 1) * 64]
    bvec[128:192] = inp["mlstm_norm_g"][l]
    bvec[192:320] = inp["diff_lambda"][l].reshape(-1)
    return {
        "w_fm": np.ascontiguousarray(w_in[:, fm_cols]),
        "w_tm": np.ascontiguousarray(w_in[:, tm_cols]),
        "ropeA": ropes[0], "ropeC": ropes[1],
        "pvec": pvec, "bvec": bvec,
        "sgu_wT": np.ascontiguousarray(inp["sgu_w"][l][h].T),
    }


def _mixer_C(S, nc, SEQ, sc_qc, sc_kc, sc_tm, identb, mask_le_b, mask_ge_b, outf):
    pats = (1, 4, 16)
    SB = 2048 if SEQ >= 2048 else SEQ
    NSB = SEQ // SB
    NBLK = SEQ // 128
    with ExitStack() as st:
        qT = _sb(st, nc, "C_qT", [64, SEQ], BF16)
        kT = _sb(st, nc, "C_kT", [64, SEQ], BF16)
        vd = [_sb(st, nc, "C_vd%d" % i, [128, NBLK, 128], BF16) for i in range(3)]
        acc = [_sb(st, nc, "C_acc%d" % i, [128, SB], F32) for i in range(2)]
        E = [_sb(st, nc, "C_E%d" % i, [128, 2, 128], BF16) for i in range(3)]
        rd = _sb(st, nc, "C_rd", [64, SB], F32)
        og = _sb(st, nc, "C_og", [64, SB], BF16)
        pss = [_ps(st, nc, "C_pss%d" % i, [128, 2, 128]) for i in range(3)]
        pso = [_ps(st, nc, "C_pso%d" % i, [128, 128]) for i in range(3)]
        for i in range(3):
            S.set_gran(vd[i], 128)
        S.sp.dma_start(out=qT[:], in_=sc_qc)
        S.sp.dma_start(out=kT[:], in_=sc_kc)
        for pi, dil in enumerate(pats):
            S.pool.memset(vd[pi][:, :, 64:128], 1.0)
            nb = SEQ // (128 * dil)
            src = sc_tm[:, 64:128].rearrange("(n j r) d -> j n r d", j=128, r=dil)
            dst = vd[pi][:, :, 0:64].rearrange("j (n r) d -> j n r d", r=dil)
            for n in range(nb):
                S.pool.dma_start(out=dst[:, n], in_=src[:, n])
        ei = 0
        for sb in range(NSB):
            a = acc[sb % 2]
            S.pool.memset(a[:], 0.0)
            for pi, dil in enumerate(pats):
                span = 128 * dil
                for n in range(sb * SB // span, (sb + 1) * SB // span):
                    for r in range(dil):
                        e = ei % 3
                        ei += 1
                        qs = slice(n * span + r, (n + 1) * span, dil)
                        kcur = qs
                        kbs = []
                        if n >= 1:
                            kprev = slice((n - 1) * span + r, n * span, dil)
                            S.pe.matmul(pss[e][:, 0, :], kT[:, kprev], qT[:, qs], start=True, stop=False)
                            S.pe.matmul(pss[e][:, 0, :], identb[:], mask_ge_b[:], start=False, stop=True)
                            kbs.append((0, (n - 1) * dil + r))
                        S.pe.matmul(pss[e][:, 1, :], kT[:, kcur], qT[:, qs], start=True, stop=False)
                        S.pe.matmul(pss[e][:, 1, :], identb[:], mask_le_b[:], start=False, stop=True)
                        kbs.append((1, n * dil + r))
                        lo = kbs[0][0]
                        S.act.activation(E[e][:, lo:2, :], pss[e][:, lo:2, :], AF.Exp, scale=0.125)
                        for ii, (slot, blk) in enumerate(kbs):
                            S.pe.matmul(pso[e][:], vd[pi][:, blk, :], E[e][:, slot, :], start=(ii == 0), stop=(ii == len(kbs) - 1))
                        loc = slice(n * span + r - sb * SB, (n + 1) * span - sb * SB, dil)
                        S.dve.tensor_tensor(a[:, loc], a[:, loc], pso[e][:], ALU.add)
            S.dve.reciprocal(rd[:], a[64:128, :])
            S.dve.tensor_tensor(og[:], a[0:64, :], rd[:], ALU.mult)
            PW = min(SB, 512)
            for pc_ in range(SB // PW):
                S.sp.dma_start(out=outf(128, sb * SB + pc_ * PW, sb * SB + (pc_ + 1) * PW), in_=og[:, pc_ * PW:(pc_ + 1) * PW])


def _mixer_D(S, nc, SEQ, sc_qkd, sc_g, sc_tm, pv, bv, ident, identb, mask_le, tri, outf):
    NCH = SEQ // 128
    with ExitStack() as st:
        qTb = _sb(st, nc, "D_qT", [64, SEQ], BF16)
        kTb = _sb(st, nc, "D_kT", [64, SEQ], BF16)
        sm = _sb(st, nc, "D_sm", [128, 8], F32)
        Mfull = _sb(st, nc, "D_Mfull", [NCH, 128], F32)
        OH = _sb(st, nc, "D_OH", [NCH, NCH, 128], F32)
        bc = _sb(st, nc, "D_bc", [128, 3, NCH], F32)
        Xcol = _sb(st, nc, "D_Xcol", [128, NCH], F32)
        rcol = _sb(st, nc, "D_rcol", [128, NCH], F32)
        wcol = _sb(st, nc, "D_wcol", [128, NCH], F32)
        mprev = bc[:, 0, :]
        aexp = bc[:, 2, :]
        S.dve.tensor_scalar_mul(sm[:, 0:1], pv[:, 9:10], -1.0)
        S.pool.memset(OH[:], 1.0)
        S.pool.affine_select(OH[:], OH[:], [[-1, NCH], [0, 128]], ALU.is_equal, 0.0, base=0, channel_multiplier=1)
        with ExitStack() as s1:
            gi = _sb(s1, nc, "D_gi", [NCH, 128], F32)
            gf = _sb(s1, nc, "D_gf", [NCH, 128], F32)
            Bc = _sb(s1, nc, "D_Bc", [NCH, 128], F32)
            rr = _sb(s1, nc, "D_rr", [NCH, 128], F32)
            Ml = _sb(s1, nc, "D_Ml", [NCH, 128], F32)
            Xn = _sb(s1, nc, "D_Xn", [NCH, 128], F32)
            zer = _sb(s1, nc, "D_zer", [NCH, 128], F32)
            col2 = _sb(s1, nc, "D_col2", [NCH, 2], F32)
            rows = _sb(s1, nc, "D_rows", [1, 2, NCH], F32)
            r3 = _sb(s1, nc, "D_r3", [1, 3, NCH], F32)
            mall = _sb(s1, nc, "D_mall", [1, NCH], F32)
            mpc = _sb(s1, nc, "D_mpc", [NCH, 1], F32)
            ones1 = _sb(s1, nc, "D_ones1", [1, 128], F32)
            pA = _ps(s1, nc, "D_pA", [128, 3 * NCH])
            pB = _ps(s1, nc, "D_pB", [128, 128])
            S.pool.memset(zer[:], 0.0)
            S.pool.memset(ones1[:], 1.0)
            S.sp.dma_start(out=gi[:], in_=sc_g[0].rearrange("(n t) -> n t", t=128))
            S.sp.dma_start(out=gf[:], in_=sc_g[1].rearrange("(n t) -> n t", t=128))
            S.act.activation(gf[:], gf[:], AF.Exp, scale=-1.0, bias=sm[0:NCH, 0:1])
            S.act.activation(gf[:], gf[:], AF.Ln, bias=1.0)
            S.dve.tensor_tensor_scan(Bc[:], gf[:], zer[:], 0.0, ALU.add, ALU.max)
            S.dve.scalar_tensor_tensor(rr[:], gi[:], pv[0:NCH, 8:9], Bc[:], ALU.add, ALU.add)
            S.dve.tensor_tensor_scan(Ml[:], rr[:], rr[:], -1e30, ALU.max, ALU.max)
            S.dve.tensor_copy(col2[:, 0:1], Ml[:, 127:128])
            S.dve.tensor_scalar_mul(col2[:, 1:2], Bc[:, 127:128], -1.0)
            S.pe.transpose(pA[0:1, 0:NCH], col2[:, 0:1], ident[0:NCH, 0:NCH])
            S.pe.transpose(pA[0:1, NCH:2 * NCH], col2[:, 1:2], ident[0:NCH, 0:NCH])
            S.dve.tensor_copy(rows[:].rearrange("p a n -> p (a n)"), pA[0:1, 0:2 * NCH])
            S.dve.tensor_tensor_scan(mall[:], rows[:, 0, :], rows[:, 1, :], 0.0, ALU.max, ALU.add)
            S.dve.memset(r3[:, 0, 0:1], 0.0)
            if NCH > 1:
                S.dve.tensor_copy(r3[:, 0, 1:NCH], mall[:, 0:NCH - 1])
            S.dve.tensor_tensor(r3[:, 1, :], r3[:, 0, :], rows[:, 0, :], ALU.max)
            S.dve.tensor_tensor(r3[:, 2, :], r3[:, 0, :], r3[:, 1, :], ALU.subtract)
            S.act.activation(r3[:, 2, :], r3[:, 2, :], AF.Exp)
            S.pe.matmul(pA[:, 0:3 * NCH], ones1[:], r3[:].rearrange("p a n -> p (a n)"), start=True, stop=True)
            S.dve.tensor_copy(bc[:].rearrange("p a n -> p (a n)"), pA[:, 0:3 * NCH])
            S.pe.transpose(pB[0:NCH, 0:1], r3[:, 0, :], ident[0:1, 0:1])
            S.dve.tensor_copy(mpc[:], pB[0:NCH, 0:1])
            S.dve.tensor_scalar(Mfull[:], Ml[:], mpc[:, 0:1], None, ALU.max)
            S.dve.tensor_tensor(Xn[:], Bc[:], Mfull[:], ALU.subtract)
            S.act.activation(Xn[:], Xn[:], AF.Exp)
            S.pe.transpose(pB[:, 0:NCH], Xn[:], ident[0:NCH, 0:NCH])
            S.dve.tensor_copy(Xcol[:], pB[:, 0:NCH])
            S.pe.transpose(pB[:, 0:NCH], rr[:], ident[0:NCH, 0:NCH])
            S.dve.tensor_copy(rcol[:], pB[:, 0:NCH])
            S.dve.tensor_tensor(wcol[:], rcol[:], bc[:, 1, :], ALU.subtract)
            S.act.activation(wcol[:], wcol[:], AF.Exp)
            S.dve.tensor_scalar_mul(wcol[:], wcol[:], 0.125)
        S.barrier()
        with ExitStack() as s2:
            xin = _sb(s2, nc, "D_xin", [128, SEQ + 4], F32)
            cacc = _sb(s2, nc, "D_cacc", [128, SEQ], F32)
            S.pool.memset(xin[:, 0:3], 0.0)
            S.sp.dma_start(out=xin[:, 3:SEQ + 3], in_=sc_qkd)
            S.dve.tensor_scalar(cacc[:], xin[:, 0:SEQ], pv[:, 0:1], None, ALU.mult)
            for j in range(1, 4):
                S.dve.scalar_tensor_tensor(cacc[:], xin[:, j:j + SEQ], pv[:, j:j + 1], cacc[:], ALU.mult, ALU.add)
            S.act.activation(qTb[:], cacc[0:64, :], AF.Silu, bias=pv[0:64, 4:5])
            S.act.activation(kTb[:], cacc[64:128, :], AF.Silu, bias=pv[64:128, 4:5])
        S.barrier()
        with ExitStack() as s3:
            vaug = _sb(s3, nc, "D_vaug", [128, NCH, 66], BF16)
            ktm = _sb(s3, nc, "D_ktm", [128, NCH, 64], BF16)
            kw = _sb(s3, nc, "D_kw", [128, NCH, 64], BF16)
            Cst = _sb(s3, nc, "D_Cst", [64, 66], F32)
            Cb = [_sb(s3, nc, "D_Cb%d" % i, [64, 66], BF16) for i in range(2)]
            tmp = [_sb(s3, nc, "D_tmp%d" % i, [128, 128], F32) for i in range(2)]
            pp = [_sb(s3, nc, "D_pp%d" % i, [128, 128], F32) for i in range(2)]
            swT = [_sb(s3, nc, "D_swT%d" % i, [128, 128], BF16) for i in range(2)]
            ech = [_sb(s3, nc, "D_ech%d" % i, [64, 128], F32) for i in range(2)]
            qe = [_sb(s3, nc, "D_qe%d" % i, [64, 128], BF16) for i in range(2)]
            dsg = [_sb(s3, nc, "D_dsg%d" % i, [128, 4, 64], F32) for i in range(2)]
            hh = [_sb(s3, nc, "D_hh%d" % i, [128, 4, 64], F32) for i in range(2)]
            dd = [_sb(s3, nc, "D_dd%d" % i, [128, 16], F32) for i in range(2)]
            stats = [_sb(s3, nc, "D_st%d" % i, [128, 4, 6], F32) for i in range(2)]
            mv = [_sb(s3, nc, "D_mv%d" % i, [128, 4, 2], F32) for i in range(2)]
            og = [_sb(s3, nc, "D_og%d" % i, [64, 512], BF16) for i in range(2)]
            pkt = [_ps(s3, nc, "D_pkt%d" % i, [128, 64], BF16) for i in range(1)]
            pqk = [_ps(s3, nc, "D_pqk%d" % i, [128, 128]) for i in range(2)]
            ph = [_ps(s3, nc, "D_ph%d" % i, [128, 4, 66]) for i in range(2)]
            pc = _ps(s3, nc, "D_pc", [64, 66])
            pt = _ps(s3, nc, "D_pt", [64, 512])
            pM = _ps(s3, nc, "D_pM", [128, 128])
            S.set_gran(vaug, 66)
            S.set_gran(ktm, 64)
            S.pool.memset(vaug[:, :, 64:66], 1.0)
            S.pool.dma_start(out=vaug[:, :, 0:64], in_=sc_tm[:, 128:192].rearrange("(n s) d -> s n d", s=128))
            S.pool.memset(Cst[:], 0.0)
            for n in range(NCH):
                c = slice(n * 128, (n + 1) * 128)
                S.pe.transpose(pkt[0][:], kTb[:, c], identb[0:64, 0:64])
                S.act.copy(ktm[:, n, :], pkt[0][:])
            wb = wcol[:].unsqueeze(2).to_broadcast([128, NCH, 64])
            S.pool.tensor_tensor(kw[:], ktm[:], wb, ALU.mult)
            def d0(n):
                b = n % 2
                c = slice(n * 128, (n + 1) * 128)
                if n % 4 == 0:
                    g4 = (n // 4) % 2
                    nn = min(4, NCH - n)
                    S.sp.dma_start(out=dsg[g4][:, 0:nn, :],
                                   in_=sc_tm[n * 128:(n + nn) * 128, 192:256].rearrange("(n s) d -> s n d", s=128))
                    S.act.activation(dsg[g4][:, 0:nn, :], dsg[g4][:, 0:nn, :], AF.Exp, scale=-1.0)
                    S.pool.tensor_scalar_add(dsg[g4][:, 0:nn, :], dsg[g4][:, 0:nn, :], 1.0)
                    S.dve.reciprocal(dsg[g4][:, 0:nn, :], dsg[g4][:, 0:nn, :])
                S.pe.matmul(pqk[b][:], kTb[:, c], qTb[:, c], start=True, stop=True)
                S.pe.matmul(pM[:], OH[:, n, :], Mfull[:], start=True, stop=True)
                S.dve.scalar_tensor_tensor(tmp[b][:], mask_le[:], rcol[:, n:n + 1], pM[:], ALU.add, ALU.subtract)
                S.act.activation(pp[b][:], tmp[b][:], AF.Exp)
                S.dve.scalar_tensor_tensor(swT[b][:], pqk[b][:], 0.125, pp[b][:], ALU.mult, ALU.mult)
                if n > 0:
                    S.act.activation(ech[b][:], pM[0:64, :], AF.Exp, scale=-1.0, bias=mprev[0:64, n:n + 1])
                    S.pool.tensor_tensor(qe[b][:], qTb[:, c], ech[b][:], ALU.mult)

            def d1(n):
                b = n % 2
                pg = ph[(n // 4) % 2]
                S.pe.matmul(pg[:, n % 4, 0:65], swT[b][:], vaug[:, n, 0:65], start=True, stop=(n == 0))
                if n > 0:
                    S.pe.matmul(pg[:, n % 4, 0:65], qe[b][:], Cb[(n - 1) % 2][:, 0:65], start=False, stop=True)
                if n < NCH - 1:
                    S.pe.matmul(pc[:, 0:65], kw[:, n, :], vaug[:, n, 0:65], start=True, stop=True)
                    S.dve.scalar_tensor_tensor(Cst[:, 0:65], Cst[:, 0:65], aexp[0:64, n:n + 1], pc[:, 0:65], ALU.mult, ALU.add)
                    S.act.copy(Cb[n % 2][:, 0:65], Cst[:, 0:65])

            def d2(n):
                if n % 4 != 3:
                    return
                g = (n // 4) % 2
                n0 = n - 3
                pg = ph[g]
                den = pg[:, :, 64]
                S.dve.tensor_scalar_mul(dd[g][:, 0:4], den, -1.0)
                S.dve.tensor_tensor(dd[g][:, 0:4], dd[g][:, 0:4], den, ALU.max)
                S.dve.tensor_tensor(dd[g][:, 0:4], dd[g][:, 0:4], Xcol[:, n0:n0 + 4], ALU.max)
                S.dve.reciprocal(dd[g][:, 4:8], dd[g][:, 0:4])
                S.dve.tensor_tensor(hh[g][:], pg[:, :, 0:64], dd[g][:, 4:8].unsqueeze(2).to_broadcast([128, 4, 64]), ALU.mult)
                for k in range(4):
                    S.dve.bn_stats(stats[g][:, k, :], hh[g][:, k, :])
                    S.dve.bn_aggr(mv[g][:, k, :], stats[g][:, k:k + 1, :])
                S.dve.tensor_scalar_add(dd[g][:, 8:12], mv[g][:, :, 1], LN_EPS)
                S.act.activation(dd[g][:, 8:12], dd[g][:, 8:12], AF.Ln)
                S.act.activation(dd[g][:, 12:16], dd[g][:, 8:12], AF.Exp, scale=-0.5)
                S.dve.tensor_tensor(hh[g][:], hh[g][:], mv[g][:, :, 0:1].to_broadcast([128, 4, 64]), ALU.subtract)
                S.dve.tensor_tensor(hh[g][:], hh[g][:], dd[g][:, 12:16].unsqueeze(2).to_broadcast([128, 4, 64]), ALU.mult)
                S.pool.tensor_tensor(hh[g][:], hh[g][:], bv[:, 128:192].unsqueeze(1).to_broadcast([128, 4, 64]), ALU.mult)
                S.pool.tensor_tensor(hh[g][:], hh[g][:], dsg[g][:], ALU.mult)

            def d3(n):
                if n % 4 != 3:
                    return
                g = (n // 4) % 2
                n0 = n - 3
                for k in range(4):
                    S.pe.transpose(pt[:, k * 128:(k + 1) * 128], hh[g][:, k, :], ident[:])
                S.act.copy(og[g][:], pt[:])
                S.sp.dma_start(out=outf(192, n0 * 128, (n + 1) * 128), in_=og[g][:])

            assert NCH % 4 == 0
            _pipeline(NCH, [d0, d1, d2, d3])


I32 = mybir.dt.int32
GROUPS = [[0, 1, 2, 3], [4, 5, 6, 7]]


def build_fused(SEQ, depth=DEPTH, do=("A", "B", "C", "D"), ffn=True, exch=True):
    T = SEQ // 4
    NTILE = SEQ // 128
    nc = bass.Bass("TRN2", target_bir_lowering=False)
    L = depth
    dr = lambda name, shape, dt=F32: nc.dram_tensor(name, shape, dt, kind="ExternalInput").ap()
    x0g = dr("x0g", [4 * D_MODEL * (T // 512), 512])
    xres0 = dr("xres0", [T, D_MODEL])
    w_fm = dr("w_fm", [L, D_MODEL, 768])
    w_tm = dr("w_tm", [L, D_MODEL, NTM])
    ropeA = dr("ropeA", [2, 64, SEQ])
    ropeC = dr("ropeC", [2, 64, SEQ])
    pvec = dr("pvec", [L, 128, 16])
    bvec = dr("bvec", [L, 704])
    sgu_wT = dr("sgu_wT", [L, 128, 128])
    w_out = dr("w_out", [L, D_MODEL, D_MODEL])
    w_gate = dr("w_gate", [L, D_MODEL, D_FF])
    w_up = dr("w_up", [L, D_MODEL, D_FF])
    w_down = dr("w_down", [L, D_FF, D_MODEL])
    lnp = dr("lnp", [L, 4, D_MODEL])
    gidx = dr("gidx", [128, 8], I32)
    y = nc.dram_tensor("y", [T, D_MODEL], F32, kind="ExternalOutput").ap()
    P0 = dict(
        sc_qa=nc.dram_tensor("sc_qa", [64, SEQ], BF16).ap(), sc_ka=nc.dram_tensor("sc_ka", [64, SEQ], BF16).ap(),
        sc_qc=nc.dram_tensor("sc_qc", [64, SEQ], BF16).ap(), sc_kc=nc.dram_tensor("sc_kc", [64, SEQ], BF16).ap(),
        sc_qkd=nc.dram_tensor("sc_qkd", [128, SEQ], F32).ap(), sc_g=nc.dram_tensor("sc_g", [2, SEQ], F32).ap(),
        sc_tm=nc.dram_tensor("sc_tm", [SEQ, NTM], F32).ap())
    NTC = T // 512
    mixb = nc.dram_tensor("mixb", [256, SEQ], BF16).ap()
    mixg = nc.dram_tensor("mixg", [4 * 256, SEQ], BF16).ap()
    yTb = nc.dram_tensor("yTb", [NTC * D_MODEL, 512], BF16).ap()
    xTg = nc.dram_tensor("xTg", [NTC * 4 * D_MODEL, 512], BF16).ap()
    xres_d = nc.dram_tensor("xres_d", [T, D_MODEL], F32).ap()
    S = Sched(nc)
    S.set_gran(mixb, 64 * SEQ)
    S.set_gran(mixg, 256 * SEQ)
    S.set_gran(yTb, D_MODEL * 512)
    S.set_gran(xTg, 4 * D_MODEL * 512)
    with ExitStack() as stc:
        _PFX[0] = ""
        C = _consts(S, nc, stc)
        gi = _sb(stc, nc, "gidx_sb", [128, 8], I32)
        S.sp.dma_start(out=gi[:], in_=gidx)
        mixtab = mixg.rearrange("f (n t) -> (f n) t", t=512)
        for l in range(L):
            _PFX[0] = "L%d_" % l
            xg, xeng = (x0g, S.pool) if l == 0 else (xTg, S.sp)

            def xsrc(tt, xg=xg, xeng=xeng):
                r, tc = divmod(tt, NTC)
                r0 = (tc * 4 + r) * D_MODEL
                return xeng, xg[r0:r0 + D_MODEL, :].rearrange("(c p) t -> p c t", p=128)

            def outf(r0, c0, c1):
                return mixb[r0:r0 + 64, c0:c1]

            def after(m):
                if exch:
                    cw = min(SEQ, 2048)
                    S.collective("AllGather", mixb[64 * m:64 * m + 64, :].rearrange("r (a b) -> (r a) b", b=cw),
                                 mixg[256 * m:256 * m + 256, :].rearrange("r (a b) -> (r a) b", b=cw), GROUPS)
            P = dict(P0)
            P.update(xsrc=xsrc, w_fm=w_fm[l], w_tm=w_tm[l], ropeA=ropeA, ropeC=ropeC, pvec=pvec[l], bvec=bvec[l],
                     sgu_wT=sgu_wT[l], outf=outf, after=after,
                     tile_order=[r * NTC + tc for tc in range(NTC) for r in range(4)])
            _mixer_all(S, nc, SEQ, P, C, do)
            S.barrier()
            last = (l == L - 1)

            def mix_gather(tile, i):
                for c in range(8):
                    m = c // 2
                    S.gather_rows(tile[:, c, :], mixtab, gi[:, c:c + 1], i * 512, dep_ap=mixg[256 * m:256 * m + 256, :])

            def yT(i):
                tc = i // 4
                return yTb[tc * D_MODEL:(tc + 1) * D_MODEL, :].rearrange("(c p) t -> p c t", p=128)[:, :, (i % 4) * 128:(i % 4 + 1) * 128]

            def after_tile(i):
                if i % 4 == 3 and exch:
                    tc = i // 4
                    S.collective("AllGather", yTb[tc * D_MODEL:(tc + 1) * D_MODEL, :],
                                 xTg[tc * 4 * D_MODEL:(tc + 1) * 4 * D_MODEL, :], GROUPS)
            PF = dict(mix_gather=mix_gather, xres=(xres0 if l == 0 else xres_d),
                      w_out=w_out[l], w_gate=w_gate[l], w_up=w_up[l], w_down=w_down[l], lnp=lnp[l],
                      y=(y if last else xres_d), yT=(None if last else yT), after_tile=after_tile)
            if ffn:
                _ffn_all(S, nc, T, PF, C["ident"])
            S.barrier()
        S.finish()
    return nc


_PROGS = {}
BATCH = 2
SEQ_FULL = 8192
NCORES = 8


def _fused_inputs(inp, SEQ, depth):
    T = SEQ // 4
    ropes = _rope_tables(SEQ)
    x = inp["x"]
    hm = [[prep_mixer_inputs(inp, l, h, SEQ, ropes) for l in range(depth)] for h in range(4)]
    w_out_p = inp["w_out"][:depth]
    lnp = np.stack([np.stack([inp["ln1_g"][l], inp["ln1_b"][l], inp["ln2_g"][l], inp["ln2_b"][l]]) for l in range(depth)])
    maps = []
    for core in range(NCORES):
        b, j = divmod(core, 4)
        xb = x[b, :SEQ]
        NTC = T // 512
        x0g = np.ascontiguousarray(xb.reshape(4, NTC, 512, D_MODEL).transpose(1, 0, 3, 2).reshape(NTC * 4 * D_MODEL, 512))
        gidx = ((np.arange(8)[None, :] * 128 + np.arange(128)[:, None]) * (SEQ // 512) + j * (T // 512)).astype(np.int32)
        m = {
            "x0g": x0g, "xres0": np.ascontiguousarray(xb[j * T:(j + 1) * T]),
            "w_fm": np.stack([hm[j][l]["w_fm"] for l in range(depth)]),
            "w_tm": np.stack([hm[j][l]["w_tm"] for l in range(depth)]),
            "ropeA": ropes[0], "ropeC": ropes[1],
            "pvec": np.stack([hm[j][l]["pvec"] for l in range(depth)]),
            "bvec": np.stack([hm[j][l]["bvec"] for l in range(depth)]),
            "sgu_wT": np.stack([hm[j][l]["sgu_wT"] for l in range(depth)]),
            "w_out": w_out_p, "w_gate": inp["w_gate"][:depth], "w_up": inp["w_up"][:depth], "w_down": inp["w_down"][:depth],
            "lnp": lnp.astype(np.float32), "gidx": gidx,
        }
        maps.append({k: np.ascontiguousarray(v) for k, v in m.items()})
    return maps


def run_fused(inp, SEQ, depth):
    key = ("fused", SEQ, depth)
    if key not in _PROGS:
        _PROGS[key] = build_fused(SEQ, depth)
    maps = _fused_inputs(inp, SEQ, depth)
    res = run_bass_kernel_spmd(_PROGS[key], maps, core_ids=list(range(NCORES)))
    T = SEQ // 4
    out = np.empty((BATCH, SEQ, D_MODEL), np.float32)
    for core in range(NCORES):
        b, j = divmod(core, 4)
        out[b, j * T:(j + 1) * T] = res.results[core]["y"]
    return out


def kernel(**inputs):
    inp = {k: np.asarray(v, dtype=np.float32) for k, v in inputs.items()}
    return run_fused(inp, SEQ_FULL, DEPTH)
```

```python
import numpy as np
import concourse.bass as bass
import concourse.mybir as mybir

F32 = mybir.dt.float32
BF16 = mybir.dt.bfloat16
AF = mybir.ActivationFunctionType
ALU = mybir.AluOpType
AX = mybir.AxisListType


class _Eng:
    def __init__(self, S, name, eng):
        self.S, self.name, self.eng = S, name, eng
        self.sem = S.nc.alloc_semaphore("es_" + name)
        self.cnt = 0
        self.seen = {}

    def __getattr__(self, op):
        fn = getattr(self.eng, op)

        def call(*args, **kw):
            return self.S._emit(self, op, fn, args, kw)
        return call


class Sched:
    def __init__(self, nc):
        self.nc = nc
        self.pe = _Eng(self, "pe", nc.tensor)
        self.act = _Eng(self, "act", nc.scalar)
        self.dve = _Eng(self, "dve", nc.vector)
        self.pool = _Eng(self, "pool", nc.gpsimd)
        self.sp = _Eng(self, "sp", nc.sync)
        self.engs = [self.pe, self.act, self.dve, self.pool, self.sp]
        self.units = {}
        self.gran = {}
        self.dma_sems = {}
        self.all_sems = {}
        self.n_inst = 0
        self.free_dma = []
        self.cc_sem = None
        self.cc_cnt = 0

    def collective(self, kind, in_ap, out_ap, groups):
        E = self.pool
        reads, writes = self._units(in_ap), self._units(out_ap)
        self._deps(E, reads, writes, same_raw=False)
        if self.cc_sem is None:
            self.cc_sem = self.nc.alloc_semaphore("cc_sem")
        inst = self.nc.gpsimd.collective_compute(kind, ALU.bypass, replica_groups=groups, ins=[in_ap], outs=[out_ap])
        self.cc_cnt += 1
        inst.then_inc(self.cc_sem, 1)
        self._record((self.cc_sem, self.cc_cnt), reads, writes)
        return inst

    def gather_rows(self, out_ap, table_ap, idx_ap, element_offset, dep_ap=None):
        E = self.pool
        reads = self._units(dep_ap if dep_ap is not None else table_ap) + self._units(idx_ap)
        writes = self._units(out_ap)
        skey = (writes[0][0], 0)
        ds = self._dma_sem(skey)
        saved = []
        for key in writes:
            u = self._u(key)
            for k_, t_ in list(u["w"].items()):
                if t_[0] is ds[0]:
                    saved.append((u, k_, t_))
                    del u["w"][k_]
        self._deps(E, reads, writes, same_raw=False)
        inst = self.nc.gpsimd.indirect_dma_start(out=out_ap, out_offset=None, in_=table_ap,
                                                 in_offset=bass.IndirectOffsetOnAxis(ap=idx_ap, axis=0),
                                                 element_offset=element_offset)
        ds[1] += 16
        inst.then_inc(ds[0], 16)
        self._record((ds[0], ds[1]), reads, writes)
        return inst

    def _dma_sem(self, skey):
        ds = self.dma_sems.get(skey)
        if ds is None:
            if self.free_dma:
                ds = self.free_dma.pop()
            else:
                ds = [self.nc.alloc_semaphore("ds%d" % len(self.all_sems)), 0]
            self.dma_sems[skey] = ds
        return ds

    def set_gran(self, t, g):
        self.gran[t.name if hasattr(t, "name") else t] = g

    def _units(self, ap):
        name = ap.tensor.name
        g = self.gran.get(name)
        if g is None:
            return [(name, 0)]
        apl = ap.ap
        space = str(ap.space)
        off = int(ap.offset)
        if "DRAM" in space:
            lo = off
            hi = off + sum((c - 1) * s for s, c in apl)
        else:
            F = 1
            for d in ap.tensor.shape[1:]:
                F *= d
            lo = off % F
            hi = lo + sum((c - 1) * s for s, c in apl[1:])
        return [(name, i) for i in range(lo // g, hi // g + 1)]

    def _u(self, key):
        u = self.units.get(key)
        if u is None:
            u = self.units[key] = {"w": {}, "r": {}}
        return u

    def _wait(self, E, tok):
        sem, val = tok
        k = id(sem)
        if E.seen.get(k, 0) >= val:
            return
        E.eng.wait_ge(sem, val)
        E.seen[k] = val

    def _deps(self, E, reads, writes, same_raw=True):
        toks = []
        for key in reads:
            u = self._u(key)
            toks += list(u["w"].values())
        for key in writes:
            u = self._u(key)
            toks += list(u["w"].values()) + list(u["r"].values())
        for sem, val in toks:
            if sem is E.sem:
                continue
            self._wait(E, (sem, val))
        if same_raw and E is not self.pe:
            for key in reads:
                u = self._u(key)
                for sem, val in u["w"].values():
                    if sem is E.sem:
                        self._wait(E, (sem, val))

    def _record(self, tok, reads, writes):
        sem, val = tok
        k = id(sem)
        for key in reads:
            self._u(key)["r"][k] = tok
        for key in writes:
            u = self._u(key)
            u["w"] = {k: tok}
            u["r"] = {}
        self.all_sems[k] = tok

    def _emit(self, E, op, fn, args, kw):
        if op in ("dma_start",):
            return self._dma(E, fn, args, kw)
        lazy = kw.pop("lazy", False)
        aps = []
        out = kw.get("out", None)
        outs = []
        first = True
        for a in list(args) + [v for k_, v in kw.items()]:
            if isinstance(a, bass.AP):
                aps.append(a)
        if out is not None:
            outs = [out]
        elif args and isinstance(args[0], bass.AP):
            outs = [args[0]]
        if kw.get("accum_out") is not None:
            outs.append(kw["accum_out"])
        out_ids = [id(o) for o in outs]
        reads, writes = [], []
        for a in aps:
            if id(a) in out_ids:
                writes += self._units(a)
            else:
                reads += self._units(a)
        self._deps(E, reads, writes)
        inst = fn(*args, **kw)
        if lazy and kw.get("stop", True) is False:
            self._record((E.sem, E.cnt + 1), reads, writes)
            self.n_inst += 1
            return inst
        E.cnt += 1
        inst.then_inc(E.sem, 1)
        self._record((E.sem, E.cnt), reads, writes)
        self.n_inst += 1
        return inst

    def _dma(self, E, fn, args, kw):
        out = kw.get("out", args[0] if args else None)
        in_ = kw.get("in_", args[1] if len(args) > 1 else None)
        writes = self._units(out)
        reads = self._units(in_)
        if "DRAM" not in str(out.space):
            skey = (writes[0][0], 0)
        elif "DRAM" not in str(in_.space):
            skey = (reads[0][0], 0)
        else:
            skey = (writes[0][0], 0)
        self._deps(E, reads, writes, same_raw=False)
        ds = self._dma_sem(skey)
        inst = fn(*args, **kw)
        ds[1] += 16
        inst.then_inc(ds[0], 16)
        self._record((ds[0], ds[1]), reads, writes)
        self.n_inst += 1
        return inst

    def barrier(self, final=False):
        toks = [t for t in self.all_sems.values() if (final or t[0] is not self.cc_sem)]
        for E in self.engs:
            for tok in toks:
                if tok[0] is E.sem:
                    continue
                self._wait(E, tok)
        keep = {}
        for key, u in self.units.items():
            w = {k: t for k, t in u["w"].items() if t[0] is self.cc_sem}
            r = {k: t for k, t in u["r"].items() if t[0] is self.cc_sem}
            if (w or r) and not final:
                keep[key] = {"w": w, "r": r}
        self.units = keep
        self.free_dma += list(self.dma_sems.values())
        self.dma_sems = {}

    def finish(self):
        self.barrier(final=True)


from contextlib import ExitStack
from concourse.bass_utils import run_bass_kernel_spmd

D_MODEL = 1024
D_FF = 2816
DEPTH = 4
ALPHA = (2 * DEPTH) ** 0.25
LN_EPS = 1e-5


_PFX = [""]


def _sb(st, nc, name, shape, dt):
    return st.enter_context(nc.sbuf_tensor(_PFX[0] + name, shape, dt))


def _ps(st, nc, name, shape, dt=F32):
    return st.enter_context(nc.psum_tensor(_PFX[0] + name, shape, dt))


def _pipeline(n, stages, lag=1):
    ns = len(stages)
    for step in range(n + (ns - 1) * lag):
        for si, f in enumerate(stages):
            i = step - si * lag
            if 0 <= i < n:
                f(i)


def _make_ident(S, nc, ident):
    S.pool.memset(ident[:], 1.0)
    S.pool.affine_select(ident[:], ident[:], [[-1, 128]], ALU.is_equal, 0.0, base=0, channel_multiplier=1)


def _layernorm_rows(S, nc, t, width, stats, mv, g_bc, b_bc, out):
    nch = width // 512
    for c in range(nch):
        S.dve.bn_stats(stats[:, c, :], t[:, c * 512:(c + 1) * 512])
    S.dve.bn_aggr(mv[:, 0:2], stats[:, 0:nch, :])
    S.dve.tensor_scalar_add(mv[:, 2:3], mv[:, 1:2], LN_EPS)
    S.act.sqrt(mv[:, 2:3], mv[:, 2:3])
    S.dve.reciprocal(mv[:, 3:4], mv[:, 2:3])
    S.dve.tensor_scalar(t, t, mv[:, 0:1], mv[:, 3:4], ALU.subtract, ALU.mult)
    S.pool.tensor_tensor(t, t, g_bc, ALU.mult)
    S.pool.tensor_tensor(out, t, b_bc, ALU.add)


def _ffn_all(S, nc, T, P, ident):
    NT = T // 128
    NTT = T // 512
    mix_gather, xres, w_out, w_gate, w_up, w_down, lnp, y, yT = (P[k] for k in (
        "mix_gather", "xres", "w_out", "w_gate", "w_up", "w_down", "lnp", "y", "yT"))
    with ExitStack() as st0:
        lnbc = _sb(st0, nc, "lnbc", [128, 4, D_MODEL], F32)
        yacc = _sb(st0, nc, "yacc", [128, NT, D_MODEL], F32)
        x1T = _sb(st0, nc, "x1T", [128, 8, T], BF16)
        S.set_gran(yacc, D_MODEL)
        S.set_gran(lnbc, D_MODEL)
        for j in range(4):
            S.sp.dma_start(out=lnbc[:, j, :], in_=lnp[j].partition_broadcast(128))
        with ExitStack() as st:
            wout_b = _sb(st, nc, "wout_b", [128, 8, D_MODEL], BF16)
            mixb = [_sb(st, nc, "mixb%d" % i, [128, 8, 512], BF16) for i in range(2)]
            xr = [_sb(st, nc, "xr%d" % i, [128, D_MODEL], F32) for i in range(2)]
            t1 = [_sb(st, nc, "t1_%d" % i, [128, D_MODEL], F32) for i in range(3)]
            stats = [_sb(st, nc, "stats%d" % i, [128, 2, 6], F32) for i in range(2)]
            mv = [_sb(st, nc, "mv%d" % i, [128, 4], F32) for i in range(2)]
            psh = [_ps(st, nc, "psh%d" % i, [128, D_MODEL]) for i in range(2)]
            pst = [_ps(st, nc, "pst%d" % i, [128, D_MODEL]) for i in range(2)]
            S.pool.dma_start(out=wout_b[:], in_=w_out.rearrange("(c p) f -> p c f", p=128))
            def f0(i):
                b = i % 2
                tsl = slice(i * 128, (i + 1) * 128)
                mb = mixb[(i // 4) % 2]
                if i == 0:
                    mix_gather(mixb[0], 0)
                if i % 4 == 0 and i // 4 + 1 < NT // 4:
                    mix_gather(mixb[(i // 4 + 1) % 2], i // 4 + 1)
                S.sp.dma_start(out=xr[b][:], in_=xres[tsl, :])
                for half in range(2):
                    for k in range(8):
                        S.pe.matmul(psh[b][:, half * 512:(half + 1) * 512], mb[:, k, (i % 4) * 128:(i % 4 + 1) * 128],
                                    wout_b[:, k, half * 512:(half + 1) * 512], start=(k == 0), stop=(k == 7), lazy=True)
                S.dve.scalar_tensor_tensor(t1[i % 3][:], xr[b][:], ALPHA, psh[b][:], ALU.mult, ALU.add)

            def f1(i):
                b = i % 2
                _layernorm_rows(S, nc, t1[i % 3][:], D_MODEL, stats[b], mv[b], lnbc[:, 0, :], lnbc[:, 1, :], t1[i % 3][:])
                S.act.mul(yacc[:, i, :], t1[i % 3][:], ALPHA)

            def f2(i):
                b = i % 2
                tsl = slice(i * 128, (i + 1) * 128)
                for c in range(8):
                    S.pe.transpose(pst[b][:, c * 128:(c + 1) * 128], t1[i % 3][:, c * 128:(c + 1) * 128], ident[:])
                S.act.copy(x1T[:, 0:4, tsl], pst[b][:, 0:512].rearrange("p (c t) -> p c t", c=4))
                S.dve.tensor_copy(x1T[:, 4:8, tsl], pst[b][:, 512:1024].rearrange("p (c t) -> p c t", c=4))

            _pipeline(NT, [f0, f1, f2])
        S.barrier()
        with ExitStack() as st:
            FB = 256
            NFB = D_FF // FB
            wg_b = [_sb(st, nc, "wg_b%d" % i, [128, 8, FB], BF16) for i in range(2)]
            wu_b = [_sb(st, nc, "wu_b%d" % i, [128, 8, FB], BF16) for i in range(2)]
            wd_b = [_sb(st, nc, "wd_b%d" % i, [128, 2, D_MODEL], BF16) for i in range(2)]
            sg = [_sb(st, nc, "sg%d" % i, [128, 512], F32) for i in range(2)]
            actT = [_sb(st, nc, "actT%d" % i, [128, 2, 512], BF16) for i in range(2)]
            psg = [_ps(st, nc, "psg%d" % i, [128, 512]) for i in range(2)]
            psu = [_ps(st, nc, "psu%d" % i, [128, 512]) for i in range(2)]
            psd = [_ps(st, nc, "psd%d" % i, [128, D_MODEL]) for i in range(2)]
            wg_v = w_gate.rearrange("(c p) f -> p c f", p=128)
            wu_v = w_up.rearrange("(c p) f -> p c f", p=128)
            wd_v = w_down.rearrange("(c p) f -> p c f", p=128)
            items = [(fb, tt) for fb in range(NFB) for tt in range(NTT)]

            def load_w(fb):
                wb = fb % 2
                fsl = slice(fb * FB, (fb + 1) * FB)
                S.pool.dma_start(out=wg_b[wb][:], in_=wg_v[:, :, fsl])
                S.pool.dma_start(out=wu_b[wb][:], in_=wu_v[:, :, fsl])
                S.pool.dma_start(out=wd_b[wb][:], in_=wd_v[:, 2 * fb:2 * fb + 2, :])

            def g0(it):
                fb, tt = items[it]
                wb = fb % 2
                ab = it % 2
                if it == 0:
                    load_w(0)
                    if NFB > 1:
                        load_w(1)
                tsl = slice(tt * 512, (tt + 1) * 512)
                for c2 in range(2):
                    for k in range(8):
                        S.pe.matmul(psg[c2][:], wg_b[wb][:, k, c2 * 128:(c2 + 1) * 128], x1T[:, k, tsl],
                                    start=(k == 0), stop=(k == 7), lazy=True)
                    for k in range(8):
                        S.pe.matmul(psu[c2][:], wu_b[wb][:, k, c2 * 128:(c2 + 1) * 128], x1T[:, k, tsl],
                                    start=(k == 0), stop=(k == 7), lazy=True)
                    S.act.activation(sg[c2][:], psg[c2][:], AF.Silu)
                    S.dve.tensor_tensor(actT[ab][:, c2, :], sg[c2][:], psu[c2][:], ALU.mult)

            def g1(it):
                fb, tt = items[it]
                wb = fb % 2
                ab = it % 2
                for s4 in range(4):
                    db = (it * 4 + s4) % 2
                    for half in range(2):
                        for c2 in range(2):
                            S.pe.matmul(psd[db][:, half * 512:(half + 1) * 512],
                                        actT[ab][:, c2, s4 * 128:(s4 + 1) * 128],
                                        wd_b[wb][:, c2, half * 512:(half + 1) * 512],
                                        start=(c2 == 0), stop=(c2 == 1), lazy=True)
                    ti = tt * 4 + s4
                    S.dve.tensor_tensor(yacc[:, ti, :], yacc[:, ti, :], psd[db][:], ALU.add)
                if tt == NTT - 1 and fb + 2 < NFB:
                    load_w(fb + 2)

            _pipeline(len(items), [g0, g1])
        S.barrier()
        with ExitStack() as st:
            stats = [_sb(st, nc, "stats3_%d" % i, [128, 2, 6], F32) for i in range(2)]
            mv = [_sb(st, nc, "mv3_%d" % i, [128, 4], F32) for i in range(2)]
            ob = [_sb(st, nc, "ob%d" % i, [128, D_MODEL], F32) for i in range(2)]
            obT = [_sb(st, nc, "obT%d" % i, [128, 8, 128], BF16) for i in range(2)]
            pst3 = [_ps(st, nc, "pst3_%d" % i, [128, D_MODEL]) for i in range(2)]
            def h0(i):
                b = i % 2
                _layernorm_rows(S, nc, yacc[:, i, :], D_MODEL, stats[b], mv[b], lnbc[:, 2, :], lnbc[:, 3, :], ob[b][:])

            def h1(i):
                b = i % 2
                S.sp.dma_start(out=y[i * 128:(i + 1) * 128, :], in_=ob[b][:])
                if yT is not None:
                    for c in range(8):
                        S.pe.transpose(pst3[b][:, c * 128:(c + 1) * 128], ob[b][:, c * 128:(c + 1) * 128], ident[:])
                    S.act.copy(obT[b][:, 0:4, :], pst3[b][:, 0:512].rearrange("p (c t) -> p c t", c=4))
                    S.dve.tensor_copy(obT[b][:, 4:8, :], pst3[b][:, 512:1024].rearrange("p (c t) -> p c t", c=4))
                    S.sp.dma_start(out=yT(i), in_=obT[b][:])
                    P["after_tile"](i)

            _pipeline(NT, [h0, h1])


HD = 64
NTM = 580
NEG = -30000.0


def _gelu_tanh(S, nc, out, x, tmp):
    S.act.activation(tmp, x, AF.Square)
    S.dve.tensor_scalar(tmp, tmp, 0.044715, 1.0, ALU.mult, ALU.add)
    S.pool.tensor_tensor(tmp, tmp, x, ALU.mult)
    S.act.activation(tmp, tmp, AF.Sigmoid, scale=2.0 * 0.7978845608028654)
    S.dve.tensor_tensor(out, x, tmp, ALU.mult)


def _consts(S, nc, st0):
    C = {}
    C["pv"] = pv = _sb(st0, nc, "pv", [128, 16], F32)
    C["bv"] = bv = _sb(st0, nc, "bv", [128, 704], F32)
    C["ident"] = ident = _sb(st0, nc, "identf", [128, 128], F32)
    C["identb"] = identb = _sb(st0, nc, "identb", [128, 128], BF16)
    C["mask_le"] = mask_le = _sb(st0, nc, "mask_le", [128, 128], F32)
    C["mask_le_b"] = mask_le_b = _sb(st0, nc, "mask_le_b", [128, 128], BF16)
    C["mask_ge_b"] = mask_ge_b = _sb(st0, nc, "mask_ge_b", [128, 128], BF16)
    C["tri"] = tri = _sb(st0, nc, "tri", [128, 128], F32)
    _make_ident(S, nc, ident)
    S.dve.tensor_copy(identb[:], ident[:])
    S.pool.memset(mask_le[:], 0.0)
    S.pool.affine_select(mask_le[:], mask_le[:], [[1, 128]], ALU.is_ge, NEG, base=0, channel_multiplier=-1)
    S.dve.tensor_copy(mask_le_b[:], mask_le[:])
    S.pool.memset(tri[:], 1.0)
    S.pool.affine_select(tri[:], tri[:], [[1, 128]], ALU.is_ge, 0.0, base=0, channel_multiplier=-1)
    S.pool.memset(mask_ge_b[:], 0.0)
    S.pool.affine_select(mask_ge_b[:], mask_ge_b[:], [[-1, 128]], ALU.is_ge, NEG, base=0, channel_multiplier=1)
    return C


def _mixer_all(S, nc, SEQ, P, C, do=("A", "B", "C", "D")):
    NTT = SEQ // 512
    xsrc, w_fm, w_tm, ropeA, ropeC, pvec, bvec, sgu_wT, outf = (P[k] for k in (
        "xsrc", "w_fm", "w_tm", "ropeA", "ropeC", "pvec", "bvec", "sgu_wT", "outf"))
    sc_qa, sc_ka, sc_qc, sc_kc, sc_qkd, sc_g, sc_tm = (P[k] for k in ("sc_qa", "sc_ka", "sc_qc", "sc_kc", "sc_qkd", "sc_g", "sc_tm"))
    pv, bv, ident, identb, mask_le, mask_le_b, mask_ge_b, tri = (C[k] for k in (
        "pv", "bv", "ident", "identb", "mask_le", "mask_le_b", "mask_ge_b", "tri"))
    S.sp.dma_start(out=pv[:], in_=pvec)
    S.sp.dma_start(out=bv[:], in_=bvec.partition_broadcast(128))
    if True:
        with ExitStack() as st:
            wfm_b = _sb(st, nc, "wfm_b", [128, 8, 768], BF16)
            wtm_b = _sb(st, nc, "wtm_b", [128, 8, NTM], BF16)
            xb = [_sb(st, nc, "xb%d" % i, [128, 8, 512], BF16) for i in range(2)]
            rA = [_sb(st, nc, "rA%d" % i, [64, 2, 512], F32) for i in range(2)]
            rC = [_sb(st, nc, "rC%d" % i, [64, 2, 512], F32) for i in range(2)]
            ta = [_sb(st, nc, "ta%d" % i, [64, 512], F32) for i in range(2)]
            tb = [_sb(st, nc, "tb%d" % i, [64, 512], F32) for i in range(2)]
            stg = [_sb(st, nc, "stg%d" % i, [64, 512], BF16) for i in range(4)]
            stg5 = [_sb(st, nc, "stg5_%d" % i, [128, 512], F32) for i in range(2)]
            stg6 = [_sb(st, nc, "stg6_%d" % i, [2, 512], F32) for i in range(2)]
            stgt = [_sb(st, nc, "stgt%d" % i, [128, NTM], F32) for i in range(2)]
            ps1 = [_ps(st, nc, "ps1_%d" % i, [128, 512]) for i in range(2)]
            ps2 = [_ps(st, nc, "ps2_%d" % i, [128, 512]) for i in range(2)]
            ps5 = _ps(st, nc, "ps5", [128, 512])
            pstm = _ps(st, nc, "pstm", [128, 1024])
            S.pool.dma_start(out=wfm_b[:], in_=w_fm.rearrange("(c p) f -> p c f", p=128))
            S.pool.dma_start(out=wtm_b[:], in_=w_tm.rearrange("(c p) f -> p c f", p=128))
            rA_v = ropeA.rearrange("two r t -> r two t")
            rC_v = ropeC.rearrange("two r t -> r two t")
            for it_, tt in enumerate(P.get("tile_order", range(NTT))):
                b = it_ % 2
                tsl = slice(tt * 512, (tt + 1) * 512)
                xe, xap = xsrc(tt)
                xe.dma_start(out=xb[b][:], in_=xap)
                S.sp.dma_start(out=rA[b][:], in_=rA_v[:, :, tsl])
                S.sp.dma_start(out=rC[b][:], in_=rC_v[:, :, tsl])
                def do_pair(pair):
                    rt, dq, dk = ((rA[b], sc_qa, sc_ka), (rC[b], sc_qc, sc_kc))[pair]
                    pb = (2 * it_ + pair) % 2
                    g1 = 2 * pair
                    for k in range(8):
                        S.pe.matmul(ps1[pb][:], wfm_b[:, k, g1 * 128:(g1 + 1) * 128], xb[b][:, k, :], start=(k == 0), stop=(k == 7), lazy=True)
                    for k in range(8):
                        S.pe.matmul(ps2[pb][:], wfm_b[:, k, (g1 + 1) * 128:(g1 + 2) * 128], xb[b][:, k, :], start=(k == 0), stop=(k == 7), lazy=True)
                    for half, dst in enumerate((dq, dk)):
                        rows = slice(half * 64, (half + 1) * 64)
                        tbuf = half
                        S.dve.tensor_tensor(ta[tbuf][:], ps2[pb][rows, :], rt[:, 1, :], ALU.mult)
                        S.dve.tensor_tensor(tb[tbuf][:], ps1[pb][rows, :], rt[:, 0, :], ALU.mult)
                        sb_ = (4 * it_ + 2 * pair + half) % 4
                        S.pool.tensor_tensor(stg[sb_][:], ta[tbuf][:], tb[tbuf][:], ALU.add)
                        S.sp.dma_start(out=dst[:, tsl], in_=stg[sb_][:])

                def do_g5():
                    for k in range(8):
                        S.pe.matmul(ps5[:], wfm_b[:, k, 512:640], xb[b][:, k, :], start=(k == 0), stop=(k == 7), lazy=True)
                    S.act.copy(stg5[b][:], ps5[:])
                    S.sp.dma_start(out=sc_qkd[:, tsl], in_=stg5[b][:])

                def do_g6():
                    for k in range(8):
                        S.pe.matmul(ps5[0:2, :], wfm_b[:, k, 640:642], xb[b][:, k, :], start=(k == 0), stop=(k == 7), lazy=True)
                    S.act.copy(stg6[b][:], ps5[0:2, :])
                    S.sp.dma_start(out=sc_g[:, tsl], in_=stg6[b][:])

                def do_sub(sub):
                    tb_ = (4 * it_ + sub) % 2
                    for k in range(8):
                        S.pe.matmul(pstm[:, 0:512], xb[b][:, k, sub * 128:(sub + 1) * 128], wtm_b[:, k, 0:512], start=(k == 0), stop=(k == 7), lazy=True)
                    for k in range(8):
                        S.pe.matmul(pstm[:, 512:NTM], xb[b][:, k, sub * 128:(sub + 1) * 128], wtm_b[:, k, 512:NTM], start=(k == 0), stop=(k == 7), lazy=True)
                    S.act.copy(stgt[tb_][:], pstm[:, 0:NTM])
                    r0 = tt * 512 + sub * 128
                    S.sp.dma_start(out=sc_tm[r0:r0 + 128, :], in_=stgt[tb_][:])

                do_pair(0)
                do_sub(0)
                do_g5()
                do_sub(1)
                do_pair(1)
                do_sub(2)
                do_g6()
                do_sub(3)
        S.barrier()
        if "B" in do:
            _mixer_B(S, nc, SEQ, sc_tm, sgu_wT, pv, bv, ident, tri, outf)
            P["after"](1)
            S.barrier()
        if "A" in do:
            _mixer_A(S, nc, SEQ, sc_qa, sc_ka, sc_tm, pv, bv, outf)
            P["after"](0)
            S.barrier()
        if "C" in do:
            _mixer_C(S, nc, SEQ, sc_qc, sc_kc, sc_tm, identb, mask_le_b, mask_ge_b, outf)
            P["after"](2)
            S.barrier()
        if "D" in do:
            _mixer_D(S, nc, SEQ, sc_qkd, sc_g, sc_tm, pv, bv, ident, identb, mask_le, tri, outf)
            P["after"](3)


def _mixer_B(S, nc, SEQ, sc_tm, sgu_wT, pv, bv, ident, tri, outf):
    NIT = SEQ // 512
    with ExitStack() as st:
        wT = _sb(st, nc, "sg_wT", [128, 128], F32)
        wTb = _sb(st, nc, "sg_wTb", [128, 128], BF16)
        uv = [_sb(st, nc, "sg_uv%d" % i, [128, 4, 320], F32) for i in range(2)]
        gl = [_sb(st, nc, "sg_gl%d" % i, [128, 4, 320], F32) for i in range(2)]
        tmp = [_sb(st, nc, "sg_tmp%d" % i, [128, 4, 320], F32) for i in range(2)]
        stats = [_sb(st, nc, "sg_st%d" % i, [128, 4, 6], F32) for i in range(2)]
        mv = [_sb(st, nc, "sg_mv%d" % i, [128, 4, 2], F32) for i in range(2)]
        rs = [_sb(st, nc, "sg_rs%d" % i, [128, 4], F32) for i in range(2)]
        vn = [_sb(st, nc, "sg_vn%d" % i, [128, 4, 64], F32) for i in range(2)]
        vnb = [_sb(st, nc, "sg_vnb%d" % i, [128, 4, 64], BF16) for i in range(2)]
        ob = [_sb(st, nc, "sg_ob%d" % i, [128, 4, 64], F32) for i in range(2)]
        og = [_sb(st, nc, "sg_og%d" % i, [64, 512], BF16) for i in range(2)]
        psz = [_ps(st, nc, "sg_psz%d" % i, [128, 4, 64]) for i in range(2)]
        pst = [_ps(st, nc, "sg_pst%d" % i, [64, 512]) for i in range(2)]
        S.sp.dma_start(out=wT[:], in_=sgu_wT)
        S.dve.tensor_tensor(wTb[:], wT[:], tri[:], ALU.mult)
        gbc = bv[:, 0:64].unsqueeze(1).to_broadcast([128, 4, 64])
        bbc = bv[:, 64:128].unsqueeze(1).to_broadcast([128, 4, 64])

        def b0(it):
            b = it % 2
            r0 = it * 512
            S.sp.dma_start(out=uv[b][:], in_=sc_tm[r0:r0 + 512, 256:576].rearrange("(n p) c -> p n c", p=128))
            _gelu_tanh(S, nc, gl[b][:], uv[b][:], tmp[b][:])
            for k in range(4):
                S.dve.bn_stats(stats[b][:, k, :], gl[b][:, k, 64:320])
                S.dve.bn_aggr(mv[b][:, k, :], stats[b][:, k:k + 1, :])
            S.dve.tensor_scalar_add(rs[b][:], mv[b][:, :, 1], LN_EPS)
            S.act.sqrt(rs[b][:], rs[b][:])
            S.dve.reciprocal(rs[b][:], rs[b][:])
            S.dve.tensor_tensor(vn[b][:], gl[b][:, :, 64:128], mv[b][:, :, 0:1].to_broadcast([128, 4, 64]), ALU.subtract)
            S.dve.tensor_tensor(vn[b][:], vn[b][:], rs[b][:].unsqueeze(2).to_broadcast([128, 4, 64]), ALU.mult)
            S.pool.tensor_tensor(vn[b][:], vn[b][:], gbc, ALU.mult)
            S.pool.tensor_tensor(vnb[b][:], vn[b][:], bbc, ALU.add)

        def b1(it):
            b = it % 2
            for k in range(4):
                S.pe.matmul(psz[b][:, k, :], wTb[:], vnb[b][:, k, :], start=True, stop=True)
            S.dve.scalar_tensor_tensor(ob[b][:], psz[b][:], pv[:, 10:11], gl[b][:, :, 0:64], ALU.add, ALU.mult)

        def b2(it):
            b = it % 2
            for k in range(4):
                S.pe.transpose(pst[b][:, k * 128:(k + 1) * 128], ob[b][:, k, :], ident[:])
            S.act.copy(og[b][:], pst[b][:])
            S.sp.dma_start(out=outf(64, it * 512, (it + 1) * 512), in_=og[b][:])

        _pipeline(NIT, [b0, b1, b2])


def _mixer_A(S, nc, SEQ, sc_qa, sc_ka, sc_tm, pv, bv, outf):
    NTT = SEQ // 512
    NKB = SEQ // 128
    scale = 32 ** -0.5
    with ExitStack() as st:
        qT = _sb(st, nc, "A_qT", [64, SEQ], BF16)
        kT = _sb(st, nc, "A_kT", [64, SEQ], BF16)
        va = _sb(st, nc, "A_va", [128, NKB, 128], BF16)
        lam = _sb(st, nc, "A_lam", [64, 8], F32)
        lt = _sb(st, nc, "A_lt", [64, 64], F32)
        ones_ms = _sb(st, nc, "A_ones", [64, 64], F32)
        E = [_sb(st, nc, "A_E%d" % i, [128, 512], BF16) for i in range(6)]
        rd = [_sb(st, nc, "A_rd%d" % i, [64, 512], F32) for i in range(2)]
        o0 = _sb(st, nc, "A_o0", [64, 512], F32)
        o1 = _sb(st, nc, "A_o1", [64, 512], F32)
        sq = _sb(st, nc, "A_sq", [64, 512], F32)
        og = [_sb(st, nc, "A_og%d" % i, [64, 512], BF16) for i in range(2)]
        pss = [_ps(st, nc, "A_pss%d" % i, [128, 512]) for i in range(6)]
        pso = [_ps(st, nc, "A_pso%d" % i, [128, 512]) for i in range(2)]
        psm = pss[0]
        S.set_gran(va, 128)
        S.sp.dma_start(out=qT[:], in_=sc_qa)
        S.sp.dma_start(out=kT[:], in_=sc_ka)
        S.pool.memset(va[:, :, 64:128], 1.0)
        S.pool.dma_start(out=va[:, :, 0:64], in_=sc_tm[:, 0:64].rearrange("(n p) d -> p n d", p=128))
        S.pool.memset(ones_ms[:], 1.0 / 64.0)
        S.dve.tensor_tensor(lt[:, 0:32], bv[0:64, 192:224], bv[0:64, 224:256], ALU.mult)
        S.dve.tensor_tensor(lt[:, 32:64], bv[0:64, 256:288], bv[0:64, 288:320], ALU.mult)
        S.dve.tensor_reduce(lam[:, 0:1], lt[:, 0:32], AX.X, ALU.add)
        S.dve.tensor_reduce(lam[:, 1:2], lt[:, 32:64], AX.X, ALU.add)
        S.act.activation(lam[:, 2:4], lam[:, 0:2], AF.Exp)
        S.dve.tensor_tensor(lam[:, 4:5], lam[:, 2:3], lam[:, 3:4], ALU.subtract)
        S.dve.tensor_tensor(lam[:, 4:5], lam[:, 4:5], pv[0:64, 7:8], ALU.add)
        S.dve.tensor_scalar_mul(lam[:, 5:6], lam[:, 4:5], -1.0)
        S.dve.tensor_tensor(lam[:, 6:7], pv[0:64, 5:6], pv[0:64, 6:7], ALU.mult)
        blocks = []
        for t in range(NTT):
            nkb = 4 * (t + 1)
            for kb in range(nkb):
                blocks.append((t, kb, nkb))
        LA = 2
        NB_ = len(blocks)

        def front(i):
            t, kb, nkb = blocks[i]
            q0 = t * 512
            j = kb - 4 * t
            c0 = max(j, 0) * 128
            for m in range(2):
                rows = slice(32 * m, 32 * m + 32)
                e = (2 * i + m) % 6
                S.pe.matmul(pss[e][:, c0:512], kT[rows, kb * 128:(kb + 1) * 128], qT[rows, q0 + c0:q0 + 512],
                            start=True, stop=True)
            for m in range(2):
                e = (2 * i + m) % 6
                S.act.activation(E[e][:, c0:512], pss[e][:, c0:512], AF.Exp, scale=scale)
                if j >= 0:
                    S.pool.affine_select(E[e][:, c0:c0 + 128], E[e][:, c0:c0 + 128], [[1, 128]], ALU.is_ge, 0.0,
                                         base=0, channel_multiplier=-1)

        def back(i):
            t, kb, nkb = blocks[i]
            j = kb - 4 * t
            c0 = max(j, 0) * 128
            for m in range(2):
                e = (2 * i + m) % 6
                po = pso[m]
                S.pe.matmul(po[:, c0:512], va[:, kb, :], E[e][:, c0:512], start=(kb == 0), stop=(kb == nkb - 1))
            if kb == nkb - 1:
                epilogue(t)

        def epilogue(t):
            q0 = t * 512
            p0 = pso[0]
            p1 = pso[1]
            S.act.activation(rd[0][:], p0[64:128, :], AF.Ln)
            S.act.activation(rd[0][:], rd[0][:], AF.Exp, scale=-1.0)
            S.dve.tensor_tensor(o0[:], p0[0:64, :], rd[0][:], ALU.mult)
            S.act.activation(rd[1][:], p1[64:128, :], AF.Ln)
            S.act.activation(rd[1][:], rd[1][:], AF.Exp, scale=-1.0)
            S.dve.tensor_tensor(o1[:], p1[0:64, :], rd[1][:], ALU.mult)
            S.dve.scalar_tensor_tensor(o0[:], o1[:], lam[:, 5:6], o0[:], ALU.mult, ALU.add)
            S.pool.tensor_tensor(sq[:], o0[:], o0[:], ALU.mult)
            S.pe.matmul(psm[0:64, :], ones_ms[:], sq[:], start=True, stop=True)
            S.dve.tensor_scalar_add(sq[:], psm[0:64, :], LN_EPS)
            S.act.activation(sq[:], sq[:], AF.Ln)
            S.act.activation(sq[:], sq[:], AF.Exp, scale=-0.5)
            S.pool.tensor_tensor(o1[:], o0[:], sq[:], ALU.mult)
            S.dve.tensor_scalar(og[t % 2][:], o1[:], lam[:, 6:7], None, ALU.mult)
            S.sp.dma_start(out=outf(0, q0, q0 + 512), in_=og[t % 2][:])

        for i in range(NB_ + LA):
            if i < NB_:
                front(i)
            if i - LA >= 0:
                back(i - LA)


import math as _math

ROPE_THETA = 500000.0


def _rope_tables(SEQ):
    pos = np.arange(SEQ, dtype=np.float32)

    def tab(rot, blk):
        half = rot // 2
        inv = (np.float32(ROPE_THETA) ** (-(np.arange(0, rot, 2, dtype=np.float32)) / np.float32(rot))).astype(np.float32)
        ang = (pos[:, None] * inv[None, :]).astype(np.float32)
        c, s = np.cos(ang).astype(np.float32).T, np.sin(ang).astype(np.float32).T
        C = np.ones((blk, SEQ), np.float32)
        Sn = np.zeros((blk, SEQ), np.float32)
        C[0:half] = c
        C[half:2 * half] = c
        Sn[0:half] = -s
        Sn[half:2 * half] = s
        return C, Sn
    Ca, Sa = tab(8, 32)
    Cc, Sc = tab(16, 64)
    ropeA = np.stack([np.concatenate([Ca, Ca], 0), np.concatenate([Sa, Sa], 0)]).astype(np.float32)
    ropeC = np.stack([Cc, Sc]).astype(np.float32)
    return np.ascontiguousarray(ropeA), np.ascontiguousarray(ropeC)


def _perm_idx(base, n, half):
    idx = np.arange(n)
    d = idx.copy()
    d[0:half] = idx[0:half] + half
    d[half:2 * half] = idx[half:2 * half] - half
    return base + d


def prep_mixer_inputs(inp, l, h, SEQ, ropes):
    w_in = inp["w_in"][l]
    c = lambda off: off + h * 64 + np.arange(64)
    aq, ak, av = c(0), c(256), c(512)
    bu = c(768)
    cq, ck, cv = c(1280), c(1536), c(1792)
    dq, dk, dv, do_ = c(2048), c(2304), c(2560), c(2816)
    aqp = np.concatenate([_perm_idx(aq[0], 32, 4), _perm_idx(aq[32], 32, 4)])
    akp = np.concatenate([_perm_idx(ak[0], 32, 4), _perm_idx(ak[32], 32, 4)])
    cqp = _perm_idx(cq[0], 64, 8)
    ckp = _perm_idx(ck[0], 64, 8)
    di, df = 3072 + h, 3076 + h
    g6 = np.concatenate([[di, df], np.full(126, di)])
    fm_cols = np.concatenate([aq, ak, aqp, akp, cq, ck, cqp, ckp, dq, dk, g6])
    bv_all = 1024 + np.concatenate([h * 64 + np.arange(64)] + [g * 64 + np.arange(64) for g in range(4) if g != h])
    tm_cols = np.concatenate([av, cv, dv, do_, bu, bv_all, [di, df, di, df]])
    assert fm_cols.size == 768 and tm_cols.size == NTM
    lam_init = 0.8 - 0.6 * _math.exp(-0.3 * l)
    pvec = np.zeros((128, 16), np.float32)
    chan = np.concatenate([h * 64 + np.arange(64), 256 + h * 64 + np.arange(64)])
    pvec[:, 0:4] = inp["mlstm_conv_w"][l][:, chan].T
    pvec[:, 4] = inp["mlstm_conv_b"][l][chan]
    pvec[:, 5] = np.tile(inp["diff_subln_g"][l], 2)
    pvec[:, 6] = 1.0 - lam_init
    pvec[:, 7] = lam_init
    pvec[:, 8] = inp["mlstm_gate_b"][l][0, h]
    pvec[:, 9] = inp["mlstm_gate_b"][l][1, h]
    pvec[:, 10] = inp["sgu_b"][l][h]
    bvec = np.zeros(704, np.float32)
    bvec[0:64] = inp["sgu_ln_g"][l][h * 64:(h + 1) * 64]
    bvec[64:128] = inp["sgu_ln_b"][l][h * 64:(h + 1) * 64]
    bvec[128:192] = inp["mlstm_norm_g"][l]
    bvec[192:320] = inp["diff_lambda"][l].reshape(-1)
    return {
        "w_fm": np.ascontiguousarray(w_in[:, fm_cols]),
        "w_tm": np.ascontiguousarray(w_in[:, tm_cols]),
        "ropeA": ropes[0], "ropeC": ropes[1],
        "pvec": pvec, "bvec": bvec,
        "sgu_wT": np.ascontiguousarray(inp["sgu_w"][l][h].T),
    }


def _mixer_C(S, nc, SEQ, sc_qc, sc_kc, sc_tm, identb, mask_le_b, mask_ge_b, outf):
    pats = (1, 4, 16)
    SB = 2048 if SEQ >= 2048 else SEQ
    NSB = SEQ // SB
    NBLK = SEQ // 128
    with ExitStack() as st:
        qT = _sb(st, nc, "C_qT", [64, SEQ], BF16)
        kT = _sb(st, nc, "C_kT", [64, SEQ], BF16)
        vd = [_sb(st, nc, "C_vd%d" % i, [128, NBLK, 128], BF16) for i in range(3)]
        acc = [_sb(st, nc, "C_acc%d" % i, [128, SB], F32) for i in range(2)]
        E = [_sb(st, nc, "C_E%d" % i, [128, 2, 128], BF16) for i in range(3)]
        rd = _sb(st, nc, "C_rd", [64, SB], F32)
        og = _sb(st, nc, "C_og", [64, SB], BF16)
        pss = [_ps(st, nc, "C_pss%d" % i, [128, 2, 128]) for i in range(3)]
        pso = [_ps(st, nc, "C_pso%d" % i, [128, 128]) for i in range(3)]
        for i in range(3):
            S.set_gran(vd[i], 128)
        S.sp.dma_start(out=qT[:], in_=sc_qc)
        S.sp.dma_start(out=kT[:], in_=sc_kc)
        for pi, dil in enumerate(pats):
            S.pool.memset(vd[pi][:, :, 64:128], 1.0)
            nb = SEQ // (128 * dil)
            src = sc_tm[:, 64:128].rearrange("(n j r) d -> j n r d", j=128, r=dil)
            dst = vd[pi][:, :, 0:64].rearrange("j (n r) d -> j n r d", r=dil)
            for n in range(nb):
                S.pool.dma_start(out=dst[:, n], in_=src[:, n])
        blocks = []
        for sb in range(NSB):
            first = True
            for pi, dil in enumerate(pats):
                span = 128 * dil
                for n in range(sb * SB // span, (sb + 1) * SB // span):
                    for r in range(dil):
                        blocks.append([sb, pi, dil, n, r, first, False])
                        first = False
            blocks[-1][6] = True

        def c0(i):
            sb, pi, dil, n, r, first, lastb = blocks[i]
            span = 128 * dil
            e = i % 3
            if first:
                S.pool.memset(acc[sb % 2][:], 0.0)
            qs = slice(n * span + r, (n + 1) * span, dil)
            if n >= 1:
                kprev = slice((n - 1) * span + r, n * span, dil)
                S.pe.matmul(pss[e][:, 0, :], kT[:, kprev], qT[:, qs], start=True, stop=False)
                S.pe.matmul(pss[e][:, 0, :], identb[:], mask_ge_b[:], start=False, stop=True)
            S.pe.matmul(pss[e][:, 1, :], kT[:, qs], qT[:, qs], start=True, stop=False)
            S.pe.matmul(pss[e][:, 1, :], identb[:], mask_le_b[:], start=False, stop=True)
            lo = 0 if n >= 1 else 1
            S.act.activation(E[e][:, lo:2, :], pss[e][:, lo:2, :], AF.Exp, scale=0.125)

        def c1(i):
            sb, pi, dil, n, r, first, lastb = blocks[i]
            span = 128 * dil
            e = i % 3
            a = acc[sb % 2]
            kbs = ([(0, (n - 1) * dil + r)] if n >= 1 else []) + [(1, n * dil + r)]
            for ii, (slot, blk) in enumerate(kbs):
                S.pe.matmul(pso[e][:], vd[pi][:, blk, :], E[e][:, slot, :], start=(ii == 0), stop=(ii == len(kbs) - 1))
            loc = slice(n * span + r - sb * SB, (n + 1) * span - sb * SB, dil)
            S.dve.tensor_tensor(a[:, loc], a[:, loc], pso[e][:], ALU.add)
            if lastb:
                S.act.activation(rd[:], a[64:128, :], AF.Ln)
                S.act.activation(rd[:], rd[:], AF.Exp, scale=-1.0)
                S.dve.tensor_tensor(og[:], a[0:64, :], rd[:], ALU.mult)
                PW = min(SB, 512)
                for pc_ in range(SB // PW):
                    S.sp.dma_start(out=outf(128, sb * SB + pc_ * PW, sb * SB + (pc_ + 1) * PW), in_=og[:, pc_ * PW:(pc_ + 1) * PW])

        _pipeline(len(blocks), [c0, c1], lag=2)


import math as _math

ROPE_THETA = 500000.0


def _rope_tables(SEQ):
    pos = np.arange(SEQ, dtype=np.float32)

    def tab(rot, blk):
        half = rot // 2
        inv = (np.float32(ROPE_THETA) ** (-(np.arange(0, rot, 2, dtype=np.float32)) / np.float32(rot))).astype(np.float32)
        ang = (pos[:, None] * inv[None, :]).astype(np.float32)
        c, s = np.cos(ang).astype(np.float32).T, np.sin(ang).astype(np.float32).T
        C = np.ones((blk, SEQ), np.float32)
        Sn = np.zeros((blk, SEQ), np.float32)
        C[0:half] = c
        C[half:2 * half] = c
        Sn[0:half] = -s
        Sn[half:2 * half] = s
        return C, Sn
    Ca, Sa = tab(8, 32)
    Cc, Sc = tab(16, 64)
    ropeA = np.stack([np.concatenate([Ca, Ca], 0), np.concatenate([Sa, Sa], 0)]).astype(np.float32)
    ropeC = np.stack([Cc, Sc]).astype(np.float32)
    return np.ascontiguousarray(ropeA), np.ascontiguousarray(ropeC)


def _perm_idx(base, n, half):
    idx = np.arange(n)
    d = idx.copy()
    d[0:half] = idx[0:half] + half
    d[half:2 * half] = idx[half:2 * half] - half
    return base + d


def prep_mixer_inputs(inp, l, h, SEQ, ropes):
    w_in = inp["w_in"][l]
    c = lambda off: off + h * 64 + np.arange(64)
    aq, ak, av = c(0), c(256), c(512)
    bu = c(768)
    cq, ck, cv = c(1280), c(1536), c(1792)
    dq, dk, dv, do_ = c(2048), c(2304), c(2560), c(2816)
    aqp = np.concatenate([_perm_idx(aq[0], 32, 4), _perm_idx(aq[32], 32, 4)])
    akp = np.concatenate([_perm_idx(ak[0], 32, 4), _perm_idx(ak[32], 32, 4)])
    cqp = _perm_idx(cq[0], 64, 8)
    ckp = _perm_idx(ck[0], 64, 8)
    di, df = 3072 + h, 3076 + h
    g6 = np.concatenate([[di, df], np.full(126, di)])
    fm_cols = np.concatenate([aq, ak, aqp, akp, cq, ck, cqp, ckp, dq, dk, g6])
    bv_all = 1024 + np.concatenate([h * 64 + np.arange(64)] + [g * 64 + np.arange(64) for g in range(4) if g != h])
    tm_cols = np.concatenate([av, cv, dv, do_, bu, bv_all, [di, df, di, df]])
    assert fm_cols.size == 768 and tm_cols.size == NTM
    lam_init = 0.8 - 0.6 * _math.exp(-0.3 * l)
    pvec = np.zeros((128, 16), np.float32)
    chan = np.concatenate([h * 64 + np.arange(64), 256 + h * 64 + np.arange(64)])
    pvec[:, 0:4] = inp["mlstm_conv_w"][l][:, chan].T
    pvec[:, 4] = inp["mlstm_conv_b"][l][chan]
    pvec[:, 5] = np.tile(inp["diff_subln_g"][l], 2)
    pvec[:, 6] = 1.0 - lam_init
    pvec[:, 7] = lam_init
    pvec[:, 8] = inp["mlstm_gate_b"][l][0, h]
    pvec[:, 9] = inp["mlstm_gate_b"][l][1, h]
    pvec[:, 10] = inp["sgu_b"][l][h]
    bvec = np.zeros(704, np.float32)
    bvec[0:64] = inp["sgu_ln_g"][l][h * 64:(h + 1) * 64]
    bvec[64:128] = inp["sgu_ln_b"][l][h * 64:(h + 1) * 64]
    bvec[128:192] = inp["mlstm_norm_g"][l]
    bvec[192:320] = inp["diff_lambda"][l].reshape(-1)
    return {
        "w_fm": np.ascontiguousarray(w_in[:, fm_cols]),
        "w_tm": np.ascontiguousarray(w_in[:, tm_cols]),
        "ropeA": ropes[0], "ropeC": ropes[1],
        "pvec": pvec, "bvec": bvec,
        "sgu_wT": np.ascontiguousarray(inp["sgu_w"][l][h].T),
    }


def _mixer_C(S, nc, SEQ, sc_qc, sc_kc, sc_tm, identb, mask_le_b, mask_ge_b, outf):
    pats = (1, 4, 16)
    SB = 2048 if SEQ >= 2048 else SEQ
    NSB = SEQ // SB
    NBLK = SEQ // 128
    with ExitStack() as st:
        qT = _sb(st, nc, "C_qT", [64, SEQ], BF16)
        kT = _sb(st, nc, "C_kT", [64, SEQ], BF16)
        vd = [_sb(st, nc, "C_vd%d" % i, [128, NBLK, 128], BF16) for i in range(3)]
        acc = [_sb(st, nc, "C_acc%d" % i, [128, SB], F32) for i in range(2)]
        E = [_sb(st, nc, "C_E%d" % i, [128, 2, 128], BF16) for i in range(3)]
        rd = _sb(st, nc, "C_rd", [64, SB], F32)
        og = _sb(st, nc, "C_og", [64, SB], BF16)
        pss = [_ps(st, nc, "C_pss%d" % i, [128, 2, 128]) for i in range(3)]
        pso = [_ps(st, nc, "C_pso%d" % i, [128, 128]) for i in range(3)]
        for i in range(3):
            S.set_gran(vd[i], 128)
        S.sp.dma_start(out=qT[:], in_=sc_qc)
        S.sp.dma_start(out=kT[:], in_=sc_kc)
        for pi, dil in enumerate(pats):
            S.pool.memset(vd[pi][:, :, 64:128], 1.0)
            nb = SEQ // (128 * dil)
            src = sc_tm[:, 64:128].rearrange("(n j r) d -> j n r d", j=128, r=dil)
            dst = vd[pi][:, :, 0:64].rearrange("j (n r) d -> j n r d", r=dil)
            for n in range(nb):
                S.pool.dma_start(out=dst[:, n], in_=src[:, n])
        ei = 0
        for sb in range(NSB):
            a = acc[sb % 2]
            S.pool.memset(a[:], 0.0)
            for pi, dil in enumerate(pats):
                span = 128 * dil
                for n in range(sb * SB // span, (sb + 1) * SB // span):
                    for r in range(dil):
                        e = ei % 3
                        ei += 1
                        qs = slice(n * span + r, (n + 1) * span, dil)
                        kcur = qs
                        kbs = []
                        if n >= 1:
                            kprev = slice((n - 1) * span + r, n * span, dil)
                            S.pe.matmul(pss[e][:, 0, :], kT[:, kprev], qT[:, qs], start=True, stop=False)
                            S.pe.matmul(pss[e][:, 0, :], identb[:], mask_ge_b[:], start=False, stop=True)
                            kbs.append((0, (n - 1) * dil + r))
                        S.pe.matmul(pss[e][:, 1, :], kT[:, kcur], qT[:, qs], start=True, stop=False)
                        S.pe.matmul(pss[e][:, 1, :], identb[:], mask_le_b[:], start=False, stop=True)
                        kbs.append((1, n * dil + r))
                        lo = kbs[0][0]
                        S.act.activation(E[e][:, lo:2, :], pss[e][:, lo:2, :], AF.Exp, scale=0.125)
                        for ii, (slot, blk) in enumerate(kbs):
                            S.pe.matmul(pso[e][:], vd[pi][:, blk, :], E[e][:, slot, :], start=(ii == 0), stop=(ii == len(kbs) - 1))
                        loc = slice(n * span + r - sb * SB, (n + 1) * span - sb * SB, dil)
                        S.dve.tensor_tensor(a[:, loc], a[:, loc], pso[e][:], ALU.add)
            S.dve.reciprocal(rd[:], a[64:128, :])
            S.dve.tensor_tensor(og[:], a[0:64, :], rd[:], ALU.mult)
            PW = min(SB, 512)
            for pc_ in range(SB // PW):
                S.sp.dma_start(out=outf(128, sb * SB + pc_ * PW, sb * SB + (pc_ + 1) * PW), in_=og[:, pc_ * PW:(pc_ + 1) * PW])


def _mixer_D(S, nc, SEQ, sc_qkd, sc_g, sc_tm, pv, bv, ident, identb, mask_le, tri, outf):
    NCH = SEQ // 128
    with ExitStack() as st:
        qTb = _sb(st, nc, "D_qT", [64, SEQ], BF16)
        kTb = _sb(st, nc, "D_kT", [64, SEQ], BF16)
        sm = _sb(st, nc, "D_sm", [128, 8], F32)
        Mfull = _sb(st, nc, "D_Mfull", [NCH, 128], F32)
        OH = _sb(st, nc, "D_OH", [NCH, NCH, 128], F32)
        bc = _sb(st, nc, "D_bc", [128, 3, NCH], F32)
        Xcol = _sb(st, nc, "D_Xcol", [128, NCH], F32)
        rcol = _sb(st, nc, "D_rcol", [128, NCH], F32)
        wcol = _sb(st, nc, "D_wcol", [128, NCH], F32)
        mprev = bc[:, 0, :]
        aexp = bc[:, 2, :]
        S.dve.tensor_scalar_mul(sm[:, 0:1], pv[:, 9:10], -1.0)
        S.pool.memset(OH[:], 1.0)
        S.pool.affine_select(OH[:], OH[:], [[-1, NCH], [0, 128]], ALU.is_equal, 0.0, base=0, channel_multiplier=1)
        with ExitStack() as s1:
            gi = _sb(s1, nc, "D_gi", [NCH, 128], F32)
            gf = _sb(s1, nc, "D_gf", [NCH, 128], F32)
            Bc = _sb(s1, nc, "D_Bc", [NCH, 128], F32)
            rr = _sb(s1, nc, "D_rr", [NCH, 128], F32)
            Ml = _sb(s1, nc, "D_Ml", [NCH, 128], F32)
            Xn = _sb(s1, nc, "D_Xn", [NCH, 128], F32)
            zer = _sb(s1, nc, "D_zer", [NCH, 128], F32)
            col2 = _sb(s1, nc, "D_col2", [NCH, 2], F32)
            rows = _sb(s1, nc, "D_rows", [1, 2, NCH], F32)
            r3 = _sb(s1, nc, "D_r3", [1, 3, NCH], F32)
            mall = _sb(s1, nc, "D_mall", [1, NCH], F32)
            mpc = _sb(s1, nc, "D_mpc", [NCH, 1], F32)
            ones1 = _sb(s1, nc, "D_ones1", [1, 128], F32)
            pA = _ps(s1, nc, "D_pA", [128, 3 * NCH])
            pB = _ps(s1, nc, "D_pB", [128, 128])
            S.pool.memset(zer[:], 0.0)
            S.pool.memset(ones1[:], 1.0)
            S.sp.dma_start(out=gi[:], in_=sc_g[0].rearrange("(n t) -> n t", t=128))
            S.sp.dma_start(out=gf[:], in_=sc_g[1].rearrange("(n t) -> n t", t=128))
            S.act.activation(gf[:], gf[:], AF.Exp, scale=-1.0, bias=sm[0:NCH, 0:1])
            S.act.activation(gf[:], gf[:], AF.Ln, bias=1.0)
            S.dve.tensor_tensor_scan(Bc[:], gf[:], zer[:], 0.0, ALU.add, ALU.max)
            S.dve.scalar_tensor_tensor(rr[:], gi[:], pv[0:NCH, 8:9], Bc[:], ALU.add, ALU.add)
            S.dve.tensor_tensor_scan(Ml[:], rr[:], rr[:], -1e30, ALU.max, ALU.max)
            S.dve.tensor_copy(col2[:, 0:1], Ml[:, 127:128])
            S.dve.tensor_scalar_mul(col2[:, 1:2], Bc[:, 127:128], -1.0)
            S.pe.transpose(pA[0:1, 0:NCH], col2[:, 0:1], ident[0:NCH, 0:NCH])
            S.pe.transpose(pA[0:1, NCH:2 * NCH], col2[:, 1:2], ident[0:NCH, 0:NCH])
            S.dve.tensor_copy(rows[:].rearrange("p a n -> p (a n)"), pA[0:1, 0:2 * NCH])
            S.dve.tensor_tensor_scan(mall[:], rows[:, 0, :], rows[:, 1, :], 0.0, ALU.max, ALU.add)
            S.dve.memset(r3[:, 0, 0:1], 0.0)
            if NCH > 1:
                S.dve.tensor_copy(r3[:, 0, 1:NCH], mall[:, 0:NCH - 1])
            S.dve.tensor_tensor(r3[:, 1, :], r3[:, 0, :], rows[:, 0, :], ALU.max)
            S.dve.tensor_tensor(r3[:, 2, :], r3[:, 0, :], r3[:, 1, :], ALU.subtract)
            S.act.activation(r3[:, 2, :], r3[:, 2, :], AF.Exp)
            S.pe.matmul(pA[:, 0:3 * NCH], ones1[:], r3[:].rearrange("p a n -> p (a n)"), start=True, stop=True)
            S.dve.tensor_copy(bc[:].rearrange("p a n -> p (a n)"), pA[:, 0:3 * NCH])
            S.pe.transpose(pB[0:NCH, 0:1], r3[:, 0, :], ident[0:1, 0:1])
            S.dve.tensor_copy(mpc[:], pB[0:NCH, 0:1])
            S.dve.tensor_scalar(Mfull[:], Ml[:], mpc[:, 0:1], None, ALU.max)
            S.dve.tensor_tensor(Xn[:], Bc[:], Mfull[:], ALU.subtract)
            S.act.activation(Xn[:], Xn[:], AF.Exp)
            S.pe.transpose(pB[:, 0:NCH], Xn[:], ident[0:NCH, 0:NCH])
            S.dve.tensor_copy(Xcol[:], pB[:, 0:NCH])
            S.pe.transpose(pB[:, 0:NCH], rr[:], ident[0:NCH, 0:NCH])
            S.dve.tensor_copy(rcol[:], pB[:, 0:NCH])
            S.dve.tensor_tensor(wcol[:], rcol[:], bc[:, 1, :], ALU.subtract)
            S.act.activation(wcol[:], wcol[:], AF.Exp)
            S.dve.tensor_scalar_mul(wcol[:], wcol[:], 0.125)
        S.barrier()
        with ExitStack() as s2:
            xin = _sb(s2, nc, "D_xin", [128, SEQ + 4], F32)
            cacc = _sb(s2, nc, "D_cacc", [128, SEQ], F32)
            S.pool.memset(xin[:, 0:3], 0.0)
            S.sp.dma_start(out=xin[:, 3:SEQ + 3], in_=sc_qkd)
            S.dve.tensor_scalar(cacc[:], xin[:, 0:SEQ], pv[:, 0:1], None, ALU.mult)
            for j in range(1, 4):
                S.dve.scalar_tensor_tensor(cacc[:], xin[:, j:j + SEQ], pv[:, j:j + 1], cacc[:], ALU.mult, ALU.add)
            S.act.activation(qTb[:], cacc[0:64, :], AF.Silu, bias=pv[0:64, 4:5])
            S.act.activation(kTb[:], cacc[64:128, :], AF.Silu, bias=pv[64:128, 4:5])
        S.barrier()
        with ExitStack() as s3:
            vaug = _sb(s3, nc, "D_vaug", [128, NCH, 66], BF16)
            ktm = _sb(s3, nc, "D_ktm", [128, NCH, 64], BF16)
            kw = _sb(s3, nc, "D_kw", [128, NCH, 64], BF16)
            Cst = _sb(s3, nc, "D_Cst", [64, 66], F32)
            Cb = [_sb(s3, nc, "D_Cb%d" % i, [64, 66], BF16) for i in range(2)]
            tmp = [_sb(s3, nc, "D_tmp%d" % i, [128, 128], F32) for i in range(2)]
            pp = [_sb(s3, nc, "D_pp%d" % i, [128, 128], F32) for i in range(2)]
            swT = [_sb(s3, nc, "D_swT%d" % i, [128, 128], BF16) for i in range(2)]
            ech = [_sb(s3, nc, "D_ech%d" % i, [64, 128], F32) for i in range(2)]
            qe = [_sb(s3, nc, "D_qe%d" % i, [64, 128], BF16) for i in range(2)]
            dsg = [_sb(s3, nc, "D_dsg%d" % i, [128, 4, 64], F32) for i in range(2)]
            hh = [_sb(s3, nc, "D_hh%d" % i, [128, 4, 64], F32) for i in range(2)]
            dd = [_sb(s3, nc, "D_dd%d" % i, [128, 16], F32) for i in range(2)]
            stats = [_sb(s3, nc, "D_st%d" % i, [128, 4, 6], F32) for i in range(2)]
            mv = [_sb(s3, nc, "D_mv%d" % i, [128, 4, 2], F32) for i in range(2)]
            og = [_sb(s3, nc, "D_og%d" % i, [64, 512], BF16) for i in range(2)]
            pkt = [_ps(s3, nc, "D_pkt%d" % i, [128, 64], BF16) for i in range(1)]
            pqk = [_ps(s3, nc, "D_pqk%d" % i, [128, 128]) for i in range(2)]
            ph = [_ps(s3, nc, "D_ph%d" % i, [128, 4, 66]) for i in range(2)]
            pc = _ps(s3, nc, "D_pc", [64, 66])
            pt = _ps(s3, nc, "D_pt", [64, 512])
            pM = _ps(s3, nc, "D_pM", [128, 128])
            S.set_gran(vaug, 66)
            S.set_gran(ktm, 64)
            S.pool.memset(vaug[:, :, 64:66], 1.0)
            S.pool.dma_start(out=vaug[:, :, 0:64], in_=sc_tm[:, 128:192].rearrange("(n s) d -> s n d", s=128))
            S.pool.memset(Cst[:], 0.0)
            for n in range(NCH):
                c = slice(n * 128, (n + 1) * 128)
                S.pe.transpose(pkt[0][:], kTb[:, c], identb[0:64, 0:64])
                S.act.copy(ktm[:, n, :], pkt[0][:])
            wb = wcol[:].unsqueeze(2).to_broadcast([128, NCH, 64])
            S.pool.tensor_tensor(kw[:], ktm[:], wb, ALU.mult)
            def d0(n):
                b = n % 2
                c = slice(n * 128, (n + 1) * 128)
                if n % 4 == 0:
                    g4 = (n // 4) % 2
                    nn = min(4, NCH - n)
                    S.sp.dma_start(out=dsg[g4][:, 0:nn, :],
                                   in_=sc_tm[n * 128:(n + nn) * 128, 192:256].rearrange("(n s) d -> s n d", s=128))
                    S.act.activation(dsg[g4][:, 0:nn, :], dsg[g4][:, 0:nn, :], AF.Exp, scale=-1.0)
                    S.pool.tensor_scalar_add(dsg[g4][:, 0:nn, :], dsg[g4][:, 0:nn, :], 1.0)
                    S.dve.reciprocal(dsg[g4][:, 0:nn, :], dsg[g4][:, 0:nn, :])
                S.pe.matmul(pqk[b][:], kTb[:, c], qTb[:, c], start=True, stop=True)
                S.pe.matmul(pM[:], OH[:, n, :], Mfull[:], start=True, stop=True)
                S.dve.scalar_tensor_tensor(tmp[b][:], mask_le[:], rcol[:, n:n + 1], pM[:], ALU.add, ALU.subtract)
                S.act.activation(pp[b][:], tmp[b][:], AF.Exp)
                S.dve.scalar_tensor_tensor(swT[b][:], pqk[b][:], 0.125, pp[b][:], ALU.mult, ALU.mult)
                if n > 0:
                    S.act.activation(ech[b][:], pM[0:64, :], AF.Exp, scale=-1.0, bias=mprev[0:64, n:n + 1])
                    S.pool.tensor_tensor(qe[b][:], qTb[:, c], ech[b][:], ALU.mult)

            def d1(n):
                b = n % 2
                pg = ph[(n // 4) % 2]
                S.pe.matmul(pg[:, n % 4, 0:65], swT[b][:], vaug[:, n, 0:65], start=True, stop=(n == 0))
                if n > 0:
                    S.pe.matmul(pg[:, n % 4, 0:65], qe[b][:], Cb[(n - 1) % 2][:, 0:65], start=False, stop=True)
                if n < NCH - 1:
                    S.pe.matmul(pc[:, 0:65], kw[:, n, :], vaug[:, n, 0:65], start=True, stop=True)
                    S.dve.scalar_tensor_tensor(Cst[:, 0:65], Cst[:, 0:65], aexp[0:64, n:n + 1], pc[:, 0:65], ALU.mult, ALU.add)
                    S.act.copy(Cb[n % 2][:, 0:65], Cst[:, 0:65])

            def d2(n):
                if n % 4 != 3:
                    return
                g = (n // 4) % 2
                n0 = n - 3
                pg = ph[g]
                den = pg[:, :, 64]
                S.dve.tensor_scalar_mul(dd[g][:, 0:4], den, -1.0)
                S.dve.tensor_tensor(dd[g][:, 0:4], dd[g][:, 0:4], den, ALU.max)
                S.dve.tensor_tensor(dd[g][:, 0:4], dd[g][:, 0:4], Xcol[:, n0:n0 + 4], ALU.max)
                S.dve.reciprocal(dd[g][:, 4:8], dd[g][:, 0:4])
                S.dve.tensor_tensor(hh[g][:], pg[:, :, 0:64], dd[g][:, 4:8].unsqueeze(2).to_broadcast([128, 4, 64]), ALU.mult)
                for k in range(4):
                    S.dve.bn_stats(stats[g][:, k, :], hh[g][:, k, :])
                    S.dve.bn_aggr(mv[g][:, k, :], stats[g][:, k:k + 1, :])
                S.dve.tensor_scalar_add(dd[g][:, 8:12], mv[g][:, :, 1], LN_EPS)
                S.act.activation(dd[g][:, 8:12], dd[g][:, 8:12], AF.Ln)
                S.act.activation(dd[g][:, 12:16], dd[g][:, 8:12], AF.Exp, scale=-0.5)
                S.dve.tensor_tensor(hh[g][:], hh[g][:], mv[g][:, :, 0:1].to_broadcast([128, 4, 64]), ALU.subtract)
                S.dve.tensor_tensor(hh[g][:], hh[g][:], dd[g][:, 12:16].unsqueeze(2).to_broadcast([128, 4, 64]), ALU.mult)
                S.pool.tensor_tensor(hh[g][:], hh[g][:], bv[:, 128:192].unsqueeze(1).to_broadcast([128, 4, 64]), ALU.mult)
                S.pool.tensor_tensor(hh[g][:], hh[g][:], dsg[g][:], ALU.mult)

            def d3(n):
                if n % 4 != 3:
                    return
                g = (n // 4) % 2
                n0 = n - 3
                for k in range(4):
                    S.pe.transpose(pt[:, k * 128:(k + 1) * 128], hh[g][:, k, :], ident[:])
                S.act.copy(og[g][:], pt[:])
                S.sp.dma_start(out=outf(192, n0 * 128, (n + 1) * 128), in_=og[g][:])

            assert NCH % 4 == 0
            _pipeline(NCH, [d0, d1, d2, d3])


I32 = mybir.dt.int32
GROUPS = [[0, 1, 2, 3], [4, 5, 6, 7]]


def build_fused(SEQ, depth=DEPTH, do=("A", "B", "C", "D"), ffn=True, exch=True):
    T = SEQ // 4
    NTILE = SEQ // 128
    nc = bass.Bass("TRN2", target_bir_lowering=False)
    L = depth
    dr = lambda name, shape, dt=F32: nc.dram_tensor(name, shape, dt, kind="ExternalInput").ap()
    x0g = dr("x0g", [4 * D_MODEL * (T // 512), 512])
    xres0 = dr("xres0", [T, D_MODEL])
    w_fm = dr("w_fm", [L, D_MODEL, 768])
    w_tm = dr("w_tm", [L, D_MODEL, NTM])
    ropeA = dr("ropeA", [2, 64, SEQ])
    ropeC = dr("ropeC", [2, 64, SEQ])
    pvec = dr("pvec", [L, 128, 16])
    bvec = dr("bvec", [L, 704])
    sgu_wT = dr("sgu_wT", [L, 128, 128])
    w_out = dr("w_out", [L, D_MODEL, D_MODEL])
    w_gate = dr("w_gate", [L, D_MODEL, D_FF])
    w_up = dr("w_up", [L, D_MODEL, D_FF])
    w_down = dr("w_down", [L, D_FF, D_MODEL])
    lnp = dr("lnp", [L, 4, D_MODEL])
    gidx = dr("gidx", [128, 8], I32)
    y = nc.dram_tensor("y", [T, D_MODEL], F32, kind="ExternalOutput").ap()
    P0 = dict(
        sc_qa=nc.dram_tensor("sc_qa", [64, SEQ], BF16).ap(), sc_ka=nc.dram_tensor("sc_ka", [64, SEQ], BF16).ap(),
        sc_qc=nc.dram_tensor("sc_qc", [64, SEQ], BF16).ap(), sc_kc=nc.dram_tensor("sc_kc", [64, SEQ], BF16).ap(),
        sc_qkd=nc.dram_tensor("sc_qkd", [128, SEQ], F32).ap(), sc_g=nc.dram_tensor("sc_g", [2, SEQ], F32).ap(),
        sc_tm=nc.dram_tensor("sc_tm", [SEQ, NTM], F32).ap())
    NTC = T // 512
    mixb = nc.dram_tensor("mixb", [256, SEQ], BF16).ap()
    mixg = nc.dram_tensor("mixg", [4 * 256, SEQ], BF16).ap()
    yTb = nc.dram_tensor("yTb", [NTC * D_MODEL, 512], BF16).ap()
    xTg = nc.dram_tensor("xTg", [NTC * 4 * D_MODEL, 512], BF16).ap()
    xres_d = nc.dram_tensor("xres_d", [T, D_MODEL], F32).ap()
    S = Sched(nc)
    S.set_gran(mixb, 64 * SEQ)
    S.set_gran(mixg, 256 * SEQ)
    S.set_gran(yTb, D_MODEL * 512)
    S.set_gran(xTg, 4 * D_MODEL * 512)
    with ExitStack() as stc:
        _PFX[0] = ""
        C = _consts(S, nc, stc)
        gi = _sb(stc, nc, "gidx_sb", [128, 8], I32)
        S.sp.dma_start(out=gi[:], in_=gidx)
        mixtab = mixg.rearrange("f (n t) -> (f n) t", t=512)
        for l in range(L):
            _PFX[0] = "L%d_" % l
            xg, xeng = (x0g, S.pool) if l == 0 else (xTg, S.sp)

            def xsrc(tt, xg=xg, xeng=xeng):
                r, tc = divmod(tt, NTC)
                r0 = (tc * 4 + r) * D_MODEL
                return xeng, xg[r0:r0 + D_MODEL, :].rearrange("(c p) t -> p c t", p=128)

            def outf(r0, c0, c1):
                return mixb[r0:r0 + 64, c0:c1]

            def after(m):
                if exch:
                    cw = min(SEQ, 2048)
                    S.collective("AllGather", mixb[64 * m:64 * m + 64, :].rearrange("r (a b) -> (r a) b", b=cw),
                                 mixg[256 * m:256 * m + 256, :].rearrange("r (a b) -> (r a) b", b=cw), GROUPS)
            P = dict(P0)
            P.update(xsrc=xsrc, w_fm=w_fm[l], w_tm=w_tm[l], ropeA=ropeA, ropeC=ropeC, pvec=pvec[l], bvec=bvec[l],
                     sgu_wT=sgu_wT[l], outf=outf, after=after,
                     tile_order=[r * NTC + tc for tc in range(NTC) for r in range(4)])
            _mixer_all(S, nc, SEQ, P, C, do)
            S.barrier()
            last = (l == L - 1)

            def mix_gather(tile, i):
                for c in range(8):
                    m = c // 2
                    S.gather_rows(tile[:, c, :], mixtab, gi[:, c:c + 1], i * 512, dep_ap=mixg[256 * m:256 * m + 256, :])

            def yT(i):
                tc = i // 4
                return yTb[tc * D_MODEL:(tc + 1) * D_MODEL, :].rearrange("(c p) t -> p c t", p=128)[:, :, (i % 4) * 128:(i % 4 + 1) * 128]

            def after_tile(i):
                if i % 4 == 3 and exch:
                    tc = i // 4
                    S.collective("AllGather", yTb[tc * D_MODEL:(tc + 1) * D_MODEL, :],
                                 xTg[tc * 4 * D_MODEL:(tc + 1) * 4 * D_MODEL, :], GROUPS)
            PF = dict(mix_gather=mix_gather, xres=(xres0 if l == 0 else xres_d),
                      w_out=w_out[l], w_gate=w_gate[l], w_up=w_up[l], w_down=w_down[l], lnp=lnp[l],
                      y=(y if last else xres_d), yT=(None if last else yT), after_tile=after_tile)
            if ffn:
                _ffn_all(S, nc, T, PF, C["ident"])
            S.barrier()
        S.finish()
    return nc


_PROGS = {}
BATCH = 2
SEQ_FULL = 8192
NCORES = 8


def _fused_inputs(inp, SEQ, depth):
    T = SEQ // 4
    ropes = _rope_tables(SEQ)
    x = inp["x"]
    hm = [[prep_mixer_inputs(inp, l, h, SEQ, ropes) for l in range(depth)] for h in range(4)]
    w_out_p = inp["w_out"][:depth]
    lnp = np.stack([np.stack([inp["ln1_g"][l], inp["ln1_b"][l], inp["ln2_g"][l], inp["ln2_b"][l]]) for l in range(depth)])
    maps = []
    for core in range(NCORES):
        b, j = divmod(core, 4)
        xb = x[b, :SEQ]
        NTC = T // 512
        x0g = np.ascontiguousarray(xb.reshape(4, NTC, 512, D_MODEL).transpose(1, 0, 3, 2).reshape(NTC * 4 * D_MODEL, 512))
        gidx = ((np.arange(8)[None, :] * 128 + np.arange(128)[:, None]) * (SEQ // 512) + j * (T // 512)).astype(np.int32)
        m = {
            "x0g": x0g, "xres0": np.ascontiguousarray(xb[j * T:(j + 1) * T]),
            "w_fm": np.stack([hm[j][l]["w_fm"] for l in range(depth)]),
            "w_tm": np.stack([hm[j][l]["w_tm"] for l in range(depth)]),
            "ropeA": ropes[0], "ropeC": ropes[1],
            "pvec": np.stack([hm[j][l]["pvec"] for l in range(depth)]),
            "bvec": np.stack([hm[j][l]["bvec"] for l in range(depth)]),
            "sgu_wT": np.stack([hm[j][l]["sgu_wT"] for l in range(depth)]),
            "w_out": w_out_p, "w_gate": inp["w_gate"][:depth], "w_up": inp["w_up"][:depth], "w_down": inp["w_down"][:depth],
            "lnp": lnp.astype(np.float32), "gidx": gidx,
        }
        maps.append({k: np.ascontiguousarray(v) for k, v in m.items()})
    return maps


def run_fused(inp, SEQ, depth):
    key = ("fused", SEQ, depth)
    if key not in _PROGS:
        _PROGS[key] = build_fused(SEQ, depth)
    maps = _fused_inputs(inp, SEQ, depth)
    res = run_bass_kernel_spmd(_PROGS[key], maps, core_ids=list(range(NCORES)))
    T = SEQ // 4
    out = np.empty((BATCH, SEQ, D_MODEL), np.float32)
    for core in range(NCORES):
        b, j = divmod(core, 4)
        out[b, j * T:(j + 1) * T] = res.results[core]["y"]
    return out


def kernel(**inputs):
    inp = {k: np.asarray(v, dtype=np.float32) for k, v in inputs.items()}
    return run_fused(inp, SEQ_FULL, DEPTH)
```

```python
import numpy as np
import concourse.bass as bass
import concourse.mybir as mybir

F32 = mybir.dt.float32
BF16 = mybir.dt.bfloat16
AF = mybir.ActivationFunctionType
ALU = mybir.AluOpType
AX = mybir.AxisListType


class _Eng:
    def __init__(self, S, name, eng):
        self.S, self.name, self.eng = S, name, eng
        self.sem = S.nc.alloc_semaphore("es_" + name)
        self.cnt = 0
        self.seen = {}

    def __getattr__(self, op):
        fn = getattr(self.eng, op)

        def call(*args, **kw):
            return self.S._emit(self, op, fn, args, kw)
        return call


class Sched:
    def __init__(self, nc):
        self.nc = nc
        self.pe = _Eng(self, "pe", nc.tensor)
        self.act = _Eng(self, "act", nc.scalar)
        self.dve = _Eng(self, "dve", nc.vector)
        self.pool = _Eng(self, "pool", nc.gpsimd)
        self.sp = _Eng(self, "sp", nc.sync)
        self.engs = [self.pe, self.act, self.dve, self.pool, self.sp]
        self.units = {}
        self.gran = {}
        self.dma_sems = {}
        self.all_sems = {}
        self.n_inst = 0
        self.free_dma = {}
        self.n_dsem = 0
        self.cc_sem = None
        self.cc_cnt = 0

    def collective(self, kind, in_ap, out_ap, groups):
        E = self.pool
        reads, writes = self._units(in_ap), self._units(out_ap)
        self._deps(E, reads, writes, same_raw=False)
        if self.cc_sem is None:
            self.cc_sem = self.nc.alloc_semaphore("cc_sem")
        inst = self.nc.gpsimd.collective_compute(kind, ALU.bypass, replica_groups=groups, ins=[in_ap], outs=[out_ap])
        self.cc_cnt += 1
        inst.then_inc(self.cc_sem, 1)
        self._record((self.cc_sem, self.cc_cnt), reads, writes)
        return inst

    def gather_rows(self, out_ap, table_ap, idx_ap, element_offset, dep_ap=None):
        E = self.pool
        reads = self._units(dep_ap if dep_ap is not None else table_ap) + self._units(idx_ap)
        writes = self._units(out_ap)
        skey = (writes[0][0], 0)
        ds = self._dma_sem(skey, "sw")
        saved = []
        for key in writes:
            u = self._u(key)
            for k_, t_ in list(u["w"].items()):
                if t_[0] is ds[0]:
                    saved.append((u, k_, t_))
                    del u["w"][k_]
        self._deps(E, reads, writes, same_raw=False)
        inst = self.nc.gpsimd.indirect_dma_start(out=out_ap, out_offset=None, in_=table_ap,
                                                 in_offset=bass.IndirectOffsetOnAxis(ap=idx_ap, axis=0),
                                                 element_offset=element_offset)
        ds[1] += 16
        inst.then_inc(ds[0], 16)
        self._record((ds[0], ds[1]), reads, writes)
        return inst

    def _dma_sem(self, skey, cls="hw"):
        skey = (skey[0], cls)
        ds = self.dma_sems.get(skey)
        if ds is None:
            fl = self.free_dma.setdefault(cls, [])
            if fl:
                ds = fl.pop()
            else:
                self.n_dsem += 1
                ds = [self.nc.alloc_semaphore("ds%s%d" % (cls, self.n_dsem)), 0, cls]
            self.dma_sems[skey] = ds
        return ds

    def set_gran(self, t, g):
        self.gran[t.name if hasattr(t, "name") else t] = g

    def _units(self, ap):
        name = ap.tensor.name
        g = self.gran.get(name)
        if g is None:
            return [(name, 0)]
        apl = ap.ap
        space = str(ap.space)
        off = int(ap.offset)
        if "DRAM" in space:
            lo = off
            hi = off + sum((c - 1) * s for s, c in apl)
        else:
            F = 1
            for d in ap.tensor.shape[1:]:
                F *= d
            lo = off % F
            hi = lo + sum((c - 1) * s for s, c in apl[1:])
        return [(name, i) for i in range(lo // g, hi // g + 1)]

    def _u(self, key):
        u = self.units.get(key)
        if u is None:
            u = self.units[key] = {"w": {}, "r": {}}
        return u

    def _wait(self, E, tok):
        sem, val = tok
        k = id(sem)
        if E.seen.get(k, 0) >= val:
            return
        E.eng.wait_ge(sem, val)
        E.seen[k] = val

    def _deps(self, E, reads, writes, same_raw=True):
        toks = []
        for key in reads:
            u = self._u(key)
            toks += list(u["w"].values())
        for key in writes:
            u = self._u(key)
            toks += list(u["w"].values()) + list(u["r"].values())
        for sem, val in toks:
            if sem is E.sem:
                continue
            self._wait(E, (sem, val))
        if same_raw and E is not self.pe:
            for key in reads:
                u = self._u(key)
                for sem, val in u["w"].values():
                    if sem is E.sem:
                        self._wait(E, (sem, val))

    def _record(self, tok, reads, writes):
        sem, val = tok
        k = id(sem)
        for key in reads:
            self._u(key)["r"][k] = tok
        for key in writes:
            u = self._u(key)
            u["w"] = {k: tok}
            u["r"] = {}
        self.all_sems[k] = tok

    def _emit(self, E, op, fn, args, kw):
        if op in ("dma_start",):
            return self._dma(E, fn, args, kw)
        lazy = kw.pop("lazy", False)
        aps = []
        out = kw.get("out", None)
        outs = []
        first = True
        for a in list(args) + [v for k_, v in kw.items()]:
            if isinstance(a, bass.AP):
                aps.append(a)
        if out is not None:
            outs = [out]
        elif args and isinstance(args[0], bass.AP):
            outs = [args[0]]
        if kw.get("accum_out") is not None:
            outs.append(kw["accum_out"])
        out_ids = [id(o) for o in outs]
        reads, writes = [], []
        for a in aps:
            if id(a) in out_ids:
                writes += self._units(a)
            else:
                reads += self._units(a)
        self._deps(E, reads, writes)
        inst = fn(*args, **kw)
        if lazy and kw.get("stop", True) is False:
            self._record((E.sem, E.cnt + 1), reads, writes)
            self.n_inst += 1
            return inst
        E.cnt += 1
        inst.then_inc(E.sem, 1)
        self._record((E.sem, E.cnt), reads, writes)
        self.n_inst += 1
        return inst

    def _dma(self, E, fn, args, kw):
        out = kw.get("out", args[0] if args else None)
        in_ = kw.get("in_", args[1] if len(args) > 1 else None)
        writes = self._units(out)
        reads = self._units(in_)
        if "DRAM" not in str(out.space):
            skey = (writes[0][0], 0)
        elif "DRAM" not in str(in_.space):
            skey = (reads[0][0], 0)
        else:
            skey = (writes[0][0], 0)
        self._deps(E, reads, writes, same_raw=False)
        ds = self._dma_sem(skey, "sw" if E is self.pool else "hw")
        inst = fn(*args, **kw)
        ds[1] += 16
        inst.then_inc(ds[0], 16)
        self._record((ds[0], ds[1]), reads, writes)
        self.n_inst += 1
        return inst

    def barrier(self, final=False):
        toks = [t for t in self.all_sems.values() if (final or t[0] is not self.cc_sem)]
        for E in self.engs:
            for tok in toks:
                if tok[0] is E.sem:
                    continue
                self._wait(E, tok)
        keep = {}
        for key, u in self.units.items():
            w = {k: t for k, t in u["w"].items() if t[0] is self.cc_sem}
            r = {k: t for k, t in u["r"].items() if t[0] is self.cc_sem}
            if (w or r) and not final:
                keep[key] = {"w": w, "r": r}
        self.units = keep
        for ds in self.dma_sems.values():
            self.free_dma.setdefault(ds[2], []).append(ds)
        self.dma_sems = {}

    def finish(self):
        self.barrier(final=True)


from contextlib import ExitStack
from concourse.bass_utils import run_bass_kernel_spmd

D_MODEL = 1024
D_FF = 2816
DEPTH = 4
ALPHA = (2 * DEPTH) ** 0.25
LN_EPS = 1e-5


_PFX = [""]


def _sb(st, nc, name, shape, dt):
    return st.enter_context(nc.sbuf_tensor(_PFX[0] + name, shape, dt))


def _ps(st, nc, name, shape, dt=F32):
    return st.enter_context(nc.psum_tensor(_PFX[0] + name, shape, dt))


def _pipeline(n, stages, lag=1):
    ns = len(stages)
    for step in range(n + (ns - 1) * lag):
        for si, f in enumerate(stages):
            i = step - si * lag
            if 0 <= i < n:
                f(i)


def _make_ident(S, nc, ident):
    S.pool.memset(ident[:], 1.0)
    S.pool.affine_select(ident[:], ident[:], [[-1, 128]], ALU.is_equal, 0.0, base=0, channel_multiplier=1)


def _layernorm_rows(S, nc, t, width, stats, mv, g_bc, b_bc, out):
    nch = width // 512
    for c in range(nch):
        S.dve.bn_stats(stats[:, c, :], t[:, c * 512:(c + 1) * 512])
    S.dve.bn_aggr(mv[:, 0:2], stats[:, 0:nch, :])
    S.dve.tensor_scalar_add(mv[:, 2:3], mv[:, 1:2], LN_EPS)
    S.act.sqrt(mv[:, 2:3], mv[:, 2:3])
    S.dve.reciprocal(mv[:, 3:4], mv[:, 2:3])
    S.dve.tensor_scalar(t, t, mv[:, 0:1], mv[:, 3:4], ALU.subtract, ALU.mult)
    S.pool.tensor_tensor(t, t, g_bc, ALU.mult)
    S.pool.tensor_tensor(out, t, b_bc, ALU.add)


def _ffn_all(S, nc, T, P, ident):
    NT = T // 128
    NTT = T // 512
    mix_gather, xres, w_out, w_gate, w_up, w_down, lnp, y, yT = (P[k] for k in (
        "mix_gather", "xres", "w_out", "w_gate", "w_up", "w_down", "lnp", "y", "yT"))
    with ExitStack() as st0:
        lnbc = _sb(st0, nc, "lnbc", [128, 4, D_MODEL], F32)
        yacc = _sb(st0, nc, "yacc", [128, NT, D_MODEL], F32)
        x1T = _sb(st0, nc, "x1T", [128, 8, T], BF16)
        S.set_gran(yacc, D_MODEL)
        S.set_gran(lnbc, D_MODEL)
        for j in range(4):
            S.sp.dma_start(out=lnbc[:, j, :], in_=lnp[j].partition_broadcast(128))
        with ExitStack() as st:
            wout_b = _sb(st, nc, "wout_b", [128, 8, D_MODEL], BF16)
            mixb = [_sb(st, nc, "mixb%d" % i, [128, 8, 512], BF16) for i in range(2)]
            xr = [_sb(st, nc, "xr%d" % i, [128, D_MODEL], F32) for i in range(2)]
            t1 = [_sb(st, nc, "t1_%d" % i, [128, D_MODEL], F32) for i in range(3)]
            stats = [_sb(st, nc, "stats%d" % i, [128, 2, 6], F32) for i in range(2)]
            mv = [_sb(st, nc, "mv%d" % i, [128, 4], F32) for i in range(2)]
            psh = [_ps(st, nc, "psh%d" % i, [128, D_MODEL]) for i in range(2)]
            pst = [_ps(st, nc, "pst%d" % i, [128, D_MODEL]) for i in range(2)]
            S.pool.dma_start(out=wout_b[:], in_=w_out.rearrange("(c p) f -> p c f", p=128))
            def f0(i):
                b = i % 2
                tsl = slice(i * 128, (i + 1) * 128)
                mb = mixb[(i // 4) % 2]
                if i == 0:
                    mix_gather(mixb[0], 0)
                if i % 4 == 0 and i // 4 + 1 < NT // 4:
                    mix_gather(mixb[(i // 4 + 1) % 2], i // 4 + 1)
                if i == 0:
                    S.sp.dma_start(out=xr[0][:], in_=xres[0:128, :])
                if i + 1 < NT:
                    S.sp.dma_start(out=xr[(i + 1) % 2][:], in_=xres[(i + 1) * 128:(i + 2) * 128, :])
                for half in range(2):
                    for k in range(8):
                        S.pe.matmul(psh[b][:, half * 512:(half + 1) * 512], mb[:, k, (i % 4) * 128:(i % 4 + 1) * 128],
                                    wout_b[:, k, half * 512:(half + 1) * 512], start=(k == 0), stop=(k == 7), lazy=True)
                S.dve.scalar_tensor_tensor(t1[i % 3][:], xr[b][:], ALPHA, psh[b][:], ALU.mult, ALU.add)

            def f1(i):
                b = i % 2
                _layernorm_rows(S, nc, t1[i % 3][:], D_MODEL, stats[b], mv[b], lnbc[:, 0, :], lnbc[:, 1, :], t1[i % 3][:])
                S.act.mul(yacc[:, i, :], t1[i % 3][:], ALPHA)

            def f2(i):
                b = i % 2
                tsl = slice(i * 128, (i + 1) * 128)
                for c in range(8):
                    S.pe.transpose(pst[b][:, c * 128:(c + 1) * 128], t1[i % 3][:, c * 128:(c + 1) * 128], ident[:])
                S.act.copy(x1T[:, 0:4, tsl], pst[b][:, 0:512].rearrange("p (c t) -> p c t", c=4))
                S.dve.tensor_copy(x1T[:, 4:8, tsl], pst[b][:, 512:1024].rearrange("p (c t) -> p c t", c=4))

            _pipeline(NT, [f0, f1, f2])
        S.barrier()
        with ExitStack() as st:
            FB = 256
            NFB = D_FF // FB
            wg_b = [_sb(st, nc, "wg_b%d" % i, [128, 8, FB], BF16) for i in range(2)]
            wu_b = [_sb(st, nc, "wu_b%d" % i, [128, 8, FB], BF16) for i in range(2)]
            wd_b = [_sb(st, nc, "wd_b%d" % i, [128, 2, D_MODEL], BF16) for i in range(2)]
            sg = [_sb(st, nc, "sg%d" % i, [128, 512], F32) for i in range(2)]
            actT = [_sb(st, nc, "actT%d" % i, [128, 2, 512], BF16) for i in range(2)]
            psg = [_ps(st, nc, "psg%d" % i, [128, 512]) for i in range(2)]
            psu = [_ps(st, nc, "psu%d" % i, [128, 512]) for i in range(2)]
            psd = [_ps(st, nc, "psd%d" % i, [128, D_MODEL]) for i in range(2)]
            wg_v = w_gate.rearrange("(c p) f -> p c f", p=128)
            wu_v = w_up.rearrange("(c p) f -> p c f", p=128)
            wd_v = w_down.rearrange("(c p) f -> p c f", p=128)
            items = [(fb, tt) for fb in range(NFB) for tt in range(NTT)]

            def load_w(fb):
                wb = fb % 2
                fsl = slice(fb * FB, (fb + 1) * FB)
                S.pool.dma_start(out=wg_b[wb][:], in_=wg_v[:, :, fsl])
                S.pool.dma_start(out=wu_b[wb][:], in_=wu_v[:, :, fsl])
                S.pool.dma_start(out=wd_b[wb][:], in_=wd_v[:, 2 * fb:2 * fb + 2, :])

            def g0(it):
                fb, tt = items[it]
                wb = fb % 2
                ab = it % 2
                if it == 0:
                    load_w(0)
                    if NFB > 1:
                        load_w(1)
                tsl = slice(tt * 512, (tt + 1) * 512)
                for c2 in range(2):
                    for k in range(8):
                        S.pe.matmul(psg[c2][:], wg_b[wb][:, k, c2 * 128:(c2 + 1) * 128], x1T[:, k, tsl],
                                    start=(k == 0), stop=(k == 7), lazy=True)
                    for k in range(8):
                        S.pe.matmul(psu[c2][:], wu_b[wb][:, k, c2 * 128:(c2 + 1) * 128], x1T[:, k, tsl],
                                    start=(k == 0), stop=(k == 7), lazy=True)
                    S.act.activation(sg[c2][:], psg[c2][:], AF.Silu)
                    S.dve.tensor_tensor(actT[ab][:, c2, :], sg[c2][:], psu[c2][:], ALU.mult)

            def g1(it):
                fb, tt = items[it]
                wb = fb % 2
                ab = it % 2
                for s4 in range(4):
                    db = (it * 4 + s4) % 2
                    for half in range(2):
                        for c2 in range(2):
                            S.pe.matmul(psd[db][:, half * 512:(half + 1) * 512],
                                        actT[ab][:, c2, s4 * 128:(s4 + 1) * 128],
                                        wd_b[wb][:, c2, half * 512:(half + 1) * 512],
                                        start=(c2 == 0), stop=(c2 == 1), lazy=True)
                    ti = tt * 4 + s4
                    S.dve.tensor_tensor(yacc[:, ti, :], yacc[:, ti, :], psd[db][:], ALU.add)
                if tt == NTT - 1 and fb + 2 < NFB:
                    load_w(fb + 2)

            _pipeline(len(items), [g0, g1])
        S.barrier()
        with ExitStack() as st:
            stats = [_sb(st, nc, "stats3_%d" % i, [128, 2, 6], F32) for i in range(2)]
            mv = [_sb(st, nc, "mv3_%d" % i, [128, 4], F32) for i in range(2)]
            ob = [_sb(st, nc, "ob%d" % i, [128, D_MODEL], F32) for i in range(2)]
            obT = [_sb(st, nc, "obT%d" % i, [128, 8, 128], BF16) for i in range(2)]
            pst3 = [_ps(st, nc, "pst3_%d" % i, [128, D_MODEL]) for i in range(2)]
            def h0(i):
                b = i % 2
                _layernorm_rows(S, nc, yacc[:, i, :], D_MODEL, stats[b], mv[b], lnbc[:, 2, :], lnbc[:, 3, :], ob[b][:])

            def h1(i):
                b = i % 2
                S.sp.dma_start(out=y[i * 128:(i + 1) * 128, :], in_=ob[b][:])
                if yT is not None:
                    for c in range(8):
                        S.pe.transpose(pst3[b][:, c * 128:(c + 1) * 128], ob[b][:, c * 128:(c + 1) * 128], ident[:])
                    S.act.copy(obT[b][:, 0:4, :], pst3[b][:, 0:512].rearrange("p (c t) -> p c t", c=4))
                    S.dve.tensor_copy(obT[b][:, 4:8, :], pst3[b][:, 512:1024].rearrange("p (c t) -> p c t", c=4))
                    S.sp.dma_start(out=yT(i), in_=obT[b][:])
                    P["after_tile"](i)

            _pipeline(NT, [h0, h1])


HD = 64
NTM = 580
NEG = -30000.0


def _gelu_tanh(S, nc, out, x, tmp):
    S.act.activation(tmp, x, AF.Square)
    S.dve.tensor_scalar(tmp, tmp, 0.044715, 1.0, ALU.mult, ALU.add)
    S.pool.tensor_tensor(tmp, tmp, x, ALU.mult)
    S.act.activation(tmp, tmp, AF.Sigmoid, scale=2.0 * 0.7978845608028654)
    S.dve.tensor_tensor(out, x, tmp, ALU.mult)


def _consts(S, nc, st0):
    C = {}
    C["pv"] = pv = _sb(st0, nc, "pv", [128, 16], F32)
    C["bv"] = bv = _sb(st0, nc, "bv", [128, 704], F32)
    C["ident"] = ident = _sb(st0, nc, "identf", [128, 128], F32)
    C["identb"] = identb = _sb(st0, nc, "identb", [128, 128], BF16)
    C["mask_le"] = mask_le = _sb(st0, nc, "mask_le", [128, 128], F32)
    C["mask_le_b"] = mask_le_b = _sb(st0, nc, "mask_le_b", [128, 128], BF16)
    C["mask_ge_b"] = mask_ge_b = _sb(st0, nc, "mask_ge_b", [128, 128], BF16)
    C["tri"] = tri = _sb(st0, nc, "tri", [128, 128], F32)
    _make_ident(S, nc, ident)
    S.dve.tensor_copy(identb[:], ident[:])
    S.pool.memset(mask_le[:], 0.0)
    S.pool.affine_select(mask_le[:], mask_le[:], [[1, 128]], ALU.is_ge, NEG, base=0, channel_multiplier=-1)
    S.dve.tensor_copy(mask_le_b[:], mask_le[:])
    S.pool.memset(tri[:], 1.0)
    S.pool.affine_select(tri[:], tri[:], [[1, 128]], ALU.is_ge, 0.0, base=0, channel_multiplier=-1)
    S.pool.memset(mask_ge_b[:], 0.0)
    S.pool.affine_select(mask_ge_b[:], mask_ge_b[:], [[-1, 128]], ALU.is_ge, NEG, base=0, channel_multiplier=1)
    return C


def _mixer_all(S, nc, SEQ, P, C, do=("A", "B", "C", "D")):
    NTT = SEQ // 512
    xsrc, w_fm, w_tm, ropeA, ropeC, pvec, bvec, sgu_wT, outf = (P[k] for k in (
        "xsrc", "w_fm", "w_tm", "ropeA", "ropeC", "pvec", "bvec", "sgu_wT", "outf"))
    sc_qa, sc_ka, sc_qc, sc_kc, sc_qkd, sc_g, sc_tm = (P[k] for k in ("sc_qa", "sc_ka", "sc_qc", "sc_kc", "sc_qkd", "sc_g", "sc_tm"))
    pv, bv, ident, identb, mask_le, mask_le_b, mask_ge_b, tri = (C[k] for k in (
        "pv", "bv", "ident", "identb", "mask_le", "mask_le_b", "mask_ge_b", "tri"))
    S.sp.dma_start(out=pv[:], in_=pvec)
    S.sp.dma_start(out=bv[:], in_=bvec.partition_broadcast(128))
    if True:
        with ExitStack() as st:
            wfm_b = _sb(st, nc, "wfm_b", [128, 8, 768], BF16)
            wtm_b = _sb(st, nc, "wtm_b", [128, 8, NTM], BF16)
            xb = [_sb(st, nc, "xb%d" % i, [128, 8, 512], BF16) for i in range(2)]
            rA = [_sb(st, nc, "rA%d" % i, [64, 2, 512], F32) for i in range(2)]
            rC = [_sb(st, nc, "rC%d" % i, [64, 2, 512], F32) for i in range(2)]
            ta = [_sb(st, nc, "ta%d" % i, [64, 512], F32) for i in range(2)]
            tb = [_sb(st, nc, "tb%d" % i, [64, 512], F32) for i in range(2)]
            stg = [_sb(st, nc, "stg%d" % i, [64, 512], BF16) for i in range(4)]
            stg5 = [_sb(st, nc, "stg5_%d" % i, [128, 512], F32) for i in range(2)]
            stg6 = [_sb(st, nc, "stg6_%d" % i, [2, 512], F32) for i in range(2)]
            stgt = [_sb(st, nc, "stgt%d" % i, [128, NTM], F32) for i in range(2)]
            ps1 = [_ps(st, nc, "ps1_%d" % i, [128, 512]) for i in range(2)]
            ps2 = [_ps(st, nc, "ps2_%d" % i, [128, 512]) for i in range(2)]
            ps5 = _ps(st, nc, "ps5", [128, 512])
            pstm = _ps(st, nc, "pstm", [128, 1024])
            S.pool.dma_start(out=wfm_b[:], in_=w_fm.rearrange("(c p) f -> p c f", p=128))
            S.pool.dma_start(out=wtm_b[:], in_=w_tm.rearrange("(c p) f -> p c f", p=128))
            rA_v = ropeA.rearrange("two r t -> r two t")
            rC_v = ropeC.rearrange("two r t -> r two t")
            order = list(P.get("tile_order", range(NTT)))

            def load_tile(j):
                tj = order[j]
                bj = j % 2
                sj = slice(tj * 512, (tj + 1) * 512)
                xe, xap = xsrc(tj)
                xe.dma_start(out=xb[bj][:], in_=xap)
                S.sp.dma_start(out=rA[bj][:], in_=rA_v[:, :, sj])
                S.sp.dma_start(out=rC[bj][:], in_=rC_v[:, :, sj])

            load_tile(0)
            for it_, tt in enumerate(order):
                b = it_ % 2
                tsl = slice(tt * 512, (tt + 1) * 512)
                if it_ + 1 < len(order):
                    load_tile(it_ + 1)
                def do_pair(pair):
                    rt, dq, dk = ((rA[b], sc_qa, sc_ka), (rC[b], sc_qc, sc_kc))[pair]
                    pb = (2 * it_ + pair) % 2
                    g1 = 2 * pair
                    for k in range(8):
                        S.pe.matmul(ps1[pb][:], wfm_b[:, k, g1 * 128:(g1 + 1) * 128], xb[b][:, k, :], start=(k == 0), stop=(k == 7), lazy=True)
                    for k in range(8):
                        S.pe.matmul(ps2[pb][:], wfm_b[:, k, (g1 + 1) * 128:(g1 + 2) * 128], xb[b][:, k, :], start=(k == 0), stop=(k == 7), lazy=True)
                    for half, dst in enumerate((dq, dk)):
                        rows = slice(half * 64, (half + 1) * 64)
                        tbuf = half
                        S.dve.tensor_tensor(ta[tbuf][:], ps2[pb][rows, :], rt[:, 1, :], ALU.mult)
                        S.dve.tensor_tensor(tb[tbuf][:], ps1[pb][rows, :], rt[:, 0, :], ALU.mult)
                        sb_ = (4 * it_ + 2 * pair + half) % 4
                        S.pool.tensor_tensor(stg[sb_][:], ta[tbuf][:], tb[tbuf][:], ALU.add)
                        S.sp.dma_start(out=dst[:, tsl], in_=stg[sb_][:])

                def do_g5():
                    for k in range(8):
                        S.pe.matmul(ps5[:], wfm_b[:, k, 512:640], xb[b][:, k, :], start=(k == 0), stop=(k == 7), lazy=True)
                    S.act.copy(stg5[b][:], ps5[:])
                    S.sp.dma_start(out=sc_qkd[:, tsl], in_=stg5[b][:])

                def do_g6():
                    for k in range(8):
                        S.pe.matmul(ps5[0:2, :], wfm_b[:, k, 640:642], xb[b][:, k, :], start=(k == 0), stop=(k == 7), lazy=True)
                    S.act.copy(stg6[b][:], ps5[0:2, :])
                    S.sp.dma_start(out=sc_g[:, tsl], in_=stg6[b][:])

                def do_sub(sub):
                    tb_ = (4 * it_ + sub) % 2
                    for k in range(8):
                        S.pe.matmul(pstm[:, 0:512], xb[b][:, k, sub * 128:(sub + 1) * 128], wtm_b[:, k, 0:512], start=(k == 0), stop=(k == 7), lazy=True)
                    for k in range(8):
                        S.pe.matmul(pstm[:, 512:NTM], xb[b][:, k, sub * 128:(sub + 1) * 128], wtm_b[:, k, 512:NTM], start=(k == 0), stop=(k == 7), lazy=True)
                    S.act.copy(stgt[tb_][:], pstm[:, 0:NTM])
                    r0 = tt * 512 + sub * 128
                    S.sp.dma_start(out=sc_tm[r0:r0 + 128, :], in_=stgt[tb_][:])

                do_pair(0)
                do_sub(0)
                do_g5()
                do_sub(1)
                do_pair(1)
                do_sub(2)
                do_g6()
                do_sub(3)
        S.barrier()
        if "B" in do:
            _mixer_B(S, nc, SEQ, sc_tm, sgu_wT, pv, bv, ident, tri, outf)
            P["after"](1)
            S.barrier()
        if "A" in do:
            _mixer_A(S, nc, SEQ, sc_qa, sc_ka, sc_tm, pv, bv, outf)
            P["after"](0)
            S.barrier()
        if "C" in do:
            _mixer_C(S, nc, SEQ, sc_qc, sc_kc, sc_tm, identb, mask_le_b, mask_ge_b, outf)
            P["after"](2)
            S.barrier()
        if "D" in do:
            _mixer_D(S, nc, SEQ, sc_qkd, sc_g, sc_tm, pv, bv, ident, identb, mask_le, tri, outf)
            P["after"](3)


def _mixer_B(S, nc, SEQ, sc_tm, sgu_wT, pv, bv, ident, tri, outf):
    NIT = SEQ // 512
    with ExitStack() as st:
        wT = _sb(st, nc, "sg_wT", [128, 128], F32)
        wTb = _sb(st, nc, "sg_wTb", [128, 128], BF16)
        uv = [_sb(st, nc, "sg_uv%d" % i, [128, 4, 320], F32) for i in range(2)]
        gl = [_sb(st, nc, "sg_gl%d" % i, [128, 4, 320], F32) for i in range(2)]
        tmp = [_sb(st, nc, "sg_tmp%d" % i, [128, 4, 320], F32) for i in range(2)]
        stats = [_sb(st, nc, "sg_st%d" % i, [128, 4, 6], F32) for i in range(2)]
        mv = [_sb(st, nc, "sg_mv%d" % i, [128, 4, 2], F32) for i in range(2)]
        rs = [_sb(st, nc, "sg_rs%d" % i, [128, 4], F32) for i in range(2)]
        vn = [_sb(st, nc, "sg_vn%d" % i, [128, 4, 64], F32) for i in range(2)]
        vnb = [_sb(st, nc, "sg_vnb%d" % i, [128, 4, 64], BF16) for i in range(2)]
        ob = [_sb(st, nc, "sg_ob%d" % i, [128, 4, 64], F32) for i in range(2)]
        og = [_sb(st, nc, "sg_og%d" % i, [64, 512], BF16) for i in range(2)]
        psz = [_ps(st, nc, "sg_psz%d" % i, [128, 4, 64]) for i in range(2)]
        pst = [_ps(st, nc, "sg_pst%d" % i, [64, 512]) for i in range(2)]
        S.sp.dma_start(out=wT[:], in_=sgu_wT)
        S.dve.tensor_tensor(wTb[:], wT[:], tri[:], ALU.mult)
        gbc = bv[:, 0:64].unsqueeze(1).to_broadcast([128, 4, 64])
        bbc = bv[:, 64:128].unsqueeze(1).to_broadcast([128, 4, 64])

        def load_uv(j):
            S.sp.dma_start(out=uv[j % 2][:], in_=sc_tm[j * 512:(j + 1) * 512, 256:576].rearrange("(n p) c -> p n c", p=128))

        def b0(it):
            b = it % 2
            if it == 0:
                load_uv(0)
            if it + 1 < NIT:
                load_uv(it + 1)
            _gelu_tanh(S, nc, gl[b][:], uv[b][:], tmp[b][:])
            for k in range(4):
                S.dve.bn_stats(stats[b][:, k, :], gl[b][:, k, 64:320])
                S.dve.bn_aggr(mv[b][:, k, :], stats[b][:, k:k + 1, :])
            S.dve.tensor_scalar_add(rs[b][:], mv[b][:, :, 1], LN_EPS)
            S.act.sqrt(rs[b][:], rs[b][:])
            S.dve.reciprocal(rs[b][:], rs[b][:])
            S.dve.tensor_tensor(vn[b][:], gl[b][:, :, 64:128], mv[b][:, :, 0:1].to_broadcast([128, 4, 64]), ALU.subtract)
            S.dve.tensor_tensor(vn[b][:], vn[b][:], rs[b][:].unsqueeze(2).to_broadcast([128, 4, 64]), ALU.mult)
            S.pool.tensor_tensor(vn[b][:], vn[b][:], gbc, ALU.mult)
            S.pool.tensor_tensor(vnb[b][:], vn[b][:], bbc, ALU.add)

        def b1(it):
            b = it % 2
            for k in range(4):
                S.pe.matmul(psz[b][:, k, :], wTb[:], vnb[b][:, k, :], start=True, stop=True)
            S.dve.scalar_tensor_tensor(ob[b][:], psz[b][:], pv[:, 10:11], gl[b][:, :, 0:64], ALU.add, ALU.mult)

        def b2(it):
            b = it % 2
            for k in range(4):
                S.pe.transpose(pst[b][:, k * 128:(k + 1) * 128], ob[b][:, k, :], ident[:])
            S.act.copy(og[b][:], pst[b][:])
            S.sp.dma_start(out=outf(64, it * 512, (it + 1) * 512), in_=og[b][:])

        _pipeline(NIT, [b0, b1, b2])


def _mixer_A(S, nc, SEQ, sc_qa, sc_ka, sc_tm, pv, bv, outf):
    NTT = SEQ // 512
    NKB = SEQ // 128
    scale = 32 ** -0.5
    with ExitStack() as st:
        qT = _sb(st, nc, "A_qT", [64, SEQ], BF16)
        kT = _sb(st, nc, "A_kT", [64, SEQ], BF16)
        va = _sb(st, nc, "A_va", [128, NKB, 128], BF16)
        lam = _sb(st, nc, "A_lam", [64, 8], F32)
        lt = _sb(st, nc, "A_lt", [64, 64], F32)
        ones_ms = _sb(st, nc, "A_ones", [64, 64], F32)
        E = [_sb(st, nc, "A_E%d" % i, [128, 512], BF16) for i in range(6)]
        rd = [_sb(st, nc, "A_rd%d" % i, [64, 512], F32) for i in range(2)]
        o0 = _sb(st, nc, "A_o0", [64, 512], F32)
        o1 = _sb(st, nc, "A_o1", [64, 512], F32)
        sq = _sb(st, nc, "A_sq", [64, 512], F32)
        og = [_sb(st, nc, "A_og%d" % i, [64, 512], BF16) for i in range(2)]
        pss = [_ps(st, nc, "A_pss%d" % i, [128, 512]) for i in range(6)]
        pso = [_ps(st, nc, "A_pso%d" % i, [128, 512]) for i in range(2)]
        psm = pss[0]
        S.set_gran(va, 128)
        S.sp.dma_start(out=qT[:], in_=sc_qa)
        S.sp.dma_start(out=kT[:], in_=sc_ka)
        S.pool.memset(va[:, :, 64:128], 1.0)
        S.pool.dma_start(out=va[:, :, 0:64], in_=sc_tm[:, 0:64].rearrange("(n p) d -> p n d", p=128))
        S.pool.memset(ones_ms[:], 1.0 / 64.0)
        S.dve.tensor_tensor(lt[:, 0:32], bv[0:64, 192:224], bv[0:64, 224:256], ALU.mult)
        S.dve.tensor_tensor(lt[:, 32:64], bv[0:64, 256:288], bv[0:64, 288:320], ALU.mult)
        S.dve.tensor_reduce(lam[:, 0:1], lt[:, 0:32], AX.X, ALU.add)
        S.dve.tensor_reduce(lam[:, 1:2], lt[:, 32:64], AX.X, ALU.add)
        S.act.activation(lam[:, 2:4], lam[:, 0:2], AF.Exp)
        S.dve.tensor_tensor(lam[:, 4:5], lam[:, 2:3], lam[:, 3:4], ALU.subtract)
        S.dve.tensor_tensor(lam[:, 4:5], lam[:, 4:5], pv[0:64, 7:8], ALU.add)
        S.dve.tensor_scalar_mul(lam[:, 5:6], lam[:, 4:5], -1.0)
        S.dve.tensor_tensor(lam[:, 6:7], pv[0:64, 5:6], pv[0:64, 6:7], ALU.mult)
        blocks = []
        for t in range(NTT):
            nkb = 4 * (t + 1)
            for kb in range(nkb):
                blocks.append((t, kb, nkb))
        LA = 2
        NB_ = len(blocks)

        def front(i):
            t, kb, nkb = blocks[i]
            q0 = t * 512
            j = kb - 4 * t
            c0 = max(j, 0) * 128
            for m in range(2):
                rows = slice(32 * m, 32 * m + 32)
                e = (2 * i + m) % 6
                S.pe.matmul(pss[e][:, c0:512], kT[rows, kb * 128:(kb + 1) * 128], qT[rows, q0 + c0:q0 + 512],
                            start=True, stop=True)
            for m in range(2):
                e = (2 * i + m) % 6
                S.act.activation(E[e][:, c0:512], pss[e][:, c0:512], AF.Exp, scale=scale)
                if j >= 0:
                    S.pool.affine_select(E[e][:, c0:c0 + 128], E[e][:, c0:c0 + 128], [[1, 128]], ALU.is_ge, 0.0,
                                         base=0, channel_multiplier=-1)

        def back(i):
            t, kb, nkb = blocks[i]
            j = kb - 4 * t
            c0 = max(j, 0) * 128
            for m in range(2):
                e = (2 * i + m) % 6
                po = pso[m]
                S.pe.matmul(po[:, c0:512], va[:, kb, :], E[e][:, c0:512], start=(kb == 0), stop=(kb == nkb - 1))
            if kb == nkb - 1:
                epilogue(t)

        def epilogue(t):
            q0 = t * 512
            p0 = pso[0]
            p1 = pso[1]
            S.act.activation(rd[0][:], p0[64:128, :], AF.Ln)
            S.act.activation(rd[0][:], rd[0][:], AF.Exp, scale=-1.0)
            S.dve.tensor_tensor(o0[:], p0[0:64, :], rd[0][:], ALU.mult)
            S.act.activation(rd[1][:], p1[64:128, :], AF.Ln)
            S.act.activation(rd[1][:], rd[1][:], AF.Exp, scale=-1.0)
            S.dve.tensor_tensor(o1[:], p1[0:64, :], rd[1][:], ALU.mult)
            S.dve.scalar_tensor_tensor(o0[:], o1[:], lam[:, 5:6], o0[:], ALU.mult, ALU.add)
            S.pool.tensor_tensor(sq[:], o0[:], o0[:], ALU.mult)
            S.pe.matmul(psm[0:64, :], ones_ms[:], sq[:], start=True, stop=True)
            S.dve.tensor_scalar_add(sq[:], psm[0:64, :], LN_EPS)
            S.act.activation(sq[:], sq[:], AF.Ln)
            S.act.activation(sq[:], sq[:], AF.Exp, scale=-0.5)
            S.pool.tensor_tensor(o1[:], o0[:], sq[:], ALU.mult)
            S.dve.tensor_scalar(og[t % 2][:], o1[:], lam[:, 6:7], None, ALU.mult)
            S.sp.dma_start(out=outf(0, q0, q0 + 512), in_=og[t % 2][:])

        for i in range(NB_ + LA):
            if i < NB_:
                front(i)
            if i - LA >= 0:
                back(i - LA)


import math as _math

ROPE_THETA = 500000.0


def _rope_tables(SEQ):
    pos = np.arange(SEQ, dtype=np.float32)

    def tab(rot, blk):
        half = rot // 2
        inv = (np.float32(ROPE_THETA) ** (-(np.arange(0, rot, 2, dtype=np.float32)) / np.float32(rot))).astype(np.float32)
        ang = (pos[:, None] * inv[None, :]).astype(np.float32)
        c, s = np.cos(ang).astype(np.float32).T, np.sin(ang).astype(np.float32).T
        C = np.ones((blk, SEQ), np.float32)
        Sn = np.zeros((blk, SEQ), np.float32)
        C[0:half] = c
        C[half:2 * half] = c
        Sn[0:half] = -s
        Sn[half:2 * half] = s
        return C, Sn
    Ca, Sa = tab(8, 32)
    Cc, Sc = tab(16, 64)
    ropeA = np.stack([np.concatenate([Ca, Ca], 0), np.concatenate([Sa, Sa], 0)]).astype(np.float32)
    ropeC = np.stack([Cc, Sc]).astype(np.float32)
    return np.ascontiguousarray(ropeA), np.ascontiguousarray(ropeC)


def _perm_idx(base, n, half):
    idx = np.arange(n)
    d = idx.copy()
    d[0:half] = idx[0:half] + half
    d[half:2 * half] = idx[half:2 * half] - half
    return base + d


def prep_mixer_inputs(inp, l, h, SEQ, ropes):
    w_in = inp["w_in"][l]
    c = lambda off: off + h * 64 + np.arange(64)
    aq, ak, av = c(0), c(256), c(512)
    bu = c(768)
    cq, ck, cv = c(1280), c(1536), c(1792)
    dq, dk, dv, do_ = c(2048), c(2304), c(2560), c(2816)
    aqp = np.concatenate([_perm_idx(aq[0], 32, 4), _perm_idx(aq[32], 32, 4)])
    akp = np.concatenate([_perm_idx(ak[0], 32, 4), _perm_idx(ak[32], 32, 4)])
    cqp = _perm_idx(cq[0], 64, 8)
    ckp = _perm_idx(ck[0], 64, 8)
    di, df = 3072 + h, 3076 + h
    g6 = np.concatenate([[di, df], np.full(126, di)])
    fm_cols = np.concatenate([aq, ak, aqp, akp, cq, ck, cqp, ckp, dq, dk, g6])
    bv_all = 1024 + np.concatenate([h * 64 + np.arange(64)] + [g * 64 + np.arange(64) for g in range(4) if g != h])
    tm_cols = np.concatenate([av, cv, dv, do_, bu, bv_all, [di, df, di, df]])
    assert fm_cols.size == 768 and tm_cols.size == NTM
    lam_init = 0.8 - 0.6 * _math.exp(-0.3 * l)
    pvec = np.zeros((128, 16), np.float32)
    chan = np.concatenate([h * 64 + np.arange(64), 256 + h * 64 + np.arange(64)])
    pvec[:, 0:4] = inp["mlstm_conv_w"][l][:, chan].T
    pvec[:, 4] = inp["mlstm_conv_b"][l][chan]
    pvec[:, 5] = np.tile(inp["diff_subln_g"][l], 2)
    pvec[:, 6] = 1.0 - lam_init
    pvec[:, 7] = lam_init
    pvec[:, 8] = inp["mlstm_gate_b"][l][0, h]
    pvec[:, 9] = inp["mlstm_gate_b"][l][1, h]
    pvec[:, 10] = inp["sgu_b"][l][h]
    bvec = np.zeros(704, np.float32)
    bvec[0:64] = inp["sgu_ln_g"][l][h * 64:(h + 1) * 64]
    bvec[64:128] = inp["sgu_ln_b"][l][h * 64:(h + 1) * 64]
    bvec[128:192] = inp["mlstm_norm_g"][l]
    bvec[192:320] = inp["diff_lambda"][l].reshape(-1)
    return {
        "w_fm": np.ascontiguousarray(w_in[:, fm_cols]),
        "w_tm": np.ascontiguousarray(w_in[:, tm_cols]),
        "ropeA": ropes[0], "ropeC": ropes[1],
        "pvec": pvec, "bvec": bvec,
        "sgu_wT": np.ascontiguousarray(inp["sgu_w"][l][h].T),
    }


def _mixer_C(S, nc, SEQ, sc_qc, sc_kc, sc_tm, identb, mask_le_b, mask_ge_b, outf):
    pats = (1, 4, 16)
    SB = 2048 if SEQ >= 2048 else SEQ
    NSB = SEQ // SB
    NBLK = SEQ // 128
    with ExitStack() as st:
        qT = _sb(st, nc, "C_qT", [64, SEQ], BF16)
        kT = _sb(st, nc, "C_kT", [64, SEQ], BF16)
        vd = [_sb(st, nc, "C_vd%d" % i, [128, NBLK, 128], BF16) for i in range(3)]
        acc = [_sb(st, nc, "C_acc%d" % i, [128, SB], F32) for i in range(2)]
        E = [_sb(st, nc, "C_E%d" % i, [128, 2, 128], BF16) for i in range(3)]
        rd = _sb(st, nc, "C_rd", [64, SB], F32)
        og = _sb(st, nc, "C_og", [64, SB], BF16)
        pss = [_ps(st, nc, "C_pss%d" % i, [128, 2, 128]) for i in range(3)]
        pso = [_ps(st, nc, "C_pso%d" % i, [128, 128]) for i in range(3)]
        for i in range(3):
            S.set_gran(vd[i], 128)
        S.sp.dma_start(out=qT[:], in_=sc_qc)
        S.sp.dma_start(out=kT[:], in_=sc_kc)
        for pi, dil in enumerate(pats):
            S.pool.memset(vd[pi][:, :, 64:128], 1.0)
            nb = SEQ // (128 * dil)
            src = sc_tm[:, 64:128].rearrange("(n j r) d -> j n r d", j=128, r=dil)
            dst = vd[pi][:, :, 0:64].rearrange("j (n r) d -> j n r d", r=dil)
            for n in range(nb):
                S.pool.dma_start(out=dst[:, n], in_=src[:, n])
        blocks = []
        for sb in range(NSB):
            first = True
            for pi, dil in enumerate(pats):
                span = 128 * dil
                for n in range(sb * SB // span, (sb + 1) * SB // span):
                    for r in range(dil):
                        blocks.append([sb, pi, dil, n, r, first, False])
                        first = False
            blocks[-1][6] = True

        def c0(i):
            sb, pi, dil, n, r, first, lastb = blocks[i]
            span = 128 * dil
            e = i % 3
            if first:
                S.pool.memset(acc[sb % 2][:], 0.0)
            qs = slice(n * span + r, (n + 1) * span, dil)
            if n >= 1:
                kprev = slice((n - 1) * span + r, n * span, dil)
                S.pe.matmul(pss[e][:, 0, :], kT[:, kprev], qT[:, qs], start=True, stop=False)
                S.pe.matmul(pss[e][:, 0, :], identb[:], mask_ge_b[:], start=False, stop=True)
            S.pe.matmul(pss[e][:, 1, :], kT[:, qs], qT[:, qs], start=True, stop=False)
            S.pe.matmul(pss[e][:, 1, :], identb[:], mask_le_b[:], start=False, stop=True)
            lo = 0 if n >= 1 else 1
            S.act.activation(E[e][:, lo:2, :], pss[e][:, lo:2, :], AF.Exp, scale=0.125)

        def c1(i):
            sb, pi, dil, n, r, first, lastb = blocks[i]
            span = 128 * dil
            e = i % 3
            a = acc[sb % 2]
            kbs = ([(0, (n - 1) * dil + r)] if n >= 1 else []) + [(1, n * dil + r)]
            for ii, (slot, blk) in enumerate(kbs):
                S.pe.matmul(pso[e][:], vd[pi][:, blk, :], E[e][:, slot, :], start=(ii == 0), stop=(ii == len(kbs) - 1))
            loc = slice(n * span + r - sb * SB, (n + 1) * span - sb * SB, dil)
            S.dve.tensor_tensor(a[:, loc], a[:, loc], pso[e][:], ALU.add)
            if lastb:
                S.act.activation(rd[:], a[64:128, :], AF.Ln)
                S.act.activation(rd[:], rd[:], AF.Exp, scale=-1.0)
                S.dve.tensor_tensor(og[:], a[0:64, :], rd[:], ALU.mult)
                PW = min(SB, 512)
                for pc_ in range(SB // PW):
                    S.sp.dma_start(out=outf(128, sb * SB + pc_ * PW, sb * SB + (pc_ + 1) * PW), in_=og[:, pc_ * PW:(pc_ + 1) * PW])

        _pipeline(len(blocks), [c0, c1], lag=2)


import math as _math

ROPE_THETA = 500000.0


def _rope_tables(SEQ):
    pos = np.arange(SEQ, dtype=np.float32)

    def tab(rot, blk):
        half = rot // 2
        inv = (np.float32(ROPE_THETA) ** (-(np.arange(0, rot, 2, dtype=np.float32)) / np.float32(rot))).astype(np.float32)
        ang = (pos[:, None] * inv[None, :]).astype(np.float32)
        c, s = np.cos(ang).astype(np.float32).T, np.sin(ang).astype(np.float32).T
        C = np.ones((blk, SEQ), np.float32)
        Sn = np.zeros((blk, SEQ), np.float32)
        C[0:half] = c
        C[half:2 * half] = c
        Sn[0:half] = -s
        Sn[half:2 * half] = s
        return C, Sn
    Ca, Sa = tab(8, 32)
    Cc, Sc = tab(16, 64)
    ropeA = np.stack([np.concatenate([Ca, Ca], 0), np.concatenate([Sa, Sa], 0)]).astype(np.float32)
    ropeC = np.stack([Cc, Sc]).astype(np.float32)
    return np.ascontiguousarray(ropeA), np.ascontiguousarray(ropeC)


def _perm_idx(base, n, half):
    idx = np.arange(n)
    d = idx.copy()
    d[0:half] = idx[0:half] + half
    d[half:2 * half] = idx[half:2 * half] - half
    return base + d


def prep_mixer_inputs(inp, l, h, SEQ, ropes):
    w_in = inp["w_in"][l]
    c = lambda off: off + h * 64 + np.arange(64)
    aq, ak, av = c(0), c(256), c(512)
    bu = c(768)
    cq, ck, cv = c(1280), c(1536), c(1792)
    dq, dk, dv, do_ = c(2048), c(2304), c(2560), c(2816)
    aqp = np.concatenate([_perm_idx(aq[0], 32, 4), _perm_idx(aq[32], 32, 4)])
    akp = np.concatenate([_perm_idx(ak[0], 32, 4), _perm_idx(ak[32], 32, 4)])
    cqp = _perm_idx(cq[0], 64, 8)
    ckp = _perm_idx(ck[0], 64, 8)
    di, df = 3072 + h, 3076 + h
    g6 = np.concatenate([[di, df], np.full(126, di)])
    fm_cols = np.concatenate([aq, ak, aqp, akp, cq, ck, cqp, ckp, dq, dk, g6])
    bv_all = 1024 + np.concatenate([h * 64 + np.arange(64)] + [g * 64 + np.arange(64) for g in range(4) if g != h])
    tm_cols = np.concatenate([av, cv, dv, do_, bu, bv_all, [di, df, di, df]])
    assert fm_cols.size == 768 and tm_cols.size == NTM
    lam_init = 0.8 - 0.6 * _math.exp(-0.3 * l)
    pvec = np.zeros((128, 16), np.float32)
    chan = np.concatenate([h * 64 + np.arange(64), 256 + h * 64 + np.arange(64)])
    pvec[:, 0:4] = inp["mlstm_conv_w"][l][:, chan].T
    pvec[:, 4] = inp["mlstm_conv_b"][l][chan]
    pvec[:, 5] = np.tile(inp["diff_subln_g"][l], 2)
    pvec[:, 6] = 1.0 - lam_init
    pvec[:, 7] = lam_init
    pvec[:, 8] = inp["mlstm_gate_b"][l][0, h]
    pvec[:, 9] = inp["mlstm_gate_b"][l][1, h]
    pvec[:, 10] = inp["sgu_b"][l][h]
    bvec = np.zeros(704, np.float32)
    bvec[0:64] = inp["sgu_ln_g"][l][h * 64:(h + 1) * 64]
    bvec[64:128] = inp["sgu_ln_b"][l][h * 64:(h + 1) * 64]
    bvec[128:192] = inp["mlstm_norm_g"][l]
    bvec[192:320] = inp["diff_lambda"][l].reshape(-1)
    return {
        "w_fm": np.ascontiguousarray(w_in[:, fm_cols]),
        "w_tm": np.ascontiguousarray(w_in[:, tm_cols]),
        "ropeA": ropes[0], "ropeC": ropes[1],
        "pvec": pvec, "bvec": bvec,
        "sgu_wT": np.ascontiguousarray(inp["sgu_w"][l][h].T),
    }


def _mixer_C(S, nc, SEQ, sc_qc, sc_kc, sc_tm, identb, mask_le_b, mask_ge_b, outf):
    pats = (1, 4, 16)
    SB = 2048 if SEQ >= 2048 else SEQ
    NSB = SEQ // SB
    NBLK = SEQ // 128
    with ExitStack() as st:
        qT = _sb(st, nc, "C_qT", [64, SEQ], BF16)
        kT = _sb(st, nc, "C_kT", [64, SEQ], BF16)
        vd = [_sb(st, nc, "C_vd%d" % i, [128, NBLK, 128], BF16) for i in range(3)]
        acc = [_sb(st, nc, "C_acc%d" % i, [128, SB], F32) for i in range(2)]
        E = [_sb(st, nc, "C_E%d" % i, [128, 2, 128], BF16) for i in range(3)]
        rd = _sb(st, nc, "C_rd", [64, SB], F32)
        og = _sb(st, nc, "C_og", [64, SB], BF16)
        pss = [_ps(st, nc, "C_pss%d" % i, [128, 2, 128]) for i in range(3)]
        pso = [_ps(st, nc, "C_pso%d" % i, [128, 128]) for i in range(3)]
        for i in range(3):
            S.set_gran(vd[i], 128)
        S.sp.dma_start(out=qT[:], in_=sc_qc)
        S.sp.dma_start(out=kT[:], in_=sc_kc)
        for pi, dil in enumerate(pats):
            S.pool.memset(vd[pi][:, :, 64:128], 1.0)
            nb = SEQ // (128 * dil)
            src = sc_tm[:, 64:128].rearrange("(n j r) d -> j n r d", j=128, r=dil)
            dst = vd[pi][:, :, 0:64].rearrange("j (n r) d -> j n r d", r=dil)
            for n in range(nb):
                S.pool.dma_start(out=dst[:, n], in_=src[:, n])
        ei = 0
        for sb in range(NSB):
            a = acc[sb % 2]
            S.pool.memset(a[:], 0.0)
            for pi, dil in enumerate(pats):
                span = 128 * dil
                for n in range(sb * SB // span, (sb + 1) * SB // span):
                    for r in range(dil):
                        e = ei % 3
                        ei += 1
                        qs = slice(n * span + r, (n + 1) * span, dil)
                        kcur = qs
                        kbs = []
                        if n >= 1:
                            kprev = slice((n - 1) * span + r, n * span, dil)
                            S.pe.matmul(pss[e][:, 0, :], kT[:, kprev], qT[:, qs], start=True, stop=False)
                            S.pe.matmul(pss[e][:, 0, :], identb[:], mask_ge_b[:], start=False, stop=True)
                            kbs.append((0, (n - 1) * dil + r))
                        S.pe.matmul(pss[e][:, 1, :], kT[:, kcur], qT[:, qs], start=True, stop=False)
                        S.pe.matmul(pss[e][:, 1, :], identb[:], mask_le_b[:], start=False, stop=True)
                        kbs.append((1, n * dil + r))
                        lo = kbs[0][0]
                        S.act.activation(E[e][:, lo:2, :], pss[e][:, lo:2, :], AF.Exp, scale=0.125)
                        for ii, (slot, blk) in enumerate(kbs):
                            S.pe.matmul(pso[e][:], vd[pi][:, blk, :], E[e][:, slot, :], start=(ii == 0), stop=(ii == len(kbs) - 1))
                        loc = slice(n * span + r - sb * SB, (n + 1) * span - sb * SB, dil)
                        S.dve.tensor_tensor(a[:, loc], a[:, loc], pso[e][:], ALU.add)
            S.dve.reciprocal(rd[:], a[64:128, :])
            S.dve.tensor_tensor(og[:], a[0:64, :], rd[:], ALU.mult)
            PW = min(SB, 512)
            for pc_ in range(SB // PW):
                S.sp.dma_start(out=outf(128, sb * SB + pc_ * PW, sb * SB + (pc_ + 1) * PW), in_=og[:, pc_ * PW:(pc_ + 1) * PW])


def _mixer_D(S, nc, SEQ, sc_qkd, sc_g, sc_tm, pv, bv, ident, identb, mask_le, tri, outf):
    NCH = SEQ // 128
    with ExitStack() as st:
        qTb = _sb(st, nc, "D_qT", [64, SEQ], BF16)
        kTb = _sb(st, nc, "D_kT", [64, SEQ], BF16)
        sm = _sb(st, nc, "D_sm", [128, 8], F32)
        Mfull = _sb(st, nc, "D_Mfull", [NCH, 128], F32)
        OH = _sb(st, nc, "D_OH", [NCH, NCH, 128], F32)
        bc = _sb(st, nc, "D_bc", [128, 3, NCH], F32)
        Xcol = _sb(st, nc, "D_Xcol", [128, NCH], F32)
        rcol = _sb(st, nc, "D_rcol", [128, NCH], F32)
        wcol = _sb(st, nc, "D_wcol", [128, NCH], F32)
        mprev = bc[:, 0, :]
        aexp = bc[:, 2, :]
        S.dve.tensor_scalar_mul(sm[:, 0:1], pv[:, 9:10], -1.0)
        S.pool.memset(OH[:], 1.0)
        S.pool.affine_select(OH[:], OH[:], [[-1, NCH], [0, 128]], ALU.is_equal, 0.0, base=0, channel_multiplier=1)
        with ExitStack() as s1:
            gi = _sb(s1, nc, "D_gi", [NCH, 128], F32)
            gf = _sb(s1, nc, "D_gf", [NCH, 128], F32)
            Bc = _sb(s1, nc, "D_Bc", [NCH, 128], F32)
            rr = _sb(s1, nc, "D_rr", [NCH, 128], F32)
            Ml = _sb(s1, nc, "D_Ml", [NCH, 128], F32)
            Xn = _sb(s1, nc, "D_Xn", [NCH, 128], F32)
            zer = _sb(s1, nc, "D_zer", [NCH, 128], F32)
            col2 = _sb(s1, nc, "D_col2", [NCH, 2], F32)
            rows = _sb(s1, nc, "D_rows", [1, 2, NCH], F32)
            r3 = _sb(s1, nc, "D_r3", [1, 3, NCH], F32)
            mall = _sb(s1, nc, "D_mall", [1, NCH], F32)
            mpc = _sb(s1, nc, "D_mpc", [NCH, 1], F32)
            ones1 = _sb(s1, nc, "D_ones1", [1, 128], F32)
            pA = _ps(s1, nc, "D_pA", [128, 3 * NCH])
            pB = _ps(s1, nc, "D_pB", [128, 128])
            S.pool.memset(zer[:], 0.0)
            S.pool.memset(ones1[:], 1.0)
            S.sp.dma_start(out=gi[:], in_=sc_g[0].rearrange("(n t) -> n t", t=128))
            S.sp.dma_start(out=gf[:], in_=sc_g[1].rearrange("(n t) -> n t", t=128))
            S.act.activation(gf[:], gf[:], AF.Exp, scale=-1.0, bias=sm[0:NCH, 0:1])
            S.act.activation(gf[:], gf[:], AF.Ln, bias=1.0)
            S.dve.tensor_tensor_scan(Bc[:], gf[:], zer[:], 0.0, ALU.add, ALU.max)
            S.dve.scalar_tensor_tensor(rr[:], gi[:], pv[0:NCH, 8:9], Bc[:], ALU.add, ALU.add)
            S.dve.tensor_tensor_scan(Ml[:], rr[:], rr[:], -1e30, ALU.max, ALU.max)
            S.dve.tensor_copy(col2[:, 0:1], Ml[:, 127:128])
            S.dve.tensor_scalar_mul(col2[:, 1:2], Bc[:, 127:128], -1.0)
            S.pe.transpose(pA[0:1, 0:NCH], col2[:, 0:1], ident[0:NCH, 0:NCH])
            S.pe.transpose(pA[0:1, NCH:2 * NCH], col2[:, 1:2], ident[0:NCH, 0:NCH])
            S.dve.tensor_copy(rows[:].rearrange("p a n -> p (a n)"), pA[0:1, 0:2 * NCH])
            S.dve.tensor_tensor_scan(mall[:], rows[:, 0, :], rows[:, 1, :], 0.0, ALU.max, ALU.add)
            S.dve.memset(r3[:, 0, 0:1], 0.0)
            if NCH > 1:
                S.dve.tensor_copy(r3[:, 0, 1:NCH], mall[:, 0:NCH - 1])
            S.dve.tensor_tensor(r3[:, 1, :], r3[:, 0, :], rows[:, 0, :], ALU.max)
            S.dve.tensor_tensor(r3[:, 2, :], r3[:, 0, :], r3[:, 1, :], ALU.subtract)
            S.act.activation(r3[:, 2, :], r3[:, 2, :], AF.Exp)
            S.pe.matmul(pA[:, 0:3 * NCH], ones1[:], r3[:].rearrange("p a n -> p (a n)"), start=True, stop=True)
            S.dve.tensor_copy(bc[:].rearrange("p a n -> p (a n)"), pA[:, 0:3 * NCH])
            S.pe.transpose(pB[0:NCH, 0:1], r3[:, 0, :], ident[0:1, 0:1])
            S.dve.tensor_copy(mpc[:], pB[0:NCH, 0:1])
            S.dve.tensor_scalar(Mfull[:], Ml[:], mpc[:, 0:1], None, ALU.max)
            S.dve.tensor_tensor(Xn[:], Bc[:], Mfull[:], ALU.subtract)
            S.act.activation(Xn[:], Xn[:], AF.Exp)
            S.pe.transpose(pB[:, 0:NCH], Xn[:], ident[0:NCH, 0:NCH])
            S.dve.tensor_copy(Xcol[:], pB[:, 0:NCH])
            S.pe.transpose(pB[:, 0:NCH], rr[:], ident[0:NCH, 0:NCH])
            S.dve.tensor_copy(rcol[:], pB[:, 0:NCH])
            S.dve.tensor_tensor(wcol[:], rcol[:], bc[:, 1, :], ALU.subtract)
            S.act.activation(wcol[:], wcol[:], AF.Exp)
            S.dve.tensor_scalar_mul(wcol[:], wcol[:], 0.125)
        S.barrier()
        with ExitStack() as s2:
            xin = _sb(s2, nc, "D_xin", [128, SEQ + 4], F32)
            cacc = _sb(s2, nc, "D_cacc", [128, SEQ], F32)
            S.pool.memset(xin[:, 0:3], 0.0)
            S.sp.dma_start(out=xin[:, 3:SEQ + 3], in_=sc_qkd)
            S.dve.tensor_scalar(cacc[:], xin[:, 0:SEQ], pv[:, 0:1], None, ALU.mult)
            for j in range(1, 4):
                S.dve.scalar_tensor_tensor(cacc[:], xin[:, j:j + SEQ], pv[:, j:j + 1], cacc[:], ALU.mult, ALU.add)
            S.act.activation(qTb[:], cacc[0:64, :], AF.Silu, bias=pv[0:64, 4:5])
            S.act.activation(kTb[:], cacc[64:128, :], AF.Silu, bias=pv[64:128, 4:5])
        S.barrier()
        with ExitStack() as s3:
            vaug = _sb(s3, nc, "D_vaug", [128, NCH, 66], BF16)
            ktm = _sb(s3, nc, "D_ktm", [128, NCH, 64], BF16)
            kw = _sb(s3, nc, "D_kw", [128, NCH, 64], BF16)
            Cst = _sb(s3, nc, "D_Cst", [64, 66], F32)
            Cb = [_sb(s3, nc, "D_Cb%d" % i, [64, 66], BF16) for i in range(2)]
            tmp = [_sb(s3, nc, "D_tmp%d" % i, [128, 128], F32) for i in range(2)]
            pp = [_sb(s3, nc, "D_pp%d" % i, [128, 128], F32) for i in range(2)]
            swT = [_sb(s3, nc, "D_swT%d" % i, [128, 128], BF16) for i in range(2)]
            ech = [_sb(s3, nc, "D_ech%d" % i, [64, 128], F32) for i in range(2)]
            qe = [_sb(s3, nc, "D_qe%d" % i, [64, 128], BF16) for i in range(2)]
            dsg = [_sb(s3, nc, "D_dsg%d" % i, [128, 4, 64], F32) for i in range(2)]
            hh = [_sb(s3, nc, "D_hh%d" % i, [128, 4, 64], F32) for i in range(2)]
            dd = [_sb(s3, nc, "D_dd%d" % i, [128, 16], F32) for i in range(2)]
            stats = [_sb(s3, nc, "D_st%d" % i, [128, 4, 6], F32) for i in range(2)]
            mv = [_sb(s3, nc, "D_mv%d" % i, [128, 4, 2], F32) for i in range(2)]
            og = [_sb(s3, nc, "D_og%d" % i, [64, 512], BF16) for i in range(2)]
            pkt = [_ps(s3, nc, "D_pkt%d" % i, [128, 64], BF16) for i in range(1)]
            pqk = [_ps(s3, nc, "D_pqk%d" % i, [128, 128]) for i in range(2)]
            ph = [_ps(s3, nc, "D_ph%d" % i, [128, 4, 66]) for i in range(2)]
            pc = _ps(s3, nc, "D_pc", [64, 66])
            pt = _ps(s3, nc, "D_pt", [64, 512])
            pM = _ps(s3, nc, "D_pM", [128, 128])
            S.set_gran(vaug, 66)
            S.set_gran(ktm, 64)
            S.pool.memset(vaug[:, :, 64:66], 1.0)
            S.pool.dma_start(out=vaug[:, :, 0:64], in_=sc_tm[:, 128:192].rearrange("(n s) d -> s n d", s=128))
            S.pool.memset(Cst[:], 0.0)
            for n in range(NCH):
                c = slice(n * 128, (n + 1) * 128)
                S.pe.transpose(pkt[0][:], kTb[:, c], identb[0:64, 0:64])
                S.act.copy(ktm[:, n, :], pkt[0][:])
            wb = wcol[:].unsqueeze(2).to_broadcast([128, NCH, 64])
            S.pool.tensor_tensor(kw[:], ktm[:], wb, ALU.mult)
            def d0(n):
                b = n % 2
                c = slice(n * 128, (n + 1) * 128)
                if n % 4 == 0:
                    g4 = (n // 4) % 2
                    nn = min(4, NCH - n)
                    S.sp.dma_start(out=dsg[g4][:, 0:nn, :],
                                   in_=sc_tm[n * 128:(n + nn) * 128, 192:256].rearrange("(n s) d -> s n d", s=128))
                    S.act.activation(dsg[g4][:, 0:nn, :], dsg[g4][:, 0:nn, :], AF.Exp, scale=-1.0)
                    S.pool.tensor_scalar_add(dsg[g4][:, 0:nn, :], dsg[g4][:, 0:nn, :], 1.0)
                    S.dve.reciprocal(dsg[g4][:, 0:nn, :], dsg[g4][:, 0:nn, :])
                S.pe.matmul(pqk[b][:], kTb[:, c], qTb[:, c], start=True, stop=True)
                S.pe.matmul(pM[:], OH[:, n, :], Mfull[:], start=True, stop=True)
                S.dve.scalar_tensor_tensor(tmp[b][:], mask_le[:], rcol[:, n:n + 1], pM[:], ALU.add, ALU.subtract)
                S.act.activation(pp[b][:], tmp[b][:], AF.Exp)
                S.dve.scalar_tensor_tensor(swT[b][:], pqk[b][:], 0.125, pp[b][:], ALU.mult, ALU.mult)
                if n > 0:
                    S.act.activation(ech[b][:], pM[0:64, :], AF.Exp, scale=-1.0, bias=mprev[0:64, n:n + 1])
                    S.pool.tensor_tensor(qe[b][:], qTb[:, c], ech[b][:], ALU.mult)

            def d1(n):
                b = n % 2
                pg = ph[(n // 4) % 2]
                S.pe.matmul(pg[:, n % 4, 0:65], swT[b][:], vaug[:, n, 0:65], start=True, stop=(n == 0))
                if n > 0:
                    S.pe.matmul(pg[:, n % 4, 0:65], qe[b][:], Cb[(n - 1) % 2][:, 0:65], start=False, stop=True)
                if n < NCH - 1:
                    S.pe.matmul(pc[:, 0:65], kw[:, n, :], vaug[:, n, 0:65], start=True, stop=True)
                    S.dve.scalar_tensor_tensor(Cst[:, 0:65], Cst[:, 0:65], aexp[0:64, n:n + 1], pc[:, 0:65], ALU.mult, ALU.add)
                    S.act.copy(Cb[n % 2][:, 0:65], Cst[:, 0:65])

            def d2(n):
                if n % 4 != 3:
                    return
                g = (n // 4) % 2
                n0 = n - 3
                pg = ph[g]
                den = pg[:, :, 64]
                S.dve.tensor_scalar_mul(dd[g][:, 0:4], den, -1.0)
                S.dve.tensor_tensor(dd[g][:, 0:4], dd[g][:, 0:4], den, ALU.max)
                S.dve.tensor_tensor(dd[g][:, 0:4], dd[g][:, 0:4], Xcol[:, n0:n0 + 4], ALU.max)
                S.dve.reciprocal(dd[g][:, 4:8], dd[g][:, 0:4])
                S.dve.tensor_tensor(hh[g][:], pg[:, :, 0:64], dd[g][:, 4:8].unsqueeze(2).to_broadcast([128, 4, 64]), ALU.mult)
                for k in range(4):
                    S.dve.bn_stats(stats[g][:, k, :], hh[g][:, k, :])
                    S.dve.bn_aggr(mv[g][:, k, :], stats[g][:, k:k + 1, :])
                S.dve.tensor_scalar_add(dd[g][:, 8:12], mv[g][:, :, 1], LN_EPS)
                S.act.activation(dd[g][:, 8:12], dd[g][:, 8:12], AF.Ln)
                S.act.activation(dd[g][:, 12:16], dd[g][:, 8:12], AF.Exp, scale=-0.5)
                S.dve.tensor_tensor(hh[g][:], hh[g][:], mv[g][:, :, 0:1].to_broadcast([128, 4, 64]), ALU.subtract)
                S.dve.tensor_tensor(hh[g][:], hh[g][:], dd[g][:, 12:16].unsqueeze(2).to_broadcast([128, 4, 64]), ALU.mult)
                S.pool.tensor_tensor(hh[g][:], hh[g][:], bv[:, 128:192].unsqueeze(1).to_broadcast([128, 4, 64]), ALU.mult)
                S.pool.tensor_tensor(hh[g][:], hh[g][:], dsg[g][:], ALU.mult)

            def d3(n):
                if n % 4 != 3:
                    return
                g = (n // 4) % 2
                n0 = n - 3
                for k in range(4):
                    S.pe.transpose(pt[:, k * 128:(k + 1) * 128], hh[g][:, k, :], ident[:])
                S.act.copy(og[g][:], pt[:])
                S.sp.dma_start(out=outf(192, n0 * 128, (n + 1) * 128), in_=og[g][:])

            assert NCH % 4 == 0
            _pipeline(NCH, [d0, d1, d2, d3])


I32 = mybir.dt.int32
GROUPS = [[0, 1, 2, 3], [4, 5, 6, 7]]


def build_fused(SEQ, depth=DEPTH, do=("A", "B", "C", "D"), ffn=True, exch=True):
    T = SEQ // 4
    NTILE = SEQ // 128
    nc = bass.Bass("TRN2", target_bir_lowering=False)
    L = depth
    dr = lambda name, shape, dt=F32: nc.dram_tensor(name, shape, dt, kind="ExternalInput").ap()
    x0g = dr("x0g", [4 * D_MODEL * (T // 512), 512])
    xres0 = dr("xres0", [T, D_MODEL])
    w_fm = dr("w_fm", [L, D_MODEL, 768])
    w_tm = dr("w_tm", [L, D_MODEL, NTM])
    ropeA = dr("ropeA", [2, 64, SEQ])
    ropeC = dr("ropeC", [2, 64, SEQ])
    pvec = dr("pvec", [L, 128, 16])
    bvec = dr("bvec", [L, 704])
    sgu_wT = dr("sgu_wT", [L, 128, 128])
    w_out = dr("w_out", [L, D_MODEL, D_MODEL])
    w_gate = dr("w_gate", [L, D_MODEL, D_FF])
    w_up = dr("w_up", [L, D_MODEL, D_FF])
    w_down = dr("w_down", [L, D_FF, D_MODEL])
    lnp = dr("lnp", [L, 4, D_MODEL])
    gidx = dr("gidx", [128, 8], I32)
    y = nc.dram_tensor("y", [T, D_MODEL], F32, kind="ExternalOutput").ap()
    P0 = dict(
        sc_qa=nc.dram_tensor("sc_qa", [64, SEQ], BF16).ap(), sc_ka=nc.dram_tensor("sc_ka", [64, SEQ], BF16).ap(),
        sc_qc=nc.dram_tensor("sc_qc", [64, SEQ], BF16).ap(), sc_kc=nc.dram_tensor("sc_kc", [64, SEQ], BF16).ap(),
        sc_qkd=nc.dram_tensor("sc_qkd", [128, SEQ], F32).ap(), sc_g=nc.dram_tensor("sc_g", [2, SEQ], F32).ap(),
        sc_tm=nc.dram_tensor("sc_tm", [SEQ, NTM], F32).ap())
    NTC = T // 512
    mixb = nc.dram_tensor("mixb", [256, SEQ], BF16).ap()
    mixg = nc.dram_tensor("mixg", [4 * 256, SEQ], BF16).ap()
    yTb = nc.dram_tensor("yTb", [NTC * D_MODEL, 512], BF16).ap()
    xTg = nc.dram_tensor("xTg", [NTC * 4 * D_MODEL, 512], BF16).ap()
    xres_d = nc.dram_tensor("xres_d", [T, D_MODEL], F32).ap()
    S = Sched(nc)
    S.set_gran(mixb, 64 * SEQ)
    S.set_gran(mixg, 256 * SEQ)
    S.set_gran(yTb, D_MODEL * 512)
    S.set_gran(xTg, 4 * D_MODEL * 512)
    with ExitStack() as stc:
        _PFX[0] = ""
        C = _consts(S, nc, stc)
        gi = _sb(stc, nc, "gidx_sb", [128, 8], I32)
        S.sp.dma_start(out=gi[:], in_=gidx)
        mixtab = mixg.rearrange("f (n t) -> (f n) t", t=512)
        for l in range(L):
            _PFX[0] = "L%d_" % l
            xg, xeng = (x0g, S.pool) if l == 0 else (xTg, S.sp)

            def xsrc(tt, xg=xg, xeng=xeng):
                r, tc = divmod(tt, NTC)
                r0 = (tc * 4 + r) * D_MODEL
                return xeng, xg[r0:r0 + D_MODEL, :].rearrange("(c p) t -> p c t", p=128)

            def outf(r0, c0, c1):
                return mixb[r0:r0 + 64, c0:c1]

            def after(m):
                if exch:
                    cw = min(SEQ, 2048)
                    S.collective("AllGather", mixb[64 * m:64 * m + 64, :].rearrange("r (a b) -> (r a) b", b=cw),
                                 mixg[256 * m:256 * m + 256, :].rearrange("r (a b) -> (r a) b", b=cw), GROUPS)
            P = dict(P0)
            P.update(xsrc=xsrc, w_fm=w_fm[l], w_tm=w_tm[l], ropeA=ropeA, ropeC=ropeC, pvec=pvec[l], bvec=bvec[l],
                     sgu_wT=sgu_wT[l], outf=outf, after=after,
                     tile_order=[r * NTC + tc for tc in range(NTC) for r in range(4)])
            _mixer_all(S, nc, SEQ, P, C, do)
            S.barrier()
            last = (l == L - 1)

            def mix_gather(tile, i):
                for c in range(8):
                    m = c // 2
                    S.gather_rows(tile[:, c, :], mixtab, gi[:, c:c + 1], i * 512, dep_ap=mixg[256 * m:256 * m + 256, :])

            def yT(i):
                tc = i // 4
                return yTb[tc * D_MODEL:(tc + 1) * D_MODEL, :].rearrange("(c p) t -> p c t", p=128)[:, :, (i % 4) * 128:(i % 4 + 1) * 128]

            def after_tile(i):
                if i % 4 == 3 and exch:
                    tc = i // 4
                    S.collective("AllGather", yTb[tc * D_MODEL:(tc + 1) * D_MODEL, :],
                                 xTg[tc * 4 * D_MODEL:(tc + 1) * 4 * D_MODEL, :], GROUPS)
            PF = dict(mix_gather=mix_gather, xres=(xres0 if l == 0 else xres_d),
                      w_out=w_out[l], w_gate=w_gate[l], w_up=w_up[l], w_down=w_down[l], lnp=lnp[l],
                      y=(y if last else xres_d), yT=(None if last else yT), after_tile=after_tile)
            if ffn:
                _ffn_all(S, nc, T, PF, C["ident"])
            S.barrier()
        S.finish()
    return nc


_PROGS = {}
BATCH = 2
SEQ_FULL = 8192
NCORES = 8


def _fused_inputs(inp, SEQ, depth):
    T = SEQ // 4
    ropes = _rope_tables(SEQ)
    x = inp["x"]
    hm = [[prep_mixer_inputs(inp, l, h, SEQ, ropes) for l in range(depth)] for h in range(4)]
    w_out_p = inp["w_out"][:depth]
    lnp = np.stack([np.stack([inp["ln1_g"][l], inp["ln1_b"][l], inp["ln2_g"][l], inp["ln2_b"][l]]) for l in range(depth)])
    maps = []
    for core in range(NCORES):
        b, j = divmod(core, 4)
        xb = x[b, :SEQ]
        NTC = T // 512
        x0g = np.ascontiguousarray(xb.reshape(4, NTC, 512, D_MODEL).transpose(1, 0, 3, 2).reshape(NTC * 4 * D_MODEL, 512))
        gidx = ((np.arange(8)[None, :] * 128 + np.arange(128)[:, None]) * (SEQ // 512) + j * (T // 512)).astype(np.int32)
        m = {
            "x0g": x0g, "xres0": np.ascontiguousarray(xb[j * T:(j + 1) * T]),
            "w_fm": np.stack([hm[j][l]["w_fm"] for l in range(depth)]),
            "w_tm": np.stack([hm[j][l]["w_tm"] for l in range(depth)]),
            "ropeA": ropes[0], "ropeC": ropes[1],
            "pvec": np.stack([hm[j][l]["pvec"] for l in range(depth)]),
            "bvec": np.stack([hm[j][l]["bvec"] for l in range(depth)]),
            "sgu_wT": np.stack([hm[j][l]["sgu_wT"] for l in range(depth)]),
            "w_out": w_out_p, "w_gate": inp["w_gate"][:depth], "w_up": inp["w_up"][:depth], "w_down": inp["w_down"][:depth],
            "lnp": lnp.astype(np.float32), "gidx": gidx,
        }
        maps.append({k: np.ascontiguousarray(v) for k, v in m.items()})
    return maps


def run_fused(inp, SEQ, depth):
    key = ("fused", SEQ, depth)
    if key not in _PROGS:
        _PROGS[key] = build_fused(SEQ, depth)
    maps = _fused_inputs(inp, SEQ, depth)
    res = run_bass_kernel_spmd(_PROGS[key], maps, core_ids=list(range(NCORES)))
    T = SEQ // 4
    out = np.empty((BATCH, SEQ, D_MODEL), np.float32)
    for core in range(NCORES):
        b, j = divmod(core, 4)
        out[b, j * T:(j + 1) * T] = res.results[core]["y"]
    return out


def kernel(**inputs):
    inp = {k: np.asarray(v, dtype=np.float32) for k, v in inputs.items()}
    return run_fused(inp, SEQ_FULL, DEPTH)
```

```python
import numpy as np
import concourse.bass as bass
import concourse.mybir as mybir

F32 = mybir.dt.float32
BF16 = mybir.dt.bfloat16
AF = mybir.ActivationFunctionType
ALU = mybir.AluOpType
AX = mybir.AxisListType


class _Eng:
    def __init__(self, S, name, eng):
        self.S, self.name, self.eng = S, name, eng
        self.sem = S.nc.alloc_semaphore("es_" + name)
        self.cnt = 0
        self.seen = {}

    def __getattr__(self, op):
        fn = getattr(self.eng, op)

        def call(*args, **kw):
            return self.S._emit(self, op, fn, args, kw)
        return call


class Sched:
    def __init__(self, nc):
        self.nc = nc
        self.pe = _Eng(self, "pe", nc.tensor)
        self.act = _Eng(self, "act", nc.scalar)
        self.dve = _Eng(self, "dve", nc.vector)
        self.pool = _Eng(self, "pool", nc.gpsimd)
        self.sp = _Eng(self, "sp", nc.sync)
        self.engs = [self.pe, self.act, self.dve, self.pool, self.sp]
        self.units = {}
        self.gran = {}
        self.dma_sems = {}
        self.all_sems = {}
        self.n_inst = 0
        self.free_dma = {}
        self.n_dsem = 0
        self.cc_sem = None
        self.cc_cnt = 0

    def collective(self, kind, in_ap, out_ap, groups):
        E = self.pool
        reads, writes = self._units(in_ap), self._units(out_ap)
        self._deps(E, reads, writes, same_raw=False)
        if self.cc_sem is None:
            self.cc_sem = self.nc.alloc_semaphore("cc_sem")
        inst = self.nc.gpsimd.collective_compute(kind, ALU.bypass, replica_groups=groups, ins=[in_ap], outs=[out_ap])
        self.cc_cnt += 1
        inst.then_inc(self.cc_sem, 1)
        self._record((self.cc_sem, self.cc_cnt), reads, writes)
        return inst

    def gather_rows(self, out_ap, table_ap, idx_ap, element_offset, dep_ap=None):
        E = self.pool
        reads = self._units(dep_ap if dep_ap is not None else table_ap) + self._units(idx_ap)
        writes = self._units(out_ap)
        skey = (writes[0][0], 0)
        ds = self._dma_sem(skey, "sw")
        saved = []
        for key in writes:
            u = self._u(key)
            for k_, t_ in list(u["w"].items()):
                if t_[0] is ds[0]:
                    saved.append((u, k_, t_))
                    del u["w"][k_]
        self._deps(E, reads, writes, same_raw=False)
        inst = self.nc.gpsimd.indirect_dma_start(out=out_ap, out_offset=None, in_=table_ap,
                                                 in_offset=bass.IndirectOffsetOnAxis(ap=idx_ap, axis=0),
                                                 element_offset=element_offset)
        ds[1] += 16
        inst.then_inc(ds[0], 16)
        self._record((ds[0], ds[1]), reads, writes)
        return inst

    def _dma_sem(self, skey, cls="hw"):
        skey = (skey[0], cls)
        ds = self.dma_sems.get(skey)
        if ds is None:
            fl = self.free_dma.setdefault(cls, [])
            if fl:
                ds = fl.pop()
            else:
                self.n_dsem += 1
                ds = [self.nc.alloc_semaphore("ds%s%d" % (cls, self.n_dsem)), 0, cls]
            self.dma_sems[skey] = ds
        return ds

    def set_gran(self, t, g):
        self.gran[t.name if hasattr(t, "name") else t] = g

    def _units(self, ap):
        name = ap.tensor.name
        g = self.gran.get(name)
        if g is None:
            return [(name, 0)]
        apl = ap.ap
        space = str(ap.space)
        off = int(ap.offset)
        if "DRAM" in space:
            lo = off
            hi = off + sum((c - 1) * s for s, c in apl)
        else:
            F = 1
            for d in ap.tensor.shape[1:]:
                F *= d
            lo = off % F
            hi = lo + sum((c - 1) * s for s, c in apl[1:])
        return [(name, i) for i in range(lo // g, hi // g + 1)]

    def _u(self, key):
        u = self.units.get(key)
        if u is None:
            u = self.units[key] = {"w": {}, "r": {}}
        return u

    def _wait(self, E, tok):
        sem, val = tok
        k = id(sem)
        if E.seen.get(k, 0) >= val:
            return
        E.eng.wait_ge(sem, val)
        E.seen[k] = val

    def _deps(self, E, reads, writes, same_raw=True):
        toks = []
        for key in reads:
            u = self._u(key)
            toks += list(u["w"].values())
        for key in writes:
            u = self._u(key)
            toks += list(u["w"].values()) + list(u["r"].values())
        for sem, val in toks:
            if sem is E.sem:
                continue
            self._wait(E, (sem, val))
        if same_raw and E is not self.pe:
            for key in reads:
                u = self._u(key)
                for sem, val in u["w"].values():
                    if sem is E.sem:
                        self._wait(E, (sem, val))

    def _record(self, tok, reads, writes):
        sem, val = tok
        k = id(sem)
        for key in reads:
            self._u(key)["r"][k] = tok
        for key in writes:
            u = self._u(key)
            u["w"] = {k: tok}
            u["r"] = {}
        self.all_sems[k] = tok

    def _emit(self, E, op, fn, args, kw):
        if op in ("dma_start",):
            return self._dma(E, fn, args, kw)
        lazy = kw.pop("lazy", False)
        aps = []
        out = kw.get("out", None)
        outs = []
        first = True
        for a in list(args) + [v for k_, v in kw.items()]:
            if isinstance(a, bass.AP):
                aps.append(a)
        if out is not None:
            outs = [out]
        elif args and isinstance(args[0], bass.AP):
            outs = [args[0]]
        if kw.get("accum_out") is not None:
            outs.append(kw["accum_out"])
        out_ids = [id(o) for o in outs]
        reads, writes = [], []
        for a in aps:
            if id(a) in out_ids:
                writes += self._units(a)
            else:
                reads += self._units(a)
        self._deps(E, reads, writes)
        inst = fn(*args, **kw)
        if lazy and kw.get("stop", True) is False:
            self._record((E.sem, E.cnt + 1), reads, writes)
            self.n_inst += 1
            return inst
        E.cnt += 1
        inst.then_inc(E.sem, 1)
        self._record((E.sem, E.cnt), reads, writes)
        self.n_inst += 1
        return inst

    def _dma(self, E, fn, args, kw):
        out = kw.get("out", args[0] if args else None)
        in_ = kw.get("in_", args[1] if len(args) > 1 else None)
        writes = self._units(out)
        reads = self._units(in_)
        if "DRAM" not in str(out.space):
            skey = (writes[0][0], 0)
        elif "DRAM" not in str(in_.space):
            skey = (reads[0][0], 0)
        else:
            skey = (writes[0][0], 0)
        self._deps(E, reads, writes, same_raw=False)
        ds = self._dma_sem(skey, "sw" if E is self.pool else "hw")
        inst = fn(*args, **kw)
        ds[1] += 16
        inst.then_inc(ds[0], 16)
        self._record((ds[0], ds[1]), reads, writes)
        self.n_inst += 1
        return inst

    def barrier(self, final=False):
        toks = [t for t in self.all_sems.values() if (final or t[0] is not self.cc_sem)]
        for E in self.engs:
            for tok in toks:
                if tok[0] is E.sem:
                    continue
                self._wait(E, tok)
        keep = {}
        for key, u in self.units.items():
            w = {k: t for k, t in u["w"].items() if t[0] is self.cc_sem}
            r = {k: t for k, t in u["r"].items() if t[0] is self.cc_sem}
            if (w or r) and not final:
                keep[key] = {"w": w, "r": r}
        self.units = keep
        for ds in self.dma_sems.values():
            self.free_dma.setdefault(ds[2], []).append(ds)
        self.dma_sems = {}

    def finish(self):
        self.barrier(final=True)


from contextlib import ExitStack
from concourse.bass_utils import run_bass_kernel_spmd

D_MODEL = 1024
D_FF = 2816
DEPTH = 4
ALPHA = (2 * DEPTH) ** 0.25
LN_EPS = 1e-5


_PFX = [""]


def _sb(st, nc, name, shape, dt):
    return st.enter_context(nc.sbuf_tensor(_PFX[0] + name, shape, dt))


def _ps(st, nc, name, shape, dt=F32):
    return st.enter_context(nc.psum_tensor(_PFX[0] + name, shape, dt))


def _pipeline(n, stages, lag=1):
    ns = len(stages)
    for step in range(n + (ns - 1) * lag):
        for si, f in enumerate(stages):
            i = step - si * lag
            if 0 <= i < n:
                f(i)


def _make_ident(S, nc, ident):
    S.pool.memset(ident[:], 1.0)
    S.pool.affine_select(ident[:], ident[:], [[-1, 128]], ALU.is_equal, 0.0, base=0, channel_multiplier=1)


def _layernorm_rows(S, nc, t, width, stats, mv, g_bc, b_bc, out):
    nch = width // 512
    for c in range(nch):
        S.dve.bn_stats(stats[:, c, :], t[:, c * 512:(c + 1) * 512])
    S.dve.bn_aggr(mv[:, 0:2], stats[:, 0:nch, :])
    S.dve.tensor_scalar_add(mv[:, 2:3], mv[:, 1:2], LN_EPS)
    S.act.sqrt(mv[:, 2:3], mv[:, 2:3])
    S.dve.reciprocal(mv[:, 3:4], mv[:, 2:3])
    S.dve.tensor_scalar(t, t, mv[:, 0:1], mv[:, 3:4], ALU.subtract, ALU.mult)
    S.pool.tensor_tensor(t, t, g_bc, ALU.mult)
    S.pool.tensor_tensor(out, t, b_bc, ALU.add)


def _ffn_all(S, nc, T, P, ident):
    NT = T // 128
    NTT = T // 512
    mix_gather, xres, w_out, w_gate, w_up, w_down, lnp, y, yT = (P[k] for k in (
        "mix_gather", "xres", "w_out", "w_gate", "w_up", "w_down", "lnp", "y", "yT"))
    with ExitStack() as st0:
        lnbc = _sb(st0, nc, "lnbc", [128, 4, D_MODEL], F32)
        yacc = _sb(st0, nc, "yacc", [128, NT, D_MODEL], F32)
        x1T = _sb(st0, nc, "x1T", [128, 8, T], BF16)
        S.set_gran(yacc, D_MODEL)
        S.set_gran(lnbc, D_MODEL)
        for j in range(4):
            S.sp.dma_start(out=lnbc[:, j, :], in_=lnp[j].partition_broadcast(128))
        with ExitStack() as st:
            wout_b = _sb(st, nc, "wout_b", [128, 8, D_MODEL], BF16)
            mixb = [_sb(st, nc, "mixb%d" % i, [128, 8, 512], BF16) for i in range(2)]
            xr = [_sb(st, nc, "xr%d" % i, [128, D_MODEL], F32) for i in range(2)]
            t1 = [_sb(st, nc, "t1_%d" % i, [128, D_MODEL], F32) for i in range(3)]
            stats = [_sb(st, nc, "stats%d" % i, [128, 2, 6], F32) for i in range(2)]
            mv = [_sb(st, nc, "mv%d" % i, [128, 4], F32) for i in range(2)]
            psh = [_ps(st, nc, "psh%d" % i, [128, D_MODEL]) for i in range(2)]
            pst = [_ps(st, nc, "pst%d" % i, [128, D_MODEL]) for i in range(2)]
            S.pool.dma_start(out=wout_b[:], in_=w_out.rearrange("(c p) f -> p c f", p=128))
            def f0(i):
                b = i % 2
                tsl = slice(i * 128, (i + 1) * 128)
                mb = mixb[(i // 4) % 2]
                if i == 0:
                    mix_gather(mixb[0], 0)
                if i % 4 == 0 and i // 4 + 1 < NT // 4:
                    mix_gather(mixb[(i // 4 + 1) % 2], i // 4 + 1)
                if i == 0:
                    S.sp.dma_start(out=xr[0][:], in_=xres[0:128, :])
                if i + 1 < NT:
                    S.sp.dma_start(out=xr[(i + 1) % 2][:], in_=xres[(i + 1) * 128:(i + 2) * 128, :])
                for half in range(2):
                    for k in range(8):
                        S.pe.matmul(psh[b][:, half * 512:(half + 1) * 512], mb[:, k, (i % 4) * 128:(i % 4 + 1) * 128],
                                    wout_b[:, k, half * 512:(half + 1) * 512], start=(k == 0), stop=(k == 7), lazy=True)
                S.dve.scalar_tensor_tensor(t1[i % 3][:], xr[b][:], ALPHA, psh[b][:], ALU.mult, ALU.add)

            def f1(i):
                b = i % 2
                _layernorm_rows(S, nc, t1[i % 3][:], D_MODEL, stats[b], mv[b], lnbc[:, 0, :], lnbc[:, 1, :], t1[i % 3][:])
                S.act.mul(yacc[:, i, :], t1[i % 3][:], ALPHA)

            def f2(i):
                b = i % 2
                tsl = slice(i * 128, (i + 1) * 128)
                for c in range(8):
                    S.pe.transpose(pst[b][:, c * 128:(c + 1) * 128], t1[i % 3][:, c * 128:(c + 1) * 128], ident[:])
                S.act.copy(x1T[:, 0:4, tsl], pst[b][:, 0:512].rearrange("p (c t) -> p c t", c=4))
                S.dve.tensor_copy(x1T[:, 4:8, tsl], pst[b][:, 512:1024].rearrange("p (c t) -> p c t", c=4))

            _pipeline(NT, [f0, f1, f2])
        S.barrier()
        with ExitStack() as st:
            FB = 256
            NFB = D_FF // FB
            wg_b = [_sb(st, nc, "wg_b%d" % i, [128, 8, FB], BF16) for i in range(2)]
            wu_b = [_sb(st, nc, "wu_b%d" % i, [128, 8, FB], BF16) for i in range(2)]
            wd_b = [_sb(st, nc, "wd_b%d" % i, [128, 2, D_MODEL], BF16) for i in range(2)]
            sg = [_sb(st, nc, "sg%d" % i, [128, 512], F32) for i in range(2)]
            actT = [_sb(st, nc, "actT%d" % i, [128, 2, 512], BF16) for i in range(2)]
            psg = [_ps(st, nc, "psg%d" % i, [128, 512]) for i in range(2)]
            psu = [_ps(st, nc, "psu%d" % i, [128, 512]) for i in range(2)]
            psd = [_ps(st, nc, "psd%d" % i, [128, D_MODEL]) for i in range(2)]
            wg_v = w_gate.rearrange("(c p) f -> p c f", p=128)
            wu_v = w_up.rearrange("(c p) f -> p c f", p=128)
            wd_v = w_down.rearrange("(c p) f -> p c f", p=128)
            items = [(fb, tt) for fb in range(NFB) for tt in range(NTT)]

            def load_w(fb):
                wb = fb % 2
                fsl = slice(fb * FB, (fb + 1) * FB)
                S.pool.dma_start(out=wg_b[wb][:], in_=wg_v[:, :, fsl])
                S.pool.dma_start(out=wu_b[wb][:], in_=wu_v[:, :, fsl])
                S.pool.dma_start(out=wd_b[wb][:], in_=wd_v[:, 2 * fb:2 * fb + 2, :])

            def g0(it):
                fb, tt = items[it]
                wb = fb % 2
                ab = it % 2
                if it == 0:
                    load_w(0)
                    if NFB > 1:
                        load_w(1)
                tsl = slice(tt * 512, (tt + 1) * 512)
                for c2 in range(2):
                    for k in range(8):
                        S.pe.matmul(psg[c2][:], wg_b[wb][:, k, c2 * 128:(c2 + 1) * 128], x1T[:, k, tsl],
                                    start=(k == 0), stop=(k == 7), lazy=True)
                    for k in range(8):
                        S.pe.matmul(psu[c2][:], wu_b[wb][:, k, c2 * 128:(c2 + 1) * 128], x1T[:, k, tsl],
                                    start=(k == 0), stop=(k == 7), lazy=True)
                    S.act.activation(sg[c2][:], psg[c2][:], AF.Silu)
                    S.dve.tensor_tensor(actT[ab][:, c2, :], sg[c2][:], psu[c2][:], ALU.mult)

            def g1(it):
                fb, tt = items[it]
                wb = fb % 2
                ab = it % 2
                for s4 in range(4):
                    db = (it * 4 + s4) % 2
                    for half in range(2):
                        for c2 in range(2):
                            S.pe.matmul(psd[db][:, half * 512:(half + 1) * 512],
                                        actT[ab][:, c2, s4 * 128:(s4 + 1) * 128],
                                        wd_b[wb][:, c2, half * 512:(half + 1) * 512],
                                        start=(c2 == 0), stop=(c2 == 1), lazy=True)
                    ti = tt * 4 + s4
                    S.dve.tensor_tensor(yacc[:, ti, :], yacc[:, ti, :], psd[db][:], ALU.add)
                if tt == NTT - 1 and fb + 2 < NFB:
                    load_w(fb + 2)

            _pipeline(len(items), [g0, g1])
        S.barrier()
        with ExitStack() as st:
            stats = [_sb(st, nc, "stats3_%d" % i, [128, 2, 6], F32) for i in range(2)]
            mv = [_sb(st, nc, "mv3_%d" % i, [128, 4], F32) for i in range(2)]
            ob = [_sb(st, nc, "ob%d" % i, [128, D_MODEL], F32) for i in range(2)]
            obT = [_sb(st, nc, "obT%d" % i, [128, 8, 128], BF16) for i in range(2)]
            pst3 = [_ps(st, nc, "pst3_%d" % i, [128, D_MODEL]) for i in range(2)]
            def h0(i):
                b = i % 2
                _layernorm_rows(S, nc, yacc[:, i, :], D_MODEL, stats[b], mv[b], lnbc[:, 2, :], lnbc[:, 3, :], ob[b][:])

            def h1(i):
                b = i % 2
                S.sp.dma_start(out=y[i * 128:(i + 1) * 128, :], in_=ob[b][:])
                if yT is not None:
                    for c in range(8):
                        S.pe.transpose(pst3[b][:, c * 128:(c + 1) * 128], ob[b][:, c * 128:(c + 1) * 128], ident[:])
                    S.act.copy(obT[b][:, 0:4, :], pst3[b][:, 0:512].rearrange("p (c t) -> p c t", c=4))
                    S.dve.tensor_copy(obT[b][:, 4:8, :], pst3[b][:, 512:1024].rearrange("p (c t) -> p c t", c=4))
                    S.sp.dma_start(out=yT(i), in_=obT[b][:])
                    P["after_tile"](i)

            _pipeline(NT, [h0, h1])


HD = 64
NTM = 580
NEG = -30000.0


def _gelu_tanh(S, nc, out, x, tmp):
    S.act.activation(tmp, x, AF.Square)
    S.dve.tensor_scalar(tmp, tmp, 0.044715, 1.0, ALU.mult, ALU.add)
    S.dve.tensor_tensor(tmp, tmp, x, ALU.mult)
    S.act.activation(tmp, tmp, AF.Sigmoid, scale=2.0 * 0.7978845608028654)
    S.dve.tensor_tensor(out, x, tmp, ALU.mult)


def _consts(S, nc, st0):
    C = {}
    C["pv"] = pv = _sb(st0, nc, "pv", [128, 16], F32)
    C["bv"] = bv = _sb(st0, nc, "bv", [128, 704], F32)
    C["ident"] = ident = _sb(st0, nc, "identf", [128, 128], F32)
    C["identb"] = identb = _sb(st0, nc, "identb", [128, 128], BF16)
    C["mask_le"] = mask_le = _sb(st0, nc, "mask_le", [128, 128], F32)
    C["mask_le_b"] = mask_le_b = _sb(st0, nc, "mask_le_b", [128, 128], BF16)
    C["mask_ge_b"] = mask_ge_b = _sb(st0, nc, "mask_ge_b", [128, 128], BF16)
    C["tri"] = tri = _sb(st0, nc, "tri", [128, 128], F32)
    _make_ident(S, nc, ident)
    S.dve.tensor_copy(identb[:], ident[:])
    S.pool.memset(mask_le[:], 0.0)
    S.pool.affine_select(mask_le[:], mask_le[:], [[1, 128]], ALU.is_ge, NEG, base=0, channel_multiplier=-1)
    S.dve.tensor_copy(mask_le_b[:], mask_le[:])
    S.pool.memset(tri[:], 1.0)
    S.pool.affine_select(tri[:], tri[:], [[1, 128]], ALU.is_ge, 0.0, base=0, channel_multiplier=-1)
    S.pool.memset(mask_ge_b[:], 0.0)
    S.pool.affine_select(mask_ge_b[:], mask_ge_b[:], [[-1, 128]], ALU.is_ge, NEG, base=0, channel_multiplier=1)
    return C


def _mixer_all(S, nc, SEQ, P, C, do=("A", "B", "C", "D")):
    NTT = SEQ // 512
    xsrc, w_fm, w_tm, ropeA, ropeC, pvec, bvec, sgu_wT, outf = (P[k] for k in (
        "xsrc", "w_fm", "w_tm", "ropeA", "ropeC", "pvec", "bvec", "sgu_wT", "outf"))
    sc_qa, sc_ka, sc_qc, sc_kc, sc_qkd, sc_g, sc_tm = (P[k] for k in ("sc_qa", "sc_ka", "sc_qc", "sc_kc", "sc_qkd", "sc_g", "sc_tm"))
    pv, bv, ident, identb, mask_le, mask_le_b, mask_ge_b, tri = (C[k] for k in (
        "pv", "bv", "ident", "identb", "mask_le", "mask_le_b", "mask_ge_b", "tri"))
    S.sp.dma_start(out=pv[:], in_=pvec)
    S.sp.dma_start(out=bv[:], in_=bvec.partition_broadcast(128))
    if True:
        with ExitStack() as st:
            wfm_b = _sb(st, nc, "wfm_b", [128, 8, 768], BF16)
            wtm_b = _sb(st, nc, "wtm_b", [128, 8, NTM], BF16)
            xb = [_sb(st, nc, "xb%d" % i, [128, 8, 512], BF16) for i in range(2)]
            rA = [_sb(st, nc, "rA%d" % i, [64, 2, 512], F32) for i in range(2)]
            rC = [_sb(st, nc, "rC%d" % i, [64, 2, 512], F32) for i in range(2)]
            ta = [_sb(st, nc, "ta%d" % i, [64, 512], F32) for i in range(2)]
            tb = [_sb(st, nc, "tb%d" % i, [64, 512], F32) for i in range(2)]
            stg = [_sb(st, nc, "stg%d" % i, [64, 512], BF16) for i in range(4)]
            stg5 = [_sb(st, nc, "stg5_%d" % i, [128, 512], F32) for i in range(2)]
            stg6 = [_sb(st, nc, "stg6_%d" % i, [2, 512], F32) for i in range(2)]
            stgt = [_sb(st, nc, "stgt%d" % i, [128, NTM], F32) for i in range(2)]
            ps1 = [_ps(st, nc, "ps1_%d" % i, [128, 512]) for i in range(2)]
            ps2 = [_ps(st, nc, "ps2_%d" % i, [128, 512]) for i in range(2)]
            ps5 = _ps(st, nc, "ps5", [128, 512])
            pstm = _ps(st, nc, "pstm", [128, 1024])
            S.pool.dma_start(out=wfm_b[:], in_=w_fm.rearrange("(c p) f -> p c f", p=128))
            S.pool.dma_start(out=wtm_b[:], in_=w_tm.rearrange("(c p) f -> p c f", p=128))
            rA_v = ropeA.rearrange("two r t -> r two t")
            rC_v = ropeC.rearrange("two r t -> r two t")
            order = list(P.get("tile_order", range(NTT)))

            def load_tile(j):
                tj = order[j]
                bj = j % 2
                sj = slice(tj * 512, (tj + 1) * 512)
                xe, xap = xsrc(tj)
                xe.dma_start(out=xb[bj][:], in_=xap)
                S.sp.dma_start(out=rA[bj][:], in_=rA_v[:, :, sj])
                S.sp.dma_start(out=rC[bj][:], in_=rC_v[:, :, sj])

            load_tile(0)
            for it_, tt in enumerate(order):
                b = it_ % 2
                tsl = slice(tt * 512, (tt + 1) * 512)
                if it_ + 1 < len(order):
                    load_tile(it_ + 1)
                def do_pair(pair):
                    rt, dq, dk = ((rA[b], sc_qa, sc_ka), (rC[b], sc_qc, sc_kc))[pair]
                    pb = (2 * it_ + pair) % 2
                    g1 = 2 * pair
                    for k in range(8):
                        S.pe.matmul(ps1[pb][:], wfm_b[:, k, g1 * 128:(g1 + 1) * 128], xb[b][:, k, :], start=(k == 0), stop=(k == 7), lazy=True)
                    for k in range(8):
                        S.pe.matmul(ps2[pb][:], wfm_b[:, k, (g1 + 1) * 128:(g1 + 2) * 128], xb[b][:, k, :], start=(k == 0), stop=(k == 7), lazy=True)
                    for half, dst in enumerate((dq, dk)):
                        rows = slice(half * 64, (half + 1) * 64)
                        tbuf = half
                        S.dve.tensor_tensor(ta[tbuf][:], ps2[pb][rows, :], rt[:, 1, :], ALU.mult)
                        S.dve.tensor_tensor(tb[tbuf][:], ps1[pb][rows, :], rt[:, 0, :], ALU.mult)
                        sb_ = (4 * it_ + 2 * pair + half) % 4
                        S.pool.tensor_tensor(stg[sb_][:], ta[tbuf][:], tb[tbuf][:], ALU.add)
                        S.sp.dma_start(out=dst[:, tsl], in_=stg[sb_][:])

                def do_g5():
                    for k in range(8):
                        S.pe.matmul(ps5[:], wfm_b[:, k, 512:640], xb[b][:, k, :], start=(k == 0), stop=(k == 7), lazy=True)
                    S.act.copy(stg5[b][:], ps5[:])
                    S.sp.dma_start(out=sc_qkd[:, tsl], in_=stg5[b][:])

                def do_g6():
                    for k in range(8):
                        S.pe.matmul(ps5[0:2, :], wfm_b[:, k, 640:642], xb[b][:, k, :], start=(k == 0), stop=(k == 7), lazy=True)
                    S.act.copy(stg6[b][:], ps5[0:2, :])
                    S.sp.dma_start(out=sc_g[:, tsl], in_=stg6[b][:])

                def do_sub(sub):
                    tb_ = (4 * it_ + sub) % 2
                    for k in range(8):
                        S.pe.matmul(pstm[:, 0:512], xb[b][:, k, sub * 128:(sub + 1) * 128], wtm_b[:, k, 0:512], start=(k == 0), stop=(k == 7), lazy=True)
                    for k in range(8):
                        S.pe.matmul(pstm[:, 512:NTM], xb[b][:, k, sub * 128:(sub + 1) * 128], wtm_b[:, k, 512:NTM], start=(k == 0), stop=(k == 7), lazy=True)
                    S.act.copy(stgt[tb_][:], pstm[:, 0:NTM])
                    r0 = tt * 512 + sub * 128
                    S.sp.dma_start(out=sc_tm[r0:r0 + 128, :], in_=stgt[tb_][:])

                do_pair(0)
                do_sub(0)
                do_g5()
                do_sub(1)
                do_pair(1)
                do_sub(2)
                do_g6()
                do_sub(3)
        S.barrier()
        if "B" in do:
            _mixer_B(S, nc, SEQ, sc_tm, sgu_wT, pv, bv, ident, tri, outf)
            P["after"](1)
            S.barrier()
        if "A" in do:
            _mixer_A(S, nc, SEQ, sc_qa, sc_ka, sc_tm, pv, bv, outf)
            P["after"](0)
            S.barrier()
        if "C" in do:
            _mixer_C(S, nc, SEQ, sc_qc, sc_kc, sc_tm, identb, mask_le_b, mask_ge_b, outf)
            P["after"](2)
            S.barrier()
        if "D" in do:
            _mixer_D(S, nc, SEQ, sc_qkd, sc_g, sc_tm, pv, bv, ident, identb, mask_le, tri, outf)
            P["after"](3)


def _mixer_B(S, nc, SEQ, sc_tm, sgu_wT, pv, bv, ident, tri, outf):
    NIT = SEQ // 512
    with ExitStack() as st:
        wT = _sb(st, nc, "sg_wT", [128, 128], F32)
        wTb = _sb(st, nc, "sg_wTb", [128, 128], BF16)
        uv = [_sb(st, nc, "sg_uv%d" % i, [128, 4, 320], F32) for i in range(2)]
        gl = [_sb(st, nc, "sg_gl%d" % i, [128, 4, 320], F32) for i in range(2)]
        tmp = [_sb(st, nc, "sg_tmp%d" % i, [128, 4, 320], F32) for i in range(2)]
        stats = [_sb(st, nc, "sg_st%d" % i, [128, 4, 6], F32) for i in range(2)]
        mv = [_sb(st, nc, "sg_mv%d" % i, [128, 4, 2], F32) for i in range(2)]
        rs = [_sb(st, nc, "sg_rs%d" % i, [128, 4], F32) for i in range(2)]
        vn = [_sb(st, nc, "sg_vn%d" % i, [128, 4, 64], F32) for i in range(2)]
        vnb = [_sb(st, nc, "sg_vnb%d" % i, [128, 4, 64], BF16) for i in range(2)]
        ob = [_sb(st, nc, "sg_ob%d" % i, [128, 4, 64], F32) for i in range(2)]
        og = [_sb(st, nc, "sg_og%d" % i, [64, 512], BF16) for i in range(2)]
        psz = [_ps(st, nc, "sg_psz%d" % i, [128, 4, 64]) for i in range(2)]
        pst = [_ps(st, nc, "sg_pst%d" % i, [64, 512]) for i in range(2)]
        S.sp.dma_start(out=wT[:], in_=sgu_wT)
        S.dve.tensor_tensor(wTb[:], wT[:], tri[:], ALU.mult)
        gbc = bv[:, 0:64].unsqueeze(1).to_broadcast([128, 4, 64])
        bbc = bv[:, 64:128].unsqueeze(1).to_broadcast([128, 4, 64])

        def load_uv(j):
            S.sp.dma_start(out=uv[j % 2][:], in_=sc_tm[j * 512:(j + 1) * 512, 256:576].rearrange("(n p) c -> p n c", p=128))

        def b0(it):
            b = it % 2
            if it == 0:
                load_uv(0)
            if it + 1 < NIT:
                load_uv(it + 1)
            _gelu_tanh(S, nc, gl[b][:], uv[b][:], tmp[b][:])
            for k in range(4):
                S.dve.bn_stats(stats[b][:, k, :], gl[b][:, k, 64:320])
                S.dve.bn_aggr(mv[b][:, k, :], stats[b][:, k:k + 1, :])
            S.dve.tensor_scalar_add(rs[b][:], mv[b][:, :, 1], LN_EPS)
            S.act.sqrt(rs[b][:], rs[b][:])
            S.dve.reciprocal(rs[b][:], rs[b][:])
            S.dve.tensor_tensor(vn[b][:], gl[b][:, :, 64:128], mv[b][:, :, 0:1].to_broadcast([128, 4, 64]), ALU.subtract)
            S.dve.tensor_tensor(vn[b][:], vn[b][:], rs[b][:].unsqueeze(2).to_broadcast([128, 4, 64]), ALU.mult)
            S.pool.tensor_tensor(vn[b][:], vn[b][:], gbc, ALU.mult)
            S.pool.tensor_tensor(vnb[b][:], vn[b][:], bbc, ALU.add)

        def b1(it):
            b = it % 2
            for k in range(4):
                S.pe.matmul(psz[b][:, k, :], wTb[:], vnb[b][:, k, :], start=True, stop=True)
            S.dve.scalar_tensor_tensor(ob[b][:], psz[b][:], pv[:, 10:11], gl[b][:, :, 0:64], ALU.add, ALU.mult)

        def b2(it):
            b = it % 2
            for k in range(4):
                S.pe.transpose(pst[b][:, k * 128:(k + 1) * 128], ob[b][:, k, :], ident[:])
            S.act.copy(og[b][:], pst[b][:])
            S.sp.dma_start(out=outf(64, it * 512, (it + 1) * 512), in_=og[b][:])

        _pipeline(NIT, [b0, b1, b2])


def _mixer_A(S, nc, SEQ, sc_qa, sc_ka, sc_tm, pv, bv, outf):
    NTT = SEQ // 512
    NKB = SEQ // 128
    scale = 32 ** -0.5
    with ExitStack() as st:
        qT = _sb(st, nc, "A_qT", [64, SEQ], BF16)
        kT = _sb(st, nc, "A_kT", [64, SEQ], BF16)
        va = _sb(st, nc, "A_va", [128, NKB, 128], BF16)
        lam = _sb(st, nc, "A_lam", [64, 8], F32)
        lt = _sb(st, nc, "A_lt", [64, 64], F32)
        ones_ms = _sb(st, nc, "A_ones", [64, 64], F32)
        E = [_sb(st, nc, "A_E%d" % i, [128, 512], BF16) for i in range(6)]
        rd = [_sb(st, nc, "A_rd%d" % i, [64, 512], F32) for i in range(2)]
        o0 = _sb(st, nc, "A_o0", [64, 512], F32)
        o1 = _sb(st, nc, "A_o1", [64, 512], F32)
        sq = _sb(st, nc, "A_sq", [64, 512], F32)
        og = [_sb(st, nc, "A_og%d" % i, [64, 512], BF16) for i in range(2)]
        pss = [_ps(st, nc, "A_pss%d" % i, [128, 512]) for i in range(6)]
        pso = [_ps(st, nc, "A_pso%d" % i, [128, 512]) for i in range(2)]
        psm = pss[0]
        S.set_gran(va, 128)
        S.sp.dma_start(out=qT[:], in_=sc_qa)
        S.sp.dma_start(out=kT[:], in_=sc_ka)
        S.pool.memset(va[:, :, 64:128], 1.0)
        S.pool.dma_start(out=va[:, :, 0:64], in_=sc_tm[:, 0:64].rearrange("(n p) d -> p n d", p=128))
        S.pool.memset(ones_ms[:], 1.0 / 64.0)
        S.dve.tensor_tensor(lt[:, 0:32], bv[0:64, 192:224], bv[0:64, 224:256], ALU.mult)
        S.dve.tensor_tensor(lt[:, 32:64], bv[0:64, 256:288], bv[0:64, 288:320], ALU.mult)
        S.dve.tensor_reduce(lam[:, 0:1], lt[:, 0:32], AX.X, ALU.add)
        S.dve.tensor_reduce(lam[:, 1:2], lt[:, 32:64], AX.X, ALU.add)
        S.act.activation(lam[:, 2:4], lam[:, 0:2], AF.Exp)
        S.dve.tensor_tensor(lam[:, 4:5], lam[:, 2:3], lam[:, 3:4], ALU.subtract)
        S.dve.tensor_tensor(lam[:, 4:5], lam[:, 4:5], pv[0:64, 7:8], ALU.add)
        S.dve.tensor_scalar_mul(lam[:, 5:6], lam[:, 4:5], -1.0)
        S.dve.tensor_tensor(lam[:, 6:7], pv[0:64, 5:6], pv[0:64, 6:7], ALU.mult)
        blocks = []
        for t in range(NTT):
            nkb = 4 * (t + 1)
            for kb in range(nkb):
                blocks.append((t, kb, nkb))
        LA = 2
        NB_ = len(blocks)

        def front(i):
            t, kb, nkb = blocks[i]
            q0 = t * 512
            j = kb - 4 * t
            c0 = max(j, 0) * 128
            for m in range(2):
                rows = slice(32 * m, 32 * m + 32)
                e = (2 * i + m) % 6
                S.pe.matmul(pss[e][:, c0:512], kT[rows, kb * 128:(kb + 1) * 128], qT[rows, q0 + c0:q0 + 512],
                            start=True, stop=True)
            for m in range(2):
                e = (2 * i + m) % 6
                S.act.activation(E[e][:, c0:512], pss[e][:, c0:512], AF.Exp, scale=scale)
                if j >= 0:
                    S.pool.affine_select(E[e][:, c0:c0 + 128], E[e][:, c0:c0 + 128], [[1, 128]], ALU.is_ge, 0.0,
                                         base=0, channel_multiplier=-1)

        def back(i):
            t, kb, nkb = blocks[i]
            j = kb - 4 * t
            c0 = max(j, 0) * 128
            for m in range(2):
                e = (2 * i + m) % 6
                po = pso[m]
                S.pe.matmul(po[:, c0:512], va[:, kb, :], E[e][:, c0:512], start=(kb == 0), stop=(kb == nkb - 1))
            if kb == nkb - 1:
                epilogue(t)

        def epilogue(t):
            q0 = t * 512
            p0 = pso[0]
            p1 = pso[1]
            S.act.activation(rd[0][:], p0[64:128, :], AF.Ln)
            S.act.activation(rd[0][:], rd[0][:], AF.Exp, scale=-1.0)
            S.dve.tensor_tensor(o0[:], p0[0:64, :], rd[0][:], ALU.mult)
            S.act.activation(rd[1][:], p1[64:128, :], AF.Ln)
            S.act.activation(rd[1][:], rd[1][:], AF.Exp, scale=-1.0)
            S.dve.tensor_tensor(o1[:], p1[0:64, :], rd[1][:], ALU.mult)
            S.dve.scalar_tensor_tensor(o0[:], o1[:], lam[:, 5:6], o0[:], ALU.mult, ALU.add)
            S.pool.tensor_tensor(sq[:], o0[:], o0[:], ALU.mult)
            S.pe.matmul(psm[0:64, :], ones_ms[:], sq[:], start=True, stop=True)
            S.dve.tensor_scalar_add(sq[:], psm[0:64, :], LN_EPS)
            S.act.activation(sq[:], sq[:], AF.Ln)
            S.act.activation(sq[:], sq[:], AF.Exp, scale=-0.5)
            S.pool.tensor_tensor(o1[:], o0[:], sq[:], ALU.mult)
            S.dve.tensor_scalar(og[t % 2][:], o1[:], lam[:, 6:7], None, ALU.mult)
            S.sp.dma_start(out=outf(0, q0, q0 + 512), in_=og[t % 2][:])

        for i in range(NB_ + LA):
            if i < NB_:
                front(i)
            if i - LA >= 0:
                back(i - LA)


import math as _math

ROPE_THETA = 500000.0


def _rope_tables(SEQ):
    pos = np.arange(SEQ, dtype=np.float32)

    def tab(rot, blk):
        half = rot // 2
        inv = (np.float32(ROPE_THETA) ** (-(np.arange(0, rot, 2, dtype=np.float32)) / np.float32(rot))).astype(np.float32)
        ang = (pos[:, None] * inv[None, :]).astype(np.float32)
        c, s = np.cos(ang).astype(np.float32).T, np.sin(ang).astype(np.float32).T
        C = np.ones((blk, SEQ), np.float32)
        Sn = np.zeros((blk, SEQ), np.float32)
        C[0:half] = c
        C[half:2 * half] = c
        Sn[0:half] = -s
        Sn[half:2 * half] = s
        return C, Sn
    Ca, Sa = tab(8, 32)
    Cc, Sc = tab(16, 64)
    ropeA = np.stack([np.concatenate([Ca, Ca], 0), np.concatenate([Sa, Sa], 0)]).astype(np.float32)
    ropeC = np.stack([Cc, Sc]).astype(np.float32)
    return np.ascontiguousarray(ropeA), np.ascontiguousarray(ropeC)


def _perm_idx(base, n, half):
    idx = np.arange(n)
    d = idx.copy()
    d[0:half] = idx[0:half] + half
    d[half:2 * half] = idx[half:2 * half] - half
    return base + d


def prep_mixer_inputs(inp, l, h, SEQ, ropes):
    w_in = inp["w_in"][l]
    c = lambda off: off + h * 64 + np.arange(64)
    aq, ak, av = c(0), c(256), c(512)
    bu = c(768)
    cq, ck, cv = c(1280), c(1536), c(1792)
    dq, dk, dv, do_ = c(2048), c(2304), c(2560), c(2816)
    aqp = np.concatenate([_perm_idx(aq[0], 32, 4), _perm_idx(aq[32], 32, 4)])
    akp = np.concatenate([_perm_idx(ak[0], 32, 4), _perm_idx(ak[32], 32, 4)])
    cqp = _perm_idx(cq[0], 64, 8)
    ckp = _perm_idx(ck[0], 64, 8)
    di, df = 3072 + h, 3076 + h
    g6 = np.concatenate([[di, df], np.full(126, di)])
    fm_cols = np.concatenate([aq, ak, aqp, akp, cq, ck, cqp, ckp, dq, dk, g6])
    bv_all = 1024 + np.concatenate([h * 64 + np.arange(64)] + [g * 64 + np.arange(64) for g in range(4) if g != h])
    tm_cols = np.concatenate([av, cv, dv, do_, bu, bv_all, [di, df, di, df]])
    assert fm_cols.size == 768 and tm_cols.size == NTM
    lam_init = 0.8 - 0.6 * _math.exp(-0.3 * l)
    pvec = np.zeros((128, 16), np.float32)
    chan = np.concatenate([h * 64 + np.arange(64), 256 + h * 64 + np.arange(64)])
    pvec[:, 0:4] = inp["mlstm_conv_w"][l][:, chan].T
    pvec[:, 4] = inp["mlstm_conv_b"][l][chan]
    pvec[:, 5] = np.tile(inp["diff_subln_g"][l], 2)
    pvec[:, 6] = 1.0 - lam_init
    pvec[:, 7] = lam_init
    pvec[:, 8] = inp["mlstm_gate_b"][l][0, h]
    pvec[:, 9] = inp["mlstm_gate_b"][l][1, h]
    pvec[:, 10] = inp["sgu_b"][l][h]
    bvec = np.zeros(704, np.float32)
    bvec[0:64] = inp["sgu_ln_g"][l][h * 64:(h + 1) * 64]
    bvec[64:128] = inp["sgu_ln_b"][l][h * 64:(h + 1) * 64]
    bvec[128:192] = inp["mlstm_norm_g"][l]
    bvec[192:320] = inp["diff_lambda"][l].reshape(-1)
    return {
        "w_fm": np.ascontiguousarray(w_in[:, fm_cols]),
        "w_tm": np.ascontiguousarray(w_in[:, tm_cols]),
        "ropeA": ropes[0], "ropeC": ropes[1],
        "pvec": pvec, "bvec": bvec,
        "sgu_wT": np.ascontiguousarray(inp["sgu_w"][l][h].T),
    }


def _mixer_C(S, nc, SEQ, sc_qc, sc_kc, sc_tm, identb, mask_le_b, mask_ge_b, outf):
    pats = (1, 4, 16)
    SB = 2048 if SEQ >= 2048 else SEQ
    NSB = SEQ // SB
    NBLK = SEQ // 128
    with ExitStack() as st:
        qT = _sb(st, nc, "C_qT", [64, SEQ], BF16)
        kT = _sb(st, nc, "C_kT", [64, SEQ], BF16)
        vd = [_sb(st, nc, "C_vd%d" % i, [128, NBLK, 128], BF16) for i in range(3)]
        acc = [_sb(st, nc, "C_acc%d" % i, [128, SB], F32) for i in range(2)]
        E = [_sb(st, nc, "C_E%d" % i, [128, 2, 128], BF16) for i in range(3)]
        rd = _sb(st, nc, "C_rd", [64, SB], F32)
        og = _sb(st, nc, "C_og", [64, SB], BF16)
        pss = [_ps(st, nc, "C_pss%d" % i, [128, 2, 128]) for i in range(3)]
        pso = [_ps(st, nc, "C_pso%d" % i, [128, 128]) for i in range(3)]
        for i in range(3):
            S.set_gran(vd[i], 128)
        S.sp.dma_start(out=qT[:], in_=sc_qc)
        S.sp.dma_start(out=kT[:], in_=sc_kc)
        for pi, dil in enumerate(pats):
            S.pool.memset(vd[pi][:, :, 64:128], 1.0)
            nb = SEQ // (128 * dil)
            src = sc_tm[:, 64:128].rearrange("(n j r) d -> j n r d", j=128, r=dil)
            dst = vd[pi][:, :, 0:64].rearrange("j (n r) d -> j n r d", r=dil)
            for n in range(nb):
                S.pool.dma_start(out=dst[:, n], in_=src[:, n])
        blocks = []
        for sb in range(NSB):
            first = True
            for pi, dil in enumerate(pats):
                span = 128 * dil
                for n in range(sb * SB // span, (sb + 1) * SB // span):
                    for r in range(dil):
                        blocks.append([sb, pi, dil, n, r, first, False])
                        first = False
            blocks[-1][6] = True

        def c0(i):
            sb, pi, dil, n, r, first, lastb = blocks[i]
            span = 128 * dil
            e = i % 3
            if first:
                S.pool.memset(acc[sb % 2][:], 0.0)
            qs = slice(n * span + r, (n + 1) * span, dil)
            if n >= 1:
                kprev = slice((n - 1) * span + r, n * span, dil)
                S.pe.matmul(pss[e][:, 0, :], kT[:, kprev], qT[:, qs], start=True, stop=False)
                S.pe.matmul(pss[e][:, 0, :], identb[:], mask_ge_b[:], start=False, stop=True)
            S.pe.matmul(pss[e][:, 1, :], kT[:, qs], qT[:, qs], start=True, stop=False)
            S.pe.matmul(pss[e][:, 1, :], identb[:], mask_le_b[:], start=False, stop=True)
            lo = 0 if n >= 1 else 1
            S.act.activation(E[e][:, lo:2, :], pss[e][:, lo:2, :], AF.Exp, scale=0.125)

        def c1(i):
            sb, pi, dil, n, r, first, lastb = blocks[i]
            span = 128 * dil
            e = i % 3
            a = acc[sb % 2]
            kbs = ([(0, (n - 1) * dil + r)] if n >= 1 else []) + [(1, n * dil + r)]
            for ii, (slot, blk) in enumerate(kbs):
                S.pe.matmul(pso[e][:], vd[pi][:, blk, :], E[e][:, slot, :], start=(ii == 0), stop=(ii == len(kbs) - 1))
            loc = slice(n * span + r - sb * SB, (n + 1) * span - sb * SB, dil)
            S.dve.tensor_tensor(a[:, loc], a[:, loc], pso[e][:], ALU.add)
            if lastb:
                S.act.activation(rd[:], a[64:128, :], AF.Ln)
                S.act.activation(rd[:], rd[:], AF.Exp, scale=-1.0)
                S.dve.tensor_tensor(og[:], a[0:64, :], rd[:], ALU.mult)
                PW = min(SB, 512)
                for pc_ in range(SB // PW):
                    S.sp.dma_start(out=outf(128, sb * SB + pc_ * PW, sb * SB + (pc_ + 1) * PW), in_=og[:, pc_ * PW:(pc_ + 1) * PW])

        _pipeline(len(blocks), [c0, c1], lag=2)


import math as _math

ROPE_THETA = 500000.0


def _rope_tables(SEQ):
    pos = np.arange(SEQ, dtype=np.float32)

    def tab(rot, blk):
        half = rot // 2
        inv = (np.float32(ROPE_THETA) ** (-(np.arange(0, rot, 2, dtype=np.float32)) / np.float32(rot))).astype(np.float32)
        ang = (pos[:, None] * inv[None, :]).astype(np.float32)
        c, s = np.cos(ang).astype(np.float32).T, np.sin(ang).astype(np.float32).T
        C = np.ones((blk, SEQ), np.float32)
        Sn = np.zeros((blk, SEQ), np.float32)
        C[0:half] = c
        C[half:2 * half] = c
        Sn[0:half] = -s
        Sn[half:2 * half] = s
        return C, Sn
    Ca, Sa = tab(8, 32)
    Cc, Sc = tab(16, 64)
    ropeA = np.stack([np.concatenate([Ca, Ca], 0), np.concatenate([Sa, Sa], 0)]).astype(np.float32)
    ropeC = np.stack([Cc, Sc]).astype(np.float32)
    return np.ascontiguousarray(ropeA), np.ascontiguousarray(ropeC)


def _perm_idx(base, n, half):
    idx = np.arange(n)
    d = idx.copy()
    d[0:half] = idx[0:half] + half
    d[half:2 * half] = idx[half:2 * half] - half
    return base + d


def prep_mixer_inputs(inp, l, h, SEQ, ropes):
    w_in = inp["w_in"][l]
    c = lambda off: off + h * 64 + np.arange(64)
    aq, ak, av = c(0), c(256), c(512)
    bu = c(768)
    cq, ck, cv = c(1280), c(1536), c(1792)
    dq, dk, dv, do_ = c(2048), c(2304), c(2560), c(2816)
    aqp = np.concatenate([_perm_idx(aq[0], 32, 4), _perm_idx(aq[32], 32, 4)])
    akp = np.concatenate([_perm_idx(ak[0], 32, 4), _perm_idx(ak[32], 32, 4)])
    cqp = _perm_idx(cq[0], 64, 8)
    ckp = _perm_idx(ck[0], 64, 8)
    di, df = 3072 + h, 3076 + h
    g6 = np.concatenate([[di, df], np.full(126, di)])
    fm_cols = np.concatenate([aq, ak, aqp, akp, cq, ck, cqp, ckp, dq, dk, g6])
    bv_all = 1024 + np.concatenate([h * 64 + np.arange(64)] + [g * 64 + np.arange(64) for g in range(4) if g != h])
    tm_cols = np.concatenate([av, cv, dv, do_, bu, bv_all, [di, df, di, df]])
    assert fm_cols.size == 768 and tm_cols.size == NTM
    lam_init = 0.8 - 0.6 * _math.exp(-0.3 * l)
    pvec = np.zeros((128, 16), np.float32)
    chan = np.concatenate([h * 64 + np.arange(64), 256 + h * 64 + np.arange(64)])
    pvec[:, 0:4] = inp["mlstm_conv_w"][l][:, chan].T
    pvec[:, 4] = inp["mlstm_conv_b"][l][chan]
    pvec[:, 5] = np.tile(inp["diff_subln_g"][l], 2)
    pvec[:, 6] = 1.0 - lam_init
    pvec[:, 7] = lam_init
    pvec[:, 8] = inp["mlstm_gate_b"][l][0, h]
    pvec[:, 9] = inp["mlstm_gate_b"][l][1, h]
    pvec[:, 10] = inp["sgu_b"][l][h]
    bvec = np.zeros(704, np.float32)
    bvec[0:64] = inp["sgu_ln_g"][l][h * 64:(h + 1) * 64]
    bvec[64:128] = inp["sgu_ln_b"][l][h * 64:(h + 1) * 64]
    bvec[128:192] = inp["mlstm_norm_g"][l]
    bvec[192:320] = inp["diff_lambda"][l].reshape(-1)
    return {
        "w_fm": np.ascontiguousarray(w_in[:, fm_cols]),
        "w_tm": np.ascontiguousarray(w_in[:, tm_cols]),
        "ropeA": ropes[0], "ropeC": ropes[1],
        "pvec": pvec, "bvec": bvec,
        "sgu_wT": np.ascontiguousarray(inp["sgu_w"][l][h].T),
    }


def _mixer_C(S, nc, SEQ, sc_qc, sc_kc, sc_tm, identb, mask_le_b, mask_ge_b, outf):
    pats = (1, 4, 16)
    SB = 2048 if SEQ >= 2048 else SEQ
    NSB = SEQ // SB
    NBLK = SEQ // 128
    with ExitStack() as st:
        qT = _sb(st, nc, "C_qT", [64, SEQ], BF16)
        kT = _sb(st, nc, "C_kT", [64, SEQ], BF16)
        vd = [_sb(st, nc, "C_vd%d" % i, [128, NBLK, 128], BF16) for i in range(3)]
        acc = [_sb(st, nc, "C_acc%d" % i, [128, SB], F32) for i in range(2)]
        E = [_sb(st, nc, "C_E%d" % i, [128, 2, 128], BF16) for i in range(3)]
        rd = _sb(st, nc, "C_rd", [64, SB], F32)
        og = _sb(st, nc, "C_og", [64, SB], BF16)
        pss = [_ps(st, nc, "C_pss%d" % i, [128, 2, 128]) for i in range(3)]
        pso = [_ps(st, nc, "C_pso%d" % i, [128, 128]) for i in range(3)]
        for i in range(3):
            S.set_gran(vd[i], 128)
        S.sp.dma_start(out=qT[:], in_=sc_qc)
        S.sp.dma_start(out=kT[:], in_=sc_kc)
        for pi, dil in enumerate(pats):
            S.pool.memset(vd[pi][:, :, 64:128], 1.0)
            nb = SEQ // (128 * dil)
            src = sc_tm[:, 64:128].rearrange("(n j r) d -> j n r d", j=128, r=dil)
            dst = vd[pi][:, :, 0:64].rearrange("j (n r) d -> j n r d", r=dil)
            for n in range(nb):
                S.pool.dma_start(out=dst[:, n], in_=src[:, n])
        ei = 0
        for sb in range(NSB):
            a = acc[sb % 2]
            S.pool.memset(a[:], 0.0)
            for pi, dil in enumerate(pats):
                span = 128 * dil
                for n in range(sb * SB // span, (sb + 1) * SB // span):
                    for r in range(dil):
                        e = ei % 3
                        ei += 1
                        qs = slice(n * span + r, (n + 1) * span, dil)
                        kcur = qs
                        kbs = []
                        if n >= 1:
                            kprev = slice((n - 1) * span + r, n * span, dil)
                            S.pe.matmul(pss[e][:, 0, :], kT[:, kprev], qT[:, qs], start=True, stop=False)
                            S.pe.matmul(pss[e][:, 0, :], identb[:], mask_ge_b[:], start=False, stop=True)
                            kbs.append((0, (n - 1) * dil + r))
                        S.pe.matmul(pss[e][:, 1, :], kT[:, kcur], qT[:, qs], start=True, stop=False)
                        S.pe.matmul(pss[e][:, 1, :], identb[:], mask_le_b[:], start=False, stop=True)
                        kbs.append((1, n * dil + r))
                        lo = kbs[0][0]
                        S.act.activation(E[e][:, lo:2, :], pss[e][:, lo:2, :], AF.Exp, scale=0.125)
                        for ii, (slot, blk) in enumerate(kbs):
                            S.pe.matmul(pso[e][:], vd[pi][:, blk, :], E[e][:, slot, :], start=(ii == 0), stop=(ii == len(kbs) - 1))
                        loc = slice(n * span + r - sb * SB, (n + 1) * span - sb * SB, dil)
                        S.dve.tensor_tensor(a[:, loc], a[:, loc], pso[e][:], ALU.add)
            S.dve.reciprocal(rd[:], a[64:128, :])
            S.dve.tensor_tensor(og[:], a[0:64, :], rd[:], ALU.mult)
            PW = min(SB, 512)
            for pc_ in range(SB // PW):
                S.sp.dma_start(out=outf(128, sb * SB + pc_ * PW, sb * SB + (pc_ + 1) * PW), in_=og[:, pc_ * PW:(pc_ + 1) * PW])


def _mixer_D(S, nc, SEQ, sc_qkd, sc_g, sc_tm, pv, bv, ident, identb, mask_le, tri, outf):
    NCH = SEQ // 128
    with ExitStack() as st:
        qTb = _sb(st, nc, "D_qT", [64, SEQ], BF16)
        kTb = _sb(st, nc, "D_kT", [64, SEQ], BF16)
        sm = _sb(st, nc, "D_sm", [128, 8], F32)
        Mfull = _sb(st, nc, "D_Mfull", [NCH, 128], F32)
        OH = _sb(st, nc, "D_OH", [NCH, NCH, 128], F32)
        bc = _sb(st, nc, "D_bc", [128, 3, NCH], F32)
        Xcol = _sb(st, nc, "D_Xcol", [128, NCH], F32)
        rcol = _sb(st, nc, "D_rcol", [128, NCH], F32)
        wcol = _sb(st, nc, "D_wcol", [128, NCH], F32)
        mprev = bc[:, 0, :]
        aexp = bc[:, 2, :]
        S.dve.tensor_scalar_mul(sm[:, 0:1], pv[:, 9:10], -1.0)
        S.pool.memset(OH[:], 1.0)
        S.pool.affine_select(OH[:], OH[:], [[-1, NCH], [0, 128]], ALU.is_equal, 0.0, base=0, channel_multiplier=1)
        with ExitStack() as s1:
            gi = _sb(s1, nc, "D_gi", [NCH, 128], F32)
            gf = _sb(s1, nc, "D_gf", [NCH, 128], F32)
            Bc = _sb(s1, nc, "D_Bc", [NCH, 128], F32)
            rr = _sb(s1, nc, "D_rr", [NCH, 128], F32)
            Ml = _sb(s1, nc, "D_Ml", [NCH, 128], F32)
            Xn = _sb(s1, nc, "D_Xn", [NCH, 128], F32)
            zer = _sb(s1, nc, "D_zer", [NCH, 128], F32)
            col2 = _sb(s1, nc, "D_col2", [NCH, 2], F32)
            rows = _sb(s1, nc, "D_rows", [1, 2, NCH], F32)
            r3 = _sb(s1, nc, "D_r3", [1, 3, NCH], F32)
            mall = _sb(s1, nc, "D_mall", [1, NCH], F32)
            mpc = _sb(s1, nc, "D_mpc", [NCH, 1], F32)
            ones1 = _sb(s1, nc, "D_ones1", [1, 128], F32)
            pA = _ps(s1, nc, "D_pA", [128, 3 * NCH])
            pB = _ps(s1, nc, "D_pB", [128, 128])
            S.pool.memset(zer[:], 0.0)
            S.pool.memset(ones1[:], 1.0)
            S.sp.dma_start(out=gi[:], in_=sc_g[0].rearrange("(n t) -> n t", t=128))
            S.sp.dma_start(out=gf[:], in_=sc_g[1].rearrange("(n t) -> n t", t=128))
            S.act.activation(gf[:], gf[:], AF.Exp, scale=-1.0, bias=sm[0:NCH, 0:1])
            S.act.activation(gf[:], gf[:], AF.Ln, bias=1.0)
            S.dve.tensor_tensor_scan(Bc[:], gf[:], zer[:], 0.0, ALU.add, ALU.max)
            S.dve.scalar_tensor_tensor(rr[:], gi[:], pv[0:NCH, 8:9], Bc[:], ALU.add, ALU.add)
            S.dve.tensor_tensor_scan(Ml[:], rr[:], rr[:], -1e30, ALU.max, ALU.max)
            S.dve.tensor_copy(col2[:, 0:1], Ml[:, 127:128])
            S.dve.tensor_scalar_mul(col2[:, 1:2], Bc[:, 127:128], -1.0)
            S.pe.transpose(pA[0:1, 0:NCH], col2[:, 0:1], ident[0:NCH, 0:NCH])
            S.pe.transpose(pA[0:1, NCH:2 * NCH], col2[:, 1:2], ident[0:NCH, 0:NCH])
            S.dve.tensor_copy(rows[:].rearrange("p a n -> p (a n)"), pA[0:1, 0:2 * NCH])
            S.dve.tensor_tensor_scan(mall[:], rows[:, 0, :], rows[:, 1, :], 0.0, ALU.max, ALU.add)
            S.dve.memset(r3[:, 0, 0:1], 0.0)
            if NCH > 1:
                S.dve.tensor_copy(r3[:, 0, 1:NCH], mall[:, 0:NCH - 1])
            S.dve.tensor_tensor(r3[:, 1, :], r3[:, 0, :], rows[:, 0, :], ALU.max)
            S.dve.tensor_tensor(r3[:, 2, :], r3[:, 0, :], r3[:, 1, :], ALU.subtract)
            S.act.activation(r3[:, 2, :], r3[:, 2, :], AF.Exp)
            S.pe.matmul(pA[:, 0:3 * NCH], ones1[:], r3[:].rearrange("p a n -> p (a n)"), start=True, stop=True)
            S.dve.tensor_copy(bc[:].rearrange("p a n -> p (a n)"), pA[:, 0:3 * NCH])
            S.pe.transpose(pB[0:NCH, 0:1], r3[:, 0, :], ident[0:1, 0:1])
            S.dve.tensor_copy(mpc[:], pB[0:NCH, 0:1])
            S.dve.tensor_scalar(Mfull[:], Ml[:], mpc[:, 0:1], None, ALU.max)
            S.dve.tensor_tensor(Xn[:], Bc[:], Mfull[:], ALU.subtract)
            S.act.activation(Xn[:], Xn[:], AF.Exp)
            S.pe.transpose(pB[:, 0:NCH], Xn[:], ident[0:NCH, 0:NCH])
            S.dve.tensor_copy(Xcol[:], pB[:, 0:NCH])
            S.pe.transpose(pB[:, 0:NCH], rr[:], ident[0:NCH, 0:NCH])
            S.dve.tensor_copy(rcol[:], pB[:, 0:NCH])
            S.dve.tensor_tensor(wcol[:], rcol[:], bc[:, 1, :], ALU.subtract)
            S.act.activation(wcol[:], wcol[:], AF.Exp)
            S.dve.tensor_scalar_mul(wcol[:], wcol[:], 0.125)
        S.barrier()
        with ExitStack() as s2:
            xin = _sb(s2, nc, "D_xin", [128, SEQ + 4], F32)
            cacc = _sb(s2, nc, "D_cacc", [128, SEQ], F32)
            S.pool.memset(xin[:, 0:3], 0.0)
            S.sp.dma_start(out=xin[:, 3:SEQ + 3], in_=sc_qkd)
            S.dve.tensor_scalar(cacc[:], xin[:, 0:SEQ], pv[:, 0:1], None, ALU.mult)
            for j in range(1, 4):
                S.dve.scalar_tensor_tensor(cacc[:], xin[:, j:j + SEQ], pv[:, j:j + 1], cacc[:], ALU.mult, ALU.add)
            S.act.activation(qTb[:], cacc[0:64, :], AF.Silu, bias=pv[0:64, 4:5])
            S.act.activation(kTb[:], cacc[64:128, :], AF.Silu, bias=pv[64:128, 4:5])
        S.barrier()
        with ExitStack() as s3:
            vaug = _sb(s3, nc, "D_vaug", [128, NCH, 66], BF16)
            ktm = _sb(s3, nc, "D_ktm", [128, NCH, 64], BF16)
            kw = _sb(s3, nc, "D_kw", [128, NCH, 64], BF16)
            Cst = _sb(s3, nc, "D_Cst", [64, 66], F32)
            Cb = [_sb(s3, nc, "D_Cb%d" % i, [64, 66], BF16) for i in range(2)]
            tmp = [_sb(s3, nc, "D_tmp%d" % i, [128, 128], F32) for i in range(2)]
            pp = [_sb(s3, nc, "D_pp%d" % i, [128, 128], F32) for i in range(2)]
            swT = [_sb(s3, nc, "D_swT%d" % i, [128, 128], BF16) for i in range(2)]
            ech = [_sb(s3, nc, "D_ech%d" % i, [64, 128], F32) for i in range(2)]
            qe = [_sb(s3, nc, "D_qe%d" % i, [64, 128], BF16) for i in range(2)]
            dsg = [_sb(s3, nc, "D_dsg%d" % i, [128, 4, 64], F32) for i in range(2)]
            hh = [_sb(s3, nc, "D_hh%d" % i, [128, 4, 64], F32) for i in range(2)]
            dd = [_sb(s3, nc, "D_dd%d" % i, [128, 16], F32) for i in range(2)]
            stats = [_sb(s3, nc, "D_st%d" % i, [128, 4, 6], F32) for i in range(2)]
            mv = [_sb(s3, nc, "D_mv%d" % i, [128, 4, 2], F32) for i in range(2)]
            og = [_sb(s3, nc, "D_og%d" % i, [64, 512], BF16) for i in range(2)]
            pkt = [_ps(s3, nc, "D_pkt%d" % i, [128, 64], BF16) for i in range(1)]
            pqk = [_ps(s3, nc, "D_pqk%d" % i, [128, 128]) for i in range(2)]
            ph = [_ps(s3, nc, "D_ph%d" % i, [128, 4, 66]) for i in range(2)]
            pc = _ps(s3, nc, "D_pc", [64, 66])
            pt = _ps(s3, nc, "D_pt", [64, 512])
            pM = _ps(s3, nc, "D_pM", [128, 128])
            S.set_gran(vaug, 66)
            S.set_gran(ktm, 64)
            S.pool.memset(vaug[:, :, 64:66], 1.0)
            S.pool.dma_start(out=vaug[:, :, 0:64], in_=sc_tm[:, 128:192].rearrange("(n s) d -> s n d", s=128))
            S.pool.memset(Cst[:], 0.0)
            for n in range(NCH):
                c = slice(n * 128, (n + 1) * 128)
                S.pe.transpose(pkt[0][:], kTb[:, c], identb[0:64, 0:64])
                S.act.copy(ktm[:, n, :], pkt[0][:])
            wb = wcol[:].unsqueeze(2).to_broadcast([128, NCH, 64])
            S.pool.tensor_tensor(kw[:], ktm[:], wb, ALU.mult)
            def d0(n):
                b = n % 2
                c = slice(n * 128, (n + 1) * 128)
                if n % 4 == 0:
                    g4 = (n // 4) % 2
                    nn = min(4, NCH - n)
                    S.sp.dma_start(out=dsg[g4][:, 0:nn, :],
                                   in_=sc_tm[n * 128:(n + nn) * 128, 192:256].rearrange("(n s) d -> s n d", s=128))
                    S.act.activation(dsg[g4][:, 0:nn, :], dsg[g4][:, 0:nn, :], AF.Exp, scale=-1.0)
                    S.pool.tensor_scalar_add(dsg[g4][:, 0:nn, :], dsg[g4][:, 0:nn, :], 1.0)
                    S.dve.reciprocal(dsg[g4][:, 0:nn, :], dsg[g4][:, 0:nn, :])
                S.pe.matmul(pqk[b][:], kTb[:, c], qTb[:, c], start=True, stop=True)
                S.pe.matmul(pM[:], OH[:, n, :], Mfull[:], start=True, stop=True)
                S.dve.scalar_tensor_tensor(tmp[b][:], mask_le[:], rcol[:, n:n + 1], pM[:], ALU.add, ALU.subtract)
                S.act.activation(pp[b][:], tmp[b][:], AF.Exp)
                S.dve.scalar_tensor_tensor(swT[b][:], pqk[b][:], 0.125, pp[b][:], ALU.mult, ALU.mult)
                if n > 0:
                    S.act.activation(ech[b][:], pM[0:64, :], AF.Exp, scale=-1.0, bias=mprev[0:64, n:n + 1])
                    S.pool.tensor_tensor(qe[b][:], qTb[:, c], ech[b][:], ALU.mult)

            def d1(n):
                b = n % 2
                pg = ph[(n // 4) % 2]
                S.pe.matmul(pg[:, n % 4, 0:65], swT[b][:], vaug[:, n, 0:65], start=True, stop=(n == 0))
                if n > 0:
                    S.pe.matmul(pg[:, n % 4, 0:65], qe[b][:], Cb[(n - 1) % 2][:, 0:65], start=False, stop=True)
                if n < NCH - 1:
                    S.pe.matmul(pc[:, 0:65], kw[:, n, :], vaug[:, n, 0:65], start=True, stop=True)
                    S.dve.scalar_tensor_tensor(Cst[:, 0:65], Cst[:, 0:65], aexp[0:64, n:n + 1], pc[:, 0:65], ALU.mult, ALU.add)
                    S.act.copy(Cb[n % 2][:, 0:65], Cst[:, 0:65])

            def d2(n):
                if n % 4 != 3:
                    return
                g = (n // 4) % 2
                n0 = n - 3
                pg = ph[g]
                den = pg[:, :, 64]
                S.dve.tensor_scalar_mul(dd[g][:, 0:4], den, -1.0)
                S.dve.tensor_tensor(dd[g][:, 0:4], dd[g][:, 0:4], den, ALU.max)
                S.dve.tensor_tensor(dd[g][:, 0:4], dd[g][:, 0:4], Xcol[:, n0:n0 + 4], ALU.max)
                S.dve.reciprocal(dd[g][:, 4:8], dd[g][:, 0:4])
                S.dve.tensor_tensor(hh[g][:], pg[:, :, 0:64], dd[g][:, 4:8].unsqueeze(2).to_broadcast([128, 4, 64]), ALU.mult)
                for k in range(4):
                    S.dve.bn_stats(stats[g][:, k, :], hh[g][:, k, :])
                    S.dve.bn_aggr(mv[g][:, k, :], stats[g][:, k:k + 1, :])
                S.dve.tensor_scalar_add(dd[g][:, 8:12], mv[g][:, :, 1], LN_EPS)
                S.act.activation(dd[g][:, 8:12], dd[g][:, 8:12], AF.Ln)
                S.act.activation(dd[g][:, 12:16], dd[g][:, 8:12], AF.Exp, scale=-0.5)
                S.dve.tensor_tensor(hh[g][:], hh[g][:], mv[g][:, :, 0:1].to_broadcast([128, 4, 64]), ALU.subtract)
                S.dve.tensor_tensor(hh[g][:], hh[g][:], dd[g][:, 12:16].unsqueeze(2).to_broadcast([128, 4, 64]), ALU.mult)
                S.pool.tensor_tensor(hh[g][:], hh[g][:], bv[:, 128:192].unsqueeze(1).to_broadcast([128, 4, 64]), ALU.mult)
                S.pool.tensor_tensor(hh[g][:], hh[g][:], dsg[g][:], ALU.mult)

            def d3(n):
                if n % 4 != 3:
                    return
                g = (n // 4) % 2
                n0 = n - 3
                for k in range(4):
                    S.pe.transpose(pt[:, k * 128:(k + 1) * 128], hh[g][:, k, :], ident[:])
                S.act.copy(og[g][:], pt[:])
                S.sp.dma_start(out=outf(192, n0 * 128, (n + 1) * 128), in_=og[g][:])

            assert NCH % 4 == 0
            _pipeline(NCH, [d0, d1, d2, d3])


I32 = mybir.dt.int32
GROUPS = [[0, 1, 2, 3], [4, 5, 6, 7]]


def build_fused(SEQ, depth=DEPTH, do=("A", "B", "C", "D"), ffn=True, exch=True):
    T = SEQ // 4
    NTILE = SEQ // 128
    nc = bass.Bass("TRN2", target_bir_lowering=False)
    L = depth
    dr = lambda name, shape, dt=F32: nc.dram_tensor(name, shape, dt, kind="ExternalInput").ap()
    x0g = dr("x0g", [4 * D_MODEL * (T // 512), 512])
    xres0 = dr("xres0", [T, D_MODEL])
    w_fm = dr("w_fm", [L, D_MODEL, 768])
    w_tm = dr("w_tm", [L, D_MODEL, NTM])
    ropeA = dr("ropeA", [2, 64, SEQ])
    ropeC = dr("ropeC", [2, 64, SEQ])
    pvec = dr("pvec", [L, 128, 16])
    bvec = dr("bvec", [L, 704])
    sgu_wT = dr("sgu_wT", [L, 128, 128])
    w_out = dr("w_out", [L, D_MODEL, D_MODEL])
    w_gate = dr("w_gate", [L, D_MODEL, D_FF])
    w_up = dr("w_up", [L, D_MODEL, D_FF])
    w_down = dr("w_down", [L, D_FF, D_MODEL])
    lnp = dr("lnp", [L, 4, D_MODEL])
    gidx = dr("gidx", [128, 8], I32)
    y = nc.dram_tensor("y", [T, D_MODEL], F32, kind="ExternalOutput").ap()
    P0 = dict(
        sc_qa=nc.dram_tensor("sc_qa", [64, SEQ], BF16).ap(), sc_ka=nc.dram_tensor("sc_ka", [64, SEQ], BF16).ap(),
        sc_qc=nc.dram_tensor("sc_qc", [64, SEQ], BF16).ap(), sc_kc=nc.dram_tensor("sc_kc", [64, SEQ], BF16).ap(),
        sc_qkd=nc.dram_tensor("sc_qkd", [128, SEQ], F32).ap(), sc_g=nc.dram_tensor("sc_g", [2, SEQ], F32).ap(),
        sc_tm=nc.dram_tensor("sc_tm", [SEQ, NTM], F32).ap())
    NTC = T // 512
    mixb = nc.dram_tensor("mixb", [256, SEQ], BF16).ap()
    mixg = nc.dram_tensor("mixg", [4 * 256, SEQ], BF16).ap()
    yTb = nc.dram_tensor("yTb", [NTC * D_MODEL, 512], BF16).ap()
    xTg = nc.dram_tensor("xTg", [NTC * 4 * D_MODEL, 512], BF16).ap()
    xres_d = nc.dram_tensor("xres_d", [T, D_MODEL], F32).ap()
    S = Sched(nc)
    S.set_gran(mixb, 64 * SEQ)
    S.set_gran(mixg, 256 * SEQ)
    S.set_gran(yTb, D_MODEL * 512)
    S.set_gran(xTg, 4 * D_MODEL * 512)
    with ExitStack() as stc:
        _PFX[0] = ""
        C = _consts(S, nc, stc)
        gi = _sb(stc, nc, "gidx_sb", [128, 8], I32)
        S.sp.dma_start(out=gi[:], in_=gidx)
        mixtab = mixg.rearrange("f (n t) -> (f n) t", t=512)
        for l in range(L):
            _PFX[0] = "L%d_" % l
            xg, xeng = (x0g, S.pool) if l == 0 else (xTg, S.sp)

            def xsrc(tt, xg=xg, xeng=xeng):
                r, tc = divmod(tt, NTC)
                r0 = (tc * 4 + r) * D_MODEL
                return xeng, xg[r0:r0 + D_MODEL, :].rearrange("(c p) t -> p c t", p=128)

            def outf(r0, c0, c1):
                return mixb[r0:r0 + 64, c0:c1]

            def after(m):
                if exch:
                    cw = min(SEQ, 2048)
                    S.collective("AllGather", mixb[64 * m:64 * m + 64, :].rearrange("r (a b) -> (r a) b", b=cw),
                                 mixg[256 * m:256 * m + 256, :].rearrange("r (a b) -> (r a) b", b=cw), GROUPS)
            P = dict(P0)
            P.update(xsrc=xsrc, w_fm=w_fm[l], w_tm=w_tm[l], ropeA=ropeA, ropeC=ropeC, pvec=pvec[l], bvec=bvec[l],
                     sgu_wT=sgu_wT[l], outf=outf, after=after,
                     tile_order=[r * NTC + tc for tc in range(NTC) for r in range(4)])
            _mixer_all(S, nc, SEQ, P, C, do)
            S.barrier()
            last = (l == L - 1)

            def mix_gather(tile, i):
                for c in range(8):
                    m = c // 2
                    S.gather_rows(tile[:, c, :], mixtab, gi[:, c:c + 1], i * 512, dep_ap=mixg[256 * m:256 * m + 256, :])

            def yT(i):
                tc = i // 4
                return yTb[tc * D_MODEL:(tc + 1) * D_MODEL, :].rearrange("(c p) t -> p c t", p=128)[:, :, (i % 4) * 128:(i % 4 + 1) * 128]

            def after_tile(i):
                if i % 4 == 3 and exch:
                    tc = i // 4
                    S.collective("AllGather", yTb[tc * D_MODEL:(tc + 1) * D_MODEL, :],
                                 xTg[tc * 4 * D_MODEL:(tc + 1) * 4 * D_MODEL, :], GROUPS)
            PF = dict(mix_gather=mix_gather, xres=(xres0 if l == 0 else xres_d),
                      w_out=w_out[l], w_gate=w_gate[l], w_up=w_up[l], w_down=w_down[l], lnp=lnp[l],
                      y=(y if last else xres_d), yT=(None if last else yT), after_tile=after_tile)
            if ffn:
                _ffn_all(S, nc, T, PF, C["ident"])
            S.barrier()
        S.finish()
    return nc


_PROGS = {}
BATCH = 2
SEQ_FULL = 8192
NCORES = 8


def _fused_inputs(inp, SEQ, depth):
    T = SEQ // 4
    ropes = _rope_tables(SEQ)
    x = inp["x"]
    hm = [[prep_mixer_inputs(inp, l, h, SEQ, ropes) for l in range(depth)] for h in range(4)]
    w_out_p = inp["w_out"][:depth]
    lnp = np.stack([np.stack([inp["ln1_g"][l], inp["ln1_b"][l], inp["ln2_g"][l], inp["ln2_b"][l]]) for l in range(depth)])
    maps = []
    for core in range(NCORES):
        b, j = divmod(core, 4)
        xb = x[b, :SEQ]
        NTC = T // 512
        x0g = np.ascontiguousarray(xb.reshape(4, NTC, 512, D_MODEL).transpose(1, 0, 3, 2).reshape(NTC * 4 * D_MODEL, 512))
        gidx = ((np.arange(8)[None, :] * 128 + np.arange(128)[:, None]) * (SEQ // 512) + j * (T // 512)).astype(np.int32)
        m = {
            "x0g": x0g, "xres0": np.ascontiguousarray(xb[j * T:(j + 1) * T]),
            "w_fm": np.stack([hm[j][l]["w_fm"] for l in range(depth)]),
            "w_tm": np.stack([hm[j][l]["w_tm"] for l in range(depth)]),
            "ropeA": ropes[0], "ropeC": ropes[1],
            "pvec": np.stack([hm[j][l]["pvec"] for l in range(depth)]),
            "bvec": np.stack([hm[j][l]["bvec"] for l in range(depth)]),
            "sgu_wT": np.stack([hm[j][l]["sgu_wT"] for l in range(depth)]),
            "w_out": w_out_p, "w_gate": inp["w_gate"][:depth], "w_up": inp["w_up"][:depth], "w_down": inp["w_down"][:depth],
            "lnp": lnp.astype(np.float32), "gidx": gidx,
        }
        maps.append({k: np.ascontiguousarray(v) for k, v in m.items()})
    return maps


def run_fused(inp, SEQ, depth):
    key = ("fused", SEQ, depth)
    if key not in _PROGS:
        _PROGS[key] = build_fused(SEQ, depth)
    maps = _fused_inputs(inp, SEQ, depth)
    res = run_bass_kernel_spmd(_PROGS[key], maps, core_ids=list(range(NCORES)))
    T = SEQ // 4
    out = np.empty((BATCH, SEQ, D_MODEL), np.float32)
    for core in range(NCORES):
        b, j = divmod(core, 4)
        out[b, j * T:(j + 1) * T] = res.results[core]["y"]
    return out


def kernel(**inputs):
    inp = {k: np.asarray(v, dtype=np.float32) for k, v in inputs.items()}
    return run_fused(inp, SEQ_FULL, DEPTH)
```

```python
import numpy as np
import concourse.bass as bass
import concourse.mybir as mybir

F32 = mybir.dt.float32
BF16 = mybir.dt.bfloat16
AF = mybir.ActivationFunctionType
ALU = mybir.AluOpType
AX = mybir.AxisListType


class _Eng:
    def __init__(self, S, name, eng):
        self.S, self.name, self.eng = S, name, eng
        self.sem = S.nc.alloc_semaphore("es_" + name)
        self.cnt = 0
        self.seen = {}

    def __getattr__(self, op):
        fn = getattr(self.eng, op)

        def call(*args, **kw):
            return self.S._emit(self, op, fn, args, kw)
        return call


class Sched:
    def __init__(self, nc):
        self.nc = nc
        self.pe = _Eng(self, "pe", nc.tensor)
        self.act = _Eng(self, "act", nc.scalar)
        self.dve = _Eng(self, "dve", nc.vector)
        self.pool = _Eng(self, "pool", nc.gpsimd)
        self.sp = _Eng(self, "sp", nc.sync)
        self.engs = [self.pe, self.act, self.dve, self.pool, self.sp]
        self.units = {}
        self.gran = {}
        self.dma_sems = {}
        self.all_sems = {}
        self.n_inst = 0
        self.free_dma = {}
        self.n_dsem = 0
        self.cc_sem = None
        self.cc_cnt = 0

    def collective(self, kind, in_ap, out_ap, groups):
        E = self.pool
        reads, writes = self._units(in_ap), self._units(out_ap)
        self._deps(E, reads, writes, same_raw=False)
        if self.cc_sem is None:
            self.cc_sem = self.nc.alloc_semaphore("cc_sem")
        inst = self.nc.gpsimd.collective_compute(kind, ALU.bypass, replica_groups=groups, ins=[in_ap], outs=[out_ap])
        self.cc_cnt += 1
        inst.then_inc(self.cc_sem, 1)
        self._record((self.cc_sem, self.cc_cnt), reads, writes)
        return inst

    def gather_rows(self, out_ap, table_ap, idx_ap, element_offset, dep_ap=None):
        E = self.pool
        reads = self._units(dep_ap if dep_ap is not None else table_ap) + self._units(idx_ap)
        writes = self._units(out_ap)
        skey = (writes[0][0], 0)
        ds = self._dma_sem(skey, "sw")
        saved = []
        for key in writes:
            u = self._u(key)
            for k_, t_ in list(u["w"].items()):
                if t_[0] is ds[0]:
                    saved.append((u, k_, t_))
                    del u["w"][k_]
        self._deps(E, reads, writes, same_raw=False)
        inst = self.nc.gpsimd.indirect_dma_start(out=out_ap, out_offset=None, in_=table_ap,
                                                 in_offset=bass.IndirectOffsetOnAxis(ap=idx_ap, axis=0),
                                                 element_offset=element_offset)
        ds[1] += 16
        inst.then_inc(ds[0], 16)
        self._record((ds[0], ds[1]), reads, writes)
        return inst

    def _dma_sem(self, skey, cls="hw"):
        skey = (skey[0], cls)
        ds = self.dma_sems.get(skey)
        if ds is None:
            fl = self.free_dma.setdefault(cls, [])
            if fl:
                ds = fl.pop()
            else:
                self.n_dsem += 1
                ds = [self.nc.alloc_semaphore("ds%s%d" % (cls, self.n_dsem)), 0, cls]
            self.dma_sems[skey] = ds
        return ds

    def set_gran(self, t, g):
        self.gran[t.name if hasattr(t, "name") else t] = g

    def _units(self, ap):
        name = ap.tensor.name
        g = self.gran.get(name)
        if g is None:
            return [(name, 0)]
        apl = ap.ap
        space = str(ap.space)
        off = int(ap.offset)
        if "DRAM" in space:
            lo = off
            hi = off + sum((c - 1) * s for s, c in apl)
        else:
            F = 1
            for d in ap.tensor.shape[1:]:
                F *= d
            lo = off % F
            hi = lo + sum((c - 1) * s for s, c in apl[1:])
        return [(name, i) for i in range(lo // g, hi // g + 1)]

    def _u(self, key):
        u = self.units.get(key)
        if u is None:
            u = self.units[key] = {"w": {}, "r": {}}
        return u

    def _wait(self, E, tok):
        sem, val = tok
        k = id(sem)
        if E.seen.get(k, 0) >= val:
            return
        E.eng.wait_ge(sem, val)
        E.seen[k] = val

    def _deps(self, E, reads, writes, same_raw=True):
        toks = []
        for key in reads:
            u = self._u(key)
            toks += list(u["w"].values())
        for key in writes:
            u = self._u(key)
            toks += list(u["w"].values()) + list(u["r"].values())
        for sem, val in toks:
            if sem is E.sem:
                continue
            self._wait(E, (sem, val))
        if same_raw and E is not self.pe:
            for key in reads:
                u = self._u(key)
                for sem, val in u["w"].values():
                    if sem is E.sem:
                        self._wait(E, (sem, val))

    def _record(self, tok, reads, writes):
        sem, val = tok
        k = id(sem)
        for key in reads:
            self._u(key)["r"][k] = tok
        for key in writes:
            u = self._u(key)
            u["w"] = {k: tok}
            u["r"] = {}
        self.all_sems[k] = tok

    def _emit(self, E, op, fn, args, kw):
        if op in ("dma_start",):
            return self._dma(E, fn, args, kw)
        lazy = kw.pop("lazy", False)
        aps = []
        out = kw.get("out", None)
        outs = []
        first = True
        for a in list(args) + [v for k_, v in kw.items()]:
            if isinstance(a, bass.AP):
                aps.append(a)
        if out is not None:
            outs = [out]
        elif args and isinstance(args[0], bass.AP):
            outs = [args[0]]
        if kw.get("accum_out") is not None:
            outs.append(kw["accum_out"])
        out_ids = [id(o) for o in outs]
        reads, writes = [], []
        for a in aps:
            if id(a) in out_ids:
                writes += self._units(a)
            else:
                reads += self._units(a)
        self._deps(E, reads, writes)
        inst = fn(*args, **kw)
        if lazy and kw.get("stop", True) is False:
            self._record((E.sem, E.cnt + 1), reads, writes)
            self.n_inst += 1
            return inst
        E.cnt += 1
        inst.then_inc(E.sem, 1)
        self._record((E.sem, E.cnt), reads, writes)
        self.n_inst += 1
        return inst

    def _dma(self, E, fn, args, kw):
        out = kw.get("out", args[0] if args else None)
        in_ = kw.get("in_", args[1] if len(args) > 1 else None)
        writes = self._units(out)
        reads = self._units(in_)
        if "DRAM" not in str(out.space):
            skey = (writes[0][0], 0)
        elif "DRAM" not in str(in_.space):
            skey = (reads[0][0], 0)
        else:
            skey = (writes[0][0], 0)
        self._deps(E, reads, writes, same_raw=False)
        ds = self._dma_sem(skey, "sw" if E is self.pool else "hw")
        inst = fn(*args, **kw)
        ds[1] += 16
        inst.then_inc(ds[0], 16)
        self._record((ds[0], ds[1]), reads, writes)
        self.n_inst += 1
        return inst

    def barrier(self, final=False):
        toks = [t for t in self.all_sems.values() if (final or t[0] is not self.cc_sem)]
        for E in self.engs:
            for tok in toks:
                if tok[0] is E.sem:
                    continue
                self._wait(E, tok)
        keep = {}
        for key, u in self.units.items():
            w = {k: t for k, t in u["w"].items() if t[0] is self.cc_sem}
            r = {k: t for k, t in u["r"].items() if t[0] is self.cc_sem}
            if (w or r) and not final:
                keep[key] = {"w": w, "r": r}
        self.units = keep
        for ds in self.dma_sems.values():
            self.free_dma.setdefault(ds[2], []).append(ds)
        self.dma_sems = {}

    def finish(self):
        self.barrier(final=True)


from contextlib import ExitStack
from concourse.bass_utils import run_bass_kernel_spmd

D_MODEL = 1024
D_FF = 2816
DEPTH = 4
ALPHA = (2 * DEPTH) ** 0.25
LN_EPS = 1e-5


_PFX = [""]


def _sb(st, nc, name, shape, dt):
    return st.enter_context(nc.sbuf_tensor(_PFX[0] + name, shape, dt))


def _ps(st, nc, name, shape, dt=F32):
    return st.enter_context(nc.psum_tensor(_PFX[0] + name, shape, dt))


def _pipeline(n, stages, lag=1):
    ns = len(stages)
    for step in range(n + (ns - 1) * lag):
        for si, f in enumerate(stages):
            i = step - si * lag
            if 0 <= i < n:
                f(i)


def _make_ident(S, nc, ident):
    S.pool.memset(ident[:], 1.0)
    S.pool.affine_select(ident[:], ident[:], [[-1, 128]], ALU.is_equal, 0.0, base=0, channel_multiplier=1)


def _layernorm_rows(S, nc, t, width, stats, mv, g_bc, b_bc, out):
    nch = width // 512
    for c in range(nch):
        S.dve.bn_stats(stats[:, c, :], t[:, c * 512:(c + 1) * 512])
    S.dve.bn_aggr(mv[:, 0:2], stats[:, 0:nch, :])
    S.dve.tensor_scalar_add(mv[:, 2:3], mv[:, 1:2], LN_EPS)
    S.act.sqrt(mv[:, 2:3], mv[:, 2:3])
    S.dve.reciprocal(mv[:, 3:4], mv[:, 2:3])
    S.dve.scalar_tensor_tensor(t, t, mv[:, 0:1], g_bc, ALU.subtract, ALU.mult)
    S.dve.scalar_tensor_tensor(out, t, mv[:, 3:4], b_bc, ALU.mult, ALU.add)


def _ffn_all(S, nc, T, P, ident):
    NT = T // 128
    NTT = T // 512
    mix_gather, xres, w_out, w_gate, w_up, w_down, lnp, y, yT = (P[k] for k in (
        "mix_gather", "xres", "w_out", "w_gate", "w_up", "w_down", "lnp", "y", "yT"))
    with ExitStack() as st0:
        lnbc = _sb(st0, nc, "lnbc", [128, 4, D_MODEL], F32)
        yacc = _sb(st0, nc, "yacc", [128, NT, D_MODEL], F32)
        x1T = _sb(st0, nc, "x1T", [128, 8, T], BF16)
        S.set_gran(yacc, D_MODEL)
        S.set_gran(lnbc, D_MODEL)
        for j in range(4):
            S.sp.dma_start(out=lnbc[:, j, :], in_=lnp[j].partition_broadcast(128))
        with ExitStack() as st:
            wout_b = _sb(st, nc, "wout_b", [128, 8, D_MODEL], BF16)
            mixb = [_sb(st, nc, "mixb%d" % i, [128, 8, 512], BF16) for i in range(2)]
            xr = [_sb(st, nc, "xr%d" % i, [128, D_MODEL], F32) for i in range(2)]
            t1 = [_sb(st, nc, "t1_%d" % i, [128, D_MODEL], F32) for i in range(3)]
            stats = [_sb(st, nc, "stats%d" % i, [128, 2, 6], F32) for i in range(2)]
            mv = [_sb(st, nc, "mv%d" % i, [128, 4], F32) for i in range(2)]
            psh = [_ps(st, nc, "psh%d" % i, [128, D_MODEL]) for i in range(2)]
            pst = [_ps(st, nc, "pst%d" % i, [128, D_MODEL]) for i in range(2)]
            S.pool.dma_start(out=wout_b[:], in_=w_out.rearrange("(c p) f -> p c f", p=128))
            def f0(i):
                b = i % 2
                tsl = slice(i * 128, (i + 1) * 128)
                mb = mixb[(i // 4) % 2]
                if i == 0:
                    mix_gather(mixb[0], 0)
                if i % 4 == 0 and i // 4 + 1 < NT // 4:
                    mix_gather(mixb[(i // 4 + 1) % 2], i // 4 + 1)
                if i == 0:
                    S.sp.dma_start(out=xr[0][:], in_=xres[0:128, :])
                if i + 1 < NT:
                    S.sp.dma_start(out=xr[(i + 1) % 2][:], in_=xres[(i + 1) * 128:(i + 2) * 128, :])
                for half in range(2):
                    for k in range(8):
                        S.pe.matmul(psh[b][:, half * 512:(half + 1) * 512], mb[:, k, (i % 4) * 128:(i % 4 + 1) * 128],
                                    wout_b[:, k, half * 512:(half + 1) * 512], start=(k == 0), stop=(k == 7), lazy=True)
                S.dve.scalar_tensor_tensor(t1[i % 3][:], xr[b][:], ALPHA, psh[b][:], ALU.mult, ALU.add)

            def f1(i):
                b = i % 2
                _layernorm_rows(S, nc, t1[i % 3][:], D_MODEL, stats[b], mv[b], lnbc[:, 0, :], lnbc[:, 1, :], t1[i % 3][:])
                S.act.mul(yacc[:, i, :], t1[i % 3][:], ALPHA)

            def f2(i):
                b = i % 2
                tsl = slice(i * 128, (i + 1) * 128)
                for c in range(8):
                    S.pe.transpose(pst[b][:, c * 128:(c + 1) * 128], t1[i % 3][:, c * 128:(c + 1) * 128], ident[:])
                S.act.copy(x1T[:, 0:4, tsl], pst[b][:, 0:512].rearrange("p (c t) -> p c t", c=4))
                S.dve.tensor_copy(x1T[:, 4:8, tsl], pst[b][:, 512:1024].rearrange("p (c t) -> p c t", c=4))

            _pipeline(NT, [f0, f1, f2])
        S.barrier()
        with ExitStack() as st:
            FB = 256
            NFB = D_FF // FB
            wg_b = [_sb(st, nc, "wg_b%d" % i, [128, 8, FB], BF16) for i in range(2)]
            wu_b = [_sb(st, nc, "wu_b%d" % i, [128, 8, FB], BF16) for i in range(2)]
            wd_b = [_sb(st, nc, "wd_b%d" % i, [128, 2, D_MODEL], BF16) for i in range(2)]
            sg = [_sb(st, nc, "sg%d" % i, [128, 512], F32) for i in range(2)]
            actT = [_sb(st, nc, "actT%d" % i, [128, 2, 512], BF16) for i in range(2)]
            psg = [_ps(st, nc, "psg%d" % i, [128, 512]) for i in range(2)]
            psu = [_ps(st, nc, "psu%d" % i, [128, 512]) for i in range(2)]
            psd = [_ps(st, nc, "psd%d" % i, [128, D_MODEL]) for i in range(2)]
            wg_v = w_gate.rearrange("(c p) f -> p c f", p=128)
            wu_v = w_up.rearrange("(c p) f -> p c f", p=128)
            wd_v = w_down.rearrange("(c p) f -> p c f", p=128)
            items = [(fb, tt) for fb in range(NFB) for tt in range(NTT)]

            def load_w(fb):
                wb = fb % 2
                fsl = slice(fb * FB, (fb + 1) * FB)
                S.pool.dma_start(out=wg_b[wb][:], in_=wg_v[:, :, fsl])
                S.pool.dma_start(out=wu_b[wb][:], in_=wu_v[:, :, fsl])
                S.pool.dma_start(out=wd_b[wb][:], in_=wd_v[:, 2 * fb:2 * fb + 2, :])

            def g0(it):
                fb, tt = items[it]
                wb = fb % 2
                ab = it % 2
                if it == 0:
                    load_w(0)
                    if NFB > 1:
                        load_w(1)
                tsl = slice(tt * 512, (tt + 1) * 512)
                for c2 in range(2):
                    for k in range(8):
                        S.pe.matmul(psg[c2][:], wg_b[wb][:, k, c2 * 128:(c2 + 1) * 128], x1T[:, k, tsl],
                                    start=(k == 0), stop=(k == 7), lazy=True)
                    for k in range(8):
                        S.pe.matmul(psu[c2][:], wu_b[wb][:, k, c2 * 128:(c2 + 1) * 128], x1T[:, k, tsl],
                                    start=(k == 0), stop=(k == 7), lazy=True)
                    S.act.activation(sg[c2][:], psg[c2][:], AF.Silu)
                    S.dve.tensor_tensor(actT[ab][:, c2, :], sg[c2][:], psu[c2][:], ALU.mult)

            def g1(it):
                fb, tt = items[it]
                wb = fb % 2
                ab = it % 2
                for s4 in range(4):
                    db = (it * 4 + s4) % 2
                    for half in range(2):
                        for c2 in range(2):
                            S.pe.matmul(psd[db][:, half * 512:(half + 1) * 512],
                                        actT[ab][:, c2, s4 * 128:(s4 + 1) * 128],
                                        wd_b[wb][:, c2, half * 512:(half + 1) * 512],
                                        start=(c2 == 0), stop=(c2 == 1), lazy=True)
                    ti = tt * 4 + s4
                    S.dve.tensor_tensor(yacc[:, ti, :], yacc[:, ti, :], psd[db][:], ALU.add)
                if tt == NTT - 1 and fb + 2 < NFB:
                    load_w(fb + 2)

            _pipeline(len(items), [g0, g1])
        S.barrier()
        with ExitStack() as st:
            stats = [_sb(st, nc, "stats3_%d" % i, [128, 2, 6], F32) for i in range(2)]
            mv = [_sb(st, nc, "mv3_%d" % i, [128, 4], F32) for i in range(2)]
            ob = [_sb(st, nc, "ob%d" % i, [128, D_MODEL], F32) for i in range(2)]
            obT = [_sb(st, nc, "obT%d" % i, [128, 8, 128], BF16) for i in range(2)]
            pst3 = [_ps(st, nc, "pst3_%d" % i, [128, D_MODEL]) for i in range(2)]
            def h0(i):
                b = i % 2
                _layernorm_rows(S, nc, yacc[:, i, :], D_MODEL, stats[b], mv[b], lnbc[:, 2, :], lnbc[:, 3, :], ob[b][:])

            def h1(i):
                b = i % 2
                S.sp.dma_start(out=y[i * 128:(i + 1) * 128, :], in_=ob[b][:])
                if yT is not None:
                    for c in range(8):
                        S.pe.transpose(pst3[b][:, c * 128:(c + 1) * 128], ob[b][:, c * 128:(c + 1) * 128], ident[:])
                    S.act.copy(obT[b][:, 0:4, :], pst3[b][:, 0:512].rearrange("p (c t) -> p c t", c=4))
                    S.dve.tensor_copy(obT[b][:, 4:8, :], pst3[b][:, 512:1024].rearrange("p (c t) -> p c t", c=4))
                    S.sp.dma_start(out=yT(i), in_=obT[b][:])
                    P["after_tile"](i)

            _pipeline(NT, [h0, h1])


HD = 64
NTM = 580
NEG = -30000.0


def _gelu_tanh(S, nc, out, x, tmp):
    S.act.activation(tmp, x, AF.Square)
    S.dve.tensor_scalar(tmp, tmp, 0.044715, 1.0, ALU.mult, ALU.add)
    S.dve.tensor_tensor(tmp, tmp, x, ALU.mult)
    S.act.activation(tmp, tmp, AF.Sigmoid, scale=2.0 * 0.7978845608028654)
    S.dve.tensor_tensor(out, x, tmp, ALU.mult)


def _consts(S, nc, st0):
    C = {}
    C["pv"] = pv = _sb(st0, nc, "pv", [128, 16], F32)
    C["bv"] = bv = _sb(st0, nc, "bv", [128, 704], F32)
    C["ident"] = ident = _sb(st0, nc, "identf", [128, 128], F32)
    C["identb"] = identb = _sb(st0, nc, "identb", [128, 128], BF16)
    C["mask_le"] = mask_le = _sb(st0, nc, "mask_le", [128, 128], F32)
    C["mask_le_b"] = mask_le_b = _sb(st0, nc, "mask_le_b", [128, 128], BF16)
    C["mask_ge_b"] = mask_ge_b = _sb(st0, nc, "mask_ge_b", [128, 128], BF16)
    C["tri"] = tri = _sb(st0, nc, "tri", [128, 128], F32)
    _make_ident(S, nc, ident)
    S.dve.tensor_copy(identb[:], ident[:])
    S.pool.memset(mask_le[:], 0.0)
    S.pool.affine_select(mask_le[:], mask_le[:], [[1, 128]], ALU.is_ge, NEG, base=0, channel_multiplier=-1)
    S.dve.tensor_copy(mask_le_b[:], mask_le[:])
    S.pool.memset(tri[:], 1.0)
    S.pool.affine_select(tri[:], tri[:], [[1, 128]], ALU.is_ge, 0.0, base=0, channel_multiplier=-1)
    S.pool.memset(mask_ge_b[:], 0.0)
    S.pool.affine_select(mask_ge_b[:], mask_ge_b[:], [[-1, 128]], ALU.is_ge, NEG, base=0, channel_multiplier=1)
    return C


def _mixer_all(S, nc, SEQ, P, C, do=("A", "B", "C", "D")):
    NTT = SEQ // 512
    xsrc, w_fm, w_tm, ropeA, ropeC, pvec, bvec, sgu_wT, outf = (P[k] for k in (
        "xsrc", "w_fm", "w_tm", "ropeA", "ropeC", "pvec", "bvec", "sgu_wT", "outf"))
    sc_qa, sc_ka, sc_qc, sc_kc, sc_qkd, sc_g, sc_tm = (P[k] for k in ("sc_qa", "sc_ka", "sc_qc", "sc_kc", "sc_qkd", "sc_g", "sc_tm"))
    pv, bv, ident, identb, mask_le, mask_le_b, mask_ge_b, tri = (C[k] for k in (
        "pv", "bv", "ident", "identb", "mask_le", "mask_le_b", "mask_ge_b", "tri"))
    S.sp.dma_start(out=pv[:], in_=pvec)
    S.sp.dma_start(out=bv[:], in_=bvec.partition_broadcast(128))
    if True:
        with ExitStack() as st:
            wfm_b = _sb(st, nc, "wfm_b", [128, 8, 768], BF16)
            wtm_b = _sb(st, nc, "wtm_b", [128, 8, NTM], BF16)
            xb = [_sb(st, nc, "xb%d" % i, [128, 8, 512], BF16) for i in range(2)]
            rA = [_sb(st, nc, "rA%d" % i, [64, 2, 512], F32) for i in range(2)]
            rC = [_sb(st, nc, "rC%d" % i, [64, 2, 512], F32) for i in range(2)]
            ta = [_sb(st, nc, "ta%d" % i, [64, 512], F32) for i in range(2)]
            tb = [_sb(st, nc, "tb%d" % i, [64, 512], F32) for i in range(2)]
            stg = [_sb(st, nc, "stg%d" % i, [64, 512], BF16) for i in range(4)]
            stg5 = [_sb(st, nc, "stg5_%d" % i, [128, 512], F32) for i in range(2)]
            stg6 = [_sb(st, nc, "stg6_%d" % i, [2, 512], F32) for i in range(2)]
            stgt = [_sb(st, nc, "stgt%d" % i, [128, NTM], F32) for i in range(2)]
            ps1 = [_ps(st, nc, "ps1_%d" % i, [128, 512]) for i in range(2)]
            ps2 = [_ps(st, nc, "ps2_%d" % i, [128, 512]) for i in range(2)]
            ps5 = _ps(st, nc, "ps5", [128, 512])
            pstm = _ps(st, nc, "pstm", [128, 1024])
            S.pool.dma_start(out=wfm_b[:], in_=w_fm.rearrange("(c p) f -> p c f", p=128))
            S.pool.dma_start(out=wtm_b[:], in_=w_tm.rearrange("(c p) f -> p c f", p=128))
            rA_v = ropeA.rearrange("two r t -> r two t")
            rC_v = ropeC.rearrange("two r t -> r two t")
            order = list(P.get("tile_order", range(NTT)))

            def load_tile(j):
                tj = order[j]
                bj = j % 2
                sj = slice(tj * 512, (tj + 1) * 512)
                xe, xap = xsrc(tj)
                xe.dma_start(out=xb[bj][:], in_=xap)
                S.sp.dma_start(out=rA[bj][:], in_=rA_v[:, :, sj])
                S.sp.dma_start(out=rC[bj][:], in_=rC_v[:, :, sj])

            load_tile(0)
            for it_, tt in enumerate(order):
                b = it_ % 2
                tsl = slice(tt * 512, (tt + 1) * 512)
                if it_ + 1 < len(order):
                    load_tile(it_ + 1)
                def do_pair(pair):
                    rt, dq, dk = ((rA[b], sc_qa, sc_ka), (rC[b], sc_qc, sc_kc))[pair]
                    pb = (2 * it_ + pair) % 2
                    g1 = 2 * pair
                    for k in range(8):
                        S.pe.matmul(ps1[pb][:], wfm_b[:, k, g1 * 128:(g1 + 1) * 128], xb[b][:, k, :], start=(k == 0), stop=(k == 7), lazy=True)
                    for k in range(8):
                        S.pe.matmul(ps2[pb][:], wfm_b[:, k, (g1 + 1) * 128:(g1 + 2) * 128], xb[b][:, k, :], start=(k == 0), stop=(k == 7), lazy=True)
                    for half, dst in enumerate((dq, dk)):
                        rows = slice(half * 64, (half + 1) * 64)
                        tbuf = half
                        S.dve.tensor_tensor(ta[tbuf][:], ps2[pb][rows, :], rt[:, 1, :], ALU.mult)
                        S.dve.tensor_tensor(tb[tbuf][:], ps1[pb][rows, :], rt[:, 0, :], ALU.mult)
                        sb_ = (4 * it_ + 2 * pair + half) % 4
                        S.pool.tensor_tensor(stg[sb_][:], ta[tbuf][:], tb[tbuf][:], ALU.add)
                        S.sp.dma_start(out=dst[:, tsl], in_=stg[sb_][:])

                def do_g5():
                    for k in range(8):
                        S.pe.matmul(ps5[:], wfm_b[:, k, 512:640], xb[b][:, k, :], start=(k == 0), stop=(k == 7), lazy=True)
                    S.act.copy(stg5[b][:], ps5[:])
                    S.sp.dma_start(out=sc_qkd[:, tsl], in_=stg5[b][:])

                def do_g6():
                    for k in range(8):
                        S.pe.matmul(ps5[0:2, :], wfm_b[:, k, 640:642], xb[b][:, k, :], start=(k == 0), stop=(k == 7), lazy=True)
                    S.act.copy(stg6[b][:], ps5[0:2, :])
                    S.sp.dma_start(out=sc_g[:, tsl], in_=stg6[b][:])

                def do_sub(sub):
                    tb_ = (4 * it_ + sub) % 2
                    for k in range(8):
                        S.pe.matmul(pstm[:, 0:512], xb[b][:, k, sub * 128:(sub + 1) * 128], wtm_b[:, k, 0:512], start=(k == 0), stop=(k == 7), lazy=True)
                    for k in range(8):
                        S.pe.matmul(pstm[:, 512:NTM], xb[b][:, k, sub * 128:(sub + 1) * 128], wtm_b[:, k, 512:NTM], start=(k == 0), stop=(k == 7), lazy=True)
                    S.act.copy(stgt[tb_][:], pstm[:, 0:NTM])
                    r0 = tt * 512 + sub * 128
                    S.sp.dma_start(out=sc_tm[r0:r0 + 128, :], in_=stgt[tb_][:])

                do_pair(0)
                do_sub(0)
                do_g5()
                do_sub(1)
                do_pair(1)
                do_sub(2)
                do_g6()
                do_sub(3)
        S.barrier()
        if "B" in do:
            _mixer_B(S, nc, SEQ, sc_tm, sgu_wT, pv, bv, ident, tri, outf)
            P["after"](1)
            S.barrier()
        if "A" in do:
            _mixer_A(S, nc, SEQ, sc_qa, sc_ka, sc_tm, pv, bv, outf)
            P["after"](0)
            S.barrier()
        if "C" in do:
            _mixer_C(S, nc, SEQ, sc_qc, sc_kc, sc_tm, identb, mask_le_b, mask_ge_b, outf)
            P["after"](2)
            S.barrier()
        if "D" in do:
            _mixer_D(S, nc, SEQ, sc_qkd, sc_g, sc_tm, pv, bv, ident, identb, mask_le, tri, outf)
            P["after"](3)


def _mixer_B(S, nc, SEQ, sc_tm, sgu_wT, pv, bv, ident, tri, outf):
    NIT = SEQ // 512
    with ExitStack() as st:
        wT = _sb(st, nc, "sg_wT", [128, 128], F32)
        wTb = _sb(st, nc, "sg_wTb", [128, 128], BF16)
        uv = [_sb(st, nc, "sg_uv%d" % i, [128, 4, 320], F32) for i in range(2)]
        gl = [_sb(st, nc, "sg_gl%d" % i, [128, 4, 320], F32) for i in range(2)]
        tmp = [_sb(st, nc, "sg_tmp%d" % i, [128, 4, 320], F32) for i in range(2)]
        stats = [_sb(st, nc, "sg_st%d" % i, [128, 4, 6], F32) for i in range(2)]
        mv = [_sb(st, nc, "sg_mv%d" % i, [128, 4, 2], F32) for i in range(2)]
        rs = [_sb(st, nc, "sg_rs%d" % i, [128, 4], F32) for i in range(2)]
        vn = [_sb(st, nc, "sg_vn%d" % i, [128, 4, 64], F32) for i in range(2)]
        vnb = [_sb(st, nc, "sg_vnb%d" % i, [128, 4, 64], BF16) for i in range(2)]
        ob = [_sb(st, nc, "sg_ob%d" % i, [128, 4, 64], F32) for i in range(2)]
        og = [_sb(st, nc, "sg_og%d" % i, [64, 512], BF16) for i in range(2)]
        psz = [_ps(st, nc, "sg_psz%d" % i, [128, 4, 64]) for i in range(2)]
        pst = [_ps(st, nc, "sg_pst%d" % i, [64, 512]) for i in range(2)]
        S.sp.dma_start(out=wT[:], in_=sgu_wT)
        S.dve.tensor_tensor(wTb[:], wT[:], tri[:], ALU.mult)
        gbc = bv[:, 0:64].unsqueeze(1).to_broadcast([128, 4, 64])
        bbc = bv[:, 64:128].unsqueeze(1).to_broadcast([128, 4, 64])

        def load_uv(j):
            S.sp.dma_start(out=uv[j % 2][:], in_=sc_tm[j * 512:(j + 1) * 512, 256:576].rearrange("(n p) c -> p n c", p=128))

        def b0(it):
            b = it % 2
            if it == 0:
                load_uv(0)
            if it + 1 < NIT:
                load_uv(it + 1)
            _gelu_tanh(S, nc, gl[b][:], uv[b][:], tmp[b][:])
            for k in range(4):
                S.dve.bn_stats(stats[b][:, k, :], gl[b][:, k, 64:320])
                S.dve.bn_aggr(mv[b][:, k, :], stats[b][:, k:k + 1, :])
            S.dve.tensor_scalar_add(rs[b][:], mv[b][:, :, 1], LN_EPS)
            S.act.sqrt(rs[b][:], rs[b][:])
            S.dve.reciprocal(rs[b][:], rs[b][:])
            S.dve.tensor_tensor(vn[b][:], gl[b][:, :, 64:128], mv[b][:, :, 0:1].to_broadcast([128, 4, 64]), ALU.subtract)
            S.dve.tensor_tensor(vn[b][:], vn[b][:], rs[b][:].unsqueeze(2).to_broadcast([128, 4, 64]), ALU.mult)
            S.pool.tensor_tensor(vn[b][:], vn[b][:], gbc, ALU.mult)
            S.pool.tensor_tensor(vnb[b][:], vn[b][:], bbc, ALU.add)

        def b1(it):
            b = it % 2
            for k in range(4):
                S.pe.matmul(psz[b][:, k, :], wTb[:], vnb[b][:, k, :], start=True, stop=True)
            S.dve.scalar_tensor_tensor(ob[b][:], psz[b][:], pv[:, 10:11], gl[b][:, :, 0:64], ALU.add, ALU.mult)

        def b2(it):
            b = it % 2
            for k in range(4):
                S.pe.transpose(pst[b][:, k * 128:(k + 1) * 128], ob[b][:, k, :], ident[:])
            S.act.copy(og[b][:], pst[b][:])
            S.sp.dma_start(out=outf(64, it * 512, (it + 1) * 512), in_=og[b][:])

        _pipeline(NIT, [b0, b1, b2])


def _mixer_A(S, nc, SEQ, sc_qa, sc_ka, sc_tm, pv, bv, outf):
    NTT = SEQ // 512
    NKB = SEQ // 128
    scale = 32 ** -0.5
    with ExitStack() as st:
        qT = _sb(st, nc, "A_qT", [64, SEQ], BF16)
        kT = _sb(st, nc, "A_kT", [64, SEQ], BF16)
        va = _sb(st, nc, "A_va", [128, NKB, 128], BF16)
        lam = _sb(st, nc, "A_lam", [64, 8], F32)
        lt = _sb(st, nc, "A_lt", [64, 64], F32)
        ones_ms = _sb(st, nc, "A_ones", [64, 64], F32)
        E = [_sb(st, nc, "A_E%d" % i, [128, 512], BF16) for i in range(6)]
        rd = [_sb(st, nc, "A_rd%d" % i, [64, 512], F32) for i in range(2)]
        o0 = _sb(st, nc, "A_o0", [64, 512], F32)
        o1 = _sb(st, nc, "A_o1", [64, 512], F32)
        sq = _sb(st, nc, "A_sq", [64, 512], F32)
        og = [_sb(st, nc, "A_og%d" % i, [64, 512], BF16) for i in range(2)]
        pss = [_ps(st, nc, "A_pss%d" % i, [128, 512]) for i in range(6)]
        pso = [_ps(st, nc, "A_pso%d" % i, [128, 512]) for i in range(2)]
        psm = pss[0]
        S.set_gran(va, 128)
        S.sp.dma_start(out=qT[:], in_=sc_qa)
        S.sp.dma_start(out=kT[:], in_=sc_ka)
        S.pool.memset(va[:, :, 64:128], 1.0)
        S.pool.dma_start(out=va[:, :, 0:64], in_=sc_tm[:, 0:64].rearrange("(n p) d -> p n d", p=128))
        S.pool.memset(ones_ms[:], 1.0 / 64.0)
        S.dve.tensor_tensor(lt[:, 0:32], bv[0:64, 192:224], bv[0:64, 224:256], ALU.mult)
        S.dve.tensor_tensor(lt[:, 32:64], bv[0:64, 256:288], bv[0:64, 288:320], ALU.mult)
        S.dve.tensor_reduce(lam[:, 0:1], lt[:, 0:32], AX.X, ALU.add)
        S.dve.tensor_reduce(lam[:, 1:2], lt[:, 32:64], AX.X, ALU.add)
        S.act.activation(lam[:, 2:4], lam[:, 0:2], AF.Exp)
        S.dve.tensor_tensor(lam[:, 4:5], lam[:, 2:3], lam[:, 3:4], ALU.subtract)
        S.dve.tensor_tensor(lam[:, 4:5], lam[:, 4:5], pv[0:64, 7:8], ALU.add)
        S.dve.tensor_scalar_mul(lam[:, 5:6], lam[:, 4:5], -1.0)
        S.dve.tensor_tensor(lam[:, 6:7], pv[0:64, 5:6], pv[0:64, 6:7], ALU.mult)
        blocks = []
        for t in range(NTT):
            nkb = 4 * (t + 1)
            for kb in range(nkb):
                blocks.append((t, kb, nkb))
        LA = 2
        NB_ = len(blocks)

        def front(i):
            t, kb, nkb = blocks[i]
            q0 = t * 512
            j = kb - 4 * t
            c0 = max(j, 0) * 128
            for m in range(2):
                rows = slice(32 * m, 32 * m + 32)
                e = (2 * i + m) % 6
                S.pe.matmul(pss[e][:, c0:512], kT[rows, kb * 128:(kb + 1) * 128], qT[rows, q0 + c0:q0 + 512],
                            start=True, stop=True)
            for m in range(2):
                e = (2 * i + m) % 6
                S.act.activation(E[e][:, c0:512], pss[e][:, c0:512], AF.Exp, scale=scale)
                if j >= 0:
                    S.pool.affine_select(E[e][:, c0:c0 + 128], E[e][:, c0:c0 + 128], [[1, 128]], ALU.is_ge, 0.0,
                                         base=0, channel_multiplier=-1)

        def back(i):
            t, kb, nkb = blocks[i]
            j = kb - 4 * t
            c0 = max(j, 0) * 128
            for m in range(2):
                e = (2 * i + m) % 6
                po = pso[m]
                S.pe.matmul(po[:, c0:512], va[:, kb, :], E[e][:, c0:512], start=(kb == 0), stop=(kb == nkb - 1))
            if kb == nkb - 1:
                epilogue(t)

        def epilogue(t):
            q0 = t * 512
            p0 = pso[0]
            p1 = pso[1]
            S.act.activation(rd[0][:], p0[64:128, :], AF.Ln)
            S.act.activation(rd[0][:], rd[0][:], AF.Exp, scale=-1.0)
            S.dve.tensor_tensor(o0[:], p0[0:64, :], rd[0][:], ALU.mult)
            S.act.activation(rd[1][:], p1[64:128, :], AF.Ln)
            S.act.activation(rd[1][:], rd[1][:], AF.Exp, scale=-1.0)
            S.dve.tensor_tensor(o1[:], p1[0:64, :], rd[1][:], ALU.mult)
            S.dve.scalar_tensor_tensor(o0[:], o1[:], lam[:, 5:6], o0[:], ALU.mult, ALU.add)
            S.pool.tensor_tensor(sq[:], o0[:], o0[:], ALU.mult)
            S.pe.matmul(psm[0:64, :], ones_ms[:], sq[:], start=True, stop=True)
            S.dve.tensor_scalar_add(sq[:], psm[0:64, :], LN_EPS)
            S.act.activation(sq[:], sq[:], AF.Ln)
            S.act.activation(sq[:], sq[:], AF.Exp, scale=-0.5)
            S.pool.tensor_tensor(o1[:], o0[:], sq[:], ALU.mult)
            S.dve.tensor_scalar(og[t % 2][:], o1[:], lam[:, 6:7], None, ALU.mult)
            S.sp.dma_start(out=outf(0, q0, q0 + 512), in_=og[t % 2][:])

        for i in range(NB_ + LA):
            if i < NB_:
                front(i)
            if i - LA >= 0:
                back(i - LA)


import math as _math

ROPE_THETA = 500000.0


def _rope_tables(SEQ):
    pos = np.arange(SEQ, dtype=np.float32)

    def tab(rot, blk):
        half = rot // 2
        inv = (np.float32(ROPE_THETA) ** (-(np.arange(0, rot, 2, dtype=np.float32)) / np.float32(rot))).astype(np.float32)
        ang = (pos[:, None] * inv[None, :]).astype(np.float32)
        c, s = np.cos(ang).astype(np.float32).T, np.sin(ang).astype(np.float32).T
        C = np.ones((blk, SEQ), np.float32)
        Sn = np.zeros((blk, SEQ), np.float32)
        C[0:half] = c
        C[half:2 * half] = c
        Sn[0:half] = -s
        Sn[half:2 * half] = s
        return C, Sn
    Ca, Sa = tab(8, 32)
    Cc, Sc = tab(16, 64)
    ropeA = np.stack([np.concatenate([Ca, Ca], 0), np.concatenate([Sa, Sa], 0)]).astype(np.float32)
    ropeC = np.stack([Cc, Sc]).astype(np.float32)
    return np.ascontiguousarray(ropeA), np.ascontiguousarray(ropeC)


def _perm_idx(base, n, half):
    idx = np.arange(n)
    d = idx.copy()
    d[0:half] = idx[0:half] + half
    d[half:2 * half] = idx[half:2 * half] - half
    return base + d


def prep_mixer_inputs(inp, l, h, SEQ, ropes):
    w_in = inp["w_in"][l]
    c = lambda off: off + h * 64 + np.arange(64)
    aq, ak, av = c(0), c(256), c(512)
    bu = c(768)
    cq, ck, cv = c(1280), c(1536), c(1792)
    dq, dk, dv, do_ = c(2048), c(2304), c(2560), c(2816)
    aqp = np.concatenate([_perm_idx(aq[0], 32, 4), _perm_idx(aq[32], 32, 4)])
    akp = np.concatenate([_perm_idx(ak[0], 32, 4), _perm_idx(ak[32], 32, 4)])
    cqp = _perm_idx(cq[0], 64, 8)
    ckp = _perm_idx(ck[0], 64, 8)
    di, df = 3072 + h, 3076 + h
    g6 = np.concatenate([[di, df], np.full(126, di)])
    fm_cols = np.concatenate([aq, ak, aqp, akp, cq, ck, cqp, ckp, dq, dk, g6])
    bv_all = 1024 + np.concatenate([h * 64 + np.arange(64)] + [g * 64 + np.arange(64) for g in range(4) if g != h])
    tm_cols = np.concatenate([av, cv, dv, do_, bu, bv_all, [di, df, di, df]])
    assert fm_cols.size == 768 and tm_cols.size == NTM
    lam_init = 0.8 - 0.6 * _math.exp(-0.3 * l)
    pvec = np.zeros((128, 16), np.float32)
    chan = np.concatenate([h * 64 + np.arange(64), 256 + h * 64 + np.arange(64)])
    pvec[:, 0:4] = inp["mlstm_conv_w"][l][:, chan].T
    pvec[:, 4] = inp["mlstm_conv_b"][l][chan]
    pvec[:, 5] = np.tile(inp["diff_subln_g"][l], 2)
    pvec[:, 6] = 1.0 - lam_init
    pvec[:, 7] = lam_init
    pvec[:, 8] = inp["mlstm_gate_b"][l][0, h]
    pvec[:, 9] = inp["mlstm_gate_b"][l][1, h]
    pvec[:, 10] = inp["sgu_b"][l][h]
    bvec = np.zeros(704, np.float32)
    bvec[0:64] = inp["sgu_ln_g"][l][h * 64:(h + 1) * 64]
    bvec[64:128] = inp["sgu_ln_b"][l][h * 64:(h + 1) * 64]
    bvec[128:192] = inp["mlstm_norm_g"][l]
    bvec[192:320] = inp["diff_lambda"][l].reshape(-1)
    return {
        "w_fm": np.ascontiguousarray(w_in[:, fm_cols]),
        "w_tm": np.ascontiguousarray(w_in[:, tm_cols]),
        "ropeA": ropes[0], "ropeC": ropes[1],
        "pvec": pvec, "bvec": bvec,
        "sgu_wT": np.ascontiguousarray(inp["sgu_w"][l][h].T),
    }


def _mixer_C(S, nc, SEQ, sc_qc, sc_kc, sc_tm, identb, mask_le_b, mask_ge_b, outf):
    pats = (1, 4, 16)
    SB = 2048 if SEQ >= 2048 else SEQ
    NSB = SEQ // SB
    NBLK = SEQ // 128
    with ExitStack() as st:
        qT = _sb(st, nc, "C_qT", [64, SEQ], BF16)
        kT = _sb(st, nc, "C_kT", [64, SEQ], BF16)
        vd = [_sb(st, nc, "C_vd%d" % i, [128, NBLK, 128], BF16) for i in range(3)]
        acc = [_sb(st, nc, "C_acc%d" % i, [128, SB], F32) for i in range(2)]
        E = [_sb(st, nc, "C_E%d" % i, [128, 2, 128], BF16) for i in range(3)]
        rd = _sb(st, nc, "C_rd", [64, SB], F32)
        og = _sb(st, nc, "C_og", [64, SB], BF16)
        pss = [_ps(st, nc, "C_pss%d" % i, [128, 2, 128]) for i in range(3)]
        pso = [_ps(st, nc, "C_pso%d" % i, [128, 128]) for i in range(3)]
        for i in range(3):
            S.set_gran(vd[i], 128)
        S.sp.dma_start(out=qT[:], in_=sc_qc)
        S.sp.dma_start(out=kT[:], in_=sc_kc)
        for pi, dil in enumerate(pats):
            S.pool.memset(vd[pi][:, :, 64:128], 1.0)
            nb = SEQ // (128 * dil)
            src = sc_tm[:, 64:128].rearrange("(n j r) d -> j n r d", j=128, r=dil)
            dst = vd[pi][:, :, 0:64].rearrange("j (n r) d -> j n r d", r=dil)
            for n in range(nb):
                S.pool.dma_start(out=dst[:, n], in_=src[:, n])
        blocks = []
        for sb in range(NSB):
            first = True
            for pi, dil in enumerate(pats):
                span = 128 * dil
                for n in range(sb * SB // span, (sb + 1) * SB // span):
                    for r in range(dil):
                        blocks.append([sb, pi, dil, n, r, first, False])
                        first = False
            blocks[-1][6] = True

        def c0(i):
            sb, pi, dil, n, r, first, lastb = blocks[i]
            span = 128 * dil
            e = i % 3
            if first:
                S.pool.memset(acc[sb % 2][:], 0.0)
            qs = slice(n * span + r, (n + 1) * span, dil)
            if n >= 1:
                kprev = slice((n - 1) * span + r, n * span, dil)
                S.pe.matmul(pss[e][:, 0, :], kT[:, kprev], qT[:, qs], start=True, stop=False)
                S.pe.matmul(pss[e][:, 0, :], identb[:], mask_ge_b[:], start=False, stop=True)
            S.pe.matmul(pss[e][:, 1, :], kT[:, qs], qT[:, qs], start=True, stop=False)
            S.pe.matmul(pss[e][:, 1, :], identb[:], mask_le_b[:], start=False, stop=True)
            lo = 0 if n >= 1 else 1
            S.act.activation(E[e][:, lo:2, :], pss[e][:, lo:2, :], AF.Exp, scale=0.125)

        def c1(i):
            sb, pi, dil, n, r, first, lastb = blocks[i]
            span = 128 * dil
            e = i % 3
            a = acc[sb % 2]
            kbs = ([(0, (n - 1) * dil + r)] if n >= 1 else []) + [(1, n * dil + r)]
            for ii, (slot, blk) in enumerate(kbs):
                S.pe.matmul(pso[e][:], vd[pi][:, blk, :], E[e][:, slot, :], start=(ii == 0), stop=(ii == len(kbs) - 1))
            loc = slice(n * span + r - sb * SB, (n + 1) * span - sb * SB, dil)
            S.dve.tensor_tensor(a[:, loc], a[:, loc], pso[e][:], ALU.add)
            if lastb:
                S.act.activation(rd[:], a[64:128, :], AF.Ln)
                S.act.activation(rd[:], rd[:], AF.Exp, scale=-1.0)
                S.dve.tensor_tensor(og[:], a[0:64, :], rd[:], ALU.mult)
                PW = min(SB, 512)
                for pc_ in range(SB // PW):
                    S.sp.dma_start(out=outf(128, sb * SB + pc_ * PW, sb * SB + (pc_ + 1) * PW), in_=og[:, pc_ * PW:(pc_ + 1) * PW])

        _pipeline(len(blocks), [c0, c1], lag=2)


import math as _math

ROPE_THETA = 500000.0


def _rope_tables(SEQ):
    pos = np.arange(SEQ, dtype=np.float32)

    def tab(rot, blk):
        half = rot // 2
        inv = (np.float32(ROPE_THETA) ** (-(np.arange(0, rot, 2, dtype=np.float32)) / np.float32(rot))).astype(np.float32)
        ang = (pos[:, None] * inv[None, :]).astype(np.float32)
        c, s = np.cos(ang).astype(np.float32).T, np.sin(ang).astype(np.float32).T
        C = np.ones((blk, SEQ), np.float32)
        Sn = np.zeros((blk, SEQ), np.float32)
        C[0:half] = c
        C[half:2 * half] = c
        Sn[0:half] = -s
        Sn[half:2 * half] = s
        return C, Sn
    Ca, Sa = tab(8, 32)
    Cc, Sc = tab(16, 64)
    ropeA = np.stack([np.concatenate([Ca, Ca], 0), np.concatenate([Sa, Sa], 0)]).astype(np.float32)
    ropeC = np.stack([Cc, Sc]).astype(np.float32)
    return np.ascontiguousarray(ropeA), np.ascontiguousarray(ropeC)


def _perm_idx(base, n, half):
    idx = np.arange(n)
    d = idx.copy()
    d[0:half] = idx[0:half] + half
    d[half:2 * half] = idx[half:2 * half] - half
    return base + d


def prep_mixer_inputs(inp, l, h, SEQ, ropes):
    w_in = inp["w_in"][l]
    c = lambda off: off + h * 64 + np.arange(64)
    aq, ak, av = c(0), c(256), c(512)
    bu = c(768)
    cq, ck, cv = c(1280), c(1536), c(1792)
    dq, dk, dv, do_ = c(2048), c(2304), c(2560), c(2816)
    aqp = np.concatenate([_perm_idx(aq[0], 32, 4), _perm_idx(aq[32], 32, 4)])
    akp = np.concatenate([_perm_idx(ak[0], 32, 4), _perm_idx(ak[32], 32, 4)])
    cqp = _perm_idx(cq[0], 64, 8)
    ckp = _perm_idx(ck[0], 64, 8)
    di, df = 3072 + h, 3076 + h
    g6 = np.concatenate([[di, df], np.full(126, di)])
    fm_cols = np.concatenate([aq, ak, aqp, akp, cq, ck, cqp, ckp, dq, dk, g6])
    bv_all = 1024 + np.concatenate([h * 64 + np.arange(64)] + [g * 64 + np.arange(64) for g in range(4) if g != h])
    tm_cols = np.concatenate([av, cv, dv, do_, bu, bv_all, [di, df, di, df]])
    assert fm_cols.size == 768 and tm_cols.size == NTM
    lam_init = 0.8 - 0.6 * _math.exp(-0.3 * l)
    pvec = np.zeros((128, 16), np.float32)
    chan = np.concatenate([h * 64 + np.arange(64), 256 + h * 64 + np.arange(64)])
    pvec[:, 0:4] = inp["mlstm_conv_w"][l][:, chan].T
    pvec[:, 4] = inp["mlstm_conv_b"][l][chan]
    pvec[:, 5] = np.tile(inp["diff_subln_g"][l], 2)
    pvec[:, 6] = 1.0 - lam_init
    pvec[:, 7] = lam_init
    pvec[:, 8] = inp["mlstm_gate_b"][l][0, h]
    pvec[:, 9] = inp["mlstm_gate_b"][l][1, h]
    pvec[:, 10] = inp["sgu_b"][l][h]
    bvec = np.zeros(704, np.float32)
    bvec[0:64] = inp["sgu_ln_g"][l][h * 64:(h + 1) * 64]
    bvec[64:128] = inp["sgu_ln_b"][l][h * 64:(h + 1) * 64]
    bvec[128:192] = inp["mlstm_norm_g"][l]
    bvec[192:320] = inp["diff_lambda"][l].reshape(-1)
    return {
        "w_fm": np.ascontiguousarray(w_in[:, fm_cols]),
        "w_tm": np.ascontiguousarray(w_in[:, tm_cols]),
        "ropeA": ropes[0], "ropeC": ropes[1],
        "pvec": pvec, "bvec": bvec,
        "sgu_wT": np.ascontiguousarray(inp["sgu_w"][l][h].T),
    }


def _mixer_C(S, nc, SEQ, sc_qc, sc_kc, sc_tm, identb, mask_le_b, mask_ge_b, outf):
    pats = (1, 4, 16)
    SB = 2048 if SEQ >= 2048 else SEQ
    NSB = SEQ // SB
    NBLK = SEQ // 128
    with ExitStack() as st:
        qT = _sb(st, nc, "C_qT", [64, SEQ], BF16)
        kT = _sb(st, nc, "C_kT", [64, SEQ], BF16)
        vd = [_sb(st, nc, "C_vd%d" % i, [128, NBLK, 128], BF16) for i in range(3)]
        acc = [_sb(st, nc, "C_acc%d" % i, [128, SB], F32) for i in range(2)]
        E = [_sb(st, nc, "C_E%d" % i, [128, 2, 128], BF16) for i in range(3)]
        rd = _sb(st, nc, "C_rd", [64, SB], F32)
        og = _sb(st, nc, "C_og", [64, SB], BF16)
        pss = [_ps(st, nc, "C_pss%d" % i, [128, 2, 128]) for i in range(3)]
        pso = [_ps(st, nc, "C_pso%d" % i, [128, 128]) for i in range(3)]
        for i in range(3):
            S.set_gran(vd[i], 128)
        S.sp.dma_start(out=qT[:], in_=sc_qc)
        S.sp.dma_start(out=kT[:], in_=sc_kc)
        for pi, dil in enumerate(pats):
            S.pool.memset(vd[pi][:, :, 64:128], 1.0)
            nb = SEQ // (128 * dil)
            src = sc_tm[:, 64:128].rearrange("(n j r) d -> j n r d", j=128, r=dil)
            dst = vd[pi][:, :, 0:64].rearrange("j (n r) d -> j n r d", r=dil)
            for n in range(nb):
                S.pool.dma_start(out=dst[:, n], in_=src[:, n])
        ei = 0
        for sb in range(NSB):
            a = acc[sb % 2]
            S.pool.memset(a[:], 0.0)
            for pi, dil in enumerate(pats):
                span = 128 * dil
                for n in range(sb * SB // span, (sb + 1) * SB // span):
                    for r in range(dil):
                        e = ei % 3
                        ei += 1
                        qs = slice(n * span + r, (n + 1) * span, dil)
                        kcur = qs
                        kbs = []
                        if n >= 1:
                            kprev = slice((n - 1) * span + r, n * span, dil)
                            S.pe.matmul(pss[e][:, 0, :], kT[:, kprev], qT[:, qs], start=True, stop=False)
                            S.pe.matmul(pss[e][:, 0, :], identb[:], mask_ge_b[:], start=False, stop=True)
                            kbs.append((0, (n - 1) * dil + r))
                        S.pe.matmul(pss[e][:, 1, :], kT[:, kcur], qT[:, qs], start=True, stop=False)
                        S.pe.matmul(pss[e][:, 1, :], identb[:], mask_le_b[:], start=False, stop=True)
                        kbs.append((1, n * dil + r))
                        lo = kbs[0][0]
                        S.act.activation(E[e][:, lo:2, :], pss[e][:, lo:2, :], AF.Exp, scale=0.125)
                        for ii, (slot, blk) in enumerate(kbs):
                            S.pe.matmul(pso[e][:], vd[pi][:, blk, :], E[e][:, slot, :], start=(ii == 0), stop=(ii == len(kbs) - 1))
                        loc = slice(n * span + r - sb * SB, (n + 1) * span - sb * SB, dil)
                        S.dve.tensor_tensor(a[:, loc], a[:, loc], pso[e][:], ALU.add)
            S.dve.reciprocal(rd[:], a[64:128, :])
            S.dve.tensor_tensor(og[:], a[0:64, :], rd[:], ALU.mult)
            PW = min(SB, 512)
            for pc_ in range(SB // PW):
                S.sp.dma_start(out=outf(128, sb * SB + pc_ * PW, sb * SB + (pc_ + 1) * PW), in_=og[:, pc_ * PW:(pc_ + 1) * PW])


def _mixer_D(S, nc, SEQ, sc_qkd, sc_g, sc_tm, pv, bv, ident, identb, mask_le, tri, outf):
    NCH = SEQ // 128
    with ExitStack() as st:
        qTb = _sb(st, nc, "D_qT", [64, SEQ], BF16)
        kTb = _sb(st, nc, "D_kT", [64, SEQ], BF16)
        sm = _sb(st, nc, "D_sm", [128, 8], F32)
        Mfull = _sb(st, nc, "D_Mfull", [NCH, 128], F32)
        OH = _sb(st, nc, "D_OH", [NCH, NCH, 128], F32)
        bc = _sb(st, nc, "D_bc", [128, 3, NCH], F32)
        Xcol = _sb(st, nc, "D_Xcol", [128, NCH], F32)
        rcol = _sb(st, nc, "D_rcol", [128, NCH], F32)
        wcol = _sb(st, nc, "D_wcol", [128, NCH], F32)
        mprev = bc[:, 0, :]
        aexp = bc[:, 2, :]
        S.dve.tensor_scalar_mul(sm[:, 0:1], pv[:, 9:10], -1.0)
        S.pool.memset(OH[:], 1.0)
        S.pool.affine_select(OH[:], OH[:], [[-1, NCH], [0, 128]], ALU.is_equal, 0.0, base=0, channel_multiplier=1)
        with ExitStack() as s1:
            gi = _sb(s1, nc, "D_gi", [NCH, 128], F32)
            gf = _sb(s1, nc, "D_gf", [NCH, 128], F32)
            Bc = _sb(s1, nc, "D_Bc", [NCH, 128], F32)
            rr = _sb(s1, nc, "D_rr", [NCH, 128], F32)
            Ml = _sb(s1, nc, "D_Ml", [NCH, 128], F32)
            Xn = _sb(s1, nc, "D_Xn", [NCH, 128], F32)
            zer = _sb(s1, nc, "D_zer", [NCH, 128], F32)
            col2 = _sb(s1, nc, "D_col2", [NCH, 2], F32)
            rows = _sb(s1, nc, "D_rows", [1, 2, NCH], F32)
            r3 = _sb(s1, nc, "D_r3", [1, 3, NCH], F32)
            mall = _sb(s1, nc, "D_mall", [1, NCH], F32)
            mpc = _sb(s1, nc, "D_mpc", [NCH, 1], F32)
            ones1 = _sb(s1, nc, "D_ones1", [1, 128], F32)
            pA = _ps(s1, nc, "D_pA", [128, 3 * NCH])
            pB = _ps(s1, nc, "D_pB", [128, 128])
            S.pool.memset(zer[:], 0.0)
            S.pool.memset(ones1[:], 1.0)
            S.sp.dma_start(out=gi[:], in_=sc_g[0].rearrange("(n t) -> n t", t=128))
            S.sp.dma_start(out=gf[:], in_=sc_g[1].rearrange("(n t) -> n t", t=128))
            S.act.activation(gf[:], gf[:], AF.Exp, scale=-1.0, bias=sm[0:NCH, 0:1])
            S.act.activation(gf[:], gf[:], AF.Ln, bias=1.0)
            S.dve.tensor_tensor_scan(Bc[:], gf[:], zer[:], 0.0, ALU.add, ALU.max)
            S.dve.scalar_tensor_tensor(rr[:], gi[:], pv[0:NCH, 8:9], Bc[:], ALU.add, ALU.add)
            S.dve.tensor_tensor_scan(Ml[:], rr[:], rr[:], -1e30, ALU.max, ALU.max)
            S.dve.tensor_copy(col2[:, 0:1], Ml[:, 127:128])
            S.dve.tensor_scalar_mul(col2[:, 1:2], Bc[:, 127:128], -1.0)
            S.pe.transpose(pA[0:1, 0:NCH], col2[:, 0:1], ident[0:NCH, 0:NCH])
            S.pe.transpose(pA[0:1, NCH:2 * NCH], col2[:, 1:2], ident[0:NCH, 0:NCH])
            S.dve.tensor_copy(rows[:].rearrange("p a n -> p (a n)"), pA[0:1, 0:2 * NCH])
            S.dve.tensor_tensor_scan(mall[:], rows[:, 0, :], rows[:, 1, :], 0.0, ALU.max, ALU.add)
            S.dve.memset(r3[:, 0, 0:1], 0.0)
            if NCH > 1:
                S.dve.tensor_copy(r3[:, 0, 1:NCH], mall[:, 0:NCH - 1])
            S.dve.tensor_tensor(r3[:, 1, :], r3[:, 0, :], rows[:, 0, :], ALU.max)
            S.dve.tensor_tensor(r3[:, 2, :], r3[:, 0, :], r3[:, 1, :], ALU.subtract)
            S.act.activation(r3[:, 2, :], r3[:, 2, :], AF.Exp)
            S.pe.matmul(pA[:, 0:3 * NCH], ones1[:], r3[:].rearrange("p a n -> p (a n)"), start=True, stop=True)
            S.dve.tensor_copy(bc[:].rearrange("p a n -> p (a n)"), pA[:, 0:3 * NCH])
            S.pe.transpose(pB[0:NCH, 0:1], r3[:, 0, :], ident[0:1, 0:1])
            S.dve.tensor_copy(mpc[:], pB[0:NCH, 0:1])
            S.dve.tensor_scalar(Mfull[:], Ml[:], mpc[:, 0:1], None, ALU.max)
            S.dve.tensor_tensor(Xn[:], Bc[:], Mfull[:], ALU.subtract)
            S.act.activation(Xn[:], Xn[:], AF.Exp)
            S.pe.transpose(pB[:, 0:NCH], Xn[:], ident[0:NCH, 0:NCH])
            S.dve.tensor_copy(Xcol[:], pB[:, 0:NCH])
            S.pe.transpose(pB[:, 0:NCH], rr[:], ident[0:NCH, 0:NCH])
            S.dve.tensor_copy(rcol[:], pB[:, 0:NCH])
            S.dve.tensor_tensor(wcol[:], rcol[:], bc[:, 1, :], ALU.subtract)
            S.act.activation(wcol[:], wcol[:], AF.Exp)
            S.dve.tensor_scalar_mul(wcol[:], wcol[:], 0.125)
        S.barrier()
        with ExitStack() as s2:
            xin = _sb(s2, nc, "D_xin", [128, SEQ + 4], F32)
            cacc = _sb(s2, nc, "D_cacc", [128, SEQ], F32)
            S.pool.memset(xin[:, 0:3], 0.0)
            S.sp.dma_start(out=xin[:, 3:SEQ + 3], in_=sc_qkd)
            S.dve.tensor_scalar(cacc[:], xin[:, 0:SEQ], pv[:, 0:1], None, ALU.mult)
            for j in range(1, 4):
                S.dve.scalar_tensor_tensor(cacc[:], xin[:, j:j + SEQ], pv[:, j:j + 1], cacc[:], ALU.mult, ALU.add)
            S.act.activation(qTb[:], cacc[0:64, :], AF.Silu, bias=pv[0:64, 4:5])
            S.act.activation(kTb[:], cacc[64:128, :], AF.Silu, bias=pv[64:128, 4:5])
        S.barrier()
        with ExitStack() as s3:
            vaug = _sb(s3, nc, "D_vaug", [128, NCH, 66], BF16)
            ktm = _sb(s3, nc, "D_ktm", [128, NCH, 64], BF16)
            kw = _sb(s3, nc, "D_kw", [128, NCH, 64], BF16)
            Cst = _sb(s3, nc, "D_Cst", [64, 66], F32)
            Cb = [_sb(s3, nc, "D_Cb%d" % i, [64, 66], BF16) for i in range(2)]
            tmp = [_sb(s3, nc, "D_tmp%d" % i, [128, 128], F32) for i in range(2)]
            pp = [_sb(s3, nc, "D_pp%d" % i, [128, 128], F32) for i in range(2)]
            swT = [_sb(s3, nc, "D_swT%d" % i, [128, 128], BF16) for i in range(2)]
            ech = [_sb(s3, nc, "D_ech%d" % i, [64, 128], F32) for i in range(2)]
            qe = [_sb(s3, nc, "D_qe%d" % i, [64, 128], BF16) for i in range(2)]
            dsg = [_sb(s3, nc, "D_dsg%d" % i, [128, 4, 64], F32) for i in range(2)]
            hh = [_sb(s3, nc, "D_hh%d" % i, [128, 4, 64], F32) for i in range(2)]
            dd = [_sb(s3, nc, "D_dd%d" % i, [128, 16], F32) for i in range(2)]
            stats = [_sb(s3, nc, "D_st%d" % i, [128, 4, 6], F32) for i in range(2)]
            mv = [_sb(s3, nc, "D_mv%d" % i, [128, 4, 2], F32) for i in range(2)]
            og = [_sb(s3, nc, "D_og%d" % i, [64, 512], BF16) for i in range(2)]
            pkt = [_ps(s3, nc, "D_pkt%d" % i, [128, 64], BF16) for i in range(1)]
            pqk = [_ps(s3, nc, "D_pqk%d" % i, [128, 128]) for i in range(2)]
            ph = [_ps(s3, nc, "D_ph%d" % i, [128, 4, 66]) for i in range(2)]
            pc = _ps(s3, nc, "D_pc", [64, 66])
            pt = _ps(s3, nc, "D_pt", [64, 512])
            pM = _ps(s3, nc, "D_pM", [128, 128])
            S.set_gran(vaug, 66)
            S.set_gran(ktm, 64)
            S.pool.memset(vaug[:, :, 64:66], 1.0)
            S.pool.dma_start(out=vaug[:, :, 0:64], in_=sc_tm[:, 128:192].rearrange("(n s) d -> s n d", s=128))
            S.pool.memset(Cst[:], 0.0)
            for n in range(NCH):
                c = slice(n * 128, (n + 1) * 128)
                S.pe.transpose(pkt[0][:], kTb[:, c], identb[0:64, 0:64])
                S.act.copy(ktm[:, n, :], pkt[0][:])
            wb = wcol[:].unsqueeze(2).to_broadcast([128, NCH, 64])
            S.pool.tensor_tensor(kw[:], ktm[:], wb, ALU.mult)
            def d0(n):
                b = n % 2
                c = slice(n * 128, (n + 1) * 128)
                if n % 4 == 0:
                    g4 = (n // 4) % 2
                    nn = min(4, NCH - n)
                    S.sp.dma_start(out=dsg[g4][:, 0:nn, :],
                                   in_=sc_tm[n * 128:(n + nn) * 128, 192:256].rearrange("(n s) d -> s n d", s=128))
                    S.act.activation(dsg[g4][:, 0:nn, :], dsg[g4][:, 0:nn, :], AF.Exp, scale=-1.0)
                    S.pool.tensor_scalar_add(dsg[g4][:, 0:nn, :], dsg[g4][:, 0:nn, :], 1.0)
                    S.dve.reciprocal(dsg[g4][:, 0:nn, :], dsg[g4][:, 0:nn, :])
                S.pe.matmul(pqk[b][:], kTb[:, c], qTb[:, c], start=True, stop=True)
                S.pe.matmul(pM[:], OH[:, n, :], Mfull[:], start=True, stop=True)
                S.dve.scalar_tensor_tensor(tmp[b][:], mask_le[:], rcol[:, n:n + 1], pM[:], ALU.add, ALU.subtract)
                S.act.activation(pp[b][:], tmp[b][:], AF.Exp)
                S.dve.scalar_tensor_tensor(swT[b][:], pqk[b][:], 0.125, pp[b][:], ALU.mult, ALU.mult)
                if n > 0:
                    S.act.activation(ech[b][:], pM[0:64, :], AF.Exp, scale=-1.0, bias=mprev[0:64, n:n + 1])
                    S.pool.tensor_tensor(qe[b][:], qTb[:, c], ech[b][:], ALU.mult)

            def d1(n):
                b = n % 2
                pg = ph[(n // 4) % 2]
                S.pe.matmul(pg[:, n % 4, 0:65], swT[b][:], vaug[:, n, 0:65], start=True, stop=(n == 0))
                if n > 0:
                    S.pe.matmul(pg[:, n % 4, 0:65], qe[b][:], Cb[(n - 1) % 2][:, 0:65], start=False, stop=True)
                if n < NCH - 1:
                    S.pe.matmul(pc[:, 0:65], kw[:, n, :], vaug[:, n, 0:65], start=True, stop=True)
                    S.dve.scalar_tensor_tensor(Cst[:, 0:65], Cst[:, 0:65], aexp[0:64, n:n + 1], pc[:, 0:65], ALU.mult, ALU.add)
                    S.act.copy(Cb[n % 2][:, 0:65], Cst[:, 0:65])

            def d2(n):
                if n % 4 != 3:
                    return
                g = (n // 4) % 2
                n0 = n - 3
                pg = ph[g]
                den = pg[:, :, 64]
                S.dve.tensor_scalar_mul(dd[g][:, 0:4], den, -1.0)
                S.dve.tensor_tensor(dd[g][:, 0:4], dd[g][:, 0:4], den, ALU.max)
                S.dve.tensor_tensor(dd[g][:, 0:4], dd[g][:, 0:4], Xcol[:, n0:n0 + 4], ALU.max)
                S.dve.reciprocal(dd[g][:, 4:8], dd[g][:, 0:4])
                S.dve.tensor_tensor(hh[g][:], pg[:, :, 0:64], dd[g][:, 4:8].unsqueeze(2).to_broadcast([128, 4, 64]), ALU.mult)
                for k in range(4):
                    S.dve.bn_stats(stats[g][:, k, :], hh[g][:, k, :])
                    S.dve.bn_aggr(mv[g][:, k, :], stats[g][:, k:k + 1, :])
                S.dve.tensor_scalar_add(dd[g][:, 8:12], mv[g][:, :, 1], LN_EPS)
                S.act.activation(dd[g][:, 8:12], dd[g][:, 8:12], AF.Ln)
                S.act.activation(dd[g][:, 12:16], dd[g][:, 8:12], AF.Exp, scale=-0.5)
                S.dve.tensor_tensor(hh[g][:], hh[g][:], mv[g][:, :, 0:1].to_broadcast([128, 4, 64]), ALU.subtract)
                S.dve.tensor_tensor(hh[g][:], hh[g][:], dd[g][:, 12:16].unsqueeze(2).to_broadcast([128, 4, 64]), ALU.mult)
                S.pool.tensor_tensor(hh[g][:], hh[g][:], bv[:, 128:192].unsqueeze(1).to_broadcast([128, 4, 64]), ALU.mult)
                S.pool.tensor_tensor(hh[g][:], hh[g][:], dsg[g][:], ALU.mult)

            def d3(n):
                if n % 4 != 3:
                    return
                g = (n // 4) % 2
                n0 = n - 3
                for k in range(4):
                    S.pe.transpose(pt[:, k * 128:(k + 1) * 128], hh[g][:, k, :], ident[:])
                S.act.copy(og[g][:], pt[:])
                S.sp.dma_start(out=outf(192, n0 * 128, (n + 1) * 128), in_=og[g][:])

            assert NCH % 4 == 0
            _pipeline(NCH, [d0, d1, d2, d3])


I32 = mybir.dt.int32
GROUPS = [[0, 1, 2, 3], [4, 5, 6, 7]]


def build_fused(SEQ, depth=DEPTH, do=("A", "B", "C", "D"), ffn=True, exch=True):
    T = SEQ // 4
    NTILE = SEQ // 128
    nc = bass.Bass("TRN2", target_bir_lowering=False)
    L = depth
    dr = lambda name, shape, dt=F32: nc.dram_tensor(name, shape, dt, kind="ExternalInput").ap()
    x0g = dr("x0g", [4 * D_MODEL * (T // 512), 512])
    xres0 = dr("xres0", [T, D_MODEL])
    w_fm = dr("w_fm", [L, D_MODEL, 768])
    w_tm = dr("w_tm", [L, D_MODEL, NTM])
    ropeA = dr("ropeA", [2, 64, SEQ])
    ropeC = dr("ropeC", [2, 64, SEQ])
    pvec = dr("pvec", [L, 128, 16])
    bvec = dr("bvec", [L, 704])
    sgu_wT = dr("sgu_wT", [L, 128, 128])
    w_out = dr("w_out", [L, D_MODEL, D_MODEL])
    w_gate = dr("w_gate", [L, D_MODEL, D_FF])
    w_up = dr("w_up", [L, D_MODEL, D_FF])
    w_down = dr("w_down", [L, D_FF, D_MODEL])
    lnp = dr("lnp", [L, 4, D_MODEL])
    gidx = dr("gidx", [128, 8], I32)
    y = nc.dram_tensor("y", [T, D_MODEL], F32, kind="ExternalOutput").ap()
    P0 = dict(
        sc_qa=nc.dram_tensor("sc_qa", [64, SEQ], BF16).ap(), sc_ka=nc.dram_tensor("sc_ka", [64, SEQ], BF16).ap(),
        sc_qc=nc.dram_tensor("sc_qc", [64, SEQ], BF16).ap(), sc_kc=nc.dram_tensor("sc_kc", [64, SEQ], BF16).ap(),
        sc_qkd=nc.dram_tensor("sc_qkd", [128, SEQ], F32).ap(), sc_g=nc.dram_tensor("sc_g", [2, SEQ], F32).ap(),
        sc_tm=nc.dram_tensor("sc_tm", [SEQ, NTM], F32).ap())
    NTC = T // 512
    mixb = nc.dram_tensor("mixb", [256, SEQ], BF16).ap()
    mixg = nc.dram_tensor("mixg", [4 * 256, SEQ], BF16).ap()
    yTb = nc.dram_tensor("yTb", [NTC * D_MODEL, 512], BF16).ap()
    xTg = nc.dram_tensor("xTg", [NTC * 4 * D_MODEL, 512], BF16).ap()
    xres_d = nc.dram_tensor("xres_d", [T, D_MODEL], F32).ap()
    S = Sched(nc)
    S.set_gran(mixb, 64 * SEQ)
    S.set_gran(mixg, 256 * SEQ)
    S.set_gran(yTb, D_MODEL * 512)
    S.set_gran(xTg, 4 * D_MODEL * 512)
    with ExitStack() as stc:
        _PFX[0] = ""
        C = _consts(S, nc, stc)
        gi = _sb(stc, nc, "gidx_sb", [128, 8], I32)
        S.sp.dma_start(out=gi[:], in_=gidx)
        mixtab = mixg.rearrange("f (n t) -> (f n) t", t=512)
        for l in range(L):
            _PFX[0] = "L%d_" % l
            xg, xeng = (x0g, S.pool) if l == 0 else (xTg, S.sp)

            def xsrc(tt, xg=xg, xeng=xeng):
                r, tc = divmod(tt, NTC)
                r0 = (tc * 4 + r) * D_MODEL
                return xeng, xg[r0:r0 + D_MODEL, :].rearrange("(c p) t -> p c t", p=128)

            def outf(r0, c0, c1):
                return mixb[r0:r0 + 64, c0:c1]

            def after(m):
                if exch:
                    cw = min(SEQ, 2048)
                    S.collective("AllGather", mixb[64 * m:64 * m + 64, :].rearrange("r (a b) -> (r a) b", b=cw),
                                 mixg[256 * m:256 * m + 256, :].rearrange("r (a b) -> (r a) b", b=cw), GROUPS)
            P = dict(P0)
            P.update(xsrc=xsrc, w_fm=w_fm[l], w_tm=w_tm[l], ropeA=ropeA, ropeC=ropeC, pvec=pvec[l], bvec=bvec[l],
                     sgu_wT=sgu_wT[l], outf=outf, after=after,
                     tile_order=[r * NTC + tc for tc in range(NTC) for r in range(4)])
            _mixer_all(S, nc, SEQ, P, C, do)
            S.barrier()
            last = (l == L - 1)

            def mix_gather(tile, i):
                for c in range(8):
                    m = c // 2
                    S.gather_rows(tile[:, c, :], mixtab, gi[:, c:c + 1], i * 512, dep_ap=mixg[256 * m:256 * m + 256, :])

            def yT(i):
                tc = i // 4
                return yTb[tc * D_MODEL:(tc + 1) * D_MODEL, :].rearrange("(c p) t -> p c t", p=128)[:, :, (i % 4) * 128:(i % 4 + 1) * 128]

            def after_tile(i):
                if i % 4 == 3 and exch:
                    tc = i // 4
                    S.collective("AllGather", yTb[tc * D_MODEL:(tc + 1) * D_MODEL, :],
                                 xTg[tc * 4 * D_MODEL:(tc + 1) * 4 * D_MODEL, :], GROUPS)
            PF = dict(mix_gather=mix_gather, xres=(xres0 if l == 0 else xres_d),
                      w_out=w_out[l], w_gate=w_gate[l], w_up=w_up[l], w_down=w_down[l], lnp=lnp[l],
                      y=(y if last else xres_d), yT=(None if last else yT), after_tile=after_tile)
            if ffn:
                _ffn_all(S, nc, T, PF, C["ident"])
            S.barrier()
        S.finish()
    return nc


_PROGS = {}
BATCH = 2
SEQ_FULL = 8192
NCORES = 8


def _fused_inputs(inp, SEQ, depth):
    T = SEQ // 4
    ropes = _rope_tables(SEQ)
    x = inp["x"]
    hm = [[prep_mixer_inputs(inp, l, h, SEQ, ropes) for l in range(depth)] for h in range(4)]
    w_out_p = inp["w_out"][:depth]
    lnp = np.stack([np.stack([inp["ln1_g"][l], inp["ln1_b"][l], inp["ln2_g"][l], inp["ln2_b"][l]]) for l in range(depth)])
    maps = []
    for core in range(NCORES):
        b, j = divmod(core, 4)
        xb = x[b, :SEQ]
        NTC = T // 512
        x0g = np.ascontiguousarray(xb.reshape(4, NTC, 512, D_MODEL).transpose(1, 0, 3, 2).reshape(NTC * 4 * D_MODEL, 512))
        gidx = ((np.arange(8)[None, :] * 128 + np.arange(128)[:, None]) * (SEQ // 512) + j * (T // 512)).astype(np.int32)
        m = {
            "x0g": x0g, "xres0": np.ascontiguousarray(xb[j * T:(j + 1) * T]),
            "w_fm": np.stack([hm[j][l]["w_fm"] for l in range(depth)]),
            "w_tm": np.stack([hm[j][l]["w_tm"] for l in range(depth)]),
            "ropeA": ropes[0], "ropeC": ropes[1],
            "pvec": np.stack([hm[j][l]["pvec"] for l in range(depth)]),
            "bvec": np.stack([hm[j][l]["bvec"] for l in range(depth)]),
            "sgu_wT": np.stack([hm[j][l]["sgu_wT"] for l in range(depth)]),
            "w_out": w_out_p, "w_gate": inp["w_gate"][:depth], "w_up": inp["w_up"][:depth], "w_down": inp["w_down"][:depth],
            "lnp": lnp.astype(np.float32), "gidx": gidx,
        }
        maps.append({k: np.ascontiguousarray(v) for k, v in m.items()})
    return maps


def run_fused(inp, SEQ, depth):
    key = ("fused", SEQ, depth)
    if key not in _PROGS:
        _PROGS[key] = build_fused(SEQ, depth)
    maps = _fused_inputs(inp, SEQ, depth)
    res = run_bass_kernel_spmd(_PROGS[key], maps, core_ids=list(range(NCORES)))
    T = SEQ // 4
    out = np.empty((BATCH, SEQ, D_MODEL), np.float32)
    for core in range(NCORES):
        b, j = divmod(core, 4)
        out[b, j * T:(j + 1) * T] = res.results[core]["y"]
    return out


def kernel(**inputs):
    inp = {k: np.asarray(v, dtype=np.float32) for k, v in inputs.items()}
    return run_fused(inp, SEQ_FULL, DEPTH)
```
